# Optimizing a Trainium2 kernel written in Bass

```python
import math
import jax, jax.numpy as jnp
from jax import lax
import numpy as np

D_MODEL = 1024
BATCH = 2
SEQ = 8192
DEPTH = 2

CTX_LEN = 256
GRID_W = 64

A_HEADS = 4
A_QK_DIM = 64
A_V_DIM = 2 * A_QK_DIM
A_WIDTH = A_HEADS * A_V_DIM
B_GROUPS = 4
B_GROUP_DIM = 64
B_WIDTH = B_GROUPS * B_GROUP_DIM
C_GROUPS = 4
C_GROUP_DIM = 64
C_WIDTH = C_GROUPS * C_GROUP_DIM
CHUNK = 128

N_BRANCH = 3
Q_BLOCK = 128
ROPE_BASE = 10000.0
AXIS_DIM = A_QK_DIM // 2
EPS = 1e-6

OFF_Q = 0
OFF_K = OFF_Q + A_WIDTH
OFF_V = OFF_K + A_WIDTH
OFF_F = OFF_V + A_WIDTH
OFF_U = OFF_F + B_WIDTH
OFF_VC = OFF_U + C_WIDTH
OFF_GATE = OFF_VC + C_WIDTH
GATE_WIDTH = A_WIDTH + B_WIDTH + C_WIDTH
OFF_MERGE = OFF_GATE + GATE_WIDTH
IN_WIDTH = OFF_MERGE + N_BRANCH * D_MODEL

kernel_name = "hybrid_diffattn_fnet_gmlp_prefix_dit"


def rms_norm(x, g):
    xf = x.astype(jnp.float32)
    y = xf * lax.rsqrt(jnp.mean(xf * xf, axis=-1, keepdims=True) + EPS)
    return (y * g.astype(jnp.float32)).astype(x.dtype)


def axial_rope_tables(n_tok):
    n_rows = n_tok // GRID_W
    row = jnp.repeat(jnp.arange(n_rows), GRID_W).astype(jnp.float32)
    col = jnp.tile(jnp.arange(GRID_W), n_rows).astype(jnp.float32)
    freqs = ROPE_BASE ** (-jnp.arange(0, AXIS_DIM, 2, dtype=jnp.float32) / AXIS_DIM)
    ang_r = row[:, None] * freqs
    ang_c = col[:, None] * freqs
    ang = jnp.concatenate([ang_r, ang_r, ang_c, ang_c], axis=-1)
    return jnp.cos(ang), jnp.sin(ang)


def rotate_half_axial(x):
    xs = x.reshape(x.shape[:-1] + (2, 2, AXIS_DIM // 2))
    x1, x2 = xs[..., 0, :], xs[..., 1, :]
    return jnp.stack([-x2, x1], axis=-2).reshape(x.shape)


def apply_rope(t, cos, sin):
    cos = cos[None, :, None, None, :].astype(t.dtype)
    sin = sin[None, :, None, None, :].astype(t.dtype)
    return t * cos + rotate_half_axial(t) * sin


def diff_attend(q, k, v, lam):
    s = jnp.einsum('bqhcd,bkhcd->bhcqk', q, k).astype(jnp.float32) * (A_QK_DIM ** -0.5)
    a = jax.nn.softmax(s, axis=-1)
    w = a[:, :, 0] - lam * a[:, :, 1]
    return jnp.einsum('bhqk,bkhe->bqhe', w.astype(v.dtype), v)


def blocked_latent_attention(q, k, v, lam):
    b, n = q.shape[:2]
    qb = q.reshape(b, n // Q_BLOCK, Q_BLOCK, A_HEADS, 2, A_QK_DIM).swapaxes(0, 1)
    ob = lax.map(lambda blk: diff_attend(blk, k, v, lam), qb)
    return ob.swapaxes(0, 1).reshape(b, n, A_HEADS, A_V_DIM)


def fourier_mix(f, w_f, b_f):
    b, n, _ = f.shape
    fg = f.reshape(b, n, B_GROUPS, B_GROUP_DIM).astype(jnp.float32)
    fr = jnp.fft.fft2(fg, axes=(1, 3), norm="ortho").real.astype(f.dtype)
    y = jnp.einsum('bngc,gcd->bngd', fr, w_f) + b_f
    return y.reshape(b, n, B_WIDTH)


def chunk_sgu(u, vc, ln_g, ln_b, w_s, b_s):
    b, n, _ = vc.shape
    vf = vc.astype(jnp.float32)
    mu = jnp.mean(vf, axis=-1, keepdims=True)
    var = jnp.mean(jnp.square(vf - mu), axis=-1, keepdims=True)
    vn = ((vf - mu) * lax.rsqrt(var + EPS) * ln_g.astype(jnp.float32)
          + ln_b.astype(jnp.float32)).astype(vc.dtype)
    vn = vn.reshape(b, n // CHUNK, CHUNK, C_GROUPS, C_GROUP_DIM)
    s = jnp.einsum('gpq,bnqgc->bnpgc', w_s, vn) + b_s.T[:, :, None]
    return u * s.reshape(b, n, C_WIDTH)


def merge_branches(p, o_a, w_f, b_f, ln_g, ln_b, w_s, b_s, w_br_a, w_br_b, w_br_c, w_out):
    o_b = fourier_mix(p[..., OFF_F:OFF_U], w_f, b_f)
    o_c = chunk_sgu(p[..., OFF_U:OFF_VC], p[..., OFF_VC:OFF_GATE], ln_g, ln_b, w_s, b_s)
    gate = jax.nn.silu(p[..., OFF_GATE:OFF_MERGE])
    ya = (o_a * gate[..., :A_WIDTH]) @ w_br_a
    yb = (o_b * gate[..., A_WIDTH:A_WIDTH + B_WIDTH]) @ w_br_b
    yc = (o_c * gate[..., A_WIDTH + B_WIDTH:]) @ w_br_c
    m = jax.nn.sigmoid(p[..., OFF_MERGE:])
    y = (m[..., :D_MODEL] * ya + m[..., D_MODEL:2 * D_MODEL] * yb
         + m[..., 2 * D_MODEL:] * yc)
    return y @ w_out


def setup_inputs(seed: int = 0) -> dict:
    key = jax.random.key(seed)
    ks = jax.random.split(key, 28)
    L, D = DEPTH, D_MODEL

    def nrm(k, shape, scale):
        return jax.random.normal(k, shape, jnp.float32) * scale

    return {
        "x": nrm(ks[0], (BATCH, SEQ, D), 1.0),
        "c": nrm(ks[1], (BATCH, D), 1.0),
        "ctx": nrm(ks[2], (BATCH, CTX_LEN, D), 1.0),
        "c_ctx": nrm(ks[3], (D,), 1.0),
        "w_ada": nrm(ks[4], (L, D, 3 * D), 0.5 * D ** -0.5),
        "b_ada": nrm(ks[5], (L, 3 * D), 0.02),
        "g_norm": 1.0 + nrm(ks[6], (L, D), 0.02),
        "w_in": nrm(ks[7], (L, D, IN_WIDTH), D ** -0.5),
        "g_q": 1.0 + nrm(ks[8], (L, A_QK_DIM), 0.02),
        "g_k": 1.0 + nrm(ks[9], (L, A_QK_DIM), 0.02),
        "lam_q1": nrm(ks[10], (L, A_QK_DIM), 0.1),
        "lam_k1": nrm(ks[11], (L, A_QK_DIM), 0.1),
        "lam_q2": nrm(ks[12], (L, A_QK_DIM), 0.1),
        "lam_k2": nrm(ks[13], (L, A_QK_DIM), 0.1),
        "g_sub": 1.0 + nrm(ks[14], (L, A_V_DIM), 0.02),
        "w_f": nrm(ks[15], (L, B_GROUPS, B_GROUP_DIM, B_GROUP_DIM), B_GROUP_DIM ** -0.5),
        "b_f": nrm(ks[16], (L, B_GROUPS, B_GROUP_DIM), 0.02),
        "ln_g": 1.0 + nrm(ks[17], (L, C_WIDTH), 0.02),
        "ln_b": nrm(ks[18], (L, C_WIDTH), 0.02),
        "w_s": nrm(ks[19], (L, C_GROUPS, CHUNK, CHUNK), CHUNK ** -0.5),
        "b_s": 1.0 + nrm(ks[20], (L, C_GROUPS, CHUNK), 0.02),
        "w_br_a": nrm(ks[21], (L, A_WIDTH, D), A_WIDTH ** -0.5),
        "w_br_b": nrm(ks[22], (L, B_WIDTH, D), B_WIDTH ** -0.5),
        "w_br_c": nrm(ks[23], (L, C_WIDTH, D), C_WIDTH ** -0.5),
        "w_out": nrm(ks[24], (L, D, D), D ** -0.5),
    }


def reference(x, c, ctx, c_ctx, w_ada, b_ada, g_norm, w_in, g_q, g_k,
              lam_q1, lam_k1, lam_q2, lam_k2, g_sub, w_f, b_f, ln_g, ln_b,
              w_s, b_s, w_br_a, w_br_b, w_br_c, w_out):
    b, n, _ = x.shape
    n_ctx = ctx.shape[1]
    cos, sin = axial_rope_tables(n)
    for l in range(DEPTH):
        last = l == DEPTH - 1
        lam_init = 0.8 - 0.6 * math.exp(-0.3 * l)
        lam = (jnp.exp(jnp.sum(lam_q1[l].astype(jnp.float32) * lam_k1[l].astype(jnp.float32)))
               - jnp.exp(jnp.sum(lam_q2[l].astype(jnp.float32) * lam_k2[l].astype(jnp.float32)))
               + lam_init)

        shift, scale, gate = jnp.split(jax.nn.silu(c) @ w_ada[l] + b_ada[l], 3, axis=-1)
        shift_c, scale_c, gate_c = jnp.split(jax.nn.silu(c_ctx) @ w_ada[l] + b_ada[l], 3, axis=-1)
        h = rms_norm(x, g_norm[l]) * (1.0 + scale[:, None, :]) + shift[:, None, :]
        h_ctx = rms_norm(ctx, g_norm[l]) * (1.0 + scale_c) + shift_c

        if last:
            kv_ctx = h_ctx @ w_in[l][:, OFF_K:OFF_F]
        else:
            p_ctx = h_ctx @ w_in[l]
            kv_ctx = p_ctx[..., OFF_K:OFF_F]
        k_ctx = rms_norm(kv_ctx[..., :A_WIDTH].reshape(b, n_ctx, A_HEADS, 2, A_QK_DIM), g_k[l])
        v_ctx = kv_ctx[..., A_WIDTH:].reshape(b, n_ctx, A_HEADS, A_V_DIM)

        p = h @ w_in[l]
        q = apply_rope(rms_norm(p[..., OFF_Q:OFF_K].reshape(b, n, A_HEADS, 2, A_QK_DIM), g_q[l]), cos, sin)
        k = apply_rope(rms_norm(p[..., OFF_K:OFF_V].reshape(b, n, A_HEADS, 2, A_QK_DIM), g_k[l]), cos, sin)
        v = p[..., OFF_V:OFF_F].reshape(b, n, A_HEADS, A_V_DIM)
        k_all = jnp.concatenate([k, k_ctx], axis=1)
        v_all = jnp.concatenate([v, v_ctx], axis=1)
        o_a = blocked_latent_attention(q, k_all, v_all, lam)
        o_a = (rms_norm(o_a, g_sub[l]) * (1.0 - lam_init)).reshape(b, n, A_WIDTH)
        out = merge_branches(p, o_a, w_f[l], b_f[l], ln_g[l], ln_b[l], w_s[l], b_s[l],
                             w_br_a[l], w_br_b[l], w_br_c[l], w_out[l])

        if not last:
            q_c = rms_norm(p_ctx[..., OFF_Q:OFF_K].reshape(b, n_ctx, A_HEADS, 2, A_QK_DIM), g_q[l])
            o_ac = diff_attend(q_c, k_ctx, v_ctx, lam)
            o_ac = (rms_norm(o_ac, g_sub[l]) * (1.0 - lam_init)).reshape(b, n_ctx, A_WIDTH)
            out_c = merge_branches(p_ctx, o_ac, w_f[l], b_f[l], ln_g[l], ln_b[l], w_s[l], b_s[l],
                                   w_br_a[l], w_br_b[l], w_br_c[l], w_out[l])
            ctx = ctx + gate_c * out_c

        x = x + gate[:, None, :] * out
    return x
```

```python
import numpy as np
import concourse.bass as bass
import concourse.mybir as mybir
from concourse.bass_utils import run_bass_kernel_spmd

F32 = mybir.dt.float32
BF16 = mybir.dt.bfloat16
AF = mybir.ActivationFunctionType
ALU = mybir.AluOpType
AX = mybir.AxisListType


class Dep:
    __slots__ = ("w", "r", "name")

    def __init__(self, name=""):
        self.w = None
        self.r = []
        self.name = name


class KB:
    COMPUTE = ("pe", "act", "dve", "pool")

    def __init__(self):
        self.nc = bass.Bass("TRN2", target_bir_lowering=False)
        nc = self.nc
        self.eng = {"pe": nc.tensor, "act": nc.scalar, "dve": nc.vector,
                    "pool": nc.gpsimd, "sp": nc.sync}
        self.sems = {}
        self.cnt = {}
        for e in self.COMPUTE:
            self.sems[e] = nc.alloc_semaphore(name="s_" + e)
            self.cnt[e] = 0
        self.seen = {e: {} for e in self.eng}
        self.n_inst = 0

    def _sem(self, key):
        if key not in self.sems:
            self.sems[key] = self.nc.alloc_semaphore(name="d_" + str(key))
            self.cnt[key] = 0
        return self.sems[key]

    def _waits(self, e, reads, writes):
        need = {}

        def add(t, war=False):
            if t is None:
                return
            sk, v = t
            if sk == e and (war or e == "pe"):
                return
            if need.get(sk, 0) < v:
                need[sk] = v
        for d in reads:
            add(d.w)
        for d in writes:
            add(d.w)
            for t in d.r:
                add(t, war=True)
        E = self.eng[e]
        for sk, v in need.items():
            if self.seen[e].get(sk, 0) >= v:
                continue
            E.wait_ge(self.sems[sk], v)
            self.seen[e][sk] = v

    def _mark(self, tok, reads, writes):
        for d in reads:
            d.r.append(tok)
            if len(d.r) > 64:
                m = {}
                for sk, v in d.r:
                    if m.get(sk, 0) < v:
                        m[sk] = v
                d.r = list(m.items())
        for d in writes:
            d.w = tok
            d.r = []

    def op(self, e, fn, reads=(), writes=()):
        self._waits(e, reads, writes)
        inst = fn(self.eng[e])
        self.cnt[e] += 1
        inst.then_inc(self.sems[e], 1)
        self._mark((e, self.cnt[e]), reads, writes)
        self.n_inst += 1
        return inst

    def mm(self, fn, reads=(), writes=(), last=True):
        return self.op("pe", fn, reads, writes)

    def dma(self, q, key, out, in_, reads=(), writes=(), **kw):
        sem = self._sem(key)
        self._waits(q, reads, writes)
        inst = self.eng[q].dma_start(out=out, in_=in_, **kw)
        self.cnt[key] += 16
        inst.then_inc(sem, 16)
        self._mark((key, self.cnt[key]), reads, writes)
        self.n_inst += 1
        return inst

    def barrier(self):
        for e, E in self.eng.items():
            for sk, sem in self.sems.items():
                v = self.cnt[sk]
                if v == 0 or self.seen[e].get(sk, 0) >= v:
                    continue
                E.wait_ge(sem, v)
                self.seen[e][sk] = v

    def wait_all(self, e, deps):
        self._waits(e, deps, ())

    def run(self, in_maps, n=8, trace=False):
        return run_bass_kernel_spmd(self.nc, in_maps, core_ids=list(range(n)), trace=trace)


import os
STOP = int(os.environ.get('STOP', '99'))

NT = 18
NLAT = 16
TOK = NT * 128
EPS = 1e-6


def build_a(kb, dbg=False):
    nc = kb.nc
    dt = nc.dram_tensor
    x = dt("x_tok", [TOK, 1024], F32, kind="ExternalInput").ap()
    cvec = dt("cvec", [128, 8, 2], F32, kind="ExternalInput").ap()
    w_ada = dt("w_ada", [1024, 3072], F32, kind="ExternalInput").ap()
    b_ada = dt("b_ada", [128, 24], F32, kind="ExternalInput").ap()
    g_norm = dt("g_norm", [128, 8], F32, kind="ExternalInput").ap()
    w_in = dt("w_in_a", [1024, 1792], F32, kind="ExternalInput").ap()
    g_q = dt("g_q", [64], F32, kind="ExternalInput").ap()
    g_k = dt("g_k", [64], F32, kind="ExternalInput").ap()
    cos = dt("cos", [NLAT * 128, 64], F32, kind="ExternalInput").ap()
    ssin = dt("ssin", [NLAT * 128, 64], F32, kind="ExternalInput").ap()
    qT_o = dt("qT", [4, 128, TOK], BF16, kind="ExternalOutput").ap()
    kT_o = dt("kT", [4, 128, TOK], BF16, kind="ExternalOutput").ap()
    v_o = dt("v", [TOK, 512], BF16, kind="ExternalOutput").ap()
    f_o = dt("f", [TOK, 256], BF16, kind="ExternalOutput").ap()
    hT_o = dt("hT", [8, 128, TOK], BF16, kind="ExternalOutput").ap()
    mod_o = dt("modT", [128, 24, 2], F32, kind="ExternalOutput").ap()

    sb = nc.alloc_sbuf_tensor
    ident = sb("ident", [128, 128], F32)
    identb = sb("identb", [128, 128], BF16)
    cT = sb("cT", [128, 8, 2], F32)
    sg = sb("sg", [128, 8, 2], F32)
    sc = sb("sc", [128, 8, 2], F32)
    bT = sb("bT", [128, 24], F32)
    gn = sb("gn", [128, 8], F32)
    modT = sb("modT_s", [128, 24, 2], F32)
    Aff = sb("Aff", [128, 8, 2], F32)
    wst = [sb(f"wst{i}", [128, 8, 512], F32) for i in range(2)]
    wA = sb("wA", [128, 8, 1792], BF16)
    GQ = sb("GQ", [128, 64], F32)
    GK = sb("GK", [128, 64], F32)
    xt = [sb(f"xt{i}", [128, 1024], F32) for i in range(2)]
    junk = sb("junk", [128, 1024], F32)
    ss = sb("ss", [128, 1], F32)
    rstd = sb("rstd", [128, 1], F32)
    hTt = [sb(f"hTt{i}", [128, 8, 128], BF16) for i in range(2)]
    cs = [sb(f"cs{i}", [128, 2, 64], F32) for i in range(2)]
    sq = sb("sq", [128, 512], F32)
    ss8 = sb("ss8", [128, 8], F32)
    qn = sb("qn", [128, 512], F32)
    t1 = sb("t1", [128, 512], F32)
    t2 = sb("t2", [128, 512], F32)
    qr = [sb(f"qr{i}", [128, 512], BF16) for i in range(2)]
    qTt = [sb(f"qTt{i}", [128, 4, 128], BF16) for i in range(2)]
    kTt = [sb(f"kTt{i}", [128, 4, 128], BF16) for i in range(2)]
    vt = [sb(f"vt{i}", [128, 512], BF16) for i in range(2)]
    ft = [sb(f"ft{i}", [128, 256], BF16) for i in range(2)]
    ps = [nc.alloc_psum_tensor(f"ps{i}", [128, 512], F32) for i in range(6)]
    ps += [nc.alloc_psum_tensor(f"ps{i}", [128, 1024], BF16) for i in (6, 7)]
    psb = ps

    D = lambda n: Dep(n)
    d_ident, d_identb, d_cT, d_sg, d_sc, d_bT, d_gn, d_modT, d_Aff = [D(n) for n in "ident identb cT sg sc bT gn modT Aff".split()]
    d_wst = [D("wst0"), D("wst1")]
    d_wA = D("wA"); d_G = D("G")
    d_xt = [D("xt0"), D("xt1")]; d_junk = D("junk"); d_ss = D("ss"); d_rstd = D("rstd")
    d_hTt = [D("hTt0"), D("hTt1")]; d_cs = [D("cs0"), D("cs1")]
    d_sq, d_ss8, d_qn, d_t1, d_t2 = D("sq"), D("ss8"), D("qn"), D("t1"), D("t2")
    d_qr = [D("qr0"), D("qr1")]; d_qTt = [D("qTt0"), D("qTt1")]; d_kTt = [D("kTt0"), D("kTt1")]
    d_vt = [D("vt0"), D("vt1")]; d_ft = [D("ft0"), D("ft1")]
    d_ps = [D(f"ps{i}") for i in range(8)]
    d_out = D("out")
    out_keys = set()

    def store(key, out, in_, reads):
        out_keys.add(key)
        kb.dma("pool", key, out, in_, reads=reads, writes=[])

    kb.op("pool", lambda E: E.memset(ident[:], 0.0), writes=[d_ident])
    kb.op("pool", lambda E: E.affine_select(out=ident[:], in_=ident[:], pattern=[[-1, 128]], compare_op=ALU.not_equal,
                                            fill=1.0, base=0, channel_multiplier=1), reads=[d_ident], writes=[d_ident])
    kb.op("pool", lambda E: E.tensor_copy(out=identb[:], in_=ident[:]), reads=[d_ident], writes=[d_identb])
    kb.dma("sp", "ld_c0", cT[:], cvec, writes=[d_cT])
    kb.dma("sp", "ld_c1", bT[:], b_ada, writes=[d_bT])
    kb.dma("sp", "ld_c2", gn[:], g_norm, writes=[d_gn])
    kb.dma("sp", "ld_c3", GQ[:], g_q.partition_broadcast(128), writes=[d_G])
    kb.dma("sp", "ld_c3", GK[:], g_k.partition_broadcast(128), writes=[d_G])
    kb.op("act", lambda E: E.activation(out=sg[:], in_=cT[:], func=AF.Sigmoid), reads=[d_cT], writes=[d_sg])
    kb.op("dve", lambda E: E.tensor_tensor(out=sc[:], in0=cT[:], in1=sg[:], op=ALU.mult), reads=[d_cT, d_sg], writes=[d_sc])
    w_ada_v = w_ada.rearrange("(k p) n -> p k n", p=128)
    for g in range(6):
        s = g % 2
        kb.dma("sp", f"ld_w{s}", wst[s][:], w_ada_v[:, :, g * 512:(g + 1) * 512], writes=[d_wst[s]])
        for jj in range(4):
            j = g * 4 + jj
            for k in range(8):
                kb.op("pe", lambda E, k=k, jj=jj, s=s, j=j: E.matmul(ps[0][:, 2 * j:2 * j + 2], lhsT=wst[s][:, k, jj * 128:(jj + 1) * 128],
                                                                    rhs=sc[:, k, :], start=(k == 0), stop=(k == 7)),
                      reads=[d_wst[s], d_sc], writes=[d_ps[0]])
    kb.op("dve", lambda E: E.tensor_tensor(out=modT[:], in0=ps[0][:, 0:48].rearrange("p (j n) -> p j n", n=2),
                                           in1=bT[:].unsqueeze(2).to_broadcast([128, 24, 2]), op=ALU.add),
          reads=[d_ps[0], d_bT], writes=[d_modT])
    store("st_mod", mod_o, modT[:], [d_modT])
    kb.op("dve", lambda E: E.tensor_scalar(out=Aff[:], in0=modT[:, 8:16, :], scalar1=1.0, scalar2=None, op0=ALU.add), reads=[d_modT], writes=[d_Aff])
    kb.op("dve", lambda E: E.tensor_tensor(out=Aff[:], in0=Aff[:], in1=gn[:].unsqueeze(2).to_broadcast([128, 8, 2]), op=ALU.mult),
          reads=[d_Aff, d_gn], writes=[d_Aff])
    def fin():
        for k in sorted(out_keys):
            nc.gpsimd.wait_ge(kb.sems[k], kb.cnt[k])
        return kb
    if STOP == 1:
        return fin()
    w_in_v = w_in.rearrange("(k p) n -> p k n", p=128)
    for g in range(4):
        s = g % 2
        n = 512 if g < 3 else 256
        kb.dma("sp", f"ld_w{s}", wst[s][:, :, 0:n], w_in_v[:, :, g * 512:g * 512 + n], reads=[], writes=[d_wst[s]])
        e = "pool" if g % 2 == 0 else "dve"
        kb.op(e, lambda E, s=s, n=n, g=g: E.tensor_copy(out=wA[:, :, g * 512:g * 512 + n], in_=wst[s][:, :, 0:n]), reads=[d_wst[s]], writes=[d_wA])

    def load_tile(i):
        s = i % 2
        kb.dma("sp", f"ld_x{s}", xt[s][:], x[i * 128:(i + 1) * 128, :], writes=[d_xt[s]])
        if i < NLAT:
            kb.dma("sp", f"ld_cs{s}", cs[s][:, 0, :], cos[i * 128:(i + 1) * 128, :], writes=[d_cs[s]])
            kb.dma("sp", f"ld_cs{s}", cs[s][:, 1, :], ssin[i * 128:(i + 1) * 128, :], writes=[d_cs[s]])

    def rstd_chain(ssrc, dsrc, dst, ddst, scale):
        kb.op("dve", lambda E: E.tensor_scalar(out=dst, in0=ssrc, scalar1=scale, scalar2=EPS, op0=ALU.mult, op1=ALU.add), reads=[dsrc], writes=[ddst])
        kb.op("act", lambda E: E.activation(out=dst, in_=dst, func=AF.Sqrt), reads=[ddst], writes=[ddst])
        kb.op("dve", lambda E: E.reciprocal(out=dst, in_=dst), reads=[ddst], writes=[ddst])

    load_tile(0)
    for i in range(NT):
        s = i % 2
        var = 0 if i < NLAT else 1
        if i + 1 < NT:
            load_tile(i + 1)
        X = xt[s]
        kb.op("act", lambda E: E.activation(out=junk[:], in_=X[:], func=AF.Square, accum_out=ss[:]), reads=[d_xt[s]], writes=[d_junk, d_ss])
        rstd_chain(ss[:], d_ss, rstd[:], d_rstd, 1.0 / 1024)
        kb.op("dve", lambda E: E.tensor_scalar(out=X[:], in0=X[:], scalar1=rstd[:, 0:1], scalar2=None, op0=ALU.mult),
              reads=[d_xt[s], d_rstd], writes=[d_xt[s]])
        for k in range(8):
            b = k // 4
            kb.op("pe", lambda E, k=k, b=b: E.transpose(out=ps[b][:, (k % 4) * 128:(k % 4 + 1) * 128], in_=X[:, k * 128:(k + 1) * 128], identity=ident[:]),
                  reads=[d_xt[s], d_ident], writes=[d_ps[b]])
        for k in range(8):
            b = k // 4
            src = ps[b][:, (k % 4) * 128:(k % 4 + 1) * 128]
            if k % 2 == 0:
                kb.op("act", lambda E, k=k, src=src: E.activation(out=hTt[s][:, k, :], in_=src, func=AF.Identity,
                                                                  scale=Aff[:, k, var:var + 1], bias=modT[:, k, var:var + 1]),
                      reads=[d_ps[b], d_Aff, d_modT], writes=[d_hTt[s]])
            else:
                kb.op("dve", lambda E, k=k, src=src: E.tensor_scalar(out=hTt[s][:, k, :], in0=src, scalar1=Aff[:, k, var:var + 1],
                                                                     scalar2=modT[:, k, var:var + 1], op0=ALU.mult, op1=ALU.add),
                      reads=[d_ps[b], d_Aff, d_modT], writes=[d_hTt[s]])
        store(f"st_h{s}", hT_o[:, :, i * 128:(i + 1) * 128].rearrange("k p t -> p k t"), hTt[s][:], [d_hTt[s]])
        if STOP == 2:
            return fin()
        for cb in range(4):
            n = 512 if cb < 3 else 256
            for k in range(8):
                kb.op("pe", lambda E, k=k, cb=cb, n=n: E.matmul(ps[2 + cb][:, 0:n], lhsT=hTt[s][:, k, :], rhs=wA[:, k, cb * 512:cb * 512 + n],
                                                               start=(k == 0), stop=(k == 7)),
                      reads=[d_hTt[s], d_wA], writes=[d_ps[2 + cb]])
        for which in range(2 if STOP != 3 else 0):
            P = ps[2 + which]; dP = d_ps[2 + which]
            G = GQ if which == 0 else GK
            kb.op("act", lambda E: E.activation(out=sq[:], in_=P[:], func=AF.Square), reads=[dP], writes=[d_sq])
            kb.op("dve", lambda E: E.tensor_reduce(out=ss8[:], in_=sq[:].rearrange("p (g c) -> p g c", c=64), axis=AX.X, op=ALU.add),
                  reads=[d_sq], writes=[d_ss8])
            rstd_chain(ss8[:], d_ss8, ss8[:], d_ss8, 1.0 / 64)
            kb.op("dve", lambda E: E.tensor_tensor(out=qn[:].rearrange("p (g c) -> p g c", c=64), in0=P[:].rearrange("p (g c) -> p g c", c=64),
                                                   in1=ss8[:].unsqueeze(2).to_broadcast([128, 8, 64]), op=ALU.mult),
                  reads=[dP, d_ss8], writes=[d_qn])
            QR = qr[which]; dQR = d_qr[which]
            if i < NLAT:
                kb.op("dve", lambda E: E.tensor_tensor(out=qn[:].rearrange("p (g c) -> p g c", c=64), in0=qn[:].rearrange("p (g c) -> p g c", c=64),
                                                        in1=G[:].unsqueeze(1).to_broadcast([128, 8, 64]), op=ALU.mult),
                      reads=[d_qn, d_G], writes=[d_qn])
                kb.op("dve", lambda E: E.tensor_tensor(out=t1[:].rearrange("p (g c) -> p g c", c=64), in0=qn[:].rearrange("p (g c) -> p g c", c=64),
                                                       in1=cs[s][:, 0, :].unsqueeze(1).to_broadcast([128, 8, 64]), op=ALU.mult),
                      reads=[d_qn, d_cs[s]], writes=[d_t1])
                qv = qn[:].rearrange("p (g a h c) -> p g a h c", a=2, h=2, c=16)
                tv = t2[:].rearrange("p (g a h c) -> p g a h c", a=2, h=2, c=16)
                sv = cs[s][:, 1, :].rearrange("p (a h c) -> p a h c", a=2, h=2)
                for hf in range(2):
                    for a in range(2):
                        kb.op("dve", lambda E, hf=hf, a=a: E.tensor_tensor(out=tv[:, :, a, hf, :], in0=qv[:, :, a, 1 - hf, :],
                                                                in1=sv[:, a, hf, :].unsqueeze(1).to_broadcast([128, 8, 16]), op=ALU.mult),
                              reads=[d_qn, d_cs[s]], writes=[d_t2])
                kb.op("dve", lambda E: E.tensor_tensor(out=QR[:], in0=t1[:], in1=t2[:], op=ALU.add), reads=[d_t1, d_t2], writes=[dQR])
            else:
                kb.op("dve", lambda E: E.tensor_tensor(out=QR[:].rearrange("p (g c) -> p g c", c=64), in0=qn[:].rearrange("p (g c) -> p g c", c=64),
                                                       in1=G[:].unsqueeze(1).to_broadcast([128, 8, 64]), op=ALU.mult),
                      reads=[d_qn, d_G], writes=[dQR])
            PT = psb[6 + which]; dPT = d_ps[6 + which]
            for hh in range(4):
                kb.op("pe", lambda E, hh=hh: E.transpose(out=PT[:, hh * 128:(hh + 1) * 128], in_=QR[:, hh * 128:(hh + 1) * 128], identity=identb[:]),
                      reads=[dQR, d_identb], writes=[dPT])
            TT = (qTt if which == 0 else kTt)[s]; dTT = (d_qTt if which == 0 else d_kTt)[s]
            kb.op("act", lambda E: E.activation(out=TT[:].rearrange("p h t -> p (h t)"), in_=PT[:, 0:512], func=AF.Copy), reads=[dPT], writes=[dTT])
            dst = (qT_o if which == 0 else kT_o)
            store(f"st_{'qk'[which]}{s}", dst[:, :, i * 128:(i + 1) * 128].rearrange("h p t -> p h t"), TT[:], [dTT])
        kb.op("act", lambda E: E.activation(out=vt[s][:], in_=ps[4][:], func=AF.Copy), reads=[d_ps[4]], writes=[d_vt[s]])
        store(f"st_v{s}", v_o[i * 128:(i + 1) * 128, :], vt[s][:], [d_vt[s]])
        kb.op("pool" if False else "act", lambda E: E.activation(out=ft[s][:], in_=ps[5][:, 0:256], func=AF.Copy), reads=[d_ps[5]], writes=[d_ft[s]])
        store(f"st_f{s}", f_o[i * 128:(i + 1) * 128, :], ft[s][:], [d_ft[s]])
        if STOP in (3, 4):
            return fin()

    for k in sorted(out_keys):
        nc.gpsimd.wait_ge(kb.sems[k], kb.cnt[k])
    return kb


import math, os
from contextlib import ExitStack

NKT = 66
NKEY = NKT * 128
TOKB = 2304
EPS = 1e-6
SKIP_ATT = int(os.environ.get("SKIP_ATT", "0"))
SKIP_FFT = int(os.environ.get("SKIP_FFT", "0"))
NQB = int(os.environ.get("NQB", "5"))


def build_b(kb, lam_init):
    nc = kb.nc
    dt = nc.dram_tensor
    qT = dt("qT", [4, 128, TOKB], BF16, kind="ExternalInput").ap()
    kT = dt("kT_all", [4, 128, NKEY], BF16, kind="ExternalInput").ap()
    v_all = dt("v_all", [NKEY, 512], BF16, kind="ExternalInput").ap()
    lamv = dt("lamv", [4, 64], F32, kind="ExternalInput").ap()
    g_sub = dt("g_sub", [128, 1], F32, kind="ExternalInput").ap()
    f_g = dt("f_g", [NKEY, 64], BF16, kind="ExternalInput").ap()
    w_f = dt("w_f", [64, 64], F32, kind="ExternalInput").ap()
    b_f = dt("b_f", [64, 1], F32, kind="ExternalInput").ap()
    cs64 = dt("cs64", [64, 128], BF16, kind="ExternalInput").ap()
    T1d = dt("T1", [128, 64, 256], BF16, kind="ExternalInput").ap()
    T2d = dt("T2", [128, 64, 256], BF16, kind="ExternalInput").ap()
    dcd = dt("dctx", [128, 2, 512], BF16, kind="ExternalInput").ap()
    ccs = dt("ccs", [64, 2, 64], F32, kind="ExternalInput").ap()
    oaT_o = dt("oaT", [4, 128, TOKB], F32, kind="ExternalOutput").ap()
    obT_o = dt("obT", [64, NKEY], F32, kind="ExternalOutput").ap()

    ps = [nc.alloc_psum_tensor(f"ps{i}", [128, 512], F32) for i in range(8)]
    d_ps = [Dep(f"ps{i}") for i in range(8)]
    out_keys = set()

    def store(key, out, in_, reads):
        out_keys.add(key)
        kb.dma("pool", key, out, in_, reads=reads, writes=[])

    if not SKIP_FFT:
      with ExitStack() as es:
        sbt = lambda name, shape, dty: es.enter_context(nc.sbuf_tensor(name, shape, dty))
        X1 = sbt("X1", [64, 8192], BF16)
        Xc = sbt("Xc", [128, 2, 64], BF16)
        CS = sbt("CS", [64, 128], BF16)
        T1 = sbt("T1s", [128, 64, 256], BF16)
        T2 = sbt("T2s", [128, 64, 256], BF16)
        DC = sbt("DC", [128, 2, 512], BF16)
        Bsb = sbt("Bsb", [128, 2, 64, 64], BF16)
        YT = sbt("YT", [64, 2, 8192], BF16)
        YC = sbt("YC", [64, 2, 256], BF16)
        CC = sbt("CC", [64, 2, 64], F32)
        WF = sbt("WF", [64, 64], F32)
        BF = sbt("BF", [64, 1], F32)
        MC = sbt("MC", [64, 2, 64], BF16)
        ob = [sbt(f"ob{i}", [64, 512], F32) for i in range(2)]
        dn = {n: Dep(n) for n in "X1 Xc CS T1 T2 DC Bsb YT YC CC WF BF MC ob0 ob1".split()}
        kb.dma("sp", "f_x1", X1[:], f_g[0:8192, :].rearrange("(t p) c -> t (p c)", p=128), writes=[dn["X1"]])
        kb.dma("sp", "f_xc", Xc[:], f_g[8192:8448, :].rearrange("(k p) c -> p k c", p=128), writes=[dn["Xc"]])
        kb.dma("sp", "f_cs", CS[:], cs64, writes=[dn["CS"]])
        kb.dma("sp", "f_cc", CC[:], ccs, writes=[dn["CC"]])
        kb.dma("sp", "f_wf", WF[:], w_f, writes=[dn["WF"]])
        kb.dma("sp", "f_bf", BF[:], b_f, writes=[dn["BF"]])
        kb.dma("sp", "f_t1", T1[:], T1d, writes=[dn["T1"]])
        kb.dma("sp", "f_t2", T2[:], T2d, writes=[dn["T2"]])
        kb.dma("sp", "f_dc", DC[:], dcd, writes=[dn["DC"]])
        for i in range(2):
            kb.op("pe", lambda E, i=i: E.matmul(ps[0][0:64, i * 64:(i + 1) * 64], lhsT=CC[:, i, :], rhs=WF[:], start=True, stop=True),
                  reads=[dn["CC"], dn["WF"]], writes=[d_ps[0]])
        kb.op("dve", lambda E: E.tensor_copy(out=MC[:].rearrange("c i d -> c (i d)"), in_=ps[0][0:64, 0:128]), reads=[d_ps[0]], writes=[dn["MC"]])
        X1v = X1[:].rearrange("t (p c) -> t p c", c=64)
        Bv = Bsb[:].rearrange("p r k c -> p c r k")
        for c4 in range(16):
            b = 1 + c4 % 2
            for cc in range(4):
                c = c4 * 4 + cc
                kb.op("pe", lambda E, c=c, cc=cc, b=b: E.matmul(ps[b][:, cc * 128:(cc + 1) * 128], lhsT=X1v[:, :, c], rhs=CS[:], start=True, stop=True),
                      reads=[dn["X1"], dn["CS"]], writes=[d_ps[b]])
            src = ps[b][:].rearrange("p (c r k) -> p c r k", c=4, r=2)
            dst = Bv[:, c4 * 4:(c4 + 1) * 4, :, :]
            for r in range(2):
                e = "act" if r == 0 else "dve"
                if e == "act":
                    kb.op("act", lambda E, src=src, dst=dst, r=r: E.activation(out=dst[:, :, r, :], in_=src[:, :, r, :], func=AF.Copy), reads=[d_ps[b]], writes=[dn["Bsb"]])
                else:
                    kb.op("dve", lambda E, src=src, dst=dst, r=r: E.tensor_copy(out=dst[:, :, r, :], in_=src[:, :, r, :]), reads=[d_ps[b]], writes=[dn["Bsb"]])
        for kb2 in range(32):
            b = 3 + kb2 % 2
            for ki in range(2):
                kbi = kb2 * 2 + ki
                o = ps[b][0:64, ki * 256:(ki + 1) * 256]
                kb.op("pe", lambda E, kbi=kbi, o=o: E.matmul(o, lhsT=Bsb[:, 0, kbi, :], rhs=T1[:, kbi, :], start=True, stop=False),
                      reads=[dn["Bsb"], dn["T1"]], writes=[d_ps[b]])
                kb.op("pe", lambda E, kbi=kbi, o=o: E.matmul(o, lhsT=Bsb[:, 1, kbi, :], rhs=T2[:, kbi, :], start=False, stop=True),
                      reads=[dn["Bsb"], dn["T2"]], writes=[d_ps[b]])
            src = ps[b][0:64, :].rearrange("c (i r a) -> c i r a", i=2, r=2)
            dst = YT[:].rearrange("c r (a k) -> c k r a", k=64)[:, kb2 * 2:kb2 * 2 + 2, :, :]
            for ki in range(2):
                if ki == 0:
                    kb.op("act", lambda E, src=src, dst=dst, ki=ki: E.activation(out=dst[:, ki, :, :], in_=src[:, ki, :, :], func=AF.Copy), reads=[d_ps[b]], writes=[dn["YT"]])
                else:
                    kb.op("dve", lambda E, src=src, dst=dst, ki=ki: E.tensor_copy(out=dst[:, ki, :, :], in_=src[:, ki, :, :]), reads=[d_ps[b]], writes=[dn["YT"]])
        for k in range(2):
            kb.op("pe", lambda E, k=k: E.matmul(ps[5][0:64, :], lhsT=Xc[:, k, :], rhs=DC[:, k, :], start=(k == 0), stop=(k == 1)),
                  reads=[dn["Xc"], dn["DC"]], writes=[d_ps[5]])
        kb.op("dve", lambda E: E.tensor_copy(out=YC[:].rearrange("c r k -> c (r k)"), in_=ps[5][0:64, :]), reads=[d_ps[5]], writes=[dn["YC"]])
        sc_lat = 1.0 / math.sqrt(8192 * 64); sc_ctx = 1.0 / math.sqrt(256 * 64)
        for blk in range(17):
            b = 6 + blk % 2
            s = blk % 2
            if blk < 16:
                n = 512; r0 = YT[:, 0, blk * 512:(blk + 1) * 512]; r1 = YT[:, 1, blk * 512:(blk + 1) * 512]; scl = sc_lat; dy = dn["YT"]
            else:
                n = 256; r0 = YC[:, 0, :]; r1 = YC[:, 1, :]; scl = sc_ctx; dy = dn["YC"]
            kb.op("pe", lambda E, r0=r0, n=n, b=b: E.matmul(ps[b][0:64, 0:n], lhsT=MC[:, 0, :], rhs=r0, start=True, stop=False), reads=[dn["MC"], dy], writes=[d_ps[b]])
            kb.op("pe", lambda E, r1=r1, n=n, b=b: E.matmul(ps[b][0:64, 0:n], lhsT=MC[:, 1, :], rhs=r1, start=False, stop=True), reads=[dn["MC"], dy], writes=[d_ps[b]])
            kb.op("act", lambda E, n=n, b=b, s=s, scl=scl: E.activation(out=ob[s][:, 0:n], in_=ps[b][0:64, 0:n], func=AF.Identity, scale=scl, bias=BF[:, 0:1]),
                  reads=[d_ps[b], dn["BF"]], writes=[dn[f"ob{s}"]])
            store(f"st_ob{s}", obT_o[:, blk * 512:blk * 512 + n], ob[s][:, 0:n], [dn[f"ob{s}"]])
        kb.barrier()

    if not SKIP_ATT:
      with ExitStack() as es:
        sbt = lambda name, shape, dty: es.enter_context(nc.sbuf_tensor(name, shape, dty))
        kTh = [sbt(f"kTh{i}", [128, NKEY], BF16) for i in range(2)]
        vh = [sbt(f"vh{i}", [128, NKT, 128], BF16) for i in range(2)]
        qTh = [sbt(f"qTh{i}", [128, TOKB], BF16) for i in range(2)]
        NPT = 3
        pt = [[sbt(f"pt{i}_{c}", [128, 512], BF16) for c in range(2)] for i in range(NPT)]
        acc = [sbt(f"acc{c}", [128, 512], F32) for c in range(2)]
        ones = sbt("ones", [128, 128], F32)
        lv = sbt("lv", [128, 4, 64], F32)
        lt = sbt("lt", [128, 2, 64], F32)
        l2 = sbt("l2", [128, 2], F32)
        nlam = sbt("nlam", [128, 1], F32)
        gs = sbt("gs", [128, 1], F32)
        rec = [sbt(f"rec{c}", [128, 512], F32) for c in range(2)]
        o0 = sbt("o0", [128, 512], F32)
        o1 = sbt("o1", [128, 512], F32)
        osq = sbt("osq", [128, 512], F32)
        rs = sbt("rs", [128, 512], F32)
        of = [sbt(f"of{i}", [128, 512], F32) for i in range(2)]
        d_kv = [Dep("kv0"), Dep("kv1")]
        d_pt = [[Dep(f"pt{i}_{c}") for c in range(2)] for i in range(NPT)]
        d_acc = [Dep("acc0"), Dep("acc1")]
        dm = {n: Dep(n) for n in "ones lv lt l2 nlam gs rec0 rec1 o0 o1 osq rs of0 of1".split()}
        kb.op("pool", lambda E: E.memset(ones[:], 1.0), writes=[dm["ones"]])
        kb.dma("sp", "a_lv", lv[:].rearrange("p a c -> p (a c)"), lamv.rearrange("a c -> (a c)").partition_broadcast(128), writes=[dm["lv"]])
        kb.dma("sp", "a_gs", gs[:], g_sub, writes=[dm["gs"]])
        kb.op("dve", lambda E: E.tensor_tensor(out=lt[:], in0=lv[:].rearrange("p (i j) c -> p i j c", j=2)[:, :, 0, :],
                                               in1=lv[:].rearrange("p (i j) c -> p i j c", j=2)[:, :, 1, :], op=ALU.mult), reads=[dm["lv"]], writes=[dm["lt"]])
        kb.op("dve", lambda E: E.tensor_reduce(out=l2[:], in_=lt[:], axis=AX.X, op=ALU.add), reads=[dm["lt"]], writes=[dm["l2"]])
        kb.op("act", lambda E: E.activation(out=l2[:], in_=l2[:], func=AF.Exp), reads=[dm["l2"]], writes=[dm["l2"]])
        kb.op("dve", lambda E: E.tensor_tensor(out=nlam[:], in0=l2[:, 1:2], in1=l2[:, 0:1], op=ALU.subtract), reads=[dm["l2"]], writes=[dm["nlam"]])
        kb.op("dve", lambda E: E.tensor_scalar(out=nlam[:], in0=nlam[:], scalar1=-lam_init, scalar2=None, op0=ALU.add), reads=[dm["nlam"]], writes=[dm["nlam"]])
        kb.op("dve", lambda E: E.tensor_scalar(out=gs[:], in0=gs[:], scalar1=(1.0 - lam_init), scalar2=None, op0=ALU.mult), reads=[dm["gs"]], writes=[dm["gs"]])

        def load_head(h):
            s = h % 2
            kb.dma("sp", f"a_k{s}", kTh[s][:], kT[h], writes=[d_kv[s]])
            kb.dma("sp", f"a_k{s}", vh[s][:], v_all[:, h * 128:(h + 1) * 128].rearrange("(t p) e -> p t e", p=128), writes=[d_kv[s]])
            kb.dma("sp", f"a_k{s}", qTh[s][:], qT[h], writes=[d_kv[s]])

        load_head(0)
        blk_id = 0
        for h in range(4):
            s = h % 2
            if h + 1 < 4:
                load_head(h + 1)
            for qb in range(NQB):
                if qb < 4:
                    q0 = qb * 512; nq = 512; kts = list(range(NKT))
                else:
                    q0 = 2048; nq = 256; kts = [64, 65]
                nk = len(kts)

                def scores(idx):
                    kt = kts[idx]; sl = idx % 2
                    for c in range(2):
                        kb.op("pe", lambda E, c=c, kt=kt, sl=sl: E.matmul(ps[sl * 2 + c][:, 0:nq], lhsT=kTh[s][c * 64:(c + 1) * 64, kt * 128:(kt + 1) * 128],
                                                                       rhs=qTh[s][c * 64:(c + 1) * 64, q0:q0 + nq], start=True, stop=True),
                              reads=[d_kv[s]], writes=[d_ps[sl * 2 + c]])

                def exps(idx):
                    sl = idx % 2; p = idx % NPT
                    for c in range(2):
                        kb.op("act", lambda E, c=c, sl=sl, p=p: E.activation(out=pt[p][c][:, 0:nq], in_=ps[sl * 2 + c][:, 0:nq], func=AF.Exp, scale=0.125),
                              reads=[d_ps[sl * 2 + c]], writes=[d_pt[p][c]])

                def av(idx):
                    kt = kts[idx]; p = idx % NPT
                    for c in range(2):
                        kb.op("pe", lambda E, c=c, kt=kt, p=p, idx=idx: E.matmul(ps[4 + c][:, 0:nq], lhsT=vh[s][:, kt, :], rhs=pt[p][c][:, 0:nq],
                                                                              start=(idx == 0), stop=(idx == nk - 1)),
                              reads=[d_kv[s], d_pt[p][c]], writes=[d_ps[4 + c]])
                        e = "dve" if c == 0 else "pool"
                        if idx == 0:
                            kb.op(e, lambda E, c=c, p=p: E.tensor_copy(out=acc[c][:, 0:nq], in_=pt[p][c][:, 0:nq]), reads=[d_pt[p][c]], writes=[d_acc[c]])
                        else:
                            kb.op(e, lambda E, c=c, p=p: E.tensor_tensor(out=acc[c][:, 0:nq], in0=acc[c][:, 0:nq], in1=pt[p][c][:, 0:nq], op=ALU.add),
                                  reads=[d_pt[p][c], d_acc[c]], writes=[d_acc[c]])

                scores(0)
                for idx in range(nk):
                    exps(idx)
                    if idx + 1 < nk:
                        scores(idx + 1)
                    av(idx)
                for c in range(2):
                    kb.op("pe", lambda E, c=c: E.matmul(ps[6 + c][:, 0:nq], lhsT=ones[:], rhs=acc[c][:, 0:nq], start=True, stop=True),
                          reads=[dm["ones"], d_acc[c]], writes=[d_ps[6 + c]])
                    kb.op("dve", lambda E, c=c: E.reciprocal(out=rec[c][:, 0:nq], in_=ps[6 + c][:, 0:nq]), reads=[d_ps[6 + c]], writes=[dm[f"rec{c}"]])
                kb.op("dve", lambda E: E.tensor_tensor(out=o0[:, 0:nq], in0=ps[4][:, 0:nq], in1=rec[0][:, 0:nq], op=ALU.mult), reads=[d_ps[4], dm["rec0"]], writes=[dm["o0"]])
                kb.op("dve", lambda E: E.tensor_tensor(out=o1[:, 0:nq], in0=ps[5][:, 0:nq], in1=rec[1][:, 0:nq], op=ALU.mult), reads=[d_ps[5], dm["rec1"]], writes=[dm["o1"]])
                kb.op("dve", lambda E: E.scalar_tensor_tensor(out=o0[:, 0:nq], in0=o1[:, 0:nq], scalar=nlam[:, 0:1], in1=o0[:, 0:nq], op0=ALU.mult, op1=ALU.add),
                      reads=[dm["o0"], dm["o1"], dm["nlam"]], writes=[dm["o0"]])
                kb.op("dve", lambda E: E.tensor_tensor(out=osq[:, 0:nq], in0=o0[:, 0:nq], in1=o0[:, 0:nq], op=ALU.mult), reads=[dm["o0"]], writes=[dm["osq"]])
                kb.op("pe", lambda E: E.matmul(ps[6][:, 0:nq], lhsT=ones[:], rhs=osq[:, 0:nq], start=True, stop=True), reads=[dm["ones"], dm["osq"]], writes=[d_ps[6]])
                kb.op("dve", lambda E: E.tensor_scalar(out=rs[:, 0:nq], in0=ps[6][:, 0:nq], scalar1=1.0 / 128, scalar2=EPS, op0=ALU.mult, op1=ALU.add),
                      reads=[d_ps[6]], writes=[dm["rs"]])
                kb.op("act", lambda E: E.activation(out=rs[:, 0:nq], in_=rs[:, 0:nq], func=AF.Ln), reads=[dm["rs"]], writes=[dm["rs"]])
                kb.op("act", lambda E: E.activation(out=rs[:, 0:nq], in_=rs[:, 0:nq], func=AF.Exp, scale=-0.5), reads=[dm["rs"]], writes=[dm["rs"]])
                so = blk_id % 2
                kb.op("dve", lambda E, so=so: E.scalar_tensor_tensor(out=of[so][:, 0:nq], in0=o0[:, 0:nq], scalar=gs[:, 0:1], in1=rs[:, 0:nq], op0=ALU.mult, op1=ALU.mult),
                      reads=[dm["o0"], dm["gs"], dm["rs"]], writes=[dm[f"of{so}"]])
                store(f"st_oa{so}", oaT_o[h, :, q0:q0 + nq], of[so][:, 0:nq], [dm[f"of{so}"]])
                blk_id += 1

    for k in sorted(out_keys):
        nc.gpsimd.wait_ge(kb.sems[k], kb.cnt[k])
    return kb


import math, os
from contextlib import ExitStack

TOKC = 2304
EPS = 1e-6
NBLK = int(os.environ.get("NBLK", "9"))
BT = 256


def build_c(kb):
    nc = kb.nc
    dt = nc.dram_tensor
    hT = dt("hT", [8, 128, TOKC], BF16, kind="ExternalInput").ap()
    x = dt("x_tok", [TOKC, 1024], F32, kind="ExternalInput").ap()
    oaT = dt("oaT", [4, 128, TOKC], F32, kind="ExternalInput").ap()
    obT = dt("obT_tok", [2, 128, TOKC], F32, kind="ExternalInput").ap()
    modT_d = dt("modT", [128, 24, 2], F32, kind="ExternalInput").ap()
    w_in = dt("w_in_c", [1024, 4608], F32, kind="ExternalInput").ap()
    ln_g = dt("ln_g", [256], F32, kind="ExternalInput").ap()
    ln_b = dt("ln_b", [256], F32, kind="ExternalInput").ap()
    w_sT = dt("w_sT", [128, 4, 128], F32, kind="ExternalInput").ap()
    bs_d = dt("bs64", [64, 4, 128], F32, kind="ExternalInput").ap()
    w_br = dt("w_br", [1024, 1024], F32, kind="ExternalInput").ap()
    w_out = dt("w_out", [1024, 1024], F32, kind="ExternalInput").ap()
    xn_o = dt("xnew", [TOKC, 1024], F32, kind="ExternalOutput").ap()

    ps = [nc.alloc_psum_tensor(f"ps{i}", [128, 512], F32) for i in range(8)]
    d_ps = [Dep(f"ps{i}") for i in range(8)]
    out_keys = set()

    def store(key, out, in_, reads):
        out_keys.add(key)
        kb.dma("pool", key, out, in_, reads=reads, writes=[])

    sb = nc.alloc_sbuf_tensor
    wC = sb("wC", [128, 8, 4608], BF16)
    wBR = sb("wBR", [128, 6, 1024], BF16)
    wBRc = sb("wBRc", [64, 4, 1024], BF16)
    wO = sb("wO", [128, 8, 1024], BF16)
    wS = sb("wS", [128, 4, 128], BF16)
    LG = sb("LG", [128, 256], F32)
    LB = sb("LB", [128, 256], F32)
    BS = sb("BS", [64, 4, 128], F32)
    modT = sb("modT_s", [128, 24, 2], F32)
    G = [sb(f"G{i}", [128, 1024], F32) for i in range(2)]
    ident = sb("ident", [128, 128], F32)
    ones = sb("ones", [128, 128], F32)
    dw = {n: Dep(n) for n in "wC wBR wBRc wO wS LG LB BS modT G ident ones".split()}

    with ExitStack() as es:
        sbt = lambda name, shape, dty: es.enter_context(nc.sbuf_tensor(name, shape, dty))
        wst = [sbt(f"wst{i}", [128, 8, 512], F32) for i in range(2)]
        d_wst = [Dep("wst0"), Dep("wst1")]
        wsf = sbt("wsf", [128, 4, 128], F32)
        diag = sbt("diag", [128, 128], F32)
        d_wsf = Dep("wsf"); d_diag = Dep("diag")
        kb.op("pool", lambda E: E.memset(ident[:], 0.0), writes=[dw["ident"]])
        kb.op("pool", lambda E: E.affine_select(out=ident[:], in_=ident[:], pattern=[[-1, 128]], compare_op=ALU.not_equal,
                                                fill=1.0, base=0, channel_multiplier=1), reads=[dw["ident"]], writes=[dw["ident"]])
        kb.op("pool", lambda E: E.memset(ones[:], 1.0), writes=[dw["ones"]])
        kb.dma("sp", "c_mod", modT[:], modT_d, writes=[dw["modT"]])
        kb.dma("sp", "c_lg", LG[:], ln_g.partition_broadcast(128), writes=[dw["LG"]])
        kb.dma("sp", "c_lb", LB[:], ln_b.partition_broadcast(128), writes=[dw["LB"]])
        kb.dma("sp", "c_bs", BS[:], bs_d, writes=[dw["BS"]])
        kb.dma("sp", "c_ws", wsf[:], w_sT, writes=[d_wsf])
        kb.op("dve", lambda E: E.tensor_copy(out=wS[:], in_=wsf[:]), reads=[d_wsf], writes=[dw["wS"]])
        for var in range(2):
            for k in range(8):
                kb.op("dve", lambda E, k=k, var=var: E.tensor_scalar(out=diag[:], in0=ident[:], scalar1=modT[:, 16 + k, var:var + 1], scalar2=None, op0=ALU.mult),
                      reads=[dw["ident"], dw["modT"]], writes=[d_diag])
                b = k // 4
                kb.op("pe", lambda E, k=k, b=b: E.matmul(ps[b][:, (k % 4) * 128:(k % 4 + 1) * 128], lhsT=ones[:], rhs=diag[:], start=True, stop=True),
                      reads=[dw["ones"], d_diag], writes=[d_ps[b]])
            for b in range(2):
                kb.op("act", lambda E, b=b, var=var: E.activation(out=G[var][:, b * 512:(b + 1) * 512], in_=ps[b][:], func=AF.Copy), reads=[d_ps[b]], writes=[dw["G"]])
        cnt = [0]

        def load_cast(src_v, ncols, dst_fn, ddst, nk=8):
            s = cnt[0] % 2; cnt[0] += 1
            kb.dma("sp", f"c_w{s}", wst[s][:, 0:nk, 0:ncols], src_v, writes=[d_wst[s]])
            e = ("dve", "pool", "act")[cnt[0] % 3]
            if e == "act":
                kb.op("act", lambda E: E.activation(out=dst_fn, in_=wst[s][:, 0:nk, 0:ncols], func=AF.Copy), reads=[d_wst[s]], writes=[ddst])
            else:
                kb.op(e, lambda E: E.tensor_copy(out=dst_fn, in_=wst[s][:, 0:nk, 0:ncols]), reads=[d_wst[s]], writes=[ddst])
        w_in_v = w_in.rearrange("(k p) n -> p k n", p=128)
        for g in range(9):
            load_cast(w_in_v[:, :, g * 512:(g + 1) * 512], 512, wC[:, :, g * 512:(g + 1) * 512], dw["wC"])
        w_br_v = w_br[0:768, :].rearrange("(k p) n -> p k n", p=128)
        for g in range(2):
            load_cast(w_br_v[:, :, g * 512:(g + 1) * 512], 512, wBR[:, :, g * 512:(g + 1) * 512], dw["wBR"], nk=6)
        w_brc_v = w_br[768:1024, :].rearrange("(g c) n -> c g n", c=64)
        for g in range(2):
            s = cnt[0] % 2; cnt[0] += 1
            kb.dma("sp", f"c_w{s}", wst[s][0:64, 0:4, 0:512], w_brc_v[:, :, g * 512:(g + 1) * 512], writes=[d_wst[s]])
            kb.op("dve", lambda E, s=s, g=g: E.tensor_copy(out=wBRc[:, :, g * 512:(g + 1) * 512], in_=wst[s][0:64, 0:4, 0:512]), reads=[d_wst[s]], writes=[dw["wBRc"]])
        w_out_v = w_out.rearrange("(k p) n -> p k n", p=128)
        for g in range(2):
            load_cast(w_out_v[:, :, g * 512:(g + 1) * 512], 512, wO[:, :, g * 512:(g + 1) * 512], dw["wO"])
        kb.barrier()

    hTb = sb("hTb", [128, 8, BT], BF16)
    oab = sb("oab", [128, 4, BT], F32)
    obb = sb("obb", [128, 2, BT], F32)
    uT = sb("uT", [64, 4, BT], F32)
    gcT = sb("gcT", [64, 4, BT], F32)
    sgt = [sb(f"sgt{i}", [128, BT], F32) for i in range(2)]
    og = sb("og", [128, 6, BT], BF16)
    ogc = sb("ogc", [64, 4, BT], BF16)
    st6 = sb("st6", [128, 6], F32)
    mv = sb("mv", [128, 2], F32)
    rstd = sb("rstd", [128, 1], F32)
    vcn = sb("vcn", [128, 256], F32)
    vnb = sb("vnb", [128, 256], BF16)
    sT = sb("sT", [64, 4, 128], F32)
    mm = [sb(f"mm{i}", [128, 3, BT], F32) for i in range(2)]
    t0 = sb("t0", [128, BT], F32)
    t1 = sb("t1", [128, BT], F32)
    yT = sb("yT", [128, 8, BT], BF16)
    xt = [sb(f"xt{i}", [128, 1024], F32) for i in range(2)]
    tmpo = sb("tmpo", [128, 1024], F32)
    dn = {n: Dep(n) for n in "hTb oab obb uT gcT sgt0 sgt1 og ogc st6 mv rstd vcn vnb sT mm0 mm1 t0 t1 yT xt0 xt1 tmpo".split()}

    def proj_fm(col0, M, nt, bank):
        for k in range(8):
            kb.op("pe", lambda E, k=k: E.matmul(ps[bank][0:M, 0:nt], lhsT=wC[:, k, col0:col0 + M], rhs=hTb[:, k, 0:nt], start=(k == 0), stop=(k == 7)),
                  reads=[dw["wC"], dn["hTb"]], writes=[d_ps[bank]])

    pj = [0]

    def next_bank():
        pj[0] += 1
        return 6 + pj[0] % 2

    for blk in range(NBLK):
        tok0 = blk * BT; nt = BT; var = 0 if tok0 < 2048 else 1
        ntile = nt // 128
        kb.dma("sp", "c_h", hTb[:, :, 0:nt], hT[:, :, tok0:tok0 + nt].rearrange("k p t -> p k t"), writes=[dn["hTb"]])
        kb.dma("sp", "c_oa", oab[:, :, 0:nt], oaT[:, :, tok0:tok0 + nt].rearrange("h p t -> p h t"), writes=[dn["oab"]])
        kb.dma("sp", "c_ob", obb[:, :, 0:nt], obT[:, :, tok0:tok0 + nt].rearrange("h p t -> p h t"), writes=[dn["obb"]])
        for g in range(4):
            b = next_bank()
            proj_fm(g * 64, 64, nt, b)
            kb.op("act", lambda E, g=g, b=b: E.activation(out=uT[:, g, 0:nt], in_=ps[b][0:64, 0:nt], func=AF.Copy), reads=[d_ps[b]], writes=[dn["uT"]])
        for g in range(4):
            b = next_bank(); s = g % 2
            proj_fm(512 + 768 + g * 64, 64, nt, b)
            kb.op("act", lambda E, b=b, s=s: E.activation(out=sgt[s][0:64, 0:nt], in_=ps[b][0:64, 0:nt], func=AF.Sigmoid), reads=[d_ps[b]], writes=[dn[f"sgt{s}"]])
            kb.op("dve", lambda E, g=g, b=b, s=s: E.tensor_tensor(out=gcT[:, g, 0:nt], in0=ps[b][0:64, 0:nt], in1=sgt[s][0:64, 0:nt], op=ALU.mult),
                  reads=[d_ps[b], dn[f"sgt{s}"]], writes=[dn["gcT"]])
        for j in range(6):
            b = next_bank(); s = j % 2
            proj_fm(512 + j * 128, 128, nt, b)
            kb.op("act", lambda E, b=b, s=s: E.activation(out=sgt[s][:, 0:nt], in_=ps[b][:, 0:nt], func=AF.Sigmoid), reads=[d_ps[b]], writes=[dn[f"sgt{s}"]])
            kb.op("dve", lambda E, b=b, s=s: E.tensor_tensor(out=sgt[s][:, 0:nt], in0=ps[b][:, 0:nt], in1=sgt[s][:, 0:nt], op=ALU.mult),
                  reads=[d_ps[b], dn[f"sgt{s}"]], writes=[dn[f"sgt{s}"]])
            src = oab[:, j, 0:nt] if j < 4 else obb[:, j - 4, 0:nt]
            dsrc = dn["oab"] if j < 4 else dn["obb"]
            kb.op("dve", lambda E, j=j, s=s, src=src: E.tensor_tensor(out=og[:, j, 0:nt], in0=sgt[s][:, 0:nt], in1=src, op=ALU.mult),
                  reads=[dn[f"sgt{s}"], dsrc], writes=[dn["og"]])
        for t in range(ntile):
            b = next_bank()
            for k in range(8):
                kb.op("pe", lambda E, k=k, t=t, b=b: E.matmul(ps[b][:, 0:256], lhsT=hTb[:, k, t * 128:(t + 1) * 128], rhs=wC[:, k, 256:512], start=(k == 0), stop=(k == 7)),
                      reads=[dw["wC"], dn["hTb"]], writes=[d_ps[b]])
            kb.op("dve", lambda E, b=b: E.bn_stats(out=st6[:], in_=ps[b][:, 0:256]), reads=[d_ps[b]], writes=[dn["st6"]])
            kb.op("dve", lambda E: E.bn_aggr(out=mv[:], in_=st6[:]), reads=[dn["st6"]], writes=[dn["mv"]])
            kb.op("dve", lambda E: E.tensor_scalar(out=rstd[:], in0=mv[:, 1:2], scalar1=EPS, scalar2=None, op0=ALU.add), reads=[dn["mv"]], writes=[dn["rstd"]])
            kb.op("act", lambda E: E.activation(out=rstd[:], in_=rstd[:], func=AF.Sqrt), reads=[dn["rstd"]], writes=[dn["rstd"]])
            kb.op("dve", lambda E: E.reciprocal(out=rstd[:], in_=rstd[:]), reads=[dn["rstd"]], writes=[dn["rstd"]])
            kb.op("dve", lambda E, b=b: E.tensor_scalar(out=vcn[:], in0=ps[b][:, 0:256], scalar1=mv[:, 0:1], scalar2=rstd[:, 0:1], op0=ALU.subtract, op1=ALU.mult),
                  reads=[d_ps[b], dn["mv"], dn["rstd"]], writes=[dn["vcn"]])
            kb.op("dve", lambda E: E.tensor_tensor(out=vcn[:], in0=vcn[:], in1=LG[:], op=ALU.mult), reads=[dn["vcn"], dw["LG"]], writes=[dn["vcn"]])
            kb.op("dve", lambda E: E.tensor_tensor(out=vnb[:], in0=vcn[:], in1=LB[:], op=ALU.add), reads=[dn["vcn"], dw["LB"]], writes=[dn["vnb"]])
            b2 = next_bank()
            for g in range(4):
                kb.op("pe", lambda E, g=g, b2=b2: E.matmul(ps[b2][0:64, g * 128:(g + 1) * 128], lhsT=vnb[:, g * 64:(g + 1) * 64], rhs=wS[:, g, :], start=True, stop=True),
                      reads=[dn["vnb"], dw["wS"]], writes=[d_ps[b2]])
            kb.op("dve", lambda E, b2=b2: E.tensor_tensor(out=sT[:].rearrange("c g p -> c (g p)"), in0=ps[b2][0:64, :], in1=BS[:].rearrange("c g p -> c (g p)"), op=ALU.add),
                  reads=[d_ps[b2], dw["BS"]], writes=[dn["sT"]])
            kb.op("dve", lambda E, t=t: E.tensor_tensor(out=sT[:], in0=sT[:], in1=uT[:, :, t * 128:(t + 1) * 128], op=ALU.mult), reads=[dn["sT"], dn["uT"]], writes=[dn["sT"]])
            kb.op("dve", lambda E, t=t: E.tensor_tensor(out=ogc[:, :, t * 128:(t + 1) * 128], in0=sT[:], in1=gcT[:, :, t * 128:(t + 1) * 128], op=ALU.mult),
                  reads=[dn["sT"], dn["gcT"]], writes=[dn["ogc"]])
        for dc in range(8):
            dsl = slice(dc * 128, (dc + 1) * 128)
            for e in range(4):
                kb.op("pe", lambda E, e=e: E.matmul(ps[0][:, 0:nt], lhsT=wBR[:, e, dsl], rhs=og[:, e, 0:nt], start=(e == 0), stop=(e == 3)),
                      reads=[dw["wBR"], dn["og"]], writes=[d_ps[0]])
            for e in range(2):
                kb.op("pe", lambda E, e=e: E.matmul(ps[1][:, 0:nt], lhsT=wBR[:, 4 + e, dsl], rhs=og[:, 4 + e, 0:nt], start=(e == 0), stop=(e == 1)),
                      reads=[dw["wBR"], dn["og"]], writes=[d_ps[1]])
            for g in range(4):
                kb.op("pe", lambda E, g=g: E.matmul(ps[2][:, 0:nt], lhsT=wBRc[:, g, dsl], rhs=ogc[:, g, 0:nt], start=(g == 0), stop=(g == 3)),
                      reads=[dw["wBRc"], dn["ogc"]], writes=[d_ps[2]])
            ms = dc % 2
            for i in range(3):
                proj_fm(1536 + i * 1024 + dc * 128, 128, nt, 3 + i)
                kb.op("act", lambda E, i=i, ms=ms: E.activation(out=mm[ms][:, i, 0:nt], in_=ps[3 + i][:, 0:nt], func=AF.Sigmoid), reads=[d_ps[3 + i]], writes=[dn[f"mm{ms}"]])
            kb.op("dve", lambda E, ms=ms: E.tensor_tensor(out=t0[:, 0:nt], in0=ps[0][:, 0:nt], in1=mm[ms][:, 0, 0:nt], op=ALU.mult), reads=[d_ps[0], dn[f"mm{ms}"]], writes=[dn["t0"]])
            kb.op("dve", lambda E, ms=ms: E.tensor_tensor(out=t1[:, 0:nt], in0=ps[1][:, 0:nt], in1=mm[ms][:, 1, 0:nt], op=ALU.mult), reads=[d_ps[1], dn[f"mm{ms}"]], writes=[dn["t1"]])
            kb.op("dve", lambda E: E.tensor_tensor(out=t0[:, 0:nt], in0=t0[:, 0:nt], in1=t1[:, 0:nt], op=ALU.add), reads=[dn["t0"], dn["t1"]], writes=[dn["t0"]])
            kb.op("dve", lambda E, ms=ms: E.tensor_tensor(out=t1[:, 0:nt], in0=ps[2][:, 0:nt], in1=mm[ms][:, 2, 0:nt], op=ALU.mult), reads=[d_ps[2], dn[f"mm{ms}"]], writes=[dn["t1"]])
            kb.op("dve", lambda E, dc=dc: E.tensor_tensor(out=yT[:, dc, 0:nt], in0=t0[:, 0:nt], in1=t1[:, 0:nt], op=ALU.add), reads=[dn["t0"], dn["t1"]], writes=[dn["yT"]])
        for t in range(ntile):
            gt = (tok0 // 128) + t
            s = gt % 2
            kb.dma("sp", f"c_x{s}", xt[s][:], x[gt * 128:(gt + 1) * 128, :], writes=[dn[f"xt{s}"]])
            for cb in range(2):
                b = next_bank()
                for k in range(8):
                    kb.op("pe", lambda E, k=k, cb=cb, b=b, t=t: E.matmul(ps[b][:], lhsT=yT[:, k, t * 128:(t + 1) * 128], rhs=wO[:, k, cb * 512:(cb + 1) * 512],
                                                                      start=(k == 0), stop=(k == 7)),
                          reads=[dn["yT"], dw["wO"]], writes=[d_ps[b]])
                kb.op("dve", lambda E, cb=cb, b=b: E.tensor_tensor(out=tmpo[:, cb * 512:(cb + 1) * 512], in0=ps[b][:], in1=G[var][:, cb * 512:(cb + 1) * 512], op=ALU.mult),
                      reads=[d_ps[b], dw["G"]], writes=[dn["tmpo"]])
            kb.op("dve", lambda E, s=s: E.tensor_tensor(out=xt[s][:], in0=xt[s][:], in1=tmpo[:], op=ALU.add), reads=[dn["tmpo"], dn[f"xt{s}"]], writes=[dn[f"xt{s}"]])
            store(f"st_x{s}", xn_o[gt * 128:(gt + 1) * 128, :], xt[s][:], [dn[f"xt{s}"]])

    for k in sorted(out_keys):
        nc.gpsimd.wait_ge(kb.sems[k], kb.cnt[k])
    return kb


import numpy as np

OFF_Q, OFF_K, OFF_V, OFF_F, OFF_U, OFF_VC, OFF_GATE, OFF_MERGE = 0, 512, 1024, 1536, 1792, 2048, 2304, 3328
N = 8192; NCTX = 256; NLT = 2048


def rope_tabs():
    n = N
    row = np.repeat(np.arange(n // 64), 64).astype(np.float32)
    col = np.tile(np.arange(64), n // 64).astype(np.float32)
    freqs = (10000.0 ** (-np.arange(0, 32, 2, dtype=np.float32) / 32)).astype(np.float32)
    ar = row[:, None] * freqs; ac = col[:, None] * freqs
    ang = np.concatenate([ar, ar, ac, ac], -1)
    cos = np.cos(ang).astype(np.float32); sin = np.sin(ang).astype(np.float32)
    sgn = np.tile(np.concatenate([-np.ones(16), np.ones(16)]), 2).astype(np.float32)
    return cos, sin * sgn


def col_layout(v, k):
    return np.ascontiguousarray(v.reshape(k, 128).T)


def stage_a_inputs(I, l, x_cur, ctx_cur):
    cos, ssin = rope_tabs()
    maps = []
    for core in range(8):
        b, j = core // 4, core % 4
        xt = np.concatenate([x_cur[b, j * NLT:(j + 1) * NLT], ctx_cur[b]], 0)
        cvec = np.stack([col_layout(I['c'][b], 8), col_layout(I['c_ctx'], 8)], -1)
        maps.append({
            "x_tok": np.ascontiguousarray(xt),
            "cvec": np.ascontiguousarray(cvec),
            "w_ada": I['w_ada'][l], "b_ada": col_layout(I['b_ada'][l], 24), "g_norm": col_layout(I['g_norm'][l], 8),
            "w_in_a": np.ascontiguousarray(I['w_in'][l][:, OFF_Q:OFF_U]),
            "g_q": I['g_q'][l], "g_k": I['g_k'][l],
            "cos": np.ascontiguousarray(cos[j * NLT:(j + 1) * NLT]), "ssin": np.ascontiguousarray(ssin[j * NLT:(j + 1) * NLT]),
        })
    return maps


import ml_dtypes
BF = ml_dtypes.bfloat16


def fourier_consts():
    t = np.arange(64)[:, None]; kb = np.arange(64)[None, :]
    a = 2 * np.pi * ((t * kb) % 64) / 64
    cs64 = np.concatenate([np.cos(a), -np.sin(a)], 1).astype(BF)
    p = np.arange(128)[:, None, None]; kbb = np.arange(64)[None, :, None]; ka = np.arange(128)[None, None, :]
    a = 2 * np.pi * ((p * (64 * ka + kbb)) % 8192) / 8192
    T1 = np.concatenate([np.cos(a), -np.sin(a)], 2).astype(BF)
    T2 = np.concatenate([np.sin(a), np.cos(a)], 2).astype(BF)
    n = np.arange(128)[:, None, None]; ch = np.arange(2)[None, :, None]; k = np.arange(256)[None, None, :]
    a = 2 * np.pi * (((ch * 128 + n) * k) % 256) / 256
    dctx = np.concatenate([np.cos(a), -np.sin(a)], 2).astype(BF)
    c1 = np.arange(64)[:, None]; c2 = np.arange(64)[None, :]
    a = 2 * np.pi * ((c1 * c2) % 64) / 64
    ccs = np.stack([np.cos(a), np.sin(a)], 1).astype(np.float32)
    return dict(cs64=cs64, T1=np.ascontiguousarray(T1), T2=np.ascontiguousarray(T2), dctx=np.ascontiguousarray(dctx), ccs=np.ascontiguousarray(ccs))


_FC = None


def stage_b_inputs(I, l, resA):
    global _FC
    if _FC is None:
        _FC = fourier_consts()
    maps = []
    for core in range(8):
        b, g = core // 4, core % 4
        rs = [resA[b * 4 + j] for j in range(4)]
        kT_all = np.concatenate([np.asarray(r["kT"])[:, :, :NLT] for r in rs] + [np.asarray(rs[0]["kT"])[:, :, NLT:]], 2)
        v_all = np.concatenate([np.asarray(r["v"])[:NLT] for r in rs] + [np.asarray(rs[0]["v"])[NLT:]], 0)
        f_g = np.concatenate([np.asarray(r["f"])[:NLT, g * 64:(g + 1) * 64] for r in rs] + [np.asarray(rs[0]["f"])[NLT:, g * 64:(g + 1) * 64]], 0)
        m = {
            "qT": np.asarray(resA[core]["qT"]), "kT_all": np.ascontiguousarray(kT_all), "v_all": np.ascontiguousarray(v_all),
            "lamv": np.stack([I['lam_q1'][l], I['lam_k1'][l], I['lam_q2'][l], I['lam_k2'][l]], 0),
            "g_sub": np.ascontiguousarray(I['g_sub'][l].reshape(128, 1)),
            "f_g": np.ascontiguousarray(f_g), "w_f": np.ascontiguousarray(I['w_f'][l][g]), "b_f": np.ascontiguousarray(I['b_f'][l][g].reshape(64, 1)),
        }
        m.update(_FC)
        maps.append(m)
    return maps


def stage_c_inputs(I, l, x_cur, ctx_cur, resA, resB):
    maps = []
    w_in_c = np.ascontiguousarray(I['w_in'][l][:, OFF_U:])
    w_br = np.ascontiguousarray(np.concatenate([I['w_br_a'][l], I['w_br_b'][l], I['w_br_c'][l]], 0))
    w_sT = np.ascontiguousarray(I['w_s'][l].transpose(2, 0, 1))
    bs64 = np.ascontiguousarray(np.broadcast_to(I['b_s'][l][None, :, :], (64, 4, 128)))
    for core in range(8):
        b, j = core // 4, core % 4
        xt = np.concatenate([x_cur[b, j * NLT:(j + 1) * NLT], ctx_cur[b]], 0)
        ob = []
        for g in range(4):
            o = np.asarray(resB[b * 4 + g]["obT"])
            ob.append(np.concatenate([o[:, j * NLT:(j + 1) * NLT], o[:, N:]], 1))
        obT_tok = np.stack(ob, 0).reshape(2, 128, NLT + NCTX)
        maps.append({
            "hT": np.asarray(resA[core]["hT"]), "x_tok": np.ascontiguousarray(xt), "oaT": np.asarray(resB[core]["oaT"]),
            "obT_tok": np.ascontiguousarray(obT_tok), "modT": np.asarray(resA[core]["modT"]),
            "w_in_c": w_in_c, "ln_g": I['ln_g'][l], "ln_b": I['ln_b'][l], "w_sT": w_sT, "bs64": bs64, "w_br": w_br, "w_out": I['w_out'][l],
        })
    return maps


import math as _math


def _np_results(res):
    return [dict((k, np.asarray(v)) for k, v in r.items()) for r in res.results]


def kernel(**inputs):
    I = {k: np.asarray(v) for k, v in inputs.items()}
    x_cur = I['x']; ctx_cur = I['ctx']
    for l in range(2):
        lam_init = 0.8 - 0.6 * _math.exp(-0.3 * l)
        kb = KB(); build_a(kb)
        resA = _np_results(kb.run(stage_a_inputs(I, l, x_cur, ctx_cur)))
        kb = KB(); build_b(kb, lam_init)
        resB = _np_results(kb.run(stage_b_inputs(I, l, resA)))
        kb = KB(); build_c(kb)
        resC = _np_results(kb.run(stage_c_inputs(I, l, x_cur, ctx_cur, resA, resB)))
        x_new = np.empty_like(x_cur); ctx_new = np.empty_like(ctx_cur)
        for core in range(8):
            b, j = core // 4, core % 4
            x_new[b, j * NLT:(j + 1) * NLT] = resC[core]["xnew"][:NLT]
            if j == 0:
                ctx_new[b] = resC[core]["xnew"][NLT:]
        x_cur, ctx_cur = x_new, ctx_new
    return x_cur.astype(np.float32)
```

```python
import numpy as np
import concourse.bass as bass
import concourse.mybir as mybir
from concourse.bass_utils import run_bass_kernel_spmd

F32 = mybir.dt.float32
BF16 = mybir.dt.bfloat16
AF = mybir.ActivationFunctionType
ALU = mybir.AluOpType
AX = mybir.AxisListType


class Dep:
    __slots__ = ("w", "r", "name")

    def __init__(self, name=""):
        self.w = None
        self.r = []
        self.name = name


class KB:
    COMPUTE = ("pe", "act", "dve", "pool")

    def __init__(self):
        self.nc = bass.Bass("TRN2", target_bir_lowering=False)
        nc = self.nc
        self.eng = {"pe": nc.tensor, "act": nc.scalar, "dve": nc.vector,
                    "pool": nc.gpsimd, "sp": nc.sync}
        self.sems = {}
        self.cnt = {}
        for e in self.COMPUTE:
            self.sems[e] = nc.alloc_semaphore(name="s_" + e)
            self.cnt[e] = 0
        self.seen = {e: {} for e in self.eng}
        self.n_inst = 0

    def _sem(self, key):
        if key not in self.sems:
            self.sems[key] = self.nc.alloc_semaphore(name="d_" + str(key))
            self.cnt[key] = 0
        return self.sems[key]

    def _waits(self, e, reads, writes):
        need = {}

        def add(t, war=False):
            if t is None:
                return
            sk, v = t
            if sk == e and (war or e == "pe"):
                return
            if need.get(sk, 0) < v:
                need[sk] = v
        for d in reads:
            add(d.w)
        for d in writes:
            add(d.w)
            for t in d.r:
                add(t, war=True)
        E = self.eng[e]
        for sk, v in need.items():
            if self.seen[e].get(sk, 0) >= v:
                continue
            E.wait_ge(self.sems[sk], v)
            self.seen[e][sk] = v

    def _mark(self, tok, reads, writes):
        for d in reads:
            d.r.append(tok)
            if len(d.r) > 64:
                m = {}
                for sk, v in d.r:
                    if m.get(sk, 0) < v:
                        m[sk] = v
                d.r = list(m.items())
        for d in writes:
            d.w = tok
            d.r = []

    def op(self, e, fn, reads=(), writes=()):
        self._waits(e, reads, writes)
        inst = fn(self.eng[e])
        self.cnt[e] += 1
        inst.then_inc(self.sems[e], 1)
        self._mark((e, self.cnt[e]), reads, writes)
        self.n_inst += 1
        return inst

    def mm(self, fn, reads=(), writes=(), last=True):
        return self.op("pe", fn, reads, writes)

    def dma(self, q, key, out, in_, reads=(), writes=(), **kw):
        sem = self._sem(key)
        self._waits(q, reads, writes)
        inst = self.eng[q].dma_start(out=out, in_=in_, **kw)
        self.cnt[key] += 16
        inst.then_inc(sem, 16)
        self._mark((key, self.cnt[key]), reads, writes)
        self.n_inst += 1
        return inst

    def barrier(self):
        for e, E in self.eng.items():
            for sk, sem in self.sems.items():
                v = self.cnt[sk]
                if v == 0 or self.seen[e].get(sk, 0) >= v:
                    continue
                E.wait_ge(sem, v)
                self.seen[e][sk] = v

    def wait_all(self, e, deps):
        self._waits(e, deps, ())

    def run(self, in_maps, n=8, trace=False):
        return run_bass_kernel_spmd(self.nc, in_maps, core_ids=list(range(n)), trace=trace)


import math
from contextlib import ExitStack

NT_A = 18
NLAT_T = 16
TOKS = NT_A * 128
NKT = 66
NKEY = NKT * 128
EPS = 1e-6
BT = 256
O_Q, O_K, O_V, O_F, O_U, O_VC, O_GATE, O_MERGE = 0, 512, 1024, 1536, 1792, 2048, 2304, 3328
RG = [[0, 1, 2, 3], [4, 5, 6, 7]]


def build_fused(kb):
    nc = kb.nc
    EI = lambda name, shape, d=F32: nc.dram_tensor(name, shape, d, kind="ExternalInput").ap()
    IN = lambda name, shape, d=F32: nc.dram_tensor(name, shape, d, kind="Internal").ap()
    x_in = EI("x_tok", [TOKS, 1024])
    cvec = EI("cvec", [128, 8, 2])
    w_ada = EI("w_ada", [2, 1024, 3072]); b_ada = EI("b_ada", [2, 128, 24]); g_norm = EI("g_norm", [2, 128, 8])
    w_in = EI("w_in", [2, 1024, 6400])
    g_q = EI("g_q", [2, 64]); g_k = EI("g_k", [2, 64])
    cos = EI("cos", [2048, 64]); ssin = EI("ssin", [2048, 64])
    lamv = EI("lamv", [2, 256]); g_sub = EI("g_sub", [2, 128, 1])
    w_f = EI("w_f", [2, 4, 64, 64]); b_f = EI("b_f", [2, 64, 4])
    cs64 = EI("cs64", [64, 128], BF16); T1d = EI("T1j", [128, 64, 64], BF16); T2d = EI("T2j", [128, 64, 64], BF16)
    dcd = EI("dctx", [128, 2, 512], BF16); ccs = EI("ccs", [64, 2, 64])
    ln_g = EI("ln_g", [2, 256]); ln_b = EI("ln_b", [2, 256])
    w_sT = EI("w_sT", [2, 128, 4, 128]); bs_d = EI("bs64", [2, 64, 4, 128])
    w_br = EI("w_br", [2, 1024, 1024]); w_out = EI("w_out", [2, 1024, 1024])
    y_out = nc.dram_tensor("y", [2048, 1024], F32, kind="ExternalOutput").ap()
    x1 = IN("x1", [TOKS, 1024])
    hT_d = IN("hT_d", [8, 128, TOKS], BF16)
    qT_d = IN("qT_d", [4, 128, TOKS], BF16)
    kT_lat = [IN(f"kT_lat{i}", [256, 2048], BF16) for i in range(2)]
    kT_ag = [IN(f"kT_ag{i}", [1024, 2048], BF16) for i in range(2)]
    kT_ctx = IN("kT_ctx", [4, 128, 256], BF16)
    v_lat = [IN(f"v_lat{i}", [1024, 512], BF16) for i in range(2)]
    v_ag = [IN(f"v_ag{i}", [4096, 512], BF16) for i in range(2)]
    v_ctx = IN("v_ctx", [256, 512], BF16)
    f_lat = IN("f_lat", [4 * 2048, 64], BF16)
    f_ag = IN("f_ag", [16 * 2048, 64], BF16)
    f_ctx = IN("f_ctx", [256, 256], BF16)
    oaT_d = IN("oaT_d", [4, 128, TOKS])
    obT_d = IN("obT_d", [4, 64, TOKS])
    modT_d = IN("modT_d", [128, 24, 2])
    out_keys = set()
    uid = [0]

    def U(name):
        uid[0] += 1
        return f"{name}_{uid[0]}"

    def store(key, out, in_, reads):
        out_keys.add(key)
        kb.dma("pool", key, out, in_, reads=reads, writes=[])

    def collective(kind, src, dst):
        sem = kb._sem("cc")
        inst = nc.gpsimd.collective_compute(kind, ALU.bypass, replica_groups=RG, ins=[src], outs=[dst])
        inst.then_inc(sem, 1)
        kb.cnt["cc"] += 1

    def rstd_chain(ssrc, dsrc, dst, ddst, scale):
        kb.op("dve", lambda E: E.tensor_scalar(out=dst, in0=ssrc, scalar1=scale, scalar2=EPS, op0=ALU.mult, op1=ALU.add), reads=[dsrc], writes=[ddst])
        kb.op("act", lambda E: E.activation(out=dst, in_=dst, func=AF.Sqrt), reads=[ddst], writes=[ddst])
        kb.op("dve", lambda E: E.reciprocal(out=dst, in_=dst), reads=[ddst], writes=[ddst])

    def make_ident(ident, dep):
        kb.op("pool", lambda E: E.memset(ident[:], 0.0), writes=[dep])
        kb.op("pool", lambda E: E.affine_select(out=ident[:], in_=ident[:], pattern=[[-1, 128]], compare_op=ALU.not_equal,
                                                fill=1.0, base=0, channel_multiplier=1), reads=[dep], writes=[dep])

    def stage_a(l, x_src):
        with ExitStack() as es:
            sb = lambda name, shape, dty: es.enter_context(nc.sbuf_tensor(U(name), shape, dty))
            ps = [es.enter_context(nc.psum_tensor(U(f"psA{i}"), [128, 512], F32)) for i in range(6)]
            ps += [es.enter_context(nc.psum_tensor(U(f"psA{i}"), [128, 1024], BF16)) for i in (6, 7)]
            ident = sb("ident", [128, 128], F32); identb = sb("identb", [128, 128], BF16)
            cT = sb("cT", [128, 8, 2], F32); sg = sb("sg", [128, 8, 2], F32); sc = sb("sc", [128, 8, 2], F32)
            bT = sb("bT", [128, 24], F32); gn = sb("gn", [128, 8], F32)
            modT = sb("modT_s", [128, 24, 2], F32); Aff = sb("Aff", [128, 8, 2], F32)
            wst = [sb(f"wst{i}", [128, 8, 512], F32) for i in range(2)]
            wA = sb("wA", [128, 8, 1792], BF16)
            GQ = sb("GQ", [128, 64], F32); GK = sb("GK", [128, 64], F32)
            xt = [sb(f"xt{i}", [128, 1024], F32) for i in range(2)]
            junk = sb("junk", [128, 1024], F32)
            ss = sb("ss", [128, 1], F32); rstd = sb("rstd", [128, 1], F32)
            hTt = [sb(f"hTt{i}", [128, 8, 128], BF16) for i in range(2)]
            cs = [sb(f"cs{i}", [128, 2, 64], F32) for i in range(2)]
            sq = sb("sq", [128, 512], F32); ss8 = sb("ss8", [128, 8], F32)
            qn = sb("qn", [128, 512], F32); t1 = sb("t1", [128, 512], F32); t2 = sb("t2", [128, 512], F32)
            qr = [sb(f"qr{i}", [128, 512], BF16) for i in range(2)]
            qTt = [sb(f"qTt{i}", [128, 4, 128], BF16) for i in range(2)]
            kTt = [sb(f"kTt{i}", [128, 4, 128], BF16) for i in range(2)]
            vt = [sb(f"vt{i}", [128, 512], BF16) for i in range(2)]
            ft = [sb(f"ft{i}", [128, 256], BF16) for i in range(2)]
            D = lambda n: Dep(n)
            d_ident, d_identb, d_cT, d_sg, d_sc, d_bT, d_gn, d_modT, d_Aff = [D(n) for n in "ident identb cT sg sc bT gn modT Aff".split()]
            d_wst = [D("wst0"), D("wst1")]; d_wA = D("wA"); d_G = D("G")
            d_xt = [D("xt0"), D("xt1")]; d_junk = D("junk"); d_ss = D("ss"); d_rstd = D("rstd")
            d_hTt = [D("hTt0"), D("hTt1")]; d_cs = [D("cs0"), D("cs1")]
            d_sq, d_ss8, d_qn, d_t1, d_t2 = D("sq"), D("ss8"), D("qn"), D("t1"), D("t2")
            d_qr = [D("qr0"), D("qr1")]; d_qTt = [D("qTt0"), D("qTt1")]; d_kTt = [D("kTt0"), D("kTt1")]
            d_vt = [D("vt0"), D("vt1")]; d_ft = [D("ft0"), D("ft1")]
            d_ps = [D(f"ps{i}") for i in range(8)]

            make_ident(ident, d_ident)
            kb.op("pool", lambda E: E.tensor_copy(out=identb[:], in_=ident[:]), reads=[d_ident], writes=[d_identb])
            kb.dma("sp", "ld_c0", cT[:], cvec, writes=[d_cT])
            kb.dma("sp", "ld_c1", bT[:], b_ada[l], writes=[d_bT])
            kb.dma("sp", "ld_c2", gn[:], g_norm[l], writes=[d_gn])
            kb.dma("sp", "ld_c3", GQ[:], g_q[l].partition_broadcast(128), writes=[d_G])
            kb.dma("sp", "ld_c3", GK[:], g_k[l].partition_broadcast(128), writes=[d_G])
            kb.op("act", lambda E: E.activation(out=sg[:], in_=cT[:], func=AF.Sigmoid), reads=[d_cT], writes=[d_sg])
            kb.op("dve", lambda E: E.tensor_tensor(out=sc[:], in0=cT[:], in1=sg[:], op=ALU.mult), reads=[d_cT, d_sg], writes=[d_sc])
            w_ada_v = w_ada[l].rearrange("(k p) n -> p k n", p=128)
            for g in range(6):
                s = g % 2
                kb.dma("sp", f"ld_w{s}", wst[s][:], w_ada_v[:, :, g * 512:(g + 1) * 512], writes=[d_wst[s]])
                for jj in range(4):
                    j = g * 4 + jj
                    for k in range(8):
                        kb.op("pe", lambda E, k=k, jj=jj, s=s, j=j: E.matmul(ps[0][:, 2 * j:2 * j + 2], lhsT=wst[s][:, k, jj * 128:(jj + 1) * 128],
                                                                            rhs=sc[:, k, :], start=(k == 0), stop=(k == 7)),
                              reads=[d_wst[s], d_sc], writes=[d_ps[0]])
            kb.op("dve", lambda E: E.tensor_tensor(out=modT[:], in0=ps[0][:, 0:48].rearrange("p (j n) -> p j n", n=2),
                                                   in1=bT[:].unsqueeze(2).to_broadcast([128, 24, 2]), op=ALU.add),
                  reads=[d_ps[0], d_bT], writes=[d_modT])
            store("st_mod", modT_d, modT[:], [d_modT])
            kb.op("dve", lambda E: E.tensor_scalar(out=Aff[:], in0=modT[:, 8:16, :], scalar1=1.0, scalar2=None, op0=ALU.add), reads=[d_modT], writes=[d_Aff])
            kb.op("dve", lambda E: E.tensor_tensor(out=Aff[:], in0=Aff[:], in1=gn[:].unsqueeze(2).to_broadcast([128, 8, 2]), op=ALU.mult),
                  reads=[d_Aff, d_gn], writes=[d_Aff])
            w_in_v = w_in[l].rearrange("(k p) n -> p k n", p=128)
            for g in range(4):
                s = g % 2
                n = 512 if g < 3 else 256
                kb.dma("sp", f"ld_w{s}", wst[s][:, :, 0:n], w_in_v[:, :, g * 512:g * 512 + n], writes=[d_wst[s]])
                e = "pool" if g % 2 == 0 else "dve"
                kb.op(e, lambda E, s=s, n=n, g=g: E.tensor_copy(out=wA[:, :, g * 512:g * 512 + n], in_=wst[s][:, :, 0:n]), reads=[d_wst[s]], writes=[d_wA])

            def load_tile(i):
                s = i % 2
                kb.dma("sp", f"ld_x{s}", xt[s][:], x_src[i * 128:(i + 1) * 128, :], writes=[d_xt[s]])
                if i < NLAT_T:
                    kb.dma("sp", f"ld_cs{s}", cs[s][:, 0, :], cos[i * 128:(i + 1) * 128, :], writes=[d_cs[s]])
                    kb.dma("sp", f"ld_cs{s}", cs[s][:, 1, :], ssin[i * 128:(i + 1) * 128, :], writes=[d_cs[s]])

            load_tile(0)
            for i in range(NT_A):
                s = i % 2
                var = 0 if i < NLAT_T else 1
                lat = i < NLAT_T
                if i + 1 < NT_A:
                    load_tile(i + 1)
                X = xt[s]
                kb.op("act", lambda E: E.activation(out=junk[:], in_=X[:], func=AF.Square, accum_out=ss[:]), reads=[d_xt[s]], writes=[d_junk, d_ss])
                rstd_chain(ss[:], d_ss, rstd[:], d_rstd, 1.0 / 1024)
                kb.op("dve", lambda E: E.tensor_scalar(out=X[:], in0=X[:], scalar1=rstd[:, 0:1], scalar2=None, op0=ALU.mult),
                      reads=[d_xt[s], d_rstd], writes=[d_xt[s]])
                for k in range(8):
                    b = k // 4
                    kb.op("pe", lambda E, k=k, b=b: E.transpose(out=ps[b][:, (k % 4) * 128:(k % 4 + 1) * 128], in_=X[:, k * 128:(k + 1) * 128], identity=ident[:]),
                          reads=[d_xt[s], d_ident], writes=[d_ps[b]])
                for k in range(8):
                    b = k // 4
                    src = ps[b][:, (k % 4) * 128:(k % 4 + 1) * 128]
                    if k % 2 == 0:
                        kb.op("act", lambda E, k=k, src=src: E.activation(out=hTt[s][:, k, :], in_=src, func=AF.Identity,
                                                                          scale=Aff[:, k, var:var + 1], bias=modT[:, k, var:var + 1]),
                              reads=[d_ps[b], d_Aff, d_modT], writes=[d_hTt[s]])
                    else:
                        kb.op("dve", lambda E, k=k, src=src: E.tensor_scalar(out=hTt[s][:, k, :], in0=src, scalar1=Aff[:, k, var:var + 1],
                                                                             scalar2=modT[:, k, var:var + 1], op0=ALU.mult, op1=ALU.add),
                              reads=[d_ps[b], d_Aff, d_modT], writes=[d_hTt[s]])
                store(f"st_h{s}", hT_d[:, :, i * 128:(i + 1) * 128].rearrange("k p t -> p k t"), hTt[s][:], [d_hTt[s]])
                for cb in range(4):
                    n = 512 if cb < 3 else 256
                    for k in range(8):
                        kb.op("pe", lambda E, k=k, cb=cb, n=n: E.matmul(ps[2 + cb][:, 0:n], lhsT=hTt[s][:, k, :], rhs=wA[:, k, cb * 512:cb * 512 + n],
                                                                       start=(k == 0), stop=(k == 7)),
                              reads=[d_hTt[s], d_wA], writes=[d_ps[2 + cb]])
                for which in range(2):
                    P = ps[2 + which]; dP = d_ps[2 + which]
                    G = GQ if which == 0 else GK
                    kb.op("act", lambda E: E.activation(out=sq[:], in_=P[:], func=AF.Square), reads=[dP], writes=[d_sq])
                    kb.op("dve", lambda E: E.tensor_reduce(out=ss8[:], in_=sq[:].rearrange("p (g c) -> p g c", c=64), axis=AX.X, op=ALU.add),
                          reads=[d_sq], writes=[d_ss8])
                    rstd_chain(ss8[:], d_ss8, ss8[:], d_ss8, 1.0 / 64)
                    kb.op("dve", lambda E: E.tensor_tensor(out=qn[:].rearrange("p (g c) -> p g c", c=64), in0=P[:].rearrange("p (g c) -> p g c", c=64),
                                                           in1=ss8[:].unsqueeze(2).to_broadcast([128, 8, 64]), op=ALU.mult),
                          reads=[dP, d_ss8], writes=[d_qn])
                    QR = qr[which]; dQR = d_qr[which]
                    if lat:
                        kb.op("dve", lambda E: E.tensor_tensor(out=qn[:].rearrange("p (g c) -> p g c", c=64), in0=qn[:].rearrange("p (g c) -> p g c", c=64),
                                                               in1=G[:].unsqueeze(1).to_broadcast([128, 8, 64]), op=ALU.mult),
                              reads=[d_qn, d_G], writes=[d_qn])
                        kb.op("dve", lambda E: E.tensor_tensor(out=t1[:].rearrange("p (g c) -> p g c", c=64), in0=qn[:].rearrange("p (g c) -> p g c", c=64),
                                                               in1=cs[s][:, 0, :].unsqueeze(1).to_broadcast([128, 8, 64]), op=ALU.mult),
                              reads=[d_qn, d_cs[s]], writes=[d_t1])
                        qv = qn[:].rearrange("p (g a h c) -> p g a h c", a=2, h=2, c=16)
                        tv = t2[:].rearrange("p (g a h c) -> p g a h c", a=2, h=2, c=16)
                        sv = cs[s][:, 1, :].rearrange("p (a h c) -> p a h c", a=2, h=2)
                        for hf in range(2):
                            for a in range(2):
                                kb.op("dve", lambda E, hf=hf, a=a: E.tensor_tensor(out=tv[:, :, a, hf, :], in0=qv[:, :, a, 1 - hf, :],
                                                                                    in1=sv[:, a, hf, :].unsqueeze(1).to_broadcast([128, 8, 16]), op=ALU.mult),
                                      reads=[d_qn, d_cs[s]], writes=[d_t2])
                        kb.op("dve", lambda E: E.tensor_tensor(out=QR[:], in0=t1[:], in1=t2[:], op=ALU.add), reads=[d_t1, d_t2], writes=[dQR])
                    else:
                        kb.op("dve", lambda E: E.tensor_tensor(out=QR[:].rearrange("p (g c) -> p g c", c=64), in0=qn[:].rearrange("p (g c) -> p g c", c=64),
                                                               in1=G[:].unsqueeze(1).to_broadcast([128, 8, 64]), op=ALU.mult),
                              reads=[d_qn, d_G], writes=[dQR])
                    PT = ps[6 + which]; dPT = d_ps[6 + which]
                    for hh in range(4):
                        kb.op("pe", lambda E, hh=hh: E.transpose(out=PT[:, hh * 128:(hh + 1) * 128], in_=QR[:, hh * 128:(hh + 1) * 128], identity=identb[:]),
                              reads=[dQR, d_identb], writes=[dPT])
                    TT = (qTt if which == 0 else kTt)[s]; dTT = (d_qTt if which == 0 else d_kTt)[s]
                    kb.op("act", lambda E: E.activation(out=TT[:].rearrange("p h t -> p (h t)"), in_=PT[:, 0:512], func=AF.Copy), reads=[dPT], writes=[dTT])
                    if which == 0:
                        store(f"st_q{s}", qT_d[:, :, i * 128:(i + 1) * 128].rearrange("h p t -> p h t"), TT[:], [dTT])
                    elif lat:
                        for hp in range(2):
                            store(f"st_k{s}", kT_lat[hp].rearrange("(h p) t -> p h t", p=128)[:, :, i * 128:(i + 1) * 128], TT[:, hp * 2:hp * 2 + 2, :], [dTT])
                    else:
                        store(f"st_k{s}", kT_ctx[:, :, (i - NLAT_T) * 128:(i - NLAT_T + 1) * 128].rearrange("h p t -> p h t"), TT[:], [dTT])
                kb.op("act", lambda E: E.activation(out=vt[s][:], in_=ps[4][:], func=AF.Copy), reads=[d_ps[4]], writes=[d_vt[s]])
                kb.op("act", lambda E: E.activation(out=ft[s][:], in_=ps[5][:, 0:256], func=AF.Copy), reads=[d_ps[5]], writes=[d_ft[s]])
                if lat:
                    store(f"st_v{s}", v_lat[i // 8][(i % 8) * 128:(i % 8 + 1) * 128, :], vt[s][:], [d_vt[s]])
                    store(f"st_f{s}", f_lat.rearrange("(g t) c -> t g c", g=4)[i * 128:(i + 1) * 128, :, :], ft[s][:].rearrange("p (g c) -> p g c", c=64), [d_ft[s]])
                else:
                    ic = i - NLAT_T
                    store(f"st_v{s}", v_ctx[ic * 128:(ic + 1) * 128, :], vt[s][:], [d_vt[s]])
                    store(f"st_f{s}", f_ctx[ic * 128:(ic + 1) * 128, :], ft[s][:], [d_ft[s]])
            kb.barrier()

    def stage_b(l, last):
        lam_init = 0.8 - 0.6 * math.exp(-0.3 * l)
        with ExitStack() as es:
            sbt = lambda name, shape, dty: es.enter_context(nc.sbuf_tensor(U(name), shape, dty))
            ps = [es.enter_context(nc.psum_tensor(U(f"psF{i}"), [128, 512], F32)) for i in range(8)]
            d_ps = [Dep(f"ps{i}") for i in range(8)]
            X1 = [sbt(f"X1_{i}", [64, 8192], BF16) for i in range(2)]
            Xc = sbt("Xc", [128, 2, 256], BF16)
            CS = sbt("CS", [64, 128], BF16)
            T1 = sbt("T1s", [128, 64, 64], BF16)
            T2 = sbt("T2s", [128, 64, 64], BF16)
            DC = sbt("DC", [128, 2, 512], BF16)
            Bsb = sbt("Bsb", [128, 2, 64, 64], BF16)
            YT = sbt("YT", [64, 2, 2048], BF16)
            YC = sbt("YC", [64, 2, 256], BF16)
            CC = sbt("CC", [64, 2, 64], F32)
            WF = sbt("WF", [64, 4, 64], F32)
            BF_ = sbt("BF", [64, 4], F32)
            MC = sbt("MC", [64, 4, 2, 64], BF16)
            ob = [sbt(f"ob{i}", [64, 512], F32) for i in range(2)]
            dn = {n: Dep(n) for n in "X1_0 X1_1 Xc CS T1 T2 DC Bsb YT YC CC WF BF MC ob0 ob1".split()}
            f_ag_v = f_ag.rearrange("(r g t p) c -> r g t (p c)", r=4, g=4, p=128)

            def load_x1(g):
                s = g % 2
                for r in range(4):
                    kb.dma("sp", f"f_x1{s}", X1[s][r * 16:(r + 1) * 16, :], f_ag_v[r, g], writes=[dn[f"X1_{s}"]])
            load_x1(0)
            if not last:
                kb.dma("sp", "f_xc", Xc[:], f_ctx.rearrange("(k p) c -> p k c", p=128), writes=[dn["Xc"]])
                kb.dma("sp", "f_dc", DC[:], dcd, writes=[dn["DC"]])
            kb.dma("sp", "f_cs", CS[:], cs64, writes=[dn["CS"]])
            kb.dma("sp", "f_cc", CC[:], ccs, writes=[dn["CC"]])
            kb.dma("sp", "f_wf", WF[:], w_f[l].rearrange("g c d -> c g d"), writes=[dn["WF"]])
            kb.dma("sp", "f_bf", BF_[:], b_f[l], writes=[dn["BF"]])
            kb.dma("sp", "f_t1", T1[:], T1d, writes=[dn["T1"]])
            kb.dma("sp", "f_t2", T2[:], T2d, writes=[dn["T2"]])
            for g in range(4):
                for i in range(2):
                    kb.op("pe", lambda E, i=i, g=g: E.matmul(ps[0][0:64, (g * 2 + i) * 64:(g * 2 + i + 1) * 64], lhsT=CC[:, i, :], rhs=WF[:, g, :], start=True, stop=True),
                          reads=[dn["CC"], dn["WF"]], writes=[d_ps[0]])
            kb.op("dve", lambda E: E.tensor_copy(out=MC[:].rearrange("c g i d -> c (g i d)"), in_=ps[0][0:64, 0:512]), reads=[d_ps[0]], writes=[dn["MC"]])
            sc_lat = 1.0 / math.sqrt(8192 * 64); sc_ctx = 1.0 / math.sqrt(256 * 64)
            blkc = [0]
            for g in range(4):
                s = g % 2
                if g + 1 < 4:
                    load_x1(g + 1)
                X1v = X1[s][:].rearrange("t (p c) -> t p c", c=64)
                Bv = Bsb[:].rearrange("p r k c -> p c r k")
                for c4 in range(16):
                    b = 1 + c4 % 2
                    for cc in range(4):
                        c = c4 * 4 + cc
                        kb.op("pe", lambda E, c=c, cc=cc, b=b: E.matmul(ps[b][:, cc * 128:(cc + 1) * 128], lhsT=X1v[:, :, c], rhs=CS[:], start=True, stop=True),
                              reads=[dn[f"X1_{s}"], dn["CS"]], writes=[d_ps[b]])
                    src = ps[b][:].rearrange("p (c r k) -> p c r k", c=4, r=2)
                    dst = Bv[:, c4 * 4:(c4 + 1) * 4, :, :]
                    kb.op("act", lambda E, src=src, dst=dst: E.activation(out=dst[:, :, 0, :], in_=src[:, :, 0, :], func=AF.Copy), reads=[d_ps[b]], writes=[dn["Bsb"]])
                    kb.op("dve", lambda E, src=src, dst=dst: E.tensor_copy(out=dst[:, :, 1, :], in_=src[:, :, 1, :]), reads=[d_ps[b]], writes=[dn["Bsb"]])
                for k8 in range(8):
                    b = 3 + k8 % 2
                    for ki in range(8):
                        kbi = k8 * 8 + ki
                        o = ps[b][0:64, ki * 64:(ki + 1) * 64]
                        kb.op("pe", lambda E, kbi=kbi, o=o: E.matmul(o, lhsT=Bsb[:, 0, kbi, :], rhs=T1[:, kbi, :], start=True, stop=False),
                              reads=[dn["Bsb"], dn["T1"]], writes=[d_ps[b]])
                        kb.op("pe", lambda E, kbi=kbi, o=o: E.matmul(o, lhsT=Bsb[:, 1, kbi, :], rhs=T2[:, kbi, :], start=False, stop=True),
                              reads=[dn["Bsb"], dn["T2"]], writes=[d_ps[b]])
                    src = ps[b][0:64, :].rearrange("c (i r a) -> c i r a", i=8, r=2)
                    dst = YT[:].rearrange("c r (a k) -> c k r a", k=64)[:, k8 * 8:(k8 + 1) * 8, :, :]
                    for r in range(2):
                        if r == 0:
                            kb.op("act", lambda E, src=src, dst=dst, r=r: E.activation(out=dst[:, :, r, :], in_=src[:, :, r, :], func=AF.Copy), reads=[d_ps[b]], writes=[dn["YT"]])
                        else:
                            kb.op("dve", lambda E, src=src, dst=dst, r=r: E.tensor_copy(out=dst[:, :, r, :], in_=src[:, :, r, :]), reads=[d_ps[b]], writes=[dn["YT"]])
                nblk = 4
                if not last:
                    for k in range(2):
                        kb.op("pe", lambda E, k=k, g=g: E.matmul(ps[5][0:64, :], lhsT=Xc[:, k, g * 64:(g + 1) * 64], rhs=DC[:, k, :], start=(k == 0), stop=(k == 1)),
                              reads=[dn["Xc"], dn["DC"]], writes=[d_ps[5]])
                    kb.op("dve", lambda E: E.tensor_copy(out=YC[:].rearrange("c r k -> c (r k)"), in_=ps[5][0:64, :]), reads=[d_ps[5]], writes=[dn["YC"]])
                    nblk = 5
                for blk in range(nblk):
                    b = 6 + blkc[0] % 2
                    so = blkc[0] % 2
                    blkc[0] += 1
                    if blk < 4:
                        n = 512; r0 = YT[:, 0, blk * 512:(blk + 1) * 512]; r1 = YT[:, 1, blk * 512:(blk + 1) * 512]; scl = sc_lat; dy = dn["YT"]
                    else:
                        n = 256; r0 = YC[:, 0, :]; r1 = YC[:, 1, :]; scl = sc_ctx; dy = dn["YC"]
                    kb.op("pe", lambda E, r0=r0, n=n, b=b, g=g: E.matmul(ps[b][0:64, 0:n], lhsT=MC[:, g, 0, :], rhs=r0, start=True, stop=False), reads=[dn["MC"], dy], writes=[d_ps[b]])
                    kb.op("pe", lambda E, r1=r1, n=n, b=b, g=g: E.matmul(ps[b][0:64, 0:n], lhsT=MC[:, g, 1, :], rhs=r1, start=False, stop=True), reads=[dn["MC"], dy], writes=[d_ps[b]])
                    kb.op("act", lambda E, n=n, b=b, so=so, scl=scl, g=g: E.activation(out=ob[so][:, 0:n], in_=ps[b][0:64, 0:n], func=AF.Identity, scale=scl, bias=BF_[:, g:g + 1]),
                          reads=[d_ps[b], dn["BF"]], writes=[dn[f"ob{so}"]])
                    store(f"st_ob{so}", obT_d[g, :, blk * 512:blk * 512 + n], ob[so][:, 0:n], [dn[f"ob{so}"]])
            kb.barrier()

        with ExitStack() as es:
            sbt = lambda name, shape, dty: es.enter_context(nc.sbuf_tensor(U(name), shape, dty))
            ps = [es.enter_context(nc.psum_tensor(U(f"psB{i}"), [128, 512], F32)) for i in range(8)]
            d_ps = [Dep(f"ps{i}") for i in range(8)]
            kTh = [sbt(f"kTh{i}", [128, NKEY], BF16) for i in range(2)]
            vh = [sbt(f"vh{i}", [128, NKT, 128], BF16) for i in range(2)]
            qTh = [sbt(f"qTh{i}", [128, TOKS], BF16) for i in range(2)]
            NPT = 3
            pt = [[sbt(f"pt{i}_{c}", [128, 512], BF16) for c in range(2)] for i in range(NPT)]
            acc = [sbt(f"acc{c}", [128, 512], F32) for c in range(2)]
            ones = sbt("ones", [128, 128], F32)
            lv = sbt("lv", [128, 4, 64], F32); lt = sbt("lt", [128, 2, 64], F32); l2 = sbt("l2", [128, 2], F32)
            nlam = sbt("nlam", [128, 1], F32); gs = sbt("gs", [128, 1], F32)
            rec = [sbt(f"rec{c}", [128, 512], F32) for c in range(2)]
            o0 = sbt("o0", [128, 512], F32); o1 = sbt("o1", [128, 512], F32)
            osq = sbt("osq", [128, 512], F32); rs = sbt("rs", [128, 512], F32)
            of = [sbt(f"of{i}", [128, 512], F32) for i in range(2)]
            d_kv = [Dep("kv0"), Dep("kv1")]
            d_pt = [[Dep(f"pt{i}_{c}") for c in range(2)] for i in range(NPT)]
            d_acc = [Dep("acc0"), Dep("acc1")]
            dm = {n: Dep(n) for n in "ones lv lt l2 nlam gs rec0 rec1 o0 o1 osq rs of0 of1".split()}
            kb.op("pool", lambda E: E.memset(ones[:], 1.0), writes=[dm["ones"]])
            kb.dma("sp", "a_lv", lv[:].rearrange("p a c -> p (a c)"), lamv[l].partition_broadcast(128), writes=[dm["lv"]])
            kb.dma("sp", "a_gs", gs[:], g_sub[l], writes=[dm["gs"]])
            lvv = lv[:].rearrange("p (i j) c -> p i j c", j=2)
            kb.op("dve", lambda E: E.tensor_tensor(out=lt[:], in0=lvv[:, :, 0, :], in1=lvv[:, :, 1, :], op=ALU.mult), reads=[dm["lv"]], writes=[dm["lt"]])
            kb.op("dve", lambda E: E.tensor_reduce(out=l2[:], in_=lt[:], axis=AX.X, op=ALU.add), reads=[dm["lt"]], writes=[dm["l2"]])
            kb.op("act", lambda E: E.activation(out=l2[:], in_=l2[:], func=AF.Exp), reads=[dm["l2"]], writes=[dm["l2"]])
            kb.op("dve", lambda E: E.tensor_tensor(out=nlam[:], in0=l2[:, 1:2], in1=l2[:, 0:1], op=ALU.subtract), reads=[dm["l2"]], writes=[dm["nlam"]])
            kb.op("dve", lambda E: E.tensor_scalar(out=nlam[:], in0=nlam[:], scalar1=-lam_init, scalar2=None, op0=ALU.add), reads=[dm["nlam"]], writes=[dm["nlam"]])
            kb.op("dve", lambda E: E.tensor_scalar(out=gs[:], in0=gs[:], scalar1=(1.0 - lam_init), scalar2=None, op0=ALU.mult), reads=[dm["gs"]], writes=[dm["gs"]])
            kT_ag_v = [a.rearrange("(r h p) t -> h p r t", r=4, h=2) for a in kT_ag]
            v_ag_v = [a.rearrange("(r t p) e -> p r t e", r=4, p=128) for a in v_ag]
            v_ctx_v = v_ctx.rearrange("(t p) e -> p t e", p=128)

            def load_head(h):
                s = h % 2
                kb.dma("sp", f"a_k{s}", kTh[s][:, 0:8192].rearrange("p (r t) -> p r t", r=4), kT_ag_v[h // 2][h % 2], writes=[d_kv[s]])
                kb.dma("sp", f"a_k{s}", kTh[s][:, 8192:NKEY], kT_ctx[h], writes=[d_kv[s]])
                for half in range(2):
                    for r in range(4):
                        kt0 = r * 16 + half * 8
                        kb.dma("sp", f"a_k{s}", vh[s][:, kt0:kt0 + 8, :], v_ag_v[half][:, r, :, h * 128:(h + 1) * 128], writes=[d_kv[s]])
                kb.dma("sp", f"a_k{s}", vh[s][:, 64:66, :], v_ctx_v[:, :, h * 128:(h + 1) * 128], writes=[d_kv[s]])
                kb.dma("sp", f"a_k{s}", qTh[s][:], qT_d[h], writes=[d_kv[s]])

            load_head(0)
            blk_id = 0
            for h in range(4):
                s = h % 2
                if h + 1 < 4:
                    load_head(h + 1)
                for qb in range(4 if last else 5):
                    if qb < 4:
                        q0 = qb * 512; nq = 512; kts = list(range(NKT))
                    else:
                        q0 = 2048; nq = 256; kts = [64, 65]
                    nk = len(kts)

                    def scores(idx):
                        kt = kts[idx]; sl = idx % 2
                        for c in range(2):
                            kb.op("pe", lambda E, c=c, kt=kt, sl=sl: E.matmul(ps[sl * 2 + c][:, 0:nq], lhsT=kTh[s][c * 64:(c + 1) * 64, kt * 128:(kt + 1) * 128],
                                                                           rhs=qTh[s][c * 64:(c + 1) * 64, q0:q0 + nq], start=True, stop=True),
                                  reads=[d_kv[s]], writes=[d_ps[sl * 2 + c]])

                    def exps(idx):
                        sl = idx % 2; p = idx % NPT
                        for c in range(2):
                            kb.op("act", lambda E, c=c, sl=sl, p=p: E.activation(out=pt[p][c][:, 0:nq], in_=ps[sl * 2 + c][:, 0:nq], func=AF.Exp, scale=0.125),
                                  reads=[d_ps[sl * 2 + c]], writes=[d_pt[p][c]])

                    def av(idx):
                        kt = kts[idx]; p = idx % NPT
                        for c in range(2):
                            kb.op("pe", lambda E, c=c, kt=kt, p=p, idx=idx: E.matmul(ps[4 + c][:, 0:nq], lhsT=vh[s][:, kt, :], rhs=pt[p][c][:, 0:nq],
                                                                                  start=(idx == 0), stop=(idx == nk - 1)),
                                  reads=[d_kv[s], d_pt[p][c]], writes=[d_ps[4 + c]])
                            e = "dve" if c == 0 else "pool"
                            if idx == 0:
                                kb.op(e, lambda E, c=c, p=p: E.tensor_copy(out=acc[c][:, 0:nq], in_=pt[p][c][:, 0:nq]), reads=[d_pt[p][c]], writes=[d_acc[c]])
                            else:
                                kb.op(e, lambda E, c=c, p=p: E.tensor_tensor(out=acc[c][:, 0:nq], in0=acc[c][:, 0:nq], in1=pt[p][c][:, 0:nq], op=ALU.add),
                                      reads=[d_pt[p][c], d_acc[c]], writes=[d_acc[c]])

                    scores(0)
                    for idx in range(nk):
                        exps(idx)
                        if idx + 1 < nk:
                            scores(idx + 1)
                        av(idx)
                    for c in range(2):
                        kb.op("pe", lambda E, c=c: E.matmul(ps[6 + c][:, 0:nq], lhsT=ones[:], rhs=acc[c][:, 0:nq], start=True, stop=True),
                              reads=[dm["ones"], d_acc[c]], writes=[d_ps[6 + c]])
                        kb.op("dve", lambda E, c=c: E.reciprocal(out=rec[c][:, 0:nq], in_=ps[6 + c][:, 0:nq]), reads=[d_ps[6 + c]], writes=[dm[f"rec{c}"]])
                    kb.op("dve", lambda E: E.tensor_tensor(out=o0[:, 0:nq], in0=ps[4][:, 0:nq], in1=rec[0][:, 0:nq], op=ALU.mult), reads=[d_ps[4], dm["rec0"]], writes=[dm["o0"]])
                    kb.op("dve", lambda E: E.tensor_tensor(out=o1[:, 0:nq], in0=ps[5][:, 0:nq], in1=rec[1][:, 0:nq], op=ALU.mult), reads=[d_ps[5], dm["rec1"]], writes=[dm["o1"]])
                    kb.op("dve", lambda E: E.scalar_tensor_tensor(out=o0[:, 0:nq], in0=o1[:, 0:nq], scalar=nlam[:, 0:1], in1=o0[:, 0:nq], op0=ALU.mult, op1=ALU.add),
                          reads=[dm["o0"], dm["o1"], dm["nlam"]], writes=[dm["o0"]])
                    kb.op("dve", lambda E: E.tensor_tensor(out=osq[:, 0:nq], in0=o0[:, 0:nq], in1=o0[:, 0:nq], op=ALU.mult), reads=[dm["o0"]], writes=[dm["osq"]])
                    kb.op("pe", lambda E: E.matmul(ps[6][:, 0:nq], lhsT=ones[:], rhs=osq[:, 0:nq], start=True, stop=True), reads=[dm["ones"], dm["osq"]], writes=[d_ps[6]])
                    kb.op("dve", lambda E: E.tensor_scalar(out=rs[:, 0:nq], in0=ps[6][:, 0:nq], scalar1=1.0 / 128, scalar2=EPS, op0=ALU.mult, op1=ALU.add),
                          reads=[d_ps[6]], writes=[dm["rs"]])
                    kb.op("act", lambda E: E.activation(out=rs[:, 0:nq], in_=rs[:, 0:nq], func=AF.Ln), reads=[dm["rs"]], writes=[dm["rs"]])
                    kb.op("act", lambda E: E.activation(out=rs[:, 0:nq], in_=rs[:, 0:nq], func=AF.Exp, scale=-0.5), reads=[dm["rs"]], writes=[dm["rs"]])
                    so = blk_id % 2
                    kb.op("dve", lambda E, so=so: E.scalar_tensor_tensor(out=of[so][:, 0:nq], in0=o0[:, 0:nq], scalar=gs[:, 0:1], in1=rs[:, 0:nq], op0=ALU.mult, op1=ALU.mult),
                          reads=[dm["o0"], dm["gs"], dm["rs"]], writes=[dm[f"of{so}"]])
                    store(f"st_oa{so}", oaT_d[h, :, q0:q0 + nq], of[so][:, 0:nq], [dm[f"of{so}"]])
                    blk_id += 1
            kb.barrier()

    def stage_c(l, last, x_src, x_dst):
        with ExitStack() as es0:
            sb = lambda name, shape, dty: es0.enter_context(nc.sbuf_tensor(U(name), shape, dty))
            ps = [es0.enter_context(nc.psum_tensor(U(f"psC{i}"), [128, 512], F32)) for i in range(8)]
            d_ps = [Dep(f"ps{i}") for i in range(8)]
            wC = sb("wC", [128, 8, 4608], BF16)
            wBR = sb("wBR", [128, 4, 1024], BF16)
            wBRb = sb("wBRb", [64, 4, 1024], BF16)
            wBRc = sb("wBRc", [64, 4, 1024], BF16)
            wO = sb("wO", [128, 8, 1024], BF16)
            wS = sb("wS", [128, 4, 128], BF16)
            LG = sb("LG", [128, 256], F32); LB = sb("LB", [128, 256], F32)
            BS = sb("BS", [64, 4, 128], F32)
            modT = sb("modT_s", [128, 24, 2], F32)
            G = [sb(f"G{i}", [128, 1024], F32) for i in range(2)]
            ident = sb("ident", [128, 128], F32); ones = sb("ones", [128, 128], F32)
            dw = {n: Dep(n) for n in "wC wBR wBRb wBRc wO wS LG LB BS modT G ident ones".split()}
            with ExitStack() as es:
                sbt = lambda name, shape, dty: es.enter_context(nc.sbuf_tensor(U(name), shape, dty))
                wst = [sbt(f"wst{i}", [128, 8, 512], F32) for i in range(2)]
                d_wst = [Dep("wst0"), Dep("wst1")]
                wsf = sbt("wsf", [128, 4, 128], F32); diag = sbt("diag", [128, 128], F32)
                d_wsf = Dep("wsf"); d_diag = Dep("diag")
                make_ident(ident, dw["ident"])
                kb.op("pool", lambda E: E.memset(ones[:], 1.0), writes=[dw["ones"]])
                kb.dma("sp", "c_mod", modT[:], modT_d, writes=[dw["modT"]])
                kb.dma("sp", "c_lg", LG[:], ln_g[l].partition_broadcast(128), writes=[dw["LG"]])
                kb.dma("sp", "c_lb", LB[:], ln_b[l].partition_broadcast(128), writes=[dw["LB"]])
                kb.dma("sp", "c_bs", BS[:], bs_d[l], writes=[dw["BS"]])
                kb.dma("sp", "c_ws", wsf[:], w_sT[l], writes=[d_wsf])
                kb.op("dve", lambda E: E.tensor_copy(out=wS[:], in_=wsf[:]), reads=[d_wsf], writes=[dw["wS"]])
                for var in range(1 if last else 2):
                    for k in range(8):
                        kb.op("dve", lambda E, k=k, var=var: E.tensor_scalar(out=diag[:], in0=ident[:], scalar1=modT[:, 16 + k, var:var + 1], scalar2=None, op0=ALU.mult),
                              reads=[dw["ident"], dw["modT"]], writes=[d_diag])
                        b = k // 4
                        kb.op("pe", lambda E, k=k, b=b: E.matmul(ps[b][:, (k % 4) * 128:(k % 4 + 1) * 128], lhsT=ones[:], rhs=diag[:], start=True, stop=True),
                              reads=[dw["ones"], d_diag], writes=[d_ps[b]])
                    for b in range(2):
                        kb.op("act", lambda E, b=b, var=var: E.activation(out=G[var][:, b * 512:(b + 1) * 512], in_=ps[b][:], func=AF.Copy), reads=[d_ps[b]], writes=[dw["G"]])
                cnt = [0]

                def load_cast(src_v, dst, ddst, np_=128, nk=8):
                    s = cnt[0] % 2; cnt[0] += 1
                    kb.dma("sp", f"c_w{s}", wst[s][0:np_, 0:nk, :], src_v, writes=[d_wst[s]])
                    e = ("dve", "pool", "act")[cnt[0] % 3]
                    if e == "act":
                        kb.op("act", lambda E: E.activation(out=dst, in_=wst[s][0:np_, 0:nk, :], func=AF.Copy), reads=[d_wst[s]], writes=[ddst])
                    else:
                        kb.op(e, lambda E: E.tensor_copy(out=dst, in_=wst[s][0:np_, 0:nk, :]), reads=[d_wst[s]], writes=[ddst])
                w_in_v = w_in[l].rearrange("(k p) n -> p k n", p=128)
                for g in range(9):
                    load_cast(w_in_v[:, :, O_U + g * 512:O_U + (g + 1) * 512], wC[:, :, g * 512:(g + 1) * 512], dw["wC"])
                w_br_v = w_br[l, 0:512, :].rearrange("(k p) n -> p k n", p=128)
                w_brb_v = w_br[l, 512:768, :].rearrange("(g c) n -> c g n", c=64)
                w_brc_v = w_br[l, 768:1024, :].rearrange("(g c) n -> c g n", c=64)
                w_out_v = w_out[l].rearrange("(k p) n -> p k n", p=128)
                for g in range(2):
                    cs_ = slice(g * 512, (g + 1) * 512)
                    load_cast(w_br_v[:, :, cs_], wBR[:, :, cs_], dw["wBR"], nk=4)
                    load_cast(w_brb_v[:, :, cs_], wBRb[:, :, cs_], dw["wBRb"], np_=64, nk=4)
                    load_cast(w_brc_v[:, :, cs_], wBRc[:, :, cs_], dw["wBRc"], np_=64, nk=4)
                    load_cast(w_out_v[:, :, cs_], wO[:, :, cs_], dw["wO"])
                kb.barrier()

            hTb = sb("hTb", [128, 8, BT], BF16)
            oab = sb("oab", [128, 4, BT], F32)
            obb = sb("obb", [64, 4, BT], F32)
            uT = sb("uT", [64, 4, BT], F32)
            gcT = sb("gcT", [64, 4, BT], F32)
            sgt = [sb(f"sgt{i}", [128, BT], F32) for i in range(2)]
            og = sb("og", [128, 4, BT], BF16)
            ogb = sb("ogb", [64, 4, BT], BF16)
            ogc = sb("ogc", [64, 4, BT], BF16)
            st6 = sb("st6", [128, 6], F32); mv = sb("mv", [128, 2], F32); rstd = sb("rstd", [128, 1], F32)
            vcn = sb("vcn", [128, 256], F32); vnb = sb("vnb", [128, 256], BF16)
            sT = sb("sT", [64, 4, 128], F32)
            mm = [sb(f"mm{i}", [128, 3, BT], F32) for i in range(2)]
            t0 = sb("t0", [128, BT], F32); t1 = sb("t1", [128, BT], F32)
            yT = sb("yT", [128, 8, BT], BF16)
            xt = [sb(f"xt{i}", [128, 1024], F32) for i in range(2)]
            tmpo = sb("tmpo", [128, 1024], F32)
            dn = {n: Dep(n) for n in "hTb oab obb uT gcT sgt0 sgt1 og ogb ogc st6 mv rstd vcn vnb sT mm0 mm1 t0 t1 yT xt0 xt1 tmpo".split()}

            def proj_fm(col0, M, nt, bank):
                for k in range(8):
                    kb.op("pe", lambda E, k=k: E.matmul(ps[bank][0:M, 0:nt], lhsT=wC[:, k, col0:col0 + M], rhs=hTb[:, k, 0:nt], start=(k == 0), stop=(k == 7)),
                          reads=[dw["wC"], dn["hTb"]], writes=[d_ps[bank]])
            pj = [0]

            def next_bank():
                pj[0] += 1
                return 6 + pj[0] % 2
            GC0 = O_GATE - O_U
            MC0 = O_MERGE - O_U
            for blk in range(8 if last else 9):
                tok0 = blk * BT; nt = BT; var = 0 if tok0 < 2048 else 1
                ntile = nt // 128
                kb.dma("sp", "c_h", hTb[:, :, 0:nt], hT_d[:, :, tok0:tok0 + nt].rearrange("k p t -> p k t"), writes=[dn["hTb"]])
                kb.dma("sp", "c_oa", oab[:, :, 0:nt], oaT_d[:, :, tok0:tok0 + nt].rearrange("h p t -> p h t"), writes=[dn["oab"]])
                kb.dma("sp", "c_ob", obb[:, :, 0:nt], obT_d[:, :, tok0:tok0 + nt].rearrange("g c t -> c g t"), writes=[dn["obb"]])
                for g in range(4):
                    b = next_bank()
                    proj_fm(g * 64, 64, nt, b)
                    kb.op("act", lambda E, g=g, b=b: E.activation(out=uT[:, g, 0:nt], in_=ps[b][0:64, 0:nt], func=AF.Copy), reads=[d_ps[b]], writes=[dn["uT"]])
                for br in range(2):
                    for g in range(4):
                        b = next_bank(); s = g % 2
                        proj_fm(GC0 + 512 + br * 256 + g * 64, 64, nt, b)
                        kb.op("act", lambda E, b=b, s=s: E.activation(out=sgt[s][0:64, 0:nt], in_=ps[b][0:64, 0:nt], func=AF.Sigmoid), reads=[d_ps[b]], writes=[dn[f"sgt{s}"]])
                        if br == 1:
                            kb.op("dve", lambda E, g=g, b=b, s=s: E.tensor_tensor(out=gcT[:, g, 0:nt], in0=ps[b][0:64, 0:nt], in1=sgt[s][0:64, 0:nt], op=ALU.mult),
                                  reads=[d_ps[b], dn[f"sgt{s}"]], writes=[dn["gcT"]])
                        else:
                            kb.op("dve", lambda E, b=b, s=s: E.tensor_tensor(out=sgt[s][0:64, 0:nt], in0=ps[b][0:64, 0:nt], in1=sgt[s][0:64, 0:nt], op=ALU.mult),
                                  reads=[d_ps[b], dn[f"sgt{s}"]], writes=[dn[f"sgt{s}"]])
                            kb.op("dve", lambda E, g=g, s=s: E.tensor_tensor(out=ogb[:, g, 0:nt], in0=sgt[s][0:64, 0:nt], in1=obb[:, g, 0:nt], op=ALU.mult),
                                  reads=[dn[f"sgt{s}"], dn["obb"]], writes=[dn["ogb"]])
                for j in range(4):
                    b = next_bank(); s = j % 2
                    proj_fm(GC0 + j * 128, 128, nt, b)
                    kb.op("act", lambda E, b=b, s=s: E.activation(out=sgt[s][:, 0:nt], in_=ps[b][:, 0:nt], func=AF.Sigmoid), reads=[d_ps[b]], writes=[dn[f"sgt{s}"]])
                    kb.op("dve", lambda E, b=b, s=s: E.tensor_tensor(out=sgt[s][:, 0:nt], in0=ps[b][:, 0:nt], in1=sgt[s][:, 0:nt], op=ALU.mult),
                          reads=[d_ps[b], dn[f"sgt{s}"]], writes=[dn[f"sgt{s}"]])
                    kb.op("dve", lambda E, j=j, s=s: E.tensor_tensor(out=og[:, j, 0:nt], in0=sgt[s][:, 0:nt], in1=oab[:, j, 0:nt], op=ALU.mult),
                          reads=[dn[f"sgt{s}"], dn["oab"]], writes=[dn["og"]])
                for t in range(ntile):
                    b = next_bank()
                    for k in range(8):
                        kb.op("pe", lambda E, k=k, t=t, b=b: E.matmul(ps[b][:, 0:256], lhsT=hTb[:, k, t * 128:(t + 1) * 128], rhs=wC[:, k, 256:512], start=(k == 0), stop=(k == 7)),
                              reads=[dw["wC"], dn["hTb"]], writes=[d_ps[b]])
                    kb.op("dve", lambda E, b=b: E.bn_stats(out=st6[:], in_=ps[b][:, 0:256]), reads=[d_ps[b]], writes=[dn["st6"]])
                    kb.op("dve", lambda E: E.bn_aggr(out=mv[:], in_=st6[:]), reads=[dn["st6"]], writes=[dn["mv"]])
                    kb.op("dve", lambda E: E.tensor_scalar(out=rstd[:], in0=mv[:, 1:2], scalar1=EPS, scalar2=None, op0=ALU.add), reads=[dn["mv"]], writes=[dn["rstd"]])
                    kb.op("act", lambda E: E.activation(out=rstd[:], in_=rstd[:], func=AF.Sqrt), reads=[dn["rstd"]], writes=[dn["rstd"]])
                    kb.op("dve", lambda E: E.reciprocal(out=rstd[:], in_=rstd[:]), reads=[dn["rstd"]], writes=[dn["rstd"]])
                    kb.op("dve", lambda E, b=b: E.tensor_scalar(out=vcn[:], in0=ps[b][:, 0:256], scalar1=mv[:, 0:1], scalar2=rstd[:, 0:1], op0=ALU.subtract, op1=ALU.mult),
                          reads=[d_ps[b], dn["mv"], dn["rstd"]], writes=[dn["vcn"]])
                    kb.op("dve", lambda E: E.tensor_tensor(out=vcn[:], in0=vcn[:], in1=LG[:], op=ALU.mult), reads=[dn["vcn"], dw["LG"]], writes=[dn["vcn"]])
                    kb.op("dve", lambda E: E.tensor_tensor(out=vnb[:], in0=vcn[:], in1=LB[:], op=ALU.add), reads=[dn["vcn"], dw["LB"]], writes=[dn["vnb"]])
                    b2 = next_bank()
                    for g in range(4):
                        kb.op("pe", lambda E, g=g, b2=b2: E.matmul(ps[b2][0:64, g * 128:(g + 1) * 128], lhsT=vnb[:, g * 64:(g + 1) * 64], rhs=wS[:, g, :], start=True, stop=True),
                              reads=[dn["vnb"], dw["wS"]], writes=[d_ps[b2]])
                    kb.op("dve", lambda E, b2=b2: E.tensor_tensor(out=sT[:].rearrange("c g p -> c (g p)"), in0=ps[b2][0:64, :], in1=BS[:].rearrange("c g p -> c (g p)"), op=ALU.add),
                          reads=[d_ps[b2], dw["BS"]], writes=[dn["sT"]])
                    kb.op("dve", lambda E, t=t: E.tensor_tensor(out=sT[:], in0=sT[:], in1=uT[:, :, t * 128:(t + 1) * 128], op=ALU.mult), reads=[dn["sT"], dn["uT"]], writes=[dn["sT"]])
                    kb.op("dve", lambda E, t=t: E.tensor_tensor(out=ogc[:, :, t * 128:(t + 1) * 128], in0=sT[:], in1=gcT[:, :, t * 128:(t + 1) * 128], op=ALU.mult),
                          reads=[dn["sT"], dn["gcT"]], writes=[dn["ogc"]])
                for dc in range(8):
                    dsl = slice(dc * 128, (dc + 1) * 128)
                    for e in range(4):
                        kb.op("pe", lambda E, e=e: E.matmul(ps[0][:, 0:nt], lhsT=wBR[:, e, dsl], rhs=og[:, e, 0:nt], start=(e == 0), stop=(e == 3)),
                              reads=[dw["wBR"], dn["og"]], writes=[d_ps[0]])
                    for g in range(4):
                        kb.op("pe", lambda E, g=g: E.matmul(ps[1][:, 0:nt], lhsT=wBRb[:, g, dsl], rhs=ogb[:, g, 0:nt], start=(g == 0), stop=(g == 3)),
                              reads=[dw["wBRb"], dn["ogb"]], writes=[d_ps[1]])
                    for g in range(4):
                        kb.op("pe", lambda E, g=g: E.matmul(ps[2][:, 0:nt], lhsT=wBRc[:, g, dsl], rhs=ogc[:, g, 0:nt], start=(g == 0), stop=(g == 3)),
                              reads=[dw["wBRc"], dn["ogc"]], writes=[d_ps[2]])
                    ms = dc % 2
                    for i in range(3):
                        proj_fm(MC0 + i * 1024 + dc * 128, 128, nt, 3 + i)
                        kb.op("act", lambda E, i=i, ms=ms: E.activation(out=mm[ms][:, i, 0:nt], in_=ps[3 + i][:, 0:nt], func=AF.Sigmoid), reads=[d_ps[3 + i]], writes=[dn[f"mm{ms}"]])
                    kb.op("dve", lambda E, ms=ms: E.tensor_tensor(out=t0[:, 0:nt], in0=ps[0][:, 0:nt], in1=mm[ms][:, 0, 0:nt], op=ALU.mult), reads=[d_ps[0], dn[f"mm{ms}"]], writes=[dn["t0"]])
                    kb.op("dve", lambda E, ms=ms: E.tensor_tensor(out=t1[:, 0:nt], in0=ps[1][:, 0:nt], in1=mm[ms][:, 1, 0:nt], op=ALU.mult), reads=[d_ps[1], dn[f"mm{ms}"]], writes=[dn["t1"]])
                    kb.op("dve", lambda E: E.tensor_tensor(out=t0[:, 0:nt], in0=t0[:, 0:nt], in1=t1[:, 0:nt], op=ALU.add), reads=[dn["t0"], dn["t1"]], writes=[dn["t0"]])
                    kb.op("dve", lambda E, ms=ms: E.tensor_tensor(out=t1[:, 0:nt], in0=ps[2][:, 0:nt], in1=mm[ms][:, 2, 0:nt], op=ALU.mult), reads=[d_ps[2], dn[f"mm{ms}"]], writes=[dn["t1"]])
                    kb.op("dve", lambda E, dc=dc: E.tensor_tensor(out=yT[:, dc, 0:nt], in0=t0[:, 0:nt], in1=t1[:, 0:nt], op=ALU.add), reads=[dn["t0"], dn["t1"]], writes=[dn["yT"]])
                for t in range(ntile):
                    gt = (tok0 // 128) + t
                    s = gt % 2
                    kb.dma("sp", f"c_x{s}", xt[s][:], x_src[gt * 128:(gt + 1) * 128, :], writes=[dn[f"xt{s}"]])
                    for cb in range(2):
                        b = next_bank()
                        for k in range(8):
                            kb.op("pe", lambda E, k=k, cb=cb, b=b, t=t: E.matmul(ps[b][:], lhsT=yT[:, k, t * 128:(t + 1) * 128], rhs=wO[:, k, cb * 512:(cb + 1) * 512],
                                                                              start=(k == 0), stop=(k == 7)),
                                  reads=[dn["yT"], dw["wO"]], writes=[d_ps[b]])
                        kb.op("dve", lambda E, cb=cb, b=b: E.tensor_tensor(out=tmpo[:, cb * 512:(cb + 1) * 512], in0=ps[b][:], in1=G[var][:, cb * 512:(cb + 1) * 512], op=ALU.mult),
                              reads=[d_ps[b], dw["G"]], writes=[dn["tmpo"]])
                    kb.op("dve", lambda E, s=s: E.tensor_tensor(out=xt[s][:], in0=xt[s][:], in1=tmpo[:], op=ALU.add), reads=[dn["tmpo"], dn[f"xt{s}"]], writes=[dn[f"xt{s}"]])
                    store(f"st_x{s}", x_dst[gt * 128:(gt + 1) * 128, :], xt[s][:], [dn[f"xt{s}"]])
            kb.barrier()

    import os
    FS = os.environ.get("FSTOP", "")
    for l in range(1 if FS else 2):
        last = l == 1
        x_src = x_in if l == 0 else x1
        stage_a(l, x_src)
        if FS == "a":
            break
        for i in range(2):
            collective("AllGather", kT_lat[i], kT_ag[i])
            collective("AllGather", v_lat[i], v_ag[i])
        collective("AllGather", f_lat, f_ag)
        kb.barrier()
        if FS == "ag":
            break
        stage_b(l, last)
        if FS == "b":
            break
        stage_c(l, last, x_src, y_out if last else x1)
    kb.barrier()
    for k in sorted(out_keys):
        nc.gpsimd.wait_ge(kb.sems[k], kb.cnt[k])
    return kb


import numpy as np
import ml_dtypes
BF = ml_dtypes.bfloat16
NLT = 2048


def rope_tabs():
    n = 8192
    row = np.repeat(np.arange(n // 64), 64).astype(np.float32)
    col = np.tile(np.arange(64), n // 64).astype(np.float32)
    freqs = (10000.0 ** (-np.arange(0, 32, 2, dtype=np.float32) / 32)).astype(np.float32)
    ar = row[:, None] * freqs; ac = col[:, None] * freqs
    ang = np.concatenate([ar, ar, ac, ac], -1)
    cos = np.cos(ang).astype(np.float32); sin = np.sin(ang).astype(np.float32)
    sgn = np.tile(np.concatenate([-np.ones(16), np.ones(16)]), 2).astype(np.float32)
    return cos, sin * sgn


def col_layout(v, k):
    return np.ascontiguousarray(v.reshape(k, 128).T)


def fourier_consts(j):
    t = np.arange(64)[:, None]; kb = np.arange(64)[None, :]
    a = 2 * np.pi * ((t * kb) % 64) / 64
    cs64 = np.concatenate([np.cos(a), -np.sin(a)], 1).astype(BF)
    p = np.arange(128)[:, None, None]; kbb = np.arange(64)[None, :, None]; ka = (32 * j + np.arange(32))[None, None, :]
    a = 2 * np.pi * ((p * (64 * ka + kbb)) % 8192) / 8192
    T1 = np.concatenate([np.cos(a), -np.sin(a)], 2).astype(BF)
    T2 = np.concatenate([np.sin(a), np.cos(a)], 2).astype(BF)
    n = np.arange(128)[:, None, None]; ch = np.arange(2)[None, :, None]; k = np.arange(256)[None, None, :]
    a = 2 * np.pi * (((ch * 128 + n) * k) % 256) / 256
    dctx = np.concatenate([np.cos(a), -np.sin(a)], 2).astype(BF)
    c1 = np.arange(64)[:, None]; c2 = np.arange(64)[None, :]
    a = 2 * np.pi * ((c1 * c2) % 64) / 64
    ccs = np.stack([np.cos(a), np.sin(a)], 1).astype(np.float32)
    return dict(cs64=cs64, T1j=np.ascontiguousarray(T1), T2j=np.ascontiguousarray(T2), dctx=np.ascontiguousarray(dctx), ccs=np.ascontiguousarray(ccs))


def fused_inputs(I):
    cos, ssin = rope_tabs()
    C = np.ascontiguousarray
    shared = {
        "w_ada": C(I['w_ada']), "b_ada": C(np.stack([col_layout(I['b_ada'][l], 24) for l in range(2)])),
        "g_norm": C(np.stack([col_layout(I['g_norm'][l], 8) for l in range(2)])),
        "w_in": C(I['w_in']), "g_q": C(I['g_q']), "g_k": C(I['g_k']),
        "lamv": C(np.concatenate([I['lam_q1'], I['lam_k1'], I['lam_q2'], I['lam_k2']], 1)),
        "g_sub": C(I['g_sub'].reshape(2, 128, 1)),
        "w_f": C(I['w_f']), "b_f": C(I['b_f'].transpose(0, 2, 1)),
        "ln_g": C(I['ln_g']), "ln_b": C(I['ln_b']),
        "w_sT": C(I['w_s'].transpose(0, 3, 1, 2)),
        "bs64": C(np.broadcast_to(I['b_s'][:, None, :, :], (2, 64, 4, 128))),
        "w_br": C(np.concatenate([I['w_br_a'], I['w_br_b'], I['w_br_c']], 1)), "w_out": C(I['w_out']),
    }
    maps = []
    for core in range(8):
        b, j = core // 4, core % 4
        m = dict(shared)
        m["x_tok"] = C(np.concatenate([I['x'][b, j * NLT:(j + 1) * NLT], I['ctx'][b]], 0))
        m["cvec"] = C(np.stack([col_layout(I['c'][b], 8), col_layout(I['c_ctx'], 8)], -1))
        m["cos"] = C(cos[j * NLT:(j + 1) * NLT]); m["ssin"] = C(ssin[j * NLT:(j + 1) * NLT])
        m.update(fourier_consts(j))
        maps.append(m)
    return maps


def kernel(**inputs):
    I = {k: np.asarray(v, dtype=np.float32) for k, v in inputs.items()}
    kb = KB()
    build_fused(kb)
    res = kb.run(fused_inputs(I))
    out = np.empty((2, 8192, 1024), np.float32)
    for core in range(8):
        b, j = core // 4, core % 4
        out[b, j * NLT:(j + 1) * NLT] = np.asarray(res.results[core]["y"])
    return out
```

```python
import numpy as np
import concourse.bass as bass
import concourse.mybir as mybir
from concourse.bass_utils import run_bass_kernel_spmd

F32 = mybir.dt.float32
BF16 = mybir.dt.bfloat16
AF = mybir.ActivationFunctionType
ALU = mybir.AluOpType
AX = mybir.AxisListType


class Dep:
    __slots__ = ("w", "r", "name")

    def __init__(self, name=""):
        self.w = None
        self.r = []
        self.name = name


class KB:
    COMPUTE = ("pe", "act", "dve", "pool")

    def __init__(self):
        self.nc = bass.Bass("TRN2", target_bir_lowering=False)
        nc = self.nc
        self.eng = {"pe": nc.tensor, "act": nc.scalar, "dve": nc.vector,
                    "pool": nc.gpsimd, "sp": nc.sync}
        self.sems = {}
        self.cnt = {}
        for e in self.COMPUTE:
            self.sems[e] = nc.alloc_semaphore(name="s_" + e)
            self.cnt[e] = 0
        self.seen = {e: {} for e in self.eng}
        self.n_inst = 0

    def _sem(self, key):
        if key not in self.sems:
            self.sems[key] = self.nc.alloc_semaphore(name="d_" + str(key))
            self.cnt[key] = 0
        return self.sems[key]

    def _waits(self, e, reads, writes):
        need = {}

        def add(t, war=False):
            if t is None:
                return
            sk, v = t
            if sk == e and (war or e == "pe"):
                return
            if need.get(sk, 0) < v:
                need[sk] = v
        for d in reads:
            add(d.w)
        for d in writes:
            add(d.w)
            for t in d.r:
                add(t, war=True)
        E = self.eng[e]
        for sk, v in need.items():
            if self.seen[e].get(sk, 0) >= v:
                continue
            E.wait_ge(self.sems[sk], v)
            self.seen[e][sk] = v

    def _mark(self, tok, reads, writes):
        for d in reads:
            d.r.append(tok)
            if len(d.r) > 64:
                m = {}
                for sk, v in d.r:
                    if m.get(sk, 0) < v:
                        m[sk] = v
                d.r = list(m.items())
        for d in writes:
            d.w = tok
            d.r = []

    def op(self, e, fn, reads=(), writes=()):
        self._waits(e, reads, writes)
        inst = fn(self.eng[e])
        self.cnt[e] += 1
        inst.then_inc(self.sems[e], 1)
        self._mark((e, self.cnt[e]), reads, writes)
        self.n_inst += 1
        return inst

    def mm(self, fn, reads=(), writes=(), last=True):
        return self.op("pe", fn, reads, writes)

    def dma(self, q, key, out, in_, reads=(), writes=(), **kw):
        sem = self._sem(key)
        self._waits(q, reads, writes)
        inst = self.eng[q].dma_start(out=out, in_=in_, **kw)
        self.cnt[key] += 16
        inst.then_inc(sem, 16)
        self._mark((key, self.cnt[key]), reads, writes)
        self.n_inst += 1
        return inst

    def barrier(self):
        for e, E in self.eng.items():
            for sk, sem in self.sems.items():
                v = self.cnt[sk]
                if v == 0 or self.seen[e].get(sk, 0) >= v:
                    continue
                E.wait_ge(sem, v)
                self.seen[e][sk] = v

    def wait_all(self, e, deps):
        self._waits(e, deps, ())

    def run(self, in_maps, n=8, trace=False):
        return run_bass_kernel_spmd(self.nc, in_maps, core_ids=list(range(n)), trace=trace)


import math
from contextlib import ExitStack

NT_A = 18
NLAT_T = 16
TOKS = NT_A * 128
NKT = 66
NKEY = NKT * 128
EPS = 1e-6
BT = 256
O_Q, O_K, O_V, O_F, O_U, O_VC, O_GATE, O_MERGE = 0, 512, 1024, 1536, 1792, 2048, 2304, 3328
RG = [[0, 1, 2, 3], [4, 5, 6, 7]]


def build_fused(kb):
    nc = kb.nc
    EI = lambda name, shape, d=F32: nc.dram_tensor(name, shape, d, kind="ExternalInput").ap()
    IN = lambda name, shape, d=F32: nc.dram_tensor(name, shape, d, kind="Internal").ap()
    x_in = EI("x_tok", [TOKS, 1024])
    cvec = EI("cvec", [128, 8, 2])
    w_ada = EI("w_ada", [2, 1024, 3072]); b_ada = EI("b_ada", [2, 128, 24]); g_norm = EI("g_norm", [2, 128, 8])
    w_in = EI("w_in", [2, 1024, 6400])
    g_q = EI("g_q", [2, 64]); g_k = EI("g_k", [2, 64])
    cos = EI("cos", [2048, 64]); ssin = EI("ssin", [2048, 64])
    lamv = EI("lamv", [2, 256]); g_sub = EI("g_sub", [2, 128, 1])
    w_f = EI("w_f", [2, 4, 64, 64]); b_f = EI("b_f", [2, 64, 4])
    cs64 = EI("cs64", [64, 128], BF16); T1d = EI("T1j", [128, 64, 64], BF16); T2d = EI("T2j", [128, 64, 64], BF16)
    dcd = EI("dctx", [128, 2, 512], BF16); ccs = EI("ccs", [64, 2, 64])
    ln_g = EI("ln_g", [2, 256]); ln_b = EI("ln_b", [2, 256])
    w_sT = EI("w_sT", [2, 128, 4, 128]); bs_d = EI("bs64", [2, 64, 4, 128])
    w_br = EI("w_br", [2, 1024, 1024]); w_out = EI("w_out", [2, 1024, 1024])
    y_out = nc.dram_tensor("y", [2048, 1024], F32, kind="ExternalOutput").ap()
    x1 = IN("x1", [TOKS, 1024])
    hT_d = IN("hT_d", [8, 128, TOKS], BF16)
    qT_d = IN("qT_d", [4, 128, TOKS], BF16)
    kT_lat = [IN(f"kT_lat{i}", [256, 2048], BF16) for i in range(2)]
    kT_ag = [IN(f"kT_ag{i}", [1024, 2048], BF16) for i in range(2)]
    kT_ctx = IN("kT_ctx", [4, 128, 256], BF16)
    v_lat = [IN(f"v_lat{i}", [1024, 512], BF16) for i in range(2)]
    v_ag = [IN(f"v_ag{i}", [4096, 512], BF16) for i in range(2)]
    v_ctx = IN("v_ctx", [256, 512], BF16)
    f_lat = IN("f_lat", [4 * 2048, 64], BF16)
    f_ag = IN("f_ag", [16 * 2048, 64], BF16)
    f_ctx = IN("f_ctx", [256, 256], BF16)
    oaT_d = IN("oaT_d", [4, 128, TOKS])
    obT_d = IN("obT_d", [4, 64, TOKS])
    modT_d = IN("modT_d", [128, 24, 2])
    out_keys = set()
    uid = [0]

    def U(name):
        uid[0] += 1
        return f"{name}_{uid[0]}"

    def store(key, out, in_, reads):
        out_keys.add(key)
        kb.dma("pool", key, out, in_, reads=reads, writes=[])

    d_fag = Dep("f_ag"); d_kvag = Dep("kv_ag")

    def collective(kind, src, dst, dep):
        sem = kb._sem("cc")
        inst = nc.gpsimd.collective_compute(kind, ALU.bypass, replica_groups=RG, ins=[src], outs=[dst])
        inst.then_inc(sem, 1)
        kb.cnt["cc"] += 1
        dep.w = ("cc", kb.cnt["cc"])

    def rstd_chain(ssrc, dsrc, dst, ddst, scale):
        kb.op("dve", lambda E: E.tensor_scalar(out=dst, in0=ssrc, scalar1=scale, scalar2=EPS, op0=ALU.mult, op1=ALU.add), reads=[dsrc], writes=[ddst])
        kb.op("act", lambda E: E.activation(out=dst, in_=dst, func=AF.Sqrt), reads=[ddst], writes=[ddst])
        kb.op("dve", lambda E: E.reciprocal(out=dst, in_=dst), reads=[ddst], writes=[ddst])

    def make_ident(ident, dep):
        kb.op("pool", lambda E: E.memset(ident[:], 0.0), writes=[dep])
        kb.op("pool", lambda E: E.affine_select(out=ident[:], in_=ident[:], pattern=[[-1, 128]], compare_op=ALU.not_equal,
                                                fill=1.0, base=0, channel_multiplier=1), reads=[dep], writes=[dep])

    def stage_a(l, x_src):
        with ExitStack() as es:
            sb = lambda name, shape, dty: es.enter_context(nc.sbuf_tensor(U(name), shape, dty))
            ps = [es.enter_context(nc.psum_tensor(U(f"psA{i}"), [128, 512], F32)) for i in range(6)]
            ps += [es.enter_context(nc.psum_tensor(U(f"psA{i}"), [128, 1024], BF16)) for i in (6, 7)]
            ident = sb("ident", [128, 128], F32); identb = sb("identb", [128, 128], BF16)
            cT = sb("cT", [128, 8, 2], F32); sg = sb("sg", [128, 8, 2], F32); sc = sb("sc", [128, 8, 2], F32)
            bT = sb("bT", [128, 24], F32); gn = sb("gn", [128, 8], F32)
            modT = sb("modT_s", [128, 24, 2], F32); Aff = sb("Aff", [128, 8, 2], F32)
            wst = [sb(f"wst{i}", [128, 8, 512], F32) for i in range(2)]
            wA = sb("wA", [128, 8, 1792], BF16)
            GQ = sb("GQ", [128, 64], F32); GK = sb("GK", [128, 64], F32)
            xt = [sb(f"xt{i}", [128, 1024], F32) for i in range(2)]
            junk = sb("junk", [128, 1024], F32)
            ss = sb("ss", [128, 1], F32); rstd = sb("rstd", [128, 1], F32)
            hTt = [sb(f"hTt{i}", [128, 8, 128], BF16) for i in range(2)]
            cs = [sb(f"cs{i}", [128, 2, 64], F32) for i in range(2)]
            sq = sb("sq", [128, 512], F32); ss8 = sb("ss8", [128, 8], F32)
            qn = sb("qn", [128, 512], F32); t1 = sb("t1", [128, 512], F32); t2 = sb("t2", [128, 512], F32)
            qr = [sb(f"qr{i}", [128, 512], BF16) for i in range(2)]
            qTt = [sb(f"qTt{i}", [128, 4, 128], BF16) for i in range(2)]
            kTt = [sb(f"kTt{i}", [128, 4, 128], BF16) for i in range(2)]
            vt = [sb(f"vt{i}", [128, 512], BF16) for i in range(2)]
            ft = [sb(f"ft{i}", [128, 256], BF16) for i in range(2)]
            D = lambda n: Dep(n)
            d_ident, d_identb, d_cT, d_sg, d_sc, d_bT, d_gn, d_modT, d_Aff = [D(n) for n in "ident identb cT sg sc bT gn modT Aff".split()]
            d_wst = [D("wst0"), D("wst1")]; d_wA = D("wA"); d_G = D("G")
            d_xt = [D("xt0"), D("xt1")]; d_junk = D("junk"); d_ss = D("ss"); d_rstd = D("rstd")
            d_hTt = [D("hTt0"), D("hTt1")]; d_cs = [D("cs0"), D("cs1")]
            d_sq, d_ss8, d_qn, d_t1, d_t2 = D("sq"), D("ss8"), D("qn"), D("t1"), D("t2")
            d_qr = [D("qr0"), D("qr1")]; d_qTt = [D("qTt0"), D("qTt1")]; d_kTt = [D("kTt0"), D("kTt1")]
            d_vt = [D("vt0"), D("vt1")]; d_ft = [D("ft0"), D("ft1")]
            d_ps = [D(f"ps{i}") for i in range(8)]

            make_ident(ident, d_ident)
            kb.op("pool", lambda E: E.tensor_copy(out=identb[:], in_=ident[:]), reads=[d_ident], writes=[d_identb])
            kb.dma("sp", "ld_c0", cT[:], cvec, writes=[d_cT])
            kb.dma("sp", "ld_c1", bT[:], b_ada[l], writes=[d_bT])
            kb.dma("sp", "ld_c2", gn[:], g_norm[l], writes=[d_gn])
            kb.dma("sp", "ld_c3", GQ[:], g_q[l].partition_broadcast(128), writes=[d_G])
            kb.dma("sp", "ld_c3", GK[:], g_k[l].partition_broadcast(128), writes=[d_G])
            kb.op("act", lambda E: E.activation(out=sg[:], in_=cT[:], func=AF.Sigmoid), reads=[d_cT], writes=[d_sg])
            kb.op("dve", lambda E: E.tensor_tensor(out=sc[:], in0=cT[:], in1=sg[:], op=ALU.mult), reads=[d_cT, d_sg], writes=[d_sc])
            w_ada_v = w_ada[l].rearrange("(k p) n -> p k n", p=128)
            for g in range(6):
                s = g % 2
                kb.dma("sp", f"ld_w{s}", wst[s][:], w_ada_v[:, :, g * 512:(g + 1) * 512], writes=[d_wst[s]])
                for jj in range(4):
                    j = g * 4 + jj
                    for k in range(8):
                        kb.op("pe", lambda E, k=k, jj=jj, s=s, j=j: E.matmul(ps[0][:, 2 * j:2 * j + 2], lhsT=wst[s][:, k, jj * 128:(jj + 1) * 128],
                                                                            rhs=sc[:, k, :], start=(k == 0), stop=(k == 7)),
                              reads=[d_wst[s], d_sc], writes=[d_ps[0]])
            kb.op("dve", lambda E: E.tensor_tensor(out=modT[:], in0=ps[0][:, 0:48].rearrange("p (j n) -> p j n", n=2),
                                                   in1=bT[:].unsqueeze(2).to_broadcast([128, 24, 2]), op=ALU.add),
                  reads=[d_ps[0], d_bT], writes=[d_modT])
            store("st_mod", modT_d, modT[:], [d_modT])
            kb.op("dve", lambda E: E.tensor_scalar(out=Aff[:], in0=modT[:, 8:16, :], scalar1=1.0, scalar2=None, op0=ALU.add), reads=[d_modT], writes=[d_Aff])
            kb.op("dve", lambda E: E.tensor_tensor(out=Aff[:], in0=Aff[:], in1=gn[:].unsqueeze(2).to_broadcast([128, 8, 2]), op=ALU.mult),
                  reads=[d_Aff, d_gn], writes=[d_Aff])
            w_in_v = w_in[l].rearrange("(k p) n -> p k n", p=128)
            for g in range(4):
                s = g % 2
                n = 512 if g < 3 else 256
                kb.dma("sp", f"ld_w{s}", wst[s][:, :, 0:n], w_in_v[:, :, g * 512:g * 512 + n], writes=[d_wst[s]])
                e = "pool" if g % 2 == 0 else "dve"
                kb.op(e, lambda E, s=s, n=n, g=g: E.tensor_copy(out=wA[:, :, g * 512:g * 512 + n], in_=wst[s][:, :, 0:n]), reads=[d_wst[s]], writes=[d_wA])

            def load_tile(i):
                s = i % 2
                kb.dma("sp", f"ld_x{s}", xt[s][:], x_src[i * 128:(i + 1) * 128, :], writes=[d_xt[s]])
                if i < NLAT_T:
                    kb.dma("sp", f"ld_cs{s}", cs[s][:, 0, :], cos[i * 128:(i + 1) * 128, :], writes=[d_cs[s]])
                    kb.dma("sp", f"ld_cs{s}", cs[s][:, 1, :], ssin[i * 128:(i + 1) * 128, :], writes=[d_cs[s]])

            load_tile(0)
            for i in range(NT_A):
                s = i % 2
                var = 0 if i < NLAT_T else 1
                lat = i < NLAT_T
                if i + 1 < NT_A:
                    load_tile(i + 1)
                X = xt[s]
                kb.op("act", lambda E: E.activation(out=junk[:], in_=X[:], func=AF.Square, accum_out=ss[:]), reads=[d_xt[s]], writes=[d_junk, d_ss])
                rstd_chain(ss[:], d_ss, rstd[:], d_rstd, 1.0 / 1024)
                kb.op("dve", lambda E: E.tensor_scalar(out=X[:], in0=X[:], scalar1=rstd[:, 0:1], scalar2=None, op0=ALU.mult),
                      reads=[d_xt[s], d_rstd], writes=[d_xt[s]])
                for k in range(8):
                    b = k // 4
                    kb.op("pe", lambda E, k=k, b=b: E.transpose(out=ps[b][:, (k % 4) * 128:(k % 4 + 1) * 128], in_=X[:, k * 128:(k + 1) * 128], identity=ident[:]),
                          reads=[d_xt[s], d_ident], writes=[d_ps[b]])
                for k in range(8):
                    b = k // 4
                    src = ps[b][:, (k % 4) * 128:(k % 4 + 1) * 128]
                    if k % 2 == 0:
                        kb.op("act", lambda E, k=k, src=src: E.activation(out=hTt[s][:, k, :], in_=src, func=AF.Identity,
                                                                          scale=Aff[:, k, var:var + 1], bias=modT[:, k, var:var + 1]),
                              reads=[d_ps[b], d_Aff, d_modT], writes=[d_hTt[s]])
                    else:
                        kb.op("dve", lambda E, k=k, src=src: E.tensor_scalar(out=hTt[s][:, k, :], in0=src, scalar1=Aff[:, k, var:var + 1],
                                                                             scalar2=modT[:, k, var:var + 1], op0=ALU.mult, op1=ALU.add),
                              reads=[d_ps[b], d_Aff, d_modT], writes=[d_hTt[s]])
                store(f"st_h{s}", hT_d[:, :, i * 128:(i + 1) * 128].rearrange("k p t -> p k t"), hTt[s][:], [d_hTt[s]])
                for cb in range(4):
                    n = 512 if cb < 3 else 256
                    for k in range(8):
                        kb.op("pe", lambda E, k=k, cb=cb, n=n: E.matmul(ps[2 + cb][:, 0:n], lhsT=hTt[s][:, k, :], rhs=wA[:, k, cb * 512:cb * 512 + n],
                                                                       start=(k == 0), stop=(k == 7)),
                              reads=[d_hTt[s], d_wA], writes=[d_ps[2 + cb]])
                for which in range(2):
                    P = ps[2 + which]; dP = d_ps[2 + which]
                    G = GQ if which == 0 else GK
                    kb.op("act", lambda E: E.activation(out=sq[:], in_=P[:], func=AF.Square), reads=[dP], writes=[d_sq])
                    kb.op("dve", lambda E: E.tensor_reduce(out=ss8[:], in_=sq[:].rearrange("p (g c) -> p g c", c=64), axis=AX.X, op=ALU.add),
                          reads=[d_sq], writes=[d_ss8])
                    rstd_chain(ss8[:], d_ss8, ss8[:], d_ss8, 1.0 / 64)
                    kb.op("dve", lambda E: E.tensor_tensor(out=qn[:].rearrange("p (g c) -> p g c", c=64), in0=P[:].rearrange("p (g c) -> p g c", c=64),
                                                           in1=ss8[:].unsqueeze(2).to_broadcast([128, 8, 64]), op=ALU.mult),
                          reads=[dP, d_ss8], writes=[d_qn])
                    QR = qr[which]; dQR = d_qr[which]
                    if lat:
                        kb.op("dve", lambda E: E.tensor_tensor(out=qn[:].rearrange("p (g c) -> p g c", c=64), in0=qn[:].rearrange("p (g c) -> p g c", c=64),
                                                               in1=G[:].unsqueeze(1).to_broadcast([128, 8, 64]), op=ALU.mult),
                              reads=[d_qn, d_G], writes=[d_qn])
                        kb.op("dve", lambda E: E.tensor_tensor(out=t1[:].rearrange("p (g c) -> p g c", c=64), in0=qn[:].rearrange("p (g c) -> p g c", c=64),
                                                               in1=cs[s][:, 0, :].unsqueeze(1).to_broadcast([128, 8, 64]), op=ALU.mult),
                              reads=[d_qn, d_cs[s]], writes=[d_t1])
                        qv = qn[:].rearrange("p (g a h c) -> p g a h c", a=2, h=2, c=16)
                        tv = t2[:].rearrange("p (g a h c) -> p g a h c", a=2, h=2, c=16)
                        sv = cs[s][:, 1, :].rearrange("p (a h c) -> p a h c", a=2, h=2)
                        for hf in range(2):
                            for a in range(2):
                                kb.op("dve", lambda E, hf=hf, a=a: E.tensor_tensor(out=tv[:, :, a, hf, :], in0=qv[:, :, a, 1 - hf, :],
                                                                                    in1=sv[:, a, hf, :].unsqueeze(1).to_broadcast([128, 8, 16]), op=ALU.mult),
                                      reads=[d_qn, d_cs[s]], writes=[d_t2])
                        kb.op("dve", lambda E: E.tensor_tensor(out=QR[:], in0=t1[:], in1=t2[:], op=ALU.add), reads=[d_t1, d_t2], writes=[dQR])
                    else:
                        kb.op("dve", lambda E: E.tensor_tensor(out=QR[:].rearrange("p (g c) -> p g c", c=64), in0=qn[:].rearrange("p (g c) -> p g c", c=64),
                                                               in1=G[:].unsqueeze(1).to_broadcast([128, 8, 64]), op=ALU.mult),
                              reads=[d_qn, d_G], writes=[dQR])
                    PT = ps[6 + which]; dPT = d_ps[6 + which]
                    for hh in range(4):
                        kb.op("pe", lambda E, hh=hh: E.transpose(out=PT[:, hh * 128:(hh + 1) * 128], in_=QR[:, hh * 128:(hh + 1) * 128], identity=identb[:]),
                              reads=[dQR, d_identb], writes=[dPT])
                    TT = (qTt if which == 0 else kTt)[s]; dTT = (d_qTt if which == 0 else d_kTt)[s]
                    kb.op("act", lambda E: E.activation(out=TT[:].rearrange("p h t -> p (h t)"), in_=PT[:, 0:512], func=AF.Copy), reads=[dPT], writes=[dTT])
                    if which == 0:
                        store(f"st_q{s}", qT_d[:, :, i * 128:(i + 1) * 128].rearrange("h p t -> p h t"), TT[:], [dTT])
                    elif lat:
                        for hp in range(2):
                            store(f"st_k{s}", kT_lat[hp].rearrange("(h p) t -> p h t", p=128)[:, :, i * 128:(i + 1) * 128], TT[:, hp * 2:hp * 2 + 2, :], [dTT])
                    else:
                        store(f"st_k{s}", kT_ctx[:, :, (i - NLAT_T) * 128:(i - NLAT_T + 1) * 128].rearrange("h p t -> p h t"), TT[:], [dTT])
                kb.op("act", lambda E: E.activation(out=vt[s][:], in_=ps[4][:], func=AF.Copy), reads=[d_ps[4]], writes=[d_vt[s]])
                kb.op("act", lambda E: E.activation(out=ft[s][:], in_=ps[5][:, 0:256], func=AF.Copy), reads=[d_ps[5]], writes=[d_ft[s]])
                if lat:
                    store(f"st_v{s}", v_lat[i // 8][(i % 8) * 128:(i % 8 + 1) * 128, :], vt[s][:], [d_vt[s]])
                    store(f"st_f{s}", f_lat.rearrange("(g t) c -> t g c", g=4)[i * 128:(i + 1) * 128, :, :], ft[s][:].rearrange("p (g c) -> p g c", c=64), [d_ft[s]])
                else:
                    ic = i - NLAT_T
                    store(f"st_v{s}", v_ctx[ic * 128:(ic + 1) * 128, :], vt[s][:], [d_vt[s]])
                    store(f"st_f{s}", f_ctx[ic * 128:(ic + 1) * 128, :], ft[s][:], [d_ft[s]])
            kb.barrier()

    def stage_b(l, last):
        lam_init = 0.8 - 0.6 * math.exp(-0.3 * l)
        with ExitStack() as es:
            sbt = lambda name, shape, dty: es.enter_context(nc.sbuf_tensor(U(name), shape, dty))
            ps = [es.enter_context(nc.psum_tensor(U(f"psF{i}"), [128, 512], F32)) for i in range(8)]
            d_ps = [Dep(f"ps{i}") for i in range(8)]
            X1 = [sbt(f"X1_{i}", [64, 8192], BF16) for i in range(2)]
            Xc = sbt("Xc", [128, 2, 256], BF16)
            CS = sbt("CS", [64, 128], BF16)
            T1 = sbt("T1s", [128, 64, 64], BF16)
            T2 = sbt("T2s", [128, 64, 64], BF16)
            DC = sbt("DC", [128, 2, 512], BF16)
            Bsb = sbt("Bsb", [128, 2, 64, 64], BF16)
            YT = sbt("YT", [64, 2, 2048], BF16)
            YC = sbt("YC", [64, 2, 256], BF16)
            CC = sbt("CC", [64, 2, 64], F32)
            WF = sbt("WF", [64, 4, 64], F32)
            BF_ = sbt("BF", [64, 4], F32)
            MC = sbt("MC", [64, 4, 2, 64], BF16)
            ob = [sbt(f"ob{i}", [64, 512], F32) for i in range(2)]
            dn = {n: Dep(n) for n in "X1_0 X1_1 Xc CS T1 T2 DC Bsb YT YC CC WF BF MC ob0 ob1".split()}
            f_ag_v = f_ag.rearrange("(r g t p) c -> r g t (p c)", r=4, g=4, p=128)

            def load_x1(g):
                s = g % 2
                for r in range(4):
                    kb.dma("sp", f"f_x1{s}", X1[s][r * 16:(r + 1) * 16, :], f_ag_v[r, g], reads=[d_fag], writes=[dn[f"X1_{s}"]])
            load_x1(0)
            if not last:
                kb.dma("sp", "f_xc", Xc[:], f_ctx.rearrange("(k p) c -> p k c", p=128), writes=[dn["Xc"]])
                kb.dma("sp", "f_dc", DC[:], dcd, writes=[dn["DC"]])
            kb.dma("sp", "f_cs", CS[:], cs64, writes=[dn["CS"]])
            kb.dma("sp", "f_cc", CC[:], ccs, writes=[dn["CC"]])
            kb.dma("sp", "f_wf", WF[:], w_f[l].rearrange("g c d -> c g d"), writes=[dn["WF"]])
            kb.dma("sp", "f_bf", BF_[:], b_f[l], writes=[dn["BF"]])
            kb.dma("sp", "f_t1", T1[:], T1d, writes=[dn["T1"]])
            kb.dma("sp", "f_t2", T2[:], T2d, writes=[dn["T2"]])
            for g in range(4):
                for i in range(2):
                    kb.op("pe", lambda E, i=i, g=g: E.matmul(ps[0][0:64, (g * 2 + i) * 64:(g * 2 + i + 1) * 64], lhsT=CC[:, i, :], rhs=WF[:, g, :], start=True, stop=True),
                          reads=[dn["CC"], dn["WF"]], writes=[d_ps[0]])
            kb.op("dve", lambda E: E.tensor_copy(out=MC[:].rearrange("c g i d -> c (g i d)"), in_=ps[0][0:64, 0:512]), reads=[d_ps[0]], writes=[dn["MC"]])
            sc_lat = 1.0 / math.sqrt(8192 * 64); sc_ctx = 1.0 / math.sqrt(256 * 64)
            blkc = [0]
            for g in range(4):
                s = g % 2
                if g + 1 < 4:
                    load_x1(g + 1)
                X1v = X1[s][:].rearrange("t (p c) -> t p c", c=64)
                Bv = Bsb[:].rearrange("p r k c -> p c r k")
                for c4 in range(16):
                    b = 1 + c4 % 2
                    for cc in range(4):
                        c = c4 * 4 + cc
                        kb.op("pe", lambda E, c=c, cc=cc, b=b: E.matmul(ps[b][:, cc * 128:(cc + 1) * 128], lhsT=X1v[:, :, c], rhs=CS[:], start=True, stop=True),
                              reads=[dn[f"X1_{s}"], dn["CS"]], writes=[d_ps[b]])
                    src = ps[b][:].rearrange("p (c r k) -> p c r k", c=4, r=2)
                    dst = Bv[:, c4 * 4:(c4 + 1) * 4, :, :]
                    kb.op("act", lambda E, src=src, dst=dst: E.activation(out=dst[:, :, 0, :], in_=src[:, :, 0, :], func=AF.Copy), reads=[d_ps[b]], writes=[dn["Bsb"]])
                    kb.op("dve", lambda E, src=src, dst=dst: E.tensor_copy(out=dst[:, :, 1, :], in_=src[:, :, 1, :]), reads=[d_ps[b]], writes=[dn["Bsb"]])
                for k8 in range(8):
                    b = 3 + k8 % 2
                    for ki in range(8):
                        kbi = k8 * 8 + ki
                        o = ps[b][0:64, ki * 64:(ki + 1) * 64]
                        kb.op("pe", lambda E, kbi=kbi, o=o: E.matmul(o, lhsT=Bsb[:, 0, kbi, :], rhs=T1[:, kbi, :], start=True, stop=False),
                              reads=[dn["Bsb"], dn["T1"]], writes=[d_ps[b]])
                        kb.op("pe", lambda E, kbi=kbi, o=o: E.matmul(o, lhsT=Bsb[:, 1, kbi, :], rhs=T2[:, kbi, :], start=False, stop=True),
                              reads=[dn["Bsb"], dn["T2"]], writes=[d_ps[b]])
                    src = ps[b][0:64, :].rearrange("c (i r a) -> c i r a", i=8, r=2)
                    dst = YT[:].rearrange("c r (a k) -> c k r a", k=64)[:, k8 * 8:(k8 + 1) * 8, :, :]
                    for r in range(2):
                        if r == 0:
                            kb.op("act", lambda E, src=src, dst=dst, r=r: E.activation(out=dst[:, :, r, :], in_=src[:, :, r, :], func=AF.Copy), reads=[d_ps[b]], writes=[dn["YT"]])
                        else:
                            kb.op("dve", lambda E, src=src, dst=dst, r=r: E.tensor_copy(out=dst[:, :, r, :], in_=src[:, :, r, :]), reads=[d_ps[b]], writes=[dn["YT"]])
                nblk = 4
                if not last:
                    for k in range(2):
                        kb.op("pe", lambda E, k=k, g=g: E.matmul(ps[5][0:64, :], lhsT=Xc[:, k, g * 64:(g + 1) * 64], rhs=DC[:, k, :], start=(k == 0), stop=(k == 1)),
                              reads=[dn["Xc"], dn["DC"]], writes=[d_ps[5]])
                    kb.op("dve", lambda E: E.tensor_copy(out=YC[:].rearrange("c r k -> c (r k)"), in_=ps[5][0:64, :]), reads=[d_ps[5]], writes=[dn["YC"]])
                    nblk = 5
                for blk in range(nblk):
                    b = 6 + blkc[0] % 2
                    so = blkc[0] % 2
                    blkc[0] += 1
                    if blk < 4:
                        n = 512; r0 = YT[:, 0, blk * 512:(blk + 1) * 512]; r1 = YT[:, 1, blk * 512:(blk + 1) * 512]; scl = sc_lat; dy = dn["YT"]
                    else:
                        n = 256; r0 = YC[:, 0, :]; r1 = YC[:, 1, :]; scl = sc_ctx; dy = dn["YC"]
                    kb.op("pe", lambda E, r0=r0, n=n, b=b, g=g: E.matmul(ps[b][0:64, 0:n], lhsT=MC[:, g, 0, :], rhs=r0, start=True, stop=False), reads=[dn["MC"], dy], writes=[d_ps[b]])
                    kb.op("pe", lambda E, r1=r1, n=n, b=b, g=g: E.matmul(ps[b][0:64, 0:n], lhsT=MC[:, g, 1, :], rhs=r1, start=False, stop=True), reads=[dn["MC"], dy], writes=[d_ps[b]])
                    kb.op("act", lambda E, n=n, b=b, so=so, scl=scl, g=g: E.activation(out=ob[so][:, 0:n], in_=ps[b][0:64, 0:n], func=AF.Identity, scale=scl, bias=BF_[:, g:g + 1]),
                          reads=[d_ps[b], dn["BF"]], writes=[dn[f"ob{so}"]])
                    store(f"st_ob{so}", obT_d[g, :, blk * 512:blk * 512 + n], ob[so][:, 0:n], [dn[f"ob{so}"]])
            kb.barrier()

        with ExitStack() as es:
            sbt = lambda name, shape, dty: es.enter_context(nc.sbuf_tensor(U(name), shape, dty))
            psS = [es.enter_context(nc.psum_tensor(U(f"psS{i}"), [128, 1024], F32)) for i in range(2)]
            ps = [None] * 4 + [es.enter_context(nc.psum_tensor(U(f"psB{i}"), [128, 512], F32)) for i in range(4, 8)]
            d_psS = [Dep("psS0"), Dep("psS1")]
            d_ps = [Dep(f"ps{i}") for i in range(8)]
            kTh = [sbt(f"kTh{i}", [128, NKEY], BF16) for i in range(2)]
            vh = [sbt(f"vh{i}", [128, NKT, 128], BF16) for i in range(2)]
            qTh = [sbt(f"qTh{i}", [128, TOKS], BF16) for i in range(2)]
            NPT = 4
            pt = [sbt(f"pt{i}", [128, 1024], BF16) for i in range(NPT)]
            ones = sbt("ones", [128, 128], F32)
            onesb = sbt("onesb", [128, 32], BF16)
            sel = sbt("sel", [128, 2, 128], F32)
            rsb = sbt("rsb", [128, 512], F32)
            lv = sbt("lv", [128, 4, 64], F32); lt = sbt("lt", [128, 2, 64], F32); l2 = sbt("l2", [128, 2], F32)
            nlam = sbt("nlam", [128, 1], F32); gs = sbt("gs", [128, 1], F32)
            rec = [sbt(f"rec{c}", [128, 512], F32) for c in range(2)]
            o0 = sbt("o0", [128, 512], F32); o1 = sbt("o1", [128, 512], F32)
            osq = sbt("osq", [128, 512], F32); rs = sbt("rs", [128, 512], F32)
            of = [sbt(f"of{i}", [128, 512], F32) for i in range(2)]
            d_kv = [Dep("kv0"), Dep("kv1")]
            d_pt = [Dep(f"pt{i}") for i in range(NPT)]
            dm = {n: Dep(n) for n in "ones onesb sel rsb lv lt l2 nlam gs rec0 rec1 o0 o1 osq rs of0 of1".split()}
            kb.op("pool", lambda E: E.memset(ones[:], 1.0), writes=[dm["ones"]])
            kb.op("pool", lambda E: E.memset(onesb[:], 1.0), writes=[dm["onesb"]])
            kb.op("pool", lambda E: E.memset(sel[:], 0.0), writes=[dm["sel"]])
            for (p0, p1, c, val) in ((0, 32, 0, 1.0 / 32), (64, 96, 0, 1.0 / 32), (32, 64, 1, 1.0 / 32), (64, 128, 1, 1.0 / 32), (64, 96, 1, 0.0)):
                kb.op("pool", lambda E, p0=p0, p1=p1, c=c, val=val: E.memset(sel[p0:p1, c, :], val), reads=[dm["sel"]], writes=[dm["sel"]])
            kb.dma("sp", "a_lv", lv[:].rearrange("p a c -> p (a c)"), lamv[l].partition_broadcast(128), writes=[dm["lv"]])
            kb.dma("sp", "a_gs", gs[:], g_sub[l], writes=[dm["gs"]])
            lvv = lv[:].rearrange("p (i j) c -> p i j c", j=2)
            kb.op("dve", lambda E: E.tensor_tensor(out=lt[:], in0=lvv[:, :, 0, :], in1=lvv[:, :, 1, :], op=ALU.mult), reads=[dm["lv"]], writes=[dm["lt"]])
            kb.op("dve", lambda E: E.tensor_reduce(out=l2[:], in_=lt[:], axis=AX.X, op=ALU.add), reads=[dm["lt"]], writes=[dm["l2"]])
            kb.op("act", lambda E: E.activation(out=l2[:], in_=l2[:], func=AF.Exp), reads=[dm["l2"]], writes=[dm["l2"]])
            kb.op("dve", lambda E: E.tensor_tensor(out=nlam[:], in0=l2[:, 1:2], in1=l2[:, 0:1], op=ALU.subtract), reads=[dm["l2"]], writes=[dm["nlam"]])
            kb.op("dve", lambda E: E.tensor_scalar(out=nlam[:], in0=nlam[:], scalar1=-lam_init, scalar2=None, op0=ALU.add), reads=[dm["nlam"]], writes=[dm["nlam"]])
            kb.op("dve", lambda E: E.tensor_scalar(out=gs[:], in0=gs[:], scalar1=(1.0 - lam_init), scalar2=None, op0=ALU.mult), reads=[dm["gs"]], writes=[dm["gs"]])
            kT_ag_v = [a.rearrange("(r h p) t -> h p r t", r=4, h=2) for a in kT_ag]
            v_ag_v = [a.rearrange("(r t p) e -> p r t e", r=4, p=128) for a in v_ag]
            v_ctx_v = v_ctx.rearrange("(t p) e -> p t e", p=128)

            def load_head(h):
                s = h % 2
                kb.dma("sp", f"a_k{s}", kTh[s][:, 0:8192].rearrange("p (r t) -> p r t", r=4), kT_ag_v[h // 2][h % 2], reads=[d_kvag], writes=[d_kv[s]])
                kb.dma("sp", f"a_k{s}", kTh[s][:, 8192:NKEY], kT_ctx[h], writes=[d_kv[s]])
                for half in range(2):
                    for r in range(4):
                        kt0 = r * 16 + half * 8
                        kb.dma("sp", f"a_k{s}", vh[s][:, kt0:kt0 + 8, :], v_ag_v[half][:, r, :, h * 128:(h + 1) * 128], reads=[d_kvag], writes=[d_kv[s]])
                kb.dma("sp", f"a_k{s}", vh[s][:, 64:66, :], v_ctx_v[:, :, h * 128:(h + 1) * 128], writes=[d_kv[s]])
                kb.dma("sp", f"a_k{s}", qTh[s][:], qT_d[h], writes=[d_kv[s]])

            load_head(0)
            blk_id = 0
            for h in range(4):
                s = h % 2
                if h + 1 < 4:
                    load_head(h + 1)
                for qb in range(4 if last else 5):
                    if qb < 4:
                        q0 = qb * 512; nq = 512; kts = list(range(NKT))
                    else:
                        q0 = 2048; nq = 256; kts = [64, 65]
                    nk = len(kts)

                    def scores(idx):
                        kt = kts[idx]; sl = idx % 2
                        for c in range(2):
                            kb.op("pe", lambda E, c=c, kt=kt, sl=sl: E.matmul(psS[sl][:, c * 512:c * 512 + nq], lhsT=kTh[s][c * 64:(c + 1) * 64, kt * 128:(kt + 1) * 128],
                                                                           rhs=qTh[s][c * 64:(c + 1) * 64, q0:q0 + nq], start=True, stop=True, tile_position=(64 * c, 0)),
                                  reads=[d_kv[s]], writes=[d_psS[sl]])

                    def exps(idx):
                        sl = idx % 2; p = idx % NPT
                        kb.op("act", lambda E, sl=sl, p=p: E.activation(out=pt[p][:].rearrange("p (c q) -> p c q", c=2)[:, :, 0:nq],
                                                                        in_=psS[sl][:].rearrange("p (c q) -> p c q", c=2)[:, :, 0:nq], func=AF.Exp, scale=0.125),
                              reads=[d_psS[sl]], writes=[d_pt[p]])

                    def av(idx):
                        kt = kts[idx]; p = idx % NPT
                        for c in range(2):
                            kb.op("pe", lambda E, c=c, kt=kt, p=p, idx=idx: E.matmul(ps[4 + c][:, 0:nq], lhsT=vh[s][:, kt, :], rhs=pt[p][:, c * 512:c * 512 + nq],
                                                                                  start=(idx == 0), stop=(idx == nk - 1)),
                                  reads=[d_kv[s], d_pt[p]], writes=[d_ps[4 + c]])

                    def rowsums(idxs):
                        for idx in idxs:
                            p = idx % NPT
                            for c in range(2):
                                g4 = (idx % 2) * 2 + c
                                kb.op("pe", lambda E, c=c, p=p, idx=idx, g4=g4: E.matmul(ps[6][32 * g4:32 * g4 + 32, 0:nq], lhsT=onesb[:], rhs=pt[p][:, c * 512:c * 512 + nq],
                                                                                       start=(idx < 2), stop=(idx >= nk - 2), tile_position=(0, 32 * g4)),
                                      reads=[dm["onesb"], d_pt[p]], writes=[d_ps[6]])

                    scores(0)
                    pend = []
                    for idx in range(nk):
                        exps(idx)
                        if idx + 1 < nk:
                            scores(idx + 1)
                        av(idx)
                        pend.append(idx)
                        if len(pend) == 2 or idx == nk - 1:
                            rowsums(pend)
                            pend = []
                    kb.op("act", lambda E: E.activation(out=rsb[:, 0:nq], in_=ps[6][:, 0:nq], func=AF.Copy), reads=[d_ps[6]], writes=[dm["rsb"]])
                    for c in range(2):
                        bnk = 7 if c == 0 else 6
                        kb.op("pe", lambda E, c=c, bnk=bnk: E.matmul(ps[bnk][:, 0:nq], lhsT=sel[:, c, :], rhs=rsb[:, 0:nq], start=True, stop=True),
                              reads=[dm["sel"], dm["rsb"]], writes=[d_ps[bnk]])
                        kb.op("dve", lambda E, c=c, bnk=bnk: E.reciprocal(out=rec[c][:, 0:nq], in_=ps[bnk][:, 0:nq]), reads=[d_ps[bnk]], writes=[dm[f"rec{c}"]])
                    kb.op("dve", lambda E: E.tensor_tensor(out=o0[:, 0:nq], in0=ps[4][:, 0:nq], in1=rec[0][:, 0:nq], op=ALU.mult), reads=[d_ps[4], dm["rec0"]], writes=[dm["o0"]])
                    kb.op("dve", lambda E: E.tensor_tensor(out=o1[:, 0:nq], in0=ps[5][:, 0:nq], in1=rec[1][:, 0:nq], op=ALU.mult), reads=[d_ps[5], dm["rec1"]], writes=[dm["o1"]])
                    kb.op("dve", lambda E: E.scalar_tensor_tensor(out=o0[:, 0:nq], in0=o1[:, 0:nq], scalar=nlam[:, 0:1], in1=o0[:, 0:nq], op0=ALU.mult, op1=ALU.add),
                          reads=[dm["o0"], dm["o1"], dm["nlam"]], writes=[dm["o0"]])
                    kb.op("dve", lambda E: E.tensor_tensor(out=osq[:, 0:nq], in0=o0[:, 0:nq], in1=o0[:, 0:nq], op=ALU.mult), reads=[dm["o0"]], writes=[dm["osq"]])
                    kb.op("pe", lambda E: E.matmul(ps[6][:, 0:nq], lhsT=ones[:], rhs=osq[:, 0:nq], start=True, stop=True), reads=[dm["ones"], dm["osq"]], writes=[d_ps[6]])
                    kb.op("dve", lambda E: E.tensor_scalar(out=rs[:, 0:nq], in0=ps[6][:, 0:nq], scalar1=1.0 / 128, scalar2=EPS, op0=ALU.mult, op1=ALU.add),
                          reads=[d_ps[6]], writes=[dm["rs"]])
                    kb.op("act", lambda E: E.activation(out=rs[:, 0:nq], in_=rs[:, 0:nq], func=AF.Ln), reads=[dm["rs"]], writes=[dm["rs"]])
                    kb.op("act", lambda E: E.activation(out=rs[:, 0:nq], in_=rs[:, 0:nq], func=AF.Exp, scale=-0.5), reads=[dm["rs"]], writes=[dm["rs"]])
                    so = blk_id % 2
                    kb.op("dve", lambda E, so=so: E.scalar_tensor_tensor(out=of[so][:, 0:nq], in0=o0[:, 0:nq], scalar=gs[:, 0:1], in1=rs[:, 0:nq], op0=ALU.mult, op1=ALU.mult),
                          reads=[dm["o0"], dm["gs"], dm["rs"]], writes=[dm[f"of{so}"]])
                    store(f"st_oa{so}", oaT_d[h, :, q0:q0 + nq], of[so][:, 0:nq], [dm[f"of{so}"]])
                    blk_id += 1
            kb.barrier()

    def stage_c(l, last, x_src, x_dst):
        with ExitStack() as es0:
            sb = lambda name, shape, dty: es0.enter_context(nc.sbuf_tensor(U(name), shape, dty))
            ps = [es0.enter_context(nc.psum_tensor(U(f"psC{i}"), [128, 512], F32)) for i in range(8)]
            d_ps = [Dep(f"ps{i}") for i in range(8)]
            wC = sb("wC", [128, 8, 4608], BF16)
            wBR = sb("wBR", [128, 4, 1024], BF16)
            wBRb = sb("wBRb", [64, 4, 1024], BF16)
            wBRc = sb("wBRc", [64, 4, 1024], BF16)
            wO = sb("wO", [128, 8, 1024], BF16)
            wS = sb("wS", [128, 4, 128], BF16)
            LG = sb("LG", [128, 256], F32); LB = sb("LB", [128, 256], F32)
            BS = sb("BS", [64, 4, 128], F32)
            modT = sb("modT_s", [128, 24, 2], F32)
            G = [sb(f"G{i}", [128, 1024], F32) for i in range(2)]
            ident = sb("ident", [128, 128], F32); ones = sb("ones", [128, 128], F32)
            dw = {n: Dep(n) for n in "wC wBR wBRb wBRc wO wS LG LB BS modT G ident ones".split()}
            with ExitStack() as es:
                sbt = lambda name, shape, dty: es.enter_context(nc.sbuf_tensor(U(name), shape, dty))
                wst = [sbt(f"wst{i}", [128, 8, 512], F32) for i in range(2)]
                d_wst = [Dep("wst0"), Dep("wst1")]
                wsf = sbt("wsf", [128, 4, 128], F32); diag = sbt("diag", [128, 128], F32)
                d_wsf = Dep("wsf"); d_diag = Dep("diag")
                make_ident(ident, dw["ident"])
                kb.op("pool", lambda E: E.memset(ones[:], 1.0), writes=[dw["ones"]])
                kb.dma("sp", "c_mod", modT[:], modT_d, writes=[dw["modT"]])
                kb.dma("sp", "c_lg", LG[:], ln_g[l].partition_broadcast(128), writes=[dw["LG"]])
                kb.dma("sp", "c_lb", LB[:], ln_b[l].partition_broadcast(128), writes=[dw["LB"]])
                kb.dma("sp", "c_bs", BS[:], bs_d[l], writes=[dw["BS"]])
                kb.dma("sp", "c_ws", wsf[:], w_sT[l], writes=[d_wsf])
                kb.op("dve", lambda E: E.tensor_copy(out=wS[:], in_=wsf[:]), reads=[d_wsf], writes=[dw["wS"]])
                for var in range(1 if last else 2):
                    for k in range(8):
                        kb.op("dve", lambda E, k=k, var=var: E.tensor_scalar(out=diag[:], in0=ident[:], scalar1=modT[:, 16 + k, var:var + 1], scalar2=None, op0=ALU.mult),
                              reads=[dw["ident"], dw["modT"]], writes=[d_diag])
                        b = k // 4
                        kb.op("pe", lambda E, k=k, b=b: E.matmul(ps[b][:, (k % 4) * 128:(k % 4 + 1) * 128], lhsT=ones[:], rhs=diag[:], start=True, stop=True),
                              reads=[dw["ones"], d_diag], writes=[d_ps[b]])
                    for b in range(2):
                        kb.op("act", lambda E, b=b, var=var: E.activation(out=G[var][:, b * 512:(b + 1) * 512], in_=ps[b][:], func=AF.Copy), reads=[d_ps[b]], writes=[dw["G"]])
                cnt = [0]

                def load_cast(src_v, dst, ddst, np_=128, nk=8):
                    s = cnt[0] % 2; cnt[0] += 1
                    kb.dma("sp", f"c_w{s}", wst[s][0:np_, 0:nk, :], src_v, writes=[d_wst[s]])
                    e = ("dve", "pool", "act")[cnt[0] % 3]
                    if e == "act":
                        kb.op("act", lambda E: E.activation(out=dst, in_=wst[s][0:np_, 0:nk, :], func=AF.Copy), reads=[d_wst[s]], writes=[ddst])
                    else:
                        kb.op(e, lambda E: E.tensor_copy(out=dst, in_=wst[s][0:np_, 0:nk, :]), reads=[d_wst[s]], writes=[ddst])
                w_in_v = w_in[l].rearrange("(k p) n -> p k n", p=128)
                for g in range(9):
                    load_cast(w_in_v[:, :, O_U + g * 512:O_U + (g + 1) * 512], wC[:, :, g * 512:(g + 1) * 512], dw["wC"])
                w_br_v = w_br[l, 0:512, :].rearrange("(k p) n -> p k n", p=128)
                w_brb_v = w_br[l, 512:768, :].rearrange("(g c) n -> c g n", c=64)
                w_brc_v = w_br[l, 768:1024, :].rearrange("(g c) n -> c g n", c=64)
                w_out_v = w_out[l].rearrange("(k p) n -> p k n", p=128)
                for g in range(2):
                    cs_ = slice(g * 512, (g + 1) * 512)
                    load_cast(w_br_v[:, :, cs_], wBR[:, :, cs_], dw["wBR"], nk=4)
                    load_cast(w_brb_v[:, :, cs_], wBRb[:, :, cs_], dw["wBRb"], np_=64, nk=4)
                    load_cast(w_brc_v[:, :, cs_], wBRc[:, :, cs_], dw["wBRc"], np_=64, nk=4)
                    load_cast(w_out_v[:, :, cs_], wO[:, :, cs_], dw["wO"])
                kb.barrier()

            hTb = sb("hTb", [128, 8, BT], BF16)
            oab = sb("oab", [128, 4, BT], F32)
            obb = sb("obb", [64, 4, BT], F32)
            uT = sb("uT", [64, 4, BT], F32)
            gcT = sb("gcT", [64, 4, BT], F32)
            sgt = [sb(f"sgt{i}", [128, BT], F32) for i in range(2)]
            og = sb("og", [128, 4, BT], BF16)
            ogb = sb("ogb", [64, 4, BT], BF16)
            ogc = sb("ogc", [64, 4, BT], BF16)
            st6 = sb("st6", [128, 6], F32); mv = sb("mv", [128, 2], F32); rstd = sb("rstd", [128, 1], F32)
            vcn = sb("vcn", [128, 256], F32); vnb = sb("vnb", [128, 256], BF16)
            sT = sb("sT", [64, 4, 128], F32)
            mm = [sb(f"mm{i}", [128, 3, BT], F32) for i in range(2)]
            t0 = sb("t0", [128, BT], F32); t1 = sb("t1", [128, BT], F32)
            yT = sb("yT", [128, 8, BT], BF16)
            xt = [sb(f"xt{i}", [128, 1024], F32) for i in range(2)]
            tmpo = sb("tmpo", [128, 1024], F32)
            dn = {n: Dep(n) for n in "hTb oab obb uT gcT sgt0 sgt1 og ogb ogc st6 mv rstd vcn vnb sT mm0 mm1 t0 t1 yT xt0 xt1 tmpo".split()}

            def proj_fm(col0, M, nt, bank):
                for k in range(8):
                    kb.op("pe", lambda E, k=k: E.matmul(ps[bank][0:M, 0:nt], lhsT=wC[:, k, col0:col0 + M], rhs=hTb[:, k, 0:nt], start=(k == 0), stop=(k == 7)),
                          reads=[dw["wC"], dn["hTb"]], writes=[d_ps[bank]])
            pj = [0]

            def next_bank():
                pj[0] += 1
                return 6 + pj[0] % 2
            GC0 = O_GATE - O_U
            MC0 = O_MERGE - O_U
            for blk in range(8 if last else 9):
                tok0 = blk * BT; nt = BT; var = 0 if tok0 < 2048 else 1
                ntile = nt // 128
                kb.dma("sp", "c_h", hTb[:, :, 0:nt], hT_d[:, :, tok0:tok0 + nt].rearrange("k p t -> p k t"), writes=[dn["hTb"]])
                kb.dma("sp", "c_oa", oab[:, :, 0:nt], oaT_d[:, :, tok0:tok0 + nt].rearrange("h p t -> p h t"), writes=[dn["oab"]])
                kb.dma("sp", "c_ob", obb[:, :, 0:nt], obT_d[:, :, tok0:tok0 + nt].rearrange("g c t -> c g t"), writes=[dn["obb"]])
                for g in range(4):
                    b = next_bank()
                    proj_fm(g * 64, 64, nt, b)
                    kb.op("act", lambda E, g=g, b=b: E.activation(out=uT[:, g, 0:nt], in_=ps[b][0:64, 0:nt], func=AF.Copy), reads=[d_ps[b]], writes=[dn["uT"]])
                for br in range(2):
                    for g in range(4):
                        b = next_bank(); s = g % 2
                        proj_fm(GC0 + 512 + br * 256 + g * 64, 64, nt, b)
                        kb.op("act", lambda E, b=b, s=s: E.activation(out=sgt[s][0:64, 0:nt], in_=ps[b][0:64, 0:nt], func=AF.Sigmoid), reads=[d_ps[b]], writes=[dn[f"sgt{s}"]])
                        if br == 1:
                            kb.op("dve", lambda E, g=g, b=b, s=s: E.tensor_tensor(out=gcT[:, g, 0:nt], in0=ps[b][0:64, 0:nt], in1=sgt[s][0:64, 0:nt], op=ALU.mult),
                                  reads=[d_ps[b], dn[f"sgt{s}"]], writes=[dn["gcT"]])
                        else:
                            kb.op("dve", lambda E, b=b, s=s: E.tensor_tensor(out=sgt[s][0:64, 0:nt], in0=ps[b][0:64, 0:nt], in1=sgt[s][0:64, 0:nt], op=ALU.mult),
                                  reads=[d_ps[b], dn[f"sgt{s}"]], writes=[dn[f"sgt{s}"]])
                            kb.op("dve", lambda E, g=g, s=s: E.tensor_tensor(out=ogb[:, g, 0:nt], in0=sgt[s][0:64, 0:nt], in1=obb[:, g, 0:nt], op=ALU.mult),
                                  reads=[dn[f"sgt{s}"], dn["obb"]], writes=[dn["ogb"]])
                for j in range(4):
                    b = next_bank(); s = j % 2
                    proj_fm(GC0 + j * 128, 128, nt, b)
                    kb.op("act", lambda E, b=b, s=s: E.activation(out=sgt[s][:, 0:nt], in_=ps[b][:, 0:nt], func=AF.Sigmoid), reads=[d_ps[b]], writes=[dn[f"sgt{s}"]])
                    kb.op("dve", lambda E, b=b, s=s: E.tensor_tensor(out=sgt[s][:, 0:nt], in0=ps[b][:, 0:nt], in1=sgt[s][:, 0:nt], op=ALU.mult),
                          reads=[d_ps[b], dn[f"sgt{s}"]], writes=[dn[f"sgt{s}"]])
                    kb.op("dve", lambda E, j=j, s=s: E.tensor_tensor(out=og[:, j, 0:nt], in0=sgt[s][:, 0:nt], in1=oab[:, j, 0:nt], op=ALU.mult),
                          reads=[dn[f"sgt{s}"], dn["oab"]], writes=[dn["og"]])
                for t in range(ntile):
                    b = next_bank()
                    for k in range(8):
                        kb.op("pe", lambda E, k=k, t=t, b=b: E.matmul(ps[b][:, 0:256], lhsT=hTb[:, k, t * 128:(t + 1) * 128], rhs=wC[:, k, 256:512], start=(k == 0), stop=(k == 7)),
                              reads=[dw["wC"], dn["hTb"]], writes=[d_ps[b]])
                    kb.op("dve", lambda E, b=b: E.bn_stats(out=st6[:], in_=ps[b][:, 0:256]), reads=[d_ps[b]], writes=[dn["st6"]])
                    kb.op("dve", lambda E: E.bn_aggr(out=mv[:], in_=st6[:]), reads=[dn["st6"]], writes=[dn["mv"]])
                    kb.op("dve", lambda E: E.tensor_scalar(out=rstd[:], in0=mv[:, 1:2], scalar1=EPS, scalar2=None, op0=ALU.add), reads=[dn["mv"]], writes=[dn["rstd"]])
                    kb.op("act", lambda E: E.activation(out=rstd[:], in_=rstd[:], func=AF.Sqrt), reads=[dn["rstd"]], writes=[dn["rstd"]])
                    kb.op("dve", lambda E: E.reciprocal(out=rstd[:], in_=rstd[:]), reads=[dn["rstd"]], writes=[dn["rstd"]])
                    kb.op("dve", lambda E, b=b: E.tensor_scalar(out=vcn[:], in0=ps[b][:, 0:256], scalar1=mv[:, 0:1], scalar2=rstd[:, 0:1], op0=ALU.subtract, op1=ALU.mult),
                          reads=[d_ps[b], dn["mv"], dn["rstd"]], writes=[dn["vcn"]])
                    kb.op("dve", lambda E: E.tensor_tensor(out=vcn[:], in0=vcn[:], in1=LG[:], op=ALU.mult), reads=[dn["vcn"], dw["LG"]], writes=[dn["vcn"]])
                    kb.op("dve", lambda E: E.tensor_tensor(out=vnb[:], in0=vcn[:], in1=LB[:], op=ALU.add), reads=[dn["vcn"], dw["LB"]], writes=[dn["vnb"]])
                    b2 = next_bank()
                    for g in range(4):
                        kb.op("pe", lambda E, g=g, b2=b2: E.matmul(ps[b2][0:64, g * 128:(g + 1) * 128], lhsT=vnb[:, g * 64:(g + 1) * 64], rhs=wS[:, g, :], start=True, stop=True),
                              reads=[dn["vnb"], dw["wS"]], writes=[d_ps[b2]])
                    kb.op("dve", lambda E, b2=b2: E.tensor_tensor(out=sT[:].rearrange("c g p -> c (g p)"), in0=ps[b2][0:64, :], in1=BS[:].rearrange("c g p -> c (g p)"), op=ALU.add),
                          reads=[d_ps[b2], dw["BS"]], writes=[dn["sT"]])
                    kb.op("dve", lambda E, t=t: E.tensor_tensor(out=sT[:], in0=sT[:], in1=uT[:, :, t * 128:(t + 1) * 128], op=ALU.mult), reads=[dn["sT"], dn["uT"]], writes=[dn["sT"]])
                    kb.op("dve", lambda E, t=t: E.tensor_tensor(out=ogc[:, :, t * 128:(t + 1) * 128], in0=sT[:], in1=gcT[:, :, t * 128:(t + 1) * 128], op=ALU.mult),
                          reads=[dn["sT"], dn["gcT"]], writes=[dn["ogc"]])
                for dc in range(8):
                    dsl = slice(dc * 128, (dc + 1) * 128)
                    for e in range(4):
                        kb.op("pe", lambda E, e=e: E.matmul(ps[0][:, 0:nt], lhsT=wBR[:, e, dsl], rhs=og[:, e, 0:nt], start=(e == 0), stop=(e == 3)),
                              reads=[dw["wBR"], dn["og"]], writes=[d_ps[0]])
                    for g in range(4):
                        kb.op("pe", lambda E, g=g: E.matmul(ps[1][:, 0:nt], lhsT=wBRb[:, g, dsl], rhs=ogb[:, g, 0:nt], start=(g == 0), stop=(g == 3)),
                              reads=[dw["wBRb"], dn["ogb"]], writes=[d_ps[1]])
                    for g in range(4):
                        kb.op("pe", lambda E, g=g: E.matmul(ps[2][:, 0:nt], lhsT=wBRc[:, g, dsl], rhs=ogc[:, g, 0:nt], start=(g == 0), stop=(g == 3)),
                              reads=[dw["wBRc"], dn["ogc"]], writes=[d_ps[2]])
                    ms = dc % 2
                    for i in range(3):
                        proj_fm(MC0 + i * 1024 + dc * 128, 128, nt, 3 + i)
                        kb.op("act", lambda E, i=i, ms=ms: E.activation(out=mm[ms][:, i, 0:nt], in_=ps[3 + i][:, 0:nt], func=AF.Sigmoid), reads=[d_ps[3 + i]], writes=[dn[f"mm{ms}"]])
                    kb.op("dve", lambda E, ms=ms: E.tensor_tensor(out=t0[:, 0:nt], in0=ps[0][:, 0:nt], in1=mm[ms][:, 0, 0:nt], op=ALU.mult), reads=[d_ps[0], dn[f"mm{ms}"]], writes=[dn["t0"]])
                    kb.op("dve", lambda E, ms=ms: E.tensor_tensor(out=t1[:, 0:nt], in0=ps[1][:, 0:nt], in1=mm[ms][:, 1, 0:nt], op=ALU.mult), reads=[d_ps[1], dn[f"mm{ms}"]], writes=[dn["t1"]])
                    kb.op("dve", lambda E: E.tensor_tensor(out=t0[:, 0:nt], in0=t0[:, 0:nt], in1=t1[:, 0:nt], op=ALU.add), reads=[dn["t0"], dn["t1"]], writes=[dn["t0"]])
                    kb.op("dve", lambda E, ms=ms: E.tensor_tensor(out=t1[:, 0:nt], in0=ps[2][:, 0:nt], in1=mm[ms][:, 2, 0:nt], op=ALU.mult), reads=[d_ps[2], dn[f"mm{ms}"]], writes=[dn["t1"]])
                    kb.op("dve", lambda E, dc=dc: E.tensor_tensor(out=yT[:, dc, 0:nt], in0=t0[:, 0:nt], in1=t1[:, 0:nt], op=ALU.add), reads=[dn["t0"], dn["t1"]], writes=[dn["yT"]])
                for t in range(ntile):
                    gt = (tok0 // 128) + t
                    s = gt % 2
                    kb.dma("sp", f"c_x{s}", xt[s][:], x_src[gt * 128:(gt + 1) * 128, :], writes=[dn[f"xt{s}"]])
                    for cb in range(2):
                        b = next_bank()
                        for k in range(8):
                            kb.op("pe", lambda E, k=k, cb=cb, b=b, t=t: E.matmul(ps[b][:], lhsT=yT[:, k, t * 128:(t + 1) * 128], rhs=wO[:, k, cb * 512:(cb + 1) * 512],
                                                                              start=(k == 0), stop=(k == 7)),
                                  reads=[dn["yT"], dw["wO"]], writes=[d_ps[b]])
                        kb.op("dve", lambda E, cb=cb, b=b: E.tensor_tensor(out=tmpo[:, cb * 512:(cb + 1) * 512], in0=ps[b][:], in1=G[var][:, cb * 512:(cb + 1) * 512], op=ALU.mult),
                              reads=[d_ps[b], dw["G"]], writes=[dn["tmpo"]])
                    kb.op("dve", lambda E, s=s: E.tensor_tensor(out=xt[s][:], in0=xt[s][:], in1=tmpo[:], op=ALU.add), reads=[dn["tmpo"], dn[f"xt{s}"]], writes=[dn[f"xt{s}"]])
                    store(f"st_x{s}", x_dst[gt * 128:(gt + 1) * 128, :], xt[s][:], [dn[f"xt{s}"]])
            kb.barrier()

    import os
    FS = os.environ.get("FSTOP", "")
    for l in range(1 if FS else 2):
        last = l == 1
        x_src = x_in if l == 0 else x1
        stage_a(l, x_src)
        if FS == "a":
            break
        collective("AllGather", f_lat, f_ag, d_fag)
        for i in range(2):
            collective("AllGather", kT_lat[i], kT_ag[i], d_kvag)
            collective("AllGather", v_lat[i], v_ag[i], d_kvag)
        if FS == "ag":
            kb.barrier()
            break
        stage_b(l, last)
        if FS == "b":
            break
        stage_c(l, last, x_src, y_out if last else x1)
    kb.barrier()
    for k in sorted(out_keys):
        nc.gpsimd.wait_ge(kb.sems[k], kb.cnt[k])
    return kb


import numpy as np
import ml_dtypes
BF = ml_dtypes.bfloat16
NLT = 2048


def rope_tabs():
    n = 8192
    row = np.repeat(np.arange(n // 64), 64).astype(np.float32)
    col = np.tile(np.arange(64), n // 64).astype(np.float32)
    freqs = (10000.0 ** (-np.arange(0, 32, 2, dtype=np.float32) / 32)).astype(np.float32)
    ar = row[:, None] * freqs; ac = col[:, None] * freqs
    ang = np.concatenate([ar, ar, ac, ac], -1)
    cos = np.cos(ang).astype(np.float32); sin = np.sin(ang).astype(np.float32)
    sgn = np.tile(np.concatenate([-np.ones(16), np.ones(16)]), 2).astype(np.float32)
    return cos, sin * sgn


def col_layout(v, k):
    return np.ascontiguousarray(v.reshape(k, 128).T)


def fourier_consts(j):
    t = np.arange(64)[:, None]; kb = np.arange(64)[None, :]
    a = 2 * np.pi * ((t * kb) % 64) / 64
    cs64 = np.concatenate([np.cos(a), -np.sin(a)], 1).astype(BF)
    p = np.arange(128)[:, None, None]; kbb = np.arange(64)[None, :, None]; ka = (32 * j + np.arange(32))[None, None, :]
    a = 2 * np.pi * ((p * (64 * ka + kbb)) % 8192) / 8192
    T1 = np.concatenate([np.cos(a), -np.sin(a)], 2).astype(BF)
    T2 = np.concatenate([np.sin(a), np.cos(a)], 2).astype(BF)
    n = np.arange(128)[:, None, None]; ch = np.arange(2)[None, :, None]; k = np.arange(256)[None, None, :]
    a = 2 * np.pi * (((ch * 128 + n) * k) % 256) / 256
    dctx = np.concatenate([np.cos(a), -np.sin(a)], 2).astype(BF)
    c1 = np.arange(64)[:, None]; c2 = np.arange(64)[None, :]
    a = 2 * np.pi * ((c1 * c2) % 64) / 64
    ccs = np.stack([np.cos(a), np.sin(a)], 1).astype(np.float32)
    return dict(cs64=cs64, T1j=np.ascontiguousarray(T1), T2j=np.ascontiguousarray(T2), dctx=np.ascontiguousarray(dctx), ccs=np.ascontiguousarray(ccs))


def fused_inputs(I):
    cos, ssin = rope_tabs()
    C = np.ascontiguousarray
    shared = {
        "w_ada": C(I['w_ada']), "b_ada": C(np.stack([col_layout(I['b_ada'][l], 24) for l in range(2)])),
        "g_norm": C(np.stack([col_layout(I['g_norm'][l], 8) for l in range(2)])),
        "w_in": C(I['w_in']), "g_q": C(I['g_q']), "g_k": C(I['g_k']),
        "lamv": C(np.concatenate([I['lam_q1'], I['lam_k1'], I['lam_q2'], I['lam_k2']], 1)),
        "g_sub": C(I['g_sub'].reshape(2, 128, 1)),
        "w_f": C(I['w_f']), "b_f": C(I['b_f'].transpose(0, 2, 1)),
        "ln_g": C(I['ln_g']), "ln_b": C(I['ln_b']),
        "w_sT": C(I['w_s'].transpose(0, 3, 1, 2)),
        "bs64": C(np.broadcast_to(I['b_s'][:, None, :, :], (2, 64, 4, 128))),
        "w_br": C(np.concatenate([I['w_br_a'], I['w_br_b'], I['w_br_c']], 1)), "w_out": C(I['w_out']),
    }
    maps = []
    for core in range(8):
        b, j = core // 4, core % 4
        m = dict(shared)
        m["x_tok"] = C(np.concatenate([I['x'][b, j * NLT:(j + 1) * NLT], I['ctx'][b]], 0))
        m["cvec"] = C(np.stack([col_layout(I['c'][b], 8), col_layout(I['c_ctx'], 8)], -1))
        m["cos"] = C(cos[j * NLT:(j + 1) * NLT]); m["ssin"] = C(ssin[j * NLT:(j + 1) * NLT])
        m.update(fourier_consts(j))
        maps.append(m)
    return maps


def kernel(**inputs):
    I = {k: np.asarray(v, dtype=np.float32) for k, v in inputs.items()}
    kb = KB()
    build_fused(kb)
    res = kb.run(fused_inputs(I))
    out = np.empty((2, 8192, 1024), np.float32)
    for core in range(8):
        b, j = core // 4, core % 4
        out[b, j * NLT:(j + 1) * NLT] = np.asarray(res.results[core]["y"])
    return out
```

```python
import numpy as np
import concourse.bass as bass
import concourse.mybir as mybir
from concourse.bass_utils import run_bass_kernel_spmd

F32 = mybir.dt.float32
BF16 = mybir.dt.bfloat16
AF = mybir.ActivationFunctionType
ALU = mybir.AluOpType
AX = mybir.AxisListType


class Dep:
    __slots__ = ("w", "r", "name")

    def __init__(self, name=""):
        self.w = None
        self.r = []
        self.name = name


class KB:
    COMPUTE = ("pe", "act", "dve", "pool")

    def __init__(self):
        self.nc = bass.Bass("TRN2", target_bir_lowering=False)
        nc = self.nc
        self.eng = {"pe": nc.tensor, "act": nc.scalar, "dve": nc.vector,
                    "pool": nc.gpsimd, "sp": nc.sync}
        self.sems = {}
        self.cnt = {}
        for e in self.COMPUTE:
            self.sems[e] = nc.alloc_semaphore(name="s_" + e)
            self.cnt[e] = 0
        self.seen = {e: {} for e in self.eng}
        self.n_inst = 0

    def _sem(self, key):
        if key not in self.sems:
            self.sems[key] = self.nc.alloc_semaphore(name="d_" + str(key))
            self.cnt[key] = 0
        return self.sems[key]

    def _waits(self, e, reads, writes):
        need = {}

        def add(t, war=False):
            if t is None:
                return
            sk, v = t
            if sk == e and (war or e == "pe"):
                return
            if need.get(sk, 0) < v:
                need[sk] = v
        for d in reads:
            add(d.w)
        for d in writes:
            add(d.w)
            for t in d.r:
                add(t, war=True)
        E = self.eng[e]
        for sk, v in need.items():
            if self.seen[e].get(sk, 0) >= v:
                continue
            E.wait_ge(self.sems[sk], v)
            self.seen[e][sk] = v

    def _mark(self, tok, reads, writes):
        for d in reads:
            d.r.append(tok)
            if len(d.r) > 64:
                m = {}
                for sk, v in d.r:
                    if m.get(sk, 0) < v:
                        m[sk] = v
                d.r = list(m.items())
        for d in writes:
            d.w = tok
            d.r = []

    def op(self, e, fn, reads=(), writes=()):
        self._waits(e, reads, writes)
        inst = fn(self.eng[e])
        self.cnt[e] += 1
        inst.then_inc(self.sems[e], 1)
        self._mark((e, self.cnt[e]), reads, writes)
        self.n_inst += 1
        return inst

    def mm(self, fn, reads=(), writes=(), last=True):
        return self.op("pe", fn, reads, writes)

    def dma(self, q, key, out, in_, reads=(), writes=(), **kw):
        sem = self._sem(key)
        self._waits(q, reads, writes)
        inst = self.eng[q].dma_start(out=out, in_=in_, **kw)
        self.cnt[key] += 16
        inst.then_inc(sem, 16)
        self._mark((key, self.cnt[key]), reads, writes)
        self.n_inst += 1
        return inst

    def barrier(self):
        for e, E in self.eng.items():
            for sk, sem in self.sems.items():
                v = self.cnt[sk]
                if v == 0 or self.seen[e].get(sk, 0) >= v:
                    continue
                E.wait_ge(sem, v)
                self.seen[e][sk] = v

    def wait_all(self, e, deps):
        self._waits(e, deps, ())

    def run(self, in_maps, n=8, trace=False):
        return run_bass_kernel_spmd(self.nc, in_maps, core_ids=list(range(n)), trace=trace)


import math
from contextlib import ExitStack

NT_A = 18
NLAT_T = 16
TOKS = NT_A * 128
NKT = 66
NKEY = NKT * 128
EPS = 1e-6
BT = 256
O_Q, O_K, O_V, O_F, O_U, O_VC, O_GATE, O_MERGE = 0, 512, 1024, 1536, 1792, 2048, 2304, 3328
RG = [[0, 1, 2, 3], [4, 5, 6, 7]]


def build_fused(kb):
    nc = kb.nc
    EI = lambda name, shape, d=F32: nc.dram_tensor(name, shape, d, kind="ExternalInput").ap()
    IN = lambda name, shape, d=F32: nc.dram_tensor(name, shape, d, kind="Internal").ap()
    x_in = EI("x_tok", [TOKS, 1024])
    cvec = EI("cvec", [128, 8, 2])
    w_ada = EI("w_ada", [2, 1024, 3072]); b_ada = EI("b_ada", [2, 128, 24]); g_norm = EI("g_norm", [2, 128, 8])
    w_in = EI("w_in", [2, 1024, 6400])
    g_q = EI("g_q", [2, 64]); g_k = EI("g_k", [2, 64])
    cos = EI("cos", [2048, 64]); ssin = EI("ssin", [2048, 64])
    lamv = EI("lamv", [2, 256]); g_sub = EI("g_sub", [2, 128, 1])
    w_f = EI("w_f", [2, 4, 64, 64]); b_f = EI("b_f", [2, 64, 4])
    cs64 = EI("cs64", [64, 128], BF16); T1d = EI("T1j", [128, 64, 64], BF16); T2d = EI("T2j", [128, 64, 64], BF16)
    dcd = EI("dctx", [128, 2, 512], BF16); ccs = EI("ccs", [64, 2, 64])
    ln_g = EI("ln_g", [2, 256]); ln_b = EI("ln_b", [2, 256])
    w_sT = EI("w_sT", [2, 128, 4, 128]); bs_d = EI("bs64", [2, 64, 4, 128])
    w_br = EI("w_br", [2, 1024, 1024]); w_out = EI("w_out", [2, 1024, 1024])
    y_out = nc.dram_tensor("y", [2048, 1024], F32, kind="ExternalOutput").ap()
    x1 = IN("x1", [TOKS, 1024])
    hT_d = IN("hT_d", [8, 128, TOKS], BF16)
    qT_d = IN("qT_d", [4, 128, TOKS], BF16)
    kT_lat = [IN(f"kT_lat{i}", [256, 2048], BF16) for i in range(2)]
    kT_ag = [IN(f"kT_ag{i}", [1024, 2048], BF16) for i in range(2)]
    kT_ctx = IN("kT_ctx", [4, 128, 256], BF16)
    v_lat = [IN(f"v_lat{i}", [1024, 512], BF16) for i in range(2)]
    v_ag = [IN(f"v_ag{i}", [4096, 512], BF16) for i in range(2)]
    v_ctx = IN("v_ctx", [256, 512], BF16)
    f_lat = IN("f_lat", [4 * 2048, 64], BF16)
    f_ag = IN("f_ag", [16 * 2048, 64], BF16)
    f_ctx = IN("f_ctx", [256, 256], BF16)
    oaT_d = IN("oaT_d", [4, 128, TOKS])
    obT_d = IN("obT_d", [4, 64, TOKS])
    modT_d = IN("modT_d", [128, 24, 2])
    out_keys = set()
    uid = [0]

    def U(name):
        uid[0] += 1
        return f"{name}_{uid[0]}"

    def store(key, out, in_, reads):
        out_keys.add(key)
        kb.dma("pool", key, out, in_, reads=reads, writes=[])

    d_fag = Dep("f_ag"); d_kvag = Dep("kv_ag")

    def collective(kind, src, dst, dep):
        sem = kb._sem("cc")
        inst = nc.gpsimd.collective_compute(kind, ALU.bypass, replica_groups=RG, ins=[src], outs=[dst])
        inst.then_inc(sem, 1)
        kb.cnt["cc"] += 1
        dep.w = ("cc", kb.cnt["cc"])

    def rstd_chain(ssrc, dsrc, dst, ddst, scale):
        kb.op("dve", lambda E: E.tensor_scalar(out=dst, in0=ssrc, scalar1=scale, scalar2=EPS, op0=ALU.mult, op1=ALU.add), reads=[dsrc], writes=[ddst])
        kb.op("act", lambda E: E.activation(out=dst, in_=dst, func=AF.Sqrt), reads=[ddst], writes=[ddst])
        kb.op("dve", lambda E: E.reciprocal(out=dst, in_=dst), reads=[ddst], writes=[ddst])

    def make_ident(ident, dep):
        kb.op("pool", lambda E: E.memset(ident[:], 0.0), writes=[dep])
        kb.op("pool", lambda E: E.affine_select(out=ident[:], in_=ident[:], pattern=[[-1, 128]], compare_op=ALU.not_equal,
                                                fill=1.0, base=0, channel_multiplier=1), reads=[dep], writes=[dep])

    def stage_a(l, x_src):
        with ExitStack() as es:
            sb = lambda name, shape, dty: es.enter_context(nc.sbuf_tensor(U(name), shape, dty))
            ps = [es.enter_context(nc.psum_tensor(U(f"psA{i}"), [128, 512], F32)) for i in range(6)]
            ps += [es.enter_context(nc.psum_tensor(U(f"psA{i}"), [128, 1024], BF16)) for i in (6, 7)]
            ident = sb("ident", [128, 128], F32); identb = sb("identb", [128, 128], BF16)
            cT = sb("cT", [128, 8, 2], F32); sg = sb("sg", [128, 8, 2], F32); sc = sb("sc", [128, 8, 2], F32)
            bT = sb("bT", [128, 24], F32); gn = sb("gn", [128, 8], F32)
            modT = sb("modT_s", [128, 24, 2], F32); Aff = sb("Aff", [128, 8, 2], F32)
            wst = [sb(f"wst{i}", [128, 8, 512], F32) for i in range(2)]
            wA = sb("wA", [128, 8, 1792], BF16)
            GQ = sb("GQ", [128, 64], F32); GK = sb("GK", [128, 64], F32)
            xt = [sb(f"xt{i}", [128, 1024], F32) for i in range(2)]
            junk = sb("junk", [128, 1024], F32)
            ss = sb("ss", [128, 1], F32); rstd = sb("rstd", [128, 1], F32)
            hTt = [sb(f"hTt{i}", [128, 8, 128], BF16) for i in range(2)]
            cs = [sb(f"cs{i}", [128, 2, 64], F32) for i in range(2)]
            sq = sb("sq", [128, 512], F32); ss8 = sb("ss8", [128, 8], F32)
            qn = sb("qn", [128, 512], F32); t1 = sb("t1", [128, 512], F32); t2 = sb("t2", [128, 512], F32)
            qr = [sb(f"qr{i}", [128, 512], BF16) for i in range(2)]
            qTt = [sb(f"qTt{i}", [128, 4, 128], BF16) for i in range(2)]
            kTt = [sb(f"kTt{i}", [128, 4, 128], BF16) for i in range(2)]
            vt = [sb(f"vt{i}", [128, 512], BF16) for i in range(2)]
            ft = [sb(f"ft{i}", [128, 256], BF16) for i in range(2)]
            D = lambda n: Dep(n)
            d_ident, d_identb, d_cT, d_sg, d_sc, d_bT, d_gn, d_modT, d_Aff = [D(n) for n in "ident identb cT sg sc bT gn modT Aff".split()]
            d_wst = [D("wst0"), D("wst1")]; d_wA = D("wA"); d_G = D("G")
            d_xt = [D("xt0"), D("xt1")]; d_junk = D("junk"); d_ss = D("ss"); d_rstd = D("rstd")
            d_hTt = [D("hTt0"), D("hTt1")]; d_cs = [D("cs0"), D("cs1")]
            d_sq, d_ss8, d_qn, d_t1, d_t2 = D("sq"), D("ss8"), D("qn"), D("t1"), D("t2")
            d_qr = [D("qr0"), D("qr1")]; d_qTt = [D("qTt0"), D("qTt1")]; d_kTt = [D("kTt0"), D("kTt1")]
            d_vt = [D("vt0"), D("vt1")]; d_ft = [D("ft0"), D("ft1")]
            d_ps = [D(f"ps{i}") for i in range(8)]

            make_ident(ident, d_ident)
            kb.op("pool", lambda E: E.tensor_copy(out=identb[:], in_=ident[:]), reads=[d_ident], writes=[d_identb])
            kb.dma("sp", "ld_c0", cT[:], cvec, writes=[d_cT])
            kb.dma("sp", "ld_c1", bT[:], b_ada[l], writes=[d_bT])
            kb.dma("sp", "ld_c2", gn[:], g_norm[l], writes=[d_gn])
            kb.dma("sp", "ld_c3", GQ[:], g_q[l].partition_broadcast(128), writes=[d_G])
            kb.dma("sp", "ld_c3", GK[:], g_k[l].partition_broadcast(128), writes=[d_G])
            kb.op("act", lambda E: E.activation(out=sg[:], in_=cT[:], func=AF.Sigmoid), reads=[d_cT], writes=[d_sg])
            kb.op("dve", lambda E: E.tensor_tensor(out=sc[:], in0=cT[:], in1=sg[:], op=ALU.mult), reads=[d_cT, d_sg], writes=[d_sc])
            w_ada_v = w_ada[l].rearrange("(k p) n -> p k n", p=128)
            for g in range(6):
                s = g % 2
                kb.dma("sp", f"ld_w{s}", wst[s][:], w_ada_v[:, :, g * 512:(g + 1) * 512], writes=[d_wst[s]])
                for jj in range(4):
                    j = g * 4 + jj
                    for k in range(8):
                        kb.op("pe", lambda E, k=k, jj=jj, s=s, j=j: E.matmul(ps[0][:, 2 * j:2 * j + 2], lhsT=wst[s][:, k, jj * 128:(jj + 1) * 128],
                                                                            rhs=sc[:, k, :], start=(k == 0), stop=(k == 7)),
                              reads=[d_wst[s], d_sc], writes=[d_ps[0]])
            kb.op("dve", lambda E: E.tensor_tensor(out=modT[:], in0=ps[0][:, 0:48].rearrange("p (j n) -> p j n", n=2),
                                                   in1=bT[:].unsqueeze(2).to_broadcast([128, 24, 2]), op=ALU.add),
                  reads=[d_ps[0], d_bT], writes=[d_modT])
            store("st_mod", modT_d, modT[:], [d_modT])
            kb.op("dve", lambda E: E.tensor_scalar(out=Aff[:], in0=modT[:, 8:16, :], scalar1=1.0, scalar2=None, op0=ALU.add), reads=[d_modT], writes=[d_Aff])
            kb.op("dve", lambda E: E.tensor_tensor(out=Aff[:], in0=Aff[:], in1=gn[:].unsqueeze(2).to_broadcast([128, 8, 2]), op=ALU.mult),
                  reads=[d_Aff, d_gn], writes=[d_Aff])
            w_in_v = w_in[l].rearrange("(k p) n -> p k n", p=128)
            for g in range(4):
                s = g % 2
                n = 512 if g < 3 else 256
                kb.dma("sp", f"ld_w{s}", wst[s][:, :, 0:n], w_in_v[:, :, g * 512:g * 512 + n], writes=[d_wst[s]])
                e = "pool" if g % 2 == 0 else "dve"
                kb.op(e, lambda E, s=s, n=n, g=g: E.tensor_copy(out=wA[:, :, g * 512:g * 512 + n], in_=wst[s][:, :, 0:n]), reads=[d_wst[s]], writes=[d_wA])

            def load_tile(i):
                s = i % 2
                kb.dma("sp", f"ld_x{s}", xt[s][:], x_src[i * 128:(i + 1) * 128, :], writes=[d_xt[s]])
                if i < NLAT_T:
                    kb.dma("sp", f"ld_cs{s}", cs[s][:, 0, :], cos[i * 128:(i + 1) * 128, :], writes=[d_cs[s]])
                    kb.dma("sp", f"ld_cs{s}", cs[s][:, 1, :], ssin[i * 128:(i + 1) * 128, :], writes=[d_cs[s]])

            G2 = sb("G2", [128, 2, 64], F32)
            sq2 = sb("sq2", [128, 1024], F32); ss16 = sb("ss16", [128, 16], F32)
            qn2 = sb("qn2", [128, 1024], F32); t1b = sb("t1b", [128, 1024], F32); t2b = sb("t2b", [128, 1024], F32)
            QR2 = sb("QR2", [128, 1024], BF16)
            TT2 = [sb(f"TT2_{i}", [128, 8, 128], BF16) for i in range(2)]
            d_G2, d_sq2, d_ss16, d_qn2, d_t1b, d_t2b, d_QR2 = [D(n) for n in "G2 sq2 ss16 qn2 t1b t2b QR2".split()]
            d_TT2 = [D("TT2_0"), D("TT2_1")]
            kb.op("dve", lambda E: E.tensor_copy(out=G2[:, 0, :], in_=GQ[:]), reads=[d_G], writes=[d_G2])
            kb.op("dve", lambda E: E.tensor_copy(out=G2[:, 1, :], in_=GK[:]), reads=[d_G], writes=[d_G2])
            psQK = [ps[2], ps[3]]
            g64 = lambda ap: ap.rearrange("p (g c) -> p g c", c=64)

            def head(i):
                s = i % 2
                var = 0 if i < NLAT_T else 1
                load_tile(i)
                X = xt[s]
                kb.op("act", lambda E: E.activation(out=junk[:], in_=X[:], func=AF.Square, accum_out=ss[:]), reads=[d_xt[s]], writes=[d_junk, d_ss])
                rstd_chain(ss[:], d_ss, rstd[:], d_rstd, 1.0 / 1024)
                kb.op("dve", lambda E: E.tensor_scalar(out=X[:], in0=X[:], scalar1=rstd[:, 0:1], scalar2=None, op0=ALU.mult),
                      reads=[d_xt[s], d_rstd], writes=[d_xt[s]])
                for k in range(8):
                    b = k // 4
                    kb.op("pe", lambda E, k=k, b=b: E.transpose(out=ps[b][:, (k % 4) * 128:(k % 4 + 1) * 128], in_=X[:, k * 128:(k + 1) * 128], identity=ident[:]),
                          reads=[d_xt[s], d_ident], writes=[d_ps[b]])
                for k in range(8):
                    b = k // 4
                    src = ps[b][:, (k % 4) * 128:(k % 4 + 1) * 128]
                    if k % 2 == 0:
                        kb.op("act", lambda E, k=k, src=src: E.activation(out=hTt[s][:, k, :], in_=src, func=AF.Identity,
                                                                          scale=Aff[:, k, var:var + 1], bias=modT[:, k, var:var + 1]),
                              reads=[d_ps[b], d_Aff, d_modT], writes=[d_hTt[s]])
                    else:
                        kb.op("dve", lambda E, k=k, src=src: E.tensor_scalar(out=hTt[s][:, k, :], in0=src, scalar1=Aff[:, k, var:var + 1],
                                                                             scalar2=modT[:, k, var:var + 1], op0=ALU.mult, op1=ALU.add),
                              reads=[d_ps[b], d_Aff, d_modT], writes=[d_hTt[s]])
                store(f"st_h{s}", hT_d[:, :, i * 128:(i + 1) * 128].rearrange("k p t -> p k t"), hTt[s][:], [d_hTt[s]])

            def mid(i):
                s = i % 2
                for cb in range(4):
                    n = 512 if cb < 3 else 256
                    for k in range(8):
                        kb.op("pe", lambda E, k=k, cb=cb, n=n: E.matmul(ps[2 + cb][:, 0:n], lhsT=hTt[s][:, k, :], rhs=wA[:, k, cb * 512:cb * 512 + n],
                                                                       start=(k == 0), stop=(k == 7)),
                              reads=[d_hTt[s], d_wA], writes=[d_ps[2 + cb]])

            def tail_a(i):
                s = i % 2
                lat = i < NLAT_T
                for w in range(2):
                    kb.op("act", lambda E, w=w: E.activation(out=sq2[:, w * 512:(w + 1) * 512], in_=psQK[w][:], func=AF.Square), reads=[d_ps[2 + w]], writes=[d_sq2])
                kb.op("dve", lambda E: E.tensor_reduce(out=ss16[:], in_=g64(sq2[:]), axis=AX.X, op=ALU.add), reads=[d_sq2], writes=[d_ss16])
                rstd_chain(ss16[:], d_ss16, ss16[:], d_ss16, 1.0 / 64)
                for w in range(2):
                    kb.op("dve", lambda E, w=w: E.tensor_tensor(out=g64(qn2[:, w * 512:(w + 1) * 512]), in0=g64(psQK[w][:]),
                                                           in1=ss16[:, w * 8:(w + 1) * 8].unsqueeze(2).to_broadcast([128, 8, 64]), op=ALU.mult),
                          reads=[d_ps[2 + w], d_ss16], writes=[d_qn2])
                kb.op("act", lambda E: E.activation(out=vt[s][:], in_=ps[4][:], func=AF.Copy), reads=[d_ps[4]], writes=[d_vt[s]])
                kb.op("act", lambda E: E.activation(out=ft[s][:], in_=ps[5][:, 0:256], func=AF.Copy), reads=[d_ps[5]], writes=[d_ft[s]])
                if lat:
                    store(f"st_v{s}", v_lat[i // 8][(i % 8) * 128:(i % 8 + 1) * 128, :], vt[s][:], [d_vt[s]])
                    store(f"st_f{s}", f_lat.rearrange("(g t) c -> t g c", g=4)[i * 128:(i + 1) * 128, :, :], ft[s][:].rearrange("p (g c) -> p g c", c=64), [d_ft[s]])
                else:
                    ic = i - NLAT_T
                    store(f"st_v{s}", v_ctx[ic * 128:(ic + 1) * 128, :], vt[s][:], [d_vt[s]])
                    store(f"st_f{s}", f_ctx[ic * 128:(ic + 1) * 128, :], ft[s][:], [d_ft[s]])

            def tail_b(i):
                s = i % 2
                lat = i < NLAT_T
                qv4 = qn2[:].rearrange("p (w g c) -> p w g c", w=2, c=64)
                G2b = G2[:].unsqueeze(2).to_broadcast([128, 2, 8, 64])
                if lat:
                    for w in range(2):
                        kb.op("dve", lambda E, w=w: E.tensor_tensor(out=qv4[:, w], in0=qv4[:, w], in1=G2[:, w, :].unsqueeze(1).to_broadcast([128, 8, 64]), op=ALU.mult),
                              reads=[d_qn2, d_G2], writes=[d_qn2])
                    kb.op("dve", lambda E: E.tensor_tensor(out=g64(t1b[:]), in0=g64(qn2[:]), in1=cs[s][:, 0, :].unsqueeze(1).to_broadcast([128, 16, 64]), op=ALU.mult),
                          reads=[d_qn2, d_cs[s]], writes=[d_t1b])
                    qv = qn2[:].rearrange("p (g a h c) -> p g a h c", a=2, h=2, c=16)
                    tv = t2b[:].rearrange("p (g a h c) -> p g a h c", a=2, h=2, c=16)
                    sv = cs[s][:, 1, :].rearrange("p (a h c) -> p a h c", a=2, h=2)
                    for hf in range(2):
                        for a in range(2):
                            kb.op("dve", lambda E, hf=hf, a=a: E.tensor_tensor(out=tv[:, :, a, hf, :], in0=qv[:, :, a, 1 - hf, :],
                                                                                in1=sv[:, a, hf, :].unsqueeze(1).to_broadcast([128, 16, 16]), op=ALU.mult),
                                  reads=[d_qn2, d_cs[s]], writes=[d_t2b])
                    kb.op("dve", lambda E: E.tensor_tensor(out=QR2[:], in0=t1b[:], in1=t2b[:], op=ALU.add), reads=[d_t1b, d_t2b], writes=[d_QR2])
                else:
                    QRv = QR2[:].rearrange("p (w g c) -> p w g c", w=2, c=64)
                    for w in range(2):
                        kb.op("dve", lambda E, w=w: E.tensor_tensor(out=QRv[:, w], in0=qv4[:, w], in1=G2[:, w, :].unsqueeze(1).to_broadcast([128, 8, 64]), op=ALU.mult),
                              reads=[d_qn2, d_G2], writes=[d_QR2])
                PT = ps[6]; dPT = d_ps[6]
                for hh in range(8):
                    kb.op("pe", lambda E, hh=hh: E.transpose(out=PT[:, hh * 128:(hh + 1) * 128], in_=QR2[:, hh * 128:(hh + 1) * 128], identity=identb[:]),
                          reads=[d_QR2, d_identb], writes=[dPT])
                TT = TT2[s]; dTT = d_TT2[s]
                kb.op("act", lambda E: E.activation(out=TT[:].rearrange("p h t -> p (h t)"), in_=PT[:, 0:1024], func=AF.Copy), reads=[dPT], writes=[dTT])
                store(f"st_q{s}", qT_d[:, :, i * 128:(i + 1) * 128].rearrange("h p t -> p h t"), TT[:, 0:4, :], [dTT])
                if lat:
                    for hp in range(2):
                        store(f"st_k{s}", kT_lat[hp].rearrange("(h p) t -> p h t", p=128)[:, :, i * 128:(i + 1) * 128], TT[:, 4 + hp * 2:4 + hp * 2 + 2, :], [dTT])
                else:
                    store(f"st_k{s}", kT_ctx[:, :, (i - NLAT_T) * 128:(i - NLAT_T + 1) * 128].rearrange("h p t -> p h t"), TT[:, 4:8, :], [dTT])

            head(0)
            mid(0)
            head(1)
            for i in range(NT_A):
                tail_a(i)
                if i + 1 < NT_A:
                    mid(i + 1)
                tail_b(i)
                if i + 2 < NT_A:
                    head(i + 2)
            kb.barrier()

    def stage_b(l, last):
        lam_init = 0.8 - 0.6 * math.exp(-0.3 * l)
        with ExitStack() as es:
            sbt = lambda name, shape, dty: es.enter_context(nc.sbuf_tensor(U(name), shape, dty))
            ps = [es.enter_context(nc.psum_tensor(U(f"psF{i}"), [128, 512], F32)) for i in range(8)]
            d_ps = [Dep(f"ps{i}") for i in range(8)]
            X1 = [sbt(f"X1_{i}", [64, 8192], BF16) for i in range(2)]
            Xc = sbt("Xc", [128, 2, 256], BF16)
            CS = sbt("CS", [64, 128], BF16)
            T1 = sbt("T1s", [128, 64, 64], BF16)
            T2 = sbt("T2s", [128, 64, 64], BF16)
            DC = sbt("DC", [128, 2, 512], BF16)
            Bsb = sbt("Bsb", [128, 2, 64, 64], BF16)
            YT = sbt("YT", [64, 2, 2048], BF16)
            YC = sbt("YC", [64, 2, 256], BF16)
            CC = sbt("CC", [64, 2, 64], F32)
            WF = sbt("WF", [64, 4, 64], F32)
            BF_ = sbt("BF", [64, 4], F32)
            MC = sbt("MC", [64, 4, 2, 64], BF16)
            ob = [sbt(f"ob{i}", [64, 512], F32) for i in range(2)]
            dn = {n: Dep(n) for n in "X1_0 X1_1 Xc CS T1 T2 DC Bsb YT YC CC WF BF MC ob0 ob1".split()}
            f_ag_v = f_ag.rearrange("(r g t p) c -> r g t (p c)", r=4, g=4, p=128)

            def load_x1(g):
                s = g % 2
                for r in range(4):
                    kb.dma("sp", f"f_x1{s}", X1[s][r * 16:(r + 1) * 16, :], f_ag_v[r, g], reads=[d_fag], writes=[dn[f"X1_{s}"]])
            load_x1(0)
            if not last:
                kb.dma("sp", "f_xc", Xc[:], f_ctx.rearrange("(k p) c -> p k c", p=128), writes=[dn["Xc"]])
                kb.dma("sp", "f_dc", DC[:], dcd, writes=[dn["DC"]])
            kb.dma("sp", "f_cs", CS[:], cs64, writes=[dn["CS"]])
            kb.dma("sp", "f_cc", CC[:], ccs, writes=[dn["CC"]])
            kb.dma("sp", "f_wf", WF[:], w_f[l].rearrange("g c d -> c g d"), writes=[dn["WF"]])
            kb.dma("sp", "f_bf", BF_[:], b_f[l], writes=[dn["BF"]])
            kb.dma("sp", "f_t1", T1[:], T1d, writes=[dn["T1"]])
            kb.dma("sp", "f_t2", T2[:], T2d, writes=[dn["T2"]])
            for g in range(4):
                for i in range(2):
                    kb.op("pe", lambda E, i=i, g=g: E.matmul(ps[0][0:64, (g * 2 + i) * 64:(g * 2 + i + 1) * 64], lhsT=CC[:, i, :], rhs=WF[:, g, :], start=True, stop=True),
                          reads=[dn["CC"], dn["WF"]], writes=[d_ps[0]])
            kb.op("dve", lambda E: E.tensor_copy(out=MC[:].rearrange("c g i d -> c (g i d)"), in_=ps[0][0:64, 0:512]), reads=[d_ps[0]], writes=[dn["MC"]])
            sc_lat = 1.0 / math.sqrt(8192 * 64); sc_ctx = 1.0 / math.sqrt(256 * 64)
            blkc = [0]
            for g in range(4):
                s = g % 2
                if g + 1 < 4:
                    load_x1(g + 1)
                X1v = X1[s][:].rearrange("t (p c) -> t p c", c=64)
                Bv = Bsb[:].rearrange("p r k c -> p c r k")
                for c4 in range(16):
                    b = 1 + c4 % 2
                    for cc in range(4):
                        c = c4 * 4 + cc
                        kb.op("pe", lambda E, c=c, cc=cc, b=b: E.matmul(ps[b][:, cc * 128:(cc + 1) * 128], lhsT=X1v[:, :, c], rhs=CS[:], start=True, stop=True),
                              reads=[dn[f"X1_{s}"], dn["CS"]], writes=[d_ps[b]])
                    src = ps[b][:].rearrange("p (c r k) -> p c r k", c=4, r=2)
                    dst = Bv[:, c4 * 4:(c4 + 1) * 4, :, :]
                    kb.op("act", lambda E, src=src, dst=dst: E.activation(out=dst[:, :, 0, :], in_=src[:, :, 0, :], func=AF.Copy), reads=[d_ps[b]], writes=[dn["Bsb"]])
                    kb.op("dve", lambda E, src=src, dst=dst: E.tensor_copy(out=dst[:, :, 1, :], in_=src[:, :, 1, :]), reads=[d_ps[b]], writes=[dn["Bsb"]])
                for k8 in range(8):
                    b = 3 + k8 % 2
                    for ki in range(8):
                        kbi = k8 * 8 + ki
                        o = ps[b][0:64, ki * 64:(ki + 1) * 64]
                        kb.op("pe", lambda E, kbi=kbi, o=o: E.matmul(o, lhsT=Bsb[:, 0, kbi, :], rhs=T1[:, kbi, :], start=True, stop=False),
                              reads=[dn["Bsb"], dn["T1"]], writes=[d_ps[b]])
                        kb.op("pe", lambda E, kbi=kbi, o=o: E.matmul(o, lhsT=Bsb[:, 1, kbi, :], rhs=T2[:, kbi, :], start=False, stop=True),
                              reads=[dn["Bsb"], dn["T2"]], writes=[d_ps[b]])
                    src = ps[b][0:64, :].rearrange("c (i r a) -> c i r a", i=8, r=2)
                    dst = YT[:].rearrange("c r (a k) -> c k r a", k=64)[:, k8 * 8:(k8 + 1) * 8, :, :]
                    for r in range(2):
                        if r == 0:
                            kb.op("act", lambda E, src=src, dst=dst, r=r: E.activation(out=dst[:, :, r, :], in_=src[:, :, r, :], func=AF.Copy), reads=[d_ps[b]], writes=[dn["YT"]])
                        else:
                            kb.op("dve", lambda E, src=src, dst=dst, r=r: E.tensor_copy(out=dst[:, :, r, :], in_=src[:, :, r, :]), reads=[d_ps[b]], writes=[dn["YT"]])
                nblk = 4
                if not last:
                    for k in range(2):
                        kb.op("pe", lambda E, k=k, g=g: E.matmul(ps[5][0:64, :], lhsT=Xc[:, k, g * 64:(g + 1) * 64], rhs=DC[:, k, :], start=(k == 0), stop=(k == 1)),
                              reads=[dn["Xc"], dn["DC"]], writes=[d_ps[5]])
                    kb.op("dve", lambda E: E.tensor_copy(out=YC[:].rearrange("c r k -> c (r k)"), in_=ps[5][0:64, :]), reads=[d_ps[5]], writes=[dn["YC"]])
                    nblk = 5
                for blk in range(nblk):
                    b = 6 + blkc[0] % 2
                    so = blkc[0] % 2
                    blkc[0] += 1
                    if blk < 4:
                        n = 512; r0 = YT[:, 0, blk * 512:(blk + 1) * 512]; r1 = YT[:, 1, blk * 512:(blk + 1) * 512]; scl = sc_lat; dy = dn["YT"]
                    else:
                        n = 256; r0 = YC[:, 0, :]; r1 = YC[:, 1, :]; scl = sc_ctx; dy = dn["YC"]
                    kb.op("pe", lambda E, r0=r0, n=n, b=b, g=g: E.matmul(ps[b][0:64, 0:n], lhsT=MC[:, g, 0, :], rhs=r0, start=True, stop=False), reads=[dn["MC"], dy], writes=[d_ps[b]])
                    kb.op("pe", lambda E, r1=r1, n=n, b=b, g=g: E.matmul(ps[b][0:64, 0:n], lhsT=MC[:, g, 1, :], rhs=r1, start=False, stop=True), reads=[dn["MC"], dy], writes=[d_ps[b]])
                    kb.op("act", lambda E, n=n, b=b, so=so, scl=scl, g=g: E.activation(out=ob[so][:, 0:n], in_=ps[b][0:64, 0:n], func=AF.Identity, scale=scl, bias=BF_[:, g:g + 1]),
                          reads=[d_ps[b], dn["BF"]], writes=[dn[f"ob{so}"]])
                    store(f"st_ob{so}", obT_d[g, :, blk * 512:blk * 512 + n], ob[so][:, 0:n], [dn[f"ob{so}"]])
            kb.barrier()

        with ExitStack() as es:
            sbt = lambda name, shape, dty: es.enter_context(nc.sbuf_tensor(U(name), shape, dty))
            psS = [es.enter_context(nc.psum_tensor(U(f"psS{i}"), [128, 1024], F32)) for i in range(2)]
            ps = [None] * 4 + [es.enter_context(nc.psum_tensor(U(f"psB{i}"), [128, 512], F32)) for i in range(4, 8)]
            d_psS = [Dep("psS0"), Dep("psS1")]
            d_ps = [Dep(f"ps{i}") for i in range(8)]
            kTh = [sbt(f"kTh{i}", [128, NKEY], BF16) for i in range(2)]
            vh = [sbt(f"vh{i}", [128, NKT, 128], BF16) for i in range(2)]
            qTh = [sbt(f"qTh{i}", [128, TOKS], BF16) for i in range(2)]
            NPT = 4
            pt = [sbt(f"pt{i}", [128, 1024], BF16) for i in range(NPT)]
            ones = sbt("ones", [128, 128], F32)
            onesb = sbt("onesb", [128, 32], BF16)
            sel = sbt("sel", [128, 2, 128], F32)
            rsb = sbt("rsb", [128, 512], F32)
            lv = sbt("lv", [128, 4, 64], F32); lt = sbt("lt", [128, 2, 64], F32); l2 = sbt("l2", [128, 2], F32)
            nlam = sbt("nlam", [128, 1], F32); gs = sbt("gs", [128, 1], F32)
            rec = [sbt(f"rec{c}", [128, 512], F32) for c in range(2)]
            o0 = sbt("o0", [128, 512], F32); o1 = sbt("o1", [128, 512], F32)
            osq = sbt("osq", [128, 512], F32); rs = sbt("rs", [128, 512], F32)
            of = [sbt(f"of{i}", [128, 512], F32) for i in range(2)]
            d_kv = [Dep("kv0"), Dep("kv1")]
            d_pt = [Dep(f"pt{i}") for i in range(NPT)]
            dm = {n: Dep(n) for n in "ones onesb sel rsb lv lt l2 nlam gs rec0 rec1 o0 o1 osq rs of0 of1".split()}
            kb.op("pool", lambda E: E.memset(ones[:], 1.0), writes=[dm["ones"]])
            kb.op("pool", lambda E: E.memset(onesb[:], 1.0), writes=[dm["onesb"]])
            kb.op("pool", lambda E: E.memset(sel[:], 0.0), writes=[dm["sel"]])
            for (p0, p1, c, val) in ((0, 32, 0, 1.0 / 32), (64, 96, 0, 1.0 / 32), (32, 64, 1, 1.0 / 32), (64, 128, 1, 1.0 / 32), (64, 96, 1, 0.0)):
                kb.op("pool", lambda E, p0=p0, p1=p1, c=c, val=val: E.memset(sel[p0:p1, c, :], val), reads=[dm["sel"]], writes=[dm["sel"]])
            kb.dma("sp", "a_lv", lv[:].rearrange("p a c -> p (a c)"), lamv[l].partition_broadcast(128), writes=[dm["lv"]])
            kb.dma("sp", "a_gs", gs[:], g_sub[l], writes=[dm["gs"]])
            lvv = lv[:].rearrange("p (i j) c -> p i j c", j=2)
            kb.op("dve", lambda E: E.tensor_tensor(out=lt[:], in0=lvv[:, :, 0, :], in1=lvv[:, :, 1, :], op=ALU.mult), reads=[dm["lv"]], writes=[dm["lt"]])
            kb.op("dve", lambda E: E.tensor_reduce(out=l2[:], in_=lt[:], axis=AX.X, op=ALU.add), reads=[dm["lt"]], writes=[dm["l2"]])
            kb.op("act", lambda E: E.activation(out=l2[:], in_=l2[:], func=AF.Exp), reads=[dm["l2"]], writes=[dm["l2"]])
            kb.op("dve", lambda E: E.tensor_tensor(out=nlam[:], in0=l2[:, 1:2], in1=l2[:, 0:1], op=ALU.subtract), reads=[dm["l2"]], writes=[dm["nlam"]])
            kb.op("dve", lambda E: E.tensor_scalar(out=nlam[:], in0=nlam[:], scalar1=-lam_init, scalar2=None, op0=ALU.add), reads=[dm["nlam"]], writes=[dm["nlam"]])
            kb.op("dve", lambda E: E.tensor_scalar(out=gs[:], in0=gs[:], scalar1=(1.0 - lam_init), scalar2=None, op0=ALU.mult), reads=[dm["gs"]], writes=[dm["gs"]])
            kT_ag_v = [a.rearrange("(r h p) t -> h p r t", r=4, h=2) for a in kT_ag]
            v_ag_v = [a.rearrange("(r t p) e -> p r t e", r=4, p=128) for a in v_ag]
            v_ctx_v = v_ctx.rearrange("(t p) e -> p t e", p=128)

            def load_head(h):
                s = h % 2
                kb.dma("sp", f"a_k{s}", kTh[s][:, 0:8192].rearrange("p (r t) -> p r t", r=4), kT_ag_v[h // 2][h % 2], reads=[d_kvag], writes=[d_kv[s]])
                kb.dma("sp", f"a_k{s}", kTh[s][:, 8192:NKEY], kT_ctx[h], writes=[d_kv[s]])
                for half in range(2):
                    for r in range(4):
                        kt0 = r * 16 + half * 8
                        kb.dma("sp", f"a_k{s}", vh[s][:, kt0:kt0 + 8, :], v_ag_v[half][:, r, :, h * 128:(h + 1) * 128], reads=[d_kvag], writes=[d_kv[s]])
                kb.dma("sp", f"a_k{s}", vh[s][:, 64:66, :], v_ctx_v[:, :, h * 128:(h + 1) * 128], writes=[d_kv[s]])
                kb.dma("sp", f"a_k{s}", qTh[s][:], qT_d[h], writes=[d_kv[s]])

            load_head(0)
            blk_id = 0
            for h in range(4):
                s = h % 2
                if h + 1 < 4:
                    load_head(h + 1)
                for qb in range(4 if last else 5):
                    if qb < 4:
                        q0 = qb * 512; nq = 512; kts = list(range(NKT))
                    else:
                        q0 = 2048; nq = 256; kts = [64, 65]
                    nk = len(kts)

                    def scores(idx):
                        kt = kts[idx]; sl = idx % 2
                        for c in range(2):
                            kb.op("pe", lambda E, c=c, kt=kt, sl=sl: E.matmul(psS[sl][:, c * 512:c * 512 + nq], lhsT=kTh[s][c * 64:(c + 1) * 64, kt * 128:(kt + 1) * 128],
                                                                           rhs=qTh[s][c * 64:(c + 1) * 64, q0:q0 + nq], start=True, stop=True, tile_position=(64 * c, 0)),
                                  reads=[d_kv[s]], writes=[d_psS[sl]])

                    def exps(idx):
                        sl = idx % 2; p = idx % NPT
                        kb.op("act", lambda E, sl=sl, p=p: E.activation(out=pt[p][:].rearrange("p (c q) -> p c q", c=2)[:, :, 0:nq],
                                                                        in_=psS[sl][:].rearrange("p (c q) -> p c q", c=2)[:, :, 0:nq], func=AF.Exp, scale=0.125),
                              reads=[d_psS[sl]], writes=[d_pt[p]])

                    def av(idx):
                        kt = kts[idx]; p = idx % NPT
                        for c in range(2):
                            kb.op("pe", lambda E, c=c, kt=kt, p=p, idx=idx: E.matmul(ps[4 + c][:, 0:nq], lhsT=vh[s][:, kt, :], rhs=pt[p][:, c * 512:c * 512 + nq],
                                                                                  start=(idx == 0), stop=(idx == nk - 1)),
                                  reads=[d_kv[s], d_pt[p]], writes=[d_ps[4 + c]])

                    def rowsums(idxs):
                        for idx in idxs:
                            p = idx % NPT
                            for c in range(2):
                                g4 = (idx % 2) * 2 + c
                                kb.op("pe", lambda E, c=c, p=p, idx=idx, g4=g4: E.matmul(ps[6][32 * g4:32 * g4 + 32, 0:nq], lhsT=onesb[:], rhs=pt[p][:, c * 512:c * 512 + nq],
                                                                                       start=(idx < 2), stop=(idx >= nk - 2), tile_position=(0, 32 * g4)),
                                      reads=[dm["onesb"], d_pt[p]], writes=[d_ps[6]])

                    scores(0)
                    pend = []

                    def av_rs(i2):
                        av(i2)
                        pend.append(i2)
                        if len(pend) == 2 or i2 == nk - 1:
                            rowsums(list(pend))
                            del pend[:]
                    for idx in range(nk):
                        exps(idx)
                        if idx + 1 < nk:
                            scores(idx + 1)
                        if idx >= 1:
                            av_rs(idx - 1)
                    av_rs(nk - 1)
                    kb.op("act", lambda E: E.activation(out=rsb[:, 0:nq], in_=ps[6][:, 0:nq], func=AF.Copy), reads=[d_ps[6]], writes=[dm["rsb"]])
                    for c in range(2):
                        bnk = 7 if c == 0 else 6
                        kb.op("pe", lambda E, c=c, bnk=bnk: E.matmul(ps[bnk][:, 0:nq], lhsT=sel[:, c, :], rhs=rsb[:, 0:nq], start=True, stop=True),
                              reads=[dm["sel"], dm["rsb"]], writes=[d_ps[bnk]])
                        kb.op("dve", lambda E, c=c, bnk=bnk: E.reciprocal(out=rec[c][:, 0:nq], in_=ps[bnk][:, 0:nq]), reads=[d_ps[bnk]], writes=[dm[f"rec{c}"]])
                    kb.op("dve", lambda E: E.tensor_tensor(out=o0[:, 0:nq], in0=ps[4][:, 0:nq], in1=rec[0][:, 0:nq], op=ALU.mult), reads=[d_ps[4], dm["rec0"]], writes=[dm["o0"]])
                    kb.op("dve", lambda E: E.tensor_tensor(out=o1[:, 0:nq], in0=ps[5][:, 0:nq], in1=rec[1][:, 0:nq], op=ALU.mult), reads=[d_ps[5], dm["rec1"]], writes=[dm["o1"]])
                    kb.op("dve", lambda E: E.scalar_tensor_tensor(out=o0[:, 0:nq], in0=o1[:, 0:nq], scalar=nlam[:, 0:1], in1=o0[:, 0:nq], op0=ALU.mult, op1=ALU.add),
                          reads=[dm["o0"], dm["o1"], dm["nlam"]], writes=[dm["o0"]])
                    kb.op("dve", lambda E: E.tensor_tensor(out=osq[:, 0:nq], in0=o0[:, 0:nq], in1=o0[:, 0:nq], op=ALU.mult), reads=[dm["o0"]], writes=[dm["osq"]])
                    kb.op("pe", lambda E: E.matmul(ps[6][:, 0:nq], lhsT=ones[:], rhs=osq[:, 0:nq], start=True, stop=True), reads=[dm["ones"], dm["osq"]], writes=[d_ps[6]])
                    kb.op("dve", lambda E: E.tensor_scalar(out=rs[:, 0:nq], in0=ps[6][:, 0:nq], scalar1=1.0 / 128, scalar2=EPS, op0=ALU.mult, op1=ALU.add),
                          reads=[d_ps[6]], writes=[dm["rs"]])
                    kb.op("act", lambda E: E.activation(out=rs[:, 0:nq], in_=rs[:, 0:nq], func=AF.Ln), reads=[dm["rs"]], writes=[dm["rs"]])
                    kb.op("act", lambda E: E.activation(out=rs[:, 0:nq], in_=rs[:, 0:nq], func=AF.Exp, scale=-0.5), reads=[dm["rs"]], writes=[dm["rs"]])
                    so = blk_id % 2
                    kb.op("dve", lambda E, so=so: E.scalar_tensor_tensor(out=of[so][:, 0:nq], in0=o0[:, 0:nq], scalar=gs[:, 0:1], in1=rs[:, 0:nq], op0=ALU.mult, op1=ALU.mult),
                          reads=[dm["o0"], dm["gs"], dm["rs"]], writes=[dm[f"of{so}"]])
                    store(f"st_oa{so}", oaT_d[h, :, q0:q0 + nq], of[so][:, 0:nq], [dm[f"of{so}"]])
                    blk_id += 1
            kb.barrier()

    def stage_c(l, last, x_src, x_dst):
        with ExitStack() as es0:
            sb = lambda name, shape, dty: es0.enter_context(nc.sbuf_tensor(U(name), shape, dty))
            ps = [es0.enter_context(nc.psum_tensor(U(f"psC{i}"), [128, 512], F32)) for i in range(8)]
            d_ps = [Dep(f"ps{i}") for i in range(8)]
            wC = sb("wC", [128, 8, 4608], BF16)
            wBR = sb("wBR", [128, 4, 1024], BF16)
            wBRb = sb("wBRb", [64, 4, 1024], BF16)
            wBRc = sb("wBRc", [64, 4, 1024], BF16)
            wO = sb("wO", [128, 8, 1024], BF16)
            wS = sb("wS", [128, 4, 128], BF16)
            LG = sb("LG", [128, 256], F32); LB = sb("LB", [128, 256], F32)
            BS = sb("BS", [64, 4, 128], F32)
            modT = sb("modT_s", [128, 24, 2], F32)
            G = [sb(f"G{i}", [128, 1024], F32) for i in range(2)]
            ident = sb("ident", [128, 128], F32); ones = sb("ones", [128, 128], F32)
            dw = {n: Dep(n) for n in "wC wBR wBRb wBRc wO wS LG LB BS modT G ident ones".split()}
            with ExitStack() as es:
                sbt = lambda name, shape, dty: es.enter_context(nc.sbuf_tensor(U(name), shape, dty))
                wst = [sbt(f"wst{i}", [128, 8, 512], F32) for i in range(2)]
                d_wst = [Dep("wst0"), Dep("wst1")]
                wsf = sbt("wsf", [128, 4, 128], F32); diag = sbt("diag", [128, 128], F32)
                d_wsf = Dep("wsf"); d_diag = Dep("diag")
                make_ident(ident, dw["ident"])
                kb.op("pool", lambda E: E.memset(ones[:], 1.0), writes=[dw["ones"]])
                kb.dma("sp", "c_mod", modT[:], modT_d, writes=[dw["modT"]])
                kb.dma("sp", "c_lg", LG[:], ln_g[l].partition_broadcast(128), writes=[dw["LG"]])
                kb.dma("sp", "c_lb", LB[:], ln_b[l].partition_broadcast(128), writes=[dw["LB"]])
                kb.dma("sp", "c_bs", BS[:], bs_d[l], writes=[dw["BS"]])
                kb.dma("sp", "c_ws", wsf[:], w_sT[l], writes=[d_wsf])
                kb.op("dve", lambda E: E.tensor_copy(out=wS[:], in_=wsf[:]), reads=[d_wsf], writes=[dw["wS"]])
                for var in range(1 if last else 2):
                    for k in range(8):
                        kb.op("dve", lambda E, k=k, var=var: E.tensor_scalar(out=diag[:], in0=ident[:], scalar1=modT[:, 16 + k, var:var + 1], scalar2=None, op0=ALU.mult),
                              reads=[dw["ident"], dw["modT"]], writes=[d_diag])
                        b = k // 4
                        kb.op("pe", lambda E, k=k, b=b: E.matmul(ps[b][:, (k % 4) * 128:(k % 4 + 1) * 128], lhsT=ones[:], rhs=diag[:], start=True, stop=True),
                              reads=[dw["ones"], d_diag], writes=[d_ps[b]])
                    for b in range(2):
                        kb.op("act", lambda E, b=b, var=var: E.activation(out=G[var][:, b * 512:(b + 1) * 512], in_=ps[b][:], func=AF.Copy), reads=[d_ps[b]], writes=[dw["G"]])
                cnt = [0]

                def load_cast(src_v, dst, ddst, np_=128, nk=8):
                    s = cnt[0] % 2; cnt[0] += 1
                    kb.dma("sp", f"c_w{s}", wst[s][0:np_, 0:nk, :], src_v, writes=[d_wst[s]])
                    e = ("dve", "pool", "act")[cnt[0] % 3]
                    if e == "act":
                        kb.op("act", lambda E: E.activation(out=dst, in_=wst[s][0:np_, 0:nk, :], func=AF.Copy), reads=[d_wst[s]], writes=[ddst])
                    else:
                        kb.op(e, lambda E: E.tensor_copy(out=dst, in_=wst[s][0:np_, 0:nk, :]), reads=[d_wst[s]], writes=[ddst])
                w_in_v = w_in[l].rearrange("(k p) n -> p k n", p=128)
                for g in range(9):
                    load_cast(w_in_v[:, :, O_U + g * 512:O_U + (g + 1) * 512], wC[:, :, g * 512:(g + 1) * 512], dw["wC"])
                w_br_v = w_br[l, 0:512, :].rearrange("(k p) n -> p k n", p=128)
                w_brb_v = w_br[l, 512:768, :].rearrange("(g c) n -> c g n", c=64)
                w_brc_v = w_br[l, 768:1024, :].rearrange("(g c) n -> c g n", c=64)
                w_out_v = w_out[l].rearrange("(k p) n -> p k n", p=128)
                for g in range(2):
                    cs_ = slice(g * 512, (g + 1) * 512)
                    load_cast(w_br_v[:, :, cs_], wBR[:, :, cs_], dw["wBR"], nk=4)
                    load_cast(w_brb_v[:, :, cs_], wBRb[:, :, cs_], dw["wBRb"], np_=64, nk=4)
                    load_cast(w_brc_v[:, :, cs_], wBRc[:, :, cs_], dw["wBRc"], np_=64, nk=4)
                    load_cast(w_out_v[:, :, cs_], wO[:, :, cs_], dw["wO"])
                kb.barrier()

            hTb = sb("hTb", [128, 8, BT], BF16)
            oab = sb("oab", [128, 4, BT], F32)
            obb = sb("obb", [64, 4, BT], F32)
            uT = sb("uT", [64, 4, BT], F32)
            gcT = sb("gcT", [64, 4, BT], F32)
            sgt = [sb(f"sgt{i}", [128, BT], F32) for i in range(2)]
            og = sb("og", [128, 4, BT], BF16)
            ogb = sb("ogb", [64, 4, BT], BF16)
            ogc = sb("ogc", [64, 4, BT], BF16)
            st6 = sb("st6", [128, 6], F32); mv = sb("mv", [128, 2], F32); rstd = sb("rstd", [128, 1], F32)
            vcn = sb("vcn", [128, 256], F32); vnb = sb("vnb", [128, 256], BF16)
            sT = sb("sT", [64, 4, 128], F32)
            mm = [sb(f"mm{i}", [128, 3, BT], F32) for i in range(2)]
            t0 = sb("t0", [128, BT], F32); t1 = sb("t1", [128, BT], F32)
            yT = sb("yT", [128, 8, BT], BF16)
            xt = [sb(f"xt{i}", [128, 1024], F32) for i in range(2)]
            tmpo = sb("tmpo", [128, 1024], F32)
            dn = {n: Dep(n) for n in "hTb oab obb uT gcT sgt0 sgt1 og ogb ogc st6 mv rstd vcn vnb sT mm0 mm1 t0 t1 yT xt0 xt1 tmpo".split()}

            def proj_fm(col0, M, nt, bank):
                for k in range(8):
                    kb.op("pe", lambda E, k=k: E.matmul(ps[bank][0:M, 0:nt], lhsT=wC[:, k, col0:col0 + M], rhs=hTb[:, k, 0:nt], start=(k == 0), stop=(k == 7)),
                          reads=[dw["wC"], dn["hTb"]], writes=[d_ps[bank]])
            pj = [0]

            def next_bank():
                pj[0] += 1
                return 6 + pj[0] % 2
            GC0 = O_GATE - O_U
            MC0 = O_MERGE - O_U
            for blk in range(8 if last else 9):
                tok0 = blk * BT; nt = BT; var = 0 if tok0 < 2048 else 1
                ntile = nt // 128
                kb.dma("sp", "c_h", hTb[:, :, 0:nt], hT_d[:, :, tok0:tok0 + nt].rearrange("k p t -> p k t"), writes=[dn["hTb"]])
                kb.dma("sp", "c_oa", oab[:, :, 0:nt], oaT_d[:, :, tok0:tok0 + nt].rearrange("h p t -> p h t"), writes=[dn["oab"]])
                kb.dma("sp", "c_ob", obb[:, :, 0:nt], obT_d[:, :, tok0:tok0 + nt].rearrange("g c t -> c g t"), writes=[dn["obb"]])
                for g in range(4):
                    b = next_bank()
                    proj_fm(g * 64, 64, nt, b)
                    kb.op("act", lambda E, g=g, b=b: E.activation(out=uT[:, g, 0:nt], in_=ps[b][0:64, 0:nt], func=AF.Copy), reads=[d_ps[b]], writes=[dn["uT"]])
                for br in range(2):
                    for g in range(4):
                        b = next_bank(); s = g % 2
                        proj_fm(GC0 + 512 + br * 256 + g * 64, 64, nt, b)
                        kb.op("act", lambda E, b=b, s=s: E.activation(out=sgt[s][0:64, 0:nt], in_=ps[b][0:64, 0:nt], func=AF.Sigmoid), reads=[d_ps[b]], writes=[dn[f"sgt{s}"]])
                        if br == 1:
                            kb.op("dve", lambda E, g=g, b=b, s=s: E.tensor_tensor(out=gcT[:, g, 0:nt], in0=ps[b][0:64, 0:nt], in1=sgt[s][0:64, 0:nt], op=ALU.mult),
                                  reads=[d_ps[b], dn[f"sgt{s}"]], writes=[dn["gcT"]])
                        else:
                            kb.op("dve", lambda E, b=b, s=s: E.tensor_tensor(out=sgt[s][0:64, 0:nt], in0=ps[b][0:64, 0:nt], in1=sgt[s][0:64, 0:nt], op=ALU.mult),
                                  reads=[d_ps[b], dn[f"sgt{s}"]], writes=[dn[f"sgt{s}"]])
                            kb.op("dve", lambda E, g=g, s=s: E.tensor_tensor(out=ogb[:, g, 0:nt], in0=sgt[s][0:64, 0:nt], in1=obb[:, g, 0:nt], op=ALU.mult),
                                  reads=[dn[f"sgt{s}"], dn["obb"]], writes=[dn["ogb"]])
                for j in range(4):
                    b = next_bank(); s = j % 2
                    proj_fm(GC0 + j * 128, 128, nt, b)
                    kb.op("act", lambda E, b=b, s=s: E.activation(out=sgt[s][:, 0:nt], in_=ps[b][:, 0:nt], func=AF.Sigmoid), reads=[d_ps[b]], writes=[dn[f"sgt{s}"]])
                    kb.op("dve", lambda E, b=b, s=s: E.tensor_tensor(out=sgt[s][:, 0:nt], in0=ps[b][:, 0:nt], in1=sgt[s][:, 0:nt], op=ALU.mult),
                          reads=[d_ps[b], dn[f"sgt{s}"]], writes=[dn[f"sgt{s}"]])
                    kb.op("dve", lambda E, j=j, s=s: E.tensor_tensor(out=og[:, j, 0:nt], in0=sgt[s][:, 0:nt], in1=oab[:, j, 0:nt], op=ALU.mult),
                          reads=[dn[f"sgt{s}"], dn["oab"]], writes=[dn["og"]])
                for t in range(ntile):
                    b = next_bank()
                    for k in range(8):
                        kb.op("pe", lambda E, k=k, t=t, b=b: E.matmul(ps[b][:, 0:256], lhsT=hTb[:, k, t * 128:(t + 1) * 128], rhs=wC[:, k, 256:512], start=(k == 0), stop=(k == 7)),
                              reads=[dw["wC"], dn["hTb"]], writes=[d_ps[b]])
                    kb.op("dve", lambda E, b=b: E.bn_stats(out=st6[:], in_=ps[b][:, 0:256]), reads=[d_ps[b]], writes=[dn["st6"]])
                    kb.op("dve", lambda E: E.bn_aggr(out=mv[:], in_=st6[:]), reads=[dn["st6"]], writes=[dn["mv"]])
                    kb.op("dve", lambda E: E.tensor_scalar(out=rstd[:], in0=mv[:, 1:2], scalar1=EPS, scalar2=None, op0=ALU.add), reads=[dn["mv"]], writes=[dn["rstd"]])
                    kb.op("act", lambda E: E.activation(out=rstd[:], in_=rstd[:], func=AF.Sqrt), reads=[dn["rstd"]], writes=[dn["rstd"]])
                    kb.op("dve", lambda E: E.reciprocal(out=rstd[:], in_=rstd[:]), reads=[dn["rstd"]], writes=[dn["rstd"]])
                    kb.op("dve", lambda E, b=b: E.tensor_scalar(out=vcn[:], in0=ps[b][:, 0:256], scalar1=mv[:, 0:1], scalar2=rstd[:, 0:1], op0=ALU.subtract, op1=ALU.mult),
                          reads=[d_ps[b], dn["mv"], dn["rstd"]], writes=[dn["vcn"]])
                    kb.op("dve", lambda E: E.tensor_tensor(out=vcn[:], in0=vcn[:], in1=LG[:], op=ALU.mult), reads=[dn["vcn"], dw["LG"]], writes=[dn["vcn"]])
                    kb.op("dve", lambda E: E.tensor_tensor(out=vnb[:], in0=vcn[:], in1=LB[:], op=ALU.add), reads=[dn["vcn"], dw["LB"]], writes=[dn["vnb"]])
                    b2 = next_bank()
                    for g in range(4):
                        kb.op("pe", lambda E, g=g, b2=b2: E.matmul(ps[b2][0:64, g * 128:(g + 1) * 128], lhsT=vnb[:, g * 64:(g + 1) * 64], rhs=wS[:, g, :], start=True, stop=True),
                              reads=[dn["vnb"], dw["wS"]], writes=[d_ps[b2]])
                    kb.op("dve", lambda E, b2=b2: E.tensor_tensor(out=sT[:].rearrange("c g p -> c (g p)"), in0=ps[b2][0:64, :], in1=BS[:].rearrange("c g p -> c (g p)"), op=ALU.add),
                          reads=[d_ps[b2], dw["BS"]], writes=[dn["sT"]])
                    kb.op("dve", lambda E, t=t: E.tensor_tensor(out=sT[:], in0=sT[:], in1=uT[:, :, t * 128:(t + 1) * 128], op=ALU.mult), reads=[dn["sT"], dn["uT"]], writes=[dn["sT"]])
                    kb.op("dve", lambda E, t=t: E.tensor_tensor(out=ogc[:, :, t * 128:(t + 1) * 128], in0=sT[:], in1=gcT[:, :, t * 128:(t + 1) * 128], op=ALU.mult),
                          reads=[dn["sT"], dn["gcT"]], writes=[dn["ogc"]])
                for dc in range(8):
                    dsl = slice(dc * 128, (dc + 1) * 128)
                    for e in range(4):
                        kb.op("pe", lambda E, e=e: E.matmul(ps[0][:, 0:nt], lhsT=wBR[:, e, dsl], rhs=og[:, e, 0:nt], start=(e == 0), stop=(e == 3)),
                              reads=[dw["wBR"], dn["og"]], writes=[d_ps[0]])
                    for g in range(4):
                        kb.op("pe", lambda E, g=g: E.matmul(ps[1][:, 0:nt], lhsT=wBRb[:, g, dsl], rhs=ogb[:, g, 0:nt], start=(g == 0), stop=(g == 3)),
                              reads=[dw["wBRb"], dn["ogb"]], writes=[d_ps[1]])
                    for g in range(4):
                        kb.op("pe", lambda E, g=g: E.matmul(ps[2][:, 0:nt], lhsT=wBRc[:, g, dsl], rhs=ogc[:, g, 0:nt], start=(g == 0), stop=(g == 3)),
                              reads=[dw["wBRc"], dn["ogc"]], writes=[d_ps[2]])
                    ms = dc % 2
                    for i in range(3):
                        proj_fm(MC0 + i * 1024 + dc * 128, 128, nt, 3 + i)
                        kb.op("act", lambda E, i=i, ms=ms: E.activation(out=mm[ms][:, i, 0:nt], in_=ps[3 + i][:, 0:nt], func=AF.Sigmoid), reads=[d_ps[3 + i]], writes=[dn[f"mm{ms}"]])
                    kb.op("dve", lambda E, ms=ms: E.tensor_tensor(out=t0[:, 0:nt], in0=ps[0][:, 0:nt], in1=mm[ms][:, 0, 0:nt], op=ALU.mult), reads=[d_ps[0], dn[f"mm{ms}"]], writes=[dn["t0"]])
                    kb.op("dve", lambda E, ms=ms: E.tensor_tensor(out=t1[:, 0:nt], in0=ps[1][:, 0:nt], in1=mm[ms][:, 1, 0:nt], op=ALU.mult), reads=[d_ps[1], dn[f"mm{ms}"]], writes=[dn["t1"]])
                    kb.op("dve", lambda E: E.tensor_tensor(out=t0[:, 0:nt], in0=t0[:, 0:nt], in1=t1[:, 0:nt], op=ALU.add), reads=[dn["t0"], dn["t1"]], writes=[dn["t0"]])
                    kb.op("dve", lambda E, ms=ms: E.tensor_tensor(out=t1[:, 0:nt], in0=ps[2][:, 0:nt], in1=mm[ms][:, 2, 0:nt], op=ALU.mult), reads=[d_ps[2], dn[f"mm{ms}"]], writes=[dn["t1"]])
                    kb.op("dve", lambda E, dc=dc: E.tensor_tensor(out=yT[:, dc, 0:nt], in0=t0[:, 0:nt], in1=t1[:, 0:nt], op=ALU.add), reads=[dn["t0"], dn["t1"]], writes=[dn["yT"]])
                for t in range(ntile):
                    gt = (tok0 // 128) + t
                    s = gt % 2
                    kb.dma("sp", f"c_x{s}", xt[s][:], x_src[gt * 128:(gt + 1) * 128, :], writes=[dn[f"xt{s}"]])
                    for cb in range(2):
                        b = next_bank()
                        for k in range(8):
                            kb.op("pe", lambda E, k=k, cb=cb, b=b, t=t: E.matmul(ps[b][:], lhsT=yT[:, k, t * 128:(t + 1) * 128], rhs=wO[:, k, cb * 512:(cb + 1) * 512],
                                                                              start=(k == 0), stop=(k == 7)),
                                  reads=[dn["yT"], dw["wO"]], writes=[d_ps[b]])
                        kb.op("dve", lambda E, cb=cb, b=b: E.tensor_tensor(out=tmpo[:, cb * 512:(cb + 1) * 512], in0=ps[b][:], in1=G[var][:, cb * 512:(cb + 1) * 512], op=ALU.mult),
                              reads=[d_ps[b], dw["G"]], writes=[dn["tmpo"]])
                    kb.op("dve", lambda E, s=s: E.tensor_tensor(out=xt[s][:], in0=xt[s][:], in1=tmpo[:], op=ALU.add), reads=[dn["tmpo"], dn[f"xt{s}"]], writes=[dn[f"xt{s}"]])
                    store(f"st_x{s}", x_dst[gt * 128:(gt + 1) * 128, :], xt[s][:], [dn[f"xt{s}"]])
            kb.barrier()

    import os
    FS = os.environ.get("FSTOP", "")
    for l in range(1 if FS else 2):
        last = l == 1
        x_src = x_in if l == 0 else x1
        stage_a(l, x_src)
        if FS == "a":
            break
        collective("AllGather", f_lat, f_ag, d_fag)
        for i in range(2):
            collective("AllGather", kT_lat[i], kT_ag[i], d_kvag)
            collective("AllGather", v_lat[i], v_ag[i], d_kvag)
        if FS == "ag":
            kb.barrier()
            break
        stage_b(l, last)
        if FS == "b":
            break
        stage_c(l, last, x_src, y_out if last else x1)
    kb.barrier()
    for k in sorted(out_keys):
        nc.gpsimd.wait_ge(kb.sems[k], kb.cnt[k])
    return kb


import numpy as np
import ml_dtypes
BF = ml_dtypes.bfloat16
NLT = 2048


def rope_tabs():
    n = 8192
    row = np.repeat(np.arange(n // 64), 64).astype(np.float32)
    col = np.tile(np.arange(64), n // 64).astype(np.float32)
    freqs = (10000.0 ** (-np.arange(0, 32, 2, dtype=np.float32) / 32)).astype(np.float32)
    ar = row[:, None] * freqs; ac = col[:, None] * freqs
    ang = np.concatenate([ar, ar, ac, ac], -1)
    cos = np.cos(ang).astype(np.float32); sin = np.sin(ang).astype(np.float32)
    sgn = np.tile(np.concatenate([-np.ones(16), np.ones(16)]), 2).astype(np.float32)
    return cos, sin * sgn


def col_layout(v, k):
    return np.ascontiguousarray(v.reshape(k, 128).T)


def fourier_consts(j):
    t = np.arange(64)[:, None]; kb = np.arange(64)[None, :]
    a = 2 * np.pi * ((t * kb) % 64) / 64
    cs64 = np.concatenate([np.cos(a), -np.sin(a)], 1).astype(BF)
    p = np.arange(128)[:, None, None]; kbb = np.arange(64)[None, :, None]; ka = (32 * j + np.arange(32))[None, None, :]
    a = 2 * np.pi * ((p * (64 * ka + kbb)) % 8192) / 8192
    T1 = np.concatenate([np.cos(a), -np.sin(a)], 2).astype(BF)
    T2 = np.concatenate([np.sin(a), np.cos(a)], 2).astype(BF)
    n = np.arange(128)[:, None, None]; ch = np.arange(2)[None, :, None]; k = np.arange(256)[None, None, :]
    a = 2 * np.pi * (((ch * 128 + n) * k) % 256) / 256
    dctx = np.concatenate([np.cos(a), -np.sin(a)], 2).astype(BF)
    c1 = np.arange(64)[:, None]; c2 = np.arange(64)[None, :]
    a = 2 * np.pi * ((c1 * c2) % 64) / 64
    ccs = np.stack([np.cos(a), np.sin(a)], 1).astype(np.float32)
    return dict(cs64=cs64, T1j=np.ascontiguousarray(T1), T2j=np.ascontiguousarray(T2), dctx=np.ascontiguousarray(dctx), ccs=np.ascontiguousarray(ccs))


def fused_inputs(I):
    cos, ssin = rope_tabs()
    C = np.ascontiguousarray
    shared = {
        "w_ada": C(I['w_ada']), "b_ada": C(np.stack([col_layout(I['b_ada'][l], 24) for l in range(2)])),
        "g_norm": C(np.stack([col_layout(I['g_norm'][l], 8) for l in range(2)])),
        "w_in": C(I['w_in']), "g_q": C(I['g_q']), "g_k": C(I['g_k']),
        "lamv": C(np.concatenate([I['lam_q1'], I['lam_k1'], I['lam_q2'], I['lam_k2']], 1)),
        "g_sub": C(I['g_sub'].reshape(2, 128, 1)),
        "w_f": C(I['w_f']), "b_f": C(I['b_f'].transpose(0, 2, 1)),
        "ln_g": C(I['ln_g']), "ln_b": C(I['ln_b']),
        "w_sT": C(I['w_s'].transpose(0, 3, 1, 2)),
        "bs64": C(np.broadcast_to(I['b_s'][:, None, :, :], (2, 64, 4, 128))),
        "w_br": C(np.concatenate([I['w_br_a'], I['w_br_b'], I['w_br_c']], 1)), "w_out": C(I['w_out']),
    }
    maps = []
    for core in range(8):
        b, j = core // 4, core % 4
        m = dict(shared)
        m["x_tok"] = C(np.concatenate([I['x'][b, j * NLT:(j + 1) * NLT], I['ctx'][b]], 0))
        m["cvec"] = C(np.stack([col_layout(I['c'][b], 8), col_layout(I['c_ctx'], 8)], -1))
        m["cos"] = C(cos[j * NLT:(j + 1) * NLT]); m["ssin"] = C(ssin[j * NLT:(j + 1) * NLT])
        m.update(fourier_consts(j))
        maps.append(m)
    return maps


def kernel(**inputs):
    I = {k: np.asarray(v, dtype=np.float32) for k, v in inputs.items()}
    kb = KB()
    build_fused(kb)
    res = kb.run(fused_inputs(I))
    out = np.empty((2, 8192, 1024), np.float32)
    for core in range(8):
        b, j = core // 4, core % 4
        out[b, j * NLT:(j + 1) * NLT] = np.asarray(res.results[core]["y"])
    return out
```

```python
import numpy as np
import concourse.bass as bass
import concourse.mybir as mybir
from concourse.bass_utils import run_bass_kernel_spmd

F32 = mybir.dt.float32
BF16 = mybir.dt.bfloat16
AF = mybir.ActivationFunctionType
ALU = mybir.AluOpType
AX = mybir.AxisListType


class Dep:
    __slots__ = ("w", "r", "name")

    def __init__(self, name=""):
        self.w = None
        self.r = []
        self.name = name


class KB:
    COMPUTE = ("pe", "act", "dve", "pool")

    def __init__(self):
        self.nc = bass.Bass("TRN2", target_bir_lowering=False)
        nc = self.nc
        self.eng = {"pe": nc.tensor, "act": nc.scalar, "dve": nc.vector,
                    "pool": nc.gpsimd, "sp": nc.sync}
        self.sems = {}
        self.cnt = {}
        for e in self.COMPUTE:
            self.sems[e] = nc.alloc_semaphore(name="s_" + e)
            self.cnt[e] = 0
        self.seen = {e: {} for e in self.eng}
        self.n_inst = 0

    def _sem(self, key):
        if key not in self.sems:
            self.sems[key] = self.nc.alloc_semaphore(name="d_" + str(key))
            self.cnt[key] = 0
        return self.sems[key]

    def _waits(self, e, reads, writes):
        need = {}

        def add(t, war=False):
            if t is None:
                return
            sk, v = t
            if sk == e and (war or e == "pe"):
                return
            if need.get(sk, 0) < v:
                need[sk] = v
        for d in reads:
            add(d.w)
        for d in writes:
            add(d.w)
            for t in d.r:
                add(t, war=True)
        E = self.eng[e]
        for sk, v in need.items():
            if self.seen[e].get(sk, 0) >= v:
                continue
            E.wait_ge(self.sems[sk], v)
            self.seen[e][sk] = v

    def _mark(self, tok, reads, writes):
        for d in reads:
            d.r.append(tok)
            if len(d.r) > 64:
                m = {}
                for sk, v in d.r:
                    if m.get(sk, 0) < v:
                        m[sk] = v
                d.r = list(m.items())
        for d in writes:
            d.w = tok
            d.r = []

    def op(self, e, fn, reads=(), writes=()):
        self._waits(e, reads, writes)
        inst = fn(self.eng[e])
        self.cnt[e] += 1
        inst.then_inc(self.sems[e], 1)
        self._mark((e, self.cnt[e]), reads, writes)
        self.n_inst += 1
        return inst

    def mm(self, fn, reads=(), writes=(), last=True):
        return self.op("pe", fn, reads, writes)

    def dma(self, q, key, out, in_, reads=(), writes=(), **kw):
        sem = self._sem(key)
        self._waits(q, reads, writes)
        inst = self.eng[q].dma_start(out=out, in_=in_, **kw)
        self.cnt[key] += 16
        inst.then_inc(sem, 16)
        self._mark((key, self.cnt[key]), reads, writes)
        self.n_inst += 1
        return inst

    def barrier(self):
        for e, E in self.eng.items():
            for sk, sem in self.sems.items():
                v = self.cnt[sk]
                if v == 0 or self.seen[e].get(sk, 0) >= v:
                    continue
                E.wait_ge(sem, v)
                self.seen[e][sk] = v

    def wait_all(self, e, deps):
        self._waits(e, deps, ())

    def run(self, in_maps, n=8, trace=False):
        return run_bass_kernel_spmd(self.nc, in_maps, core_ids=list(range(n)), trace=trace)


import math
from contextlib import ExitStack

NT_A = 18
NLAT_T = 16
TOKS = NT_A * 128
NKT = 66
NKEY = NKT * 128
EPS = 1e-6
BT = 256
O_Q, O_K, O_V, O_F, O_U, O_VC, O_GATE, O_MERGE = 0, 512, 1024, 1536, 1792, 2048, 2304, 3328
RG = [[0, 1, 2, 3], [4, 5, 6, 7]]


def build_fused(kb):
    nc = kb.nc
    EI = lambda name, shape, d=F32: nc.dram_tensor(name, shape, d, kind="ExternalInput").ap()
    IN = lambda name, shape, d=F32: nc.dram_tensor(name, shape, d, kind="Internal").ap()
    x_in = EI("x_tok", [TOKS, 1024])
    cvec = EI("cvec", [128, 8, 2])
    w_ada = EI("w_ada", [2, 1024, 3072]); b_ada = EI("b_ada", [2, 128, 24]); g_norm = EI("g_norm", [2, 128, 8])
    w_in = EI("w_in", [2, 1024, 6400])
    g_q = EI("g_q", [2, 64]); g_k = EI("g_k", [2, 64])
    cos = EI("cos", [2048, 64]); ssin = EI("ssin", [2048, 64])
    lamv = EI("lamv", [2, 256]); g_sub = EI("g_sub", [2, 128, 1])
    w_f = EI("w_f", [2, 4, 64, 64]); b_f = EI("b_f", [2, 64, 4])
    cs64 = EI("cs64", [64, 128], BF16); T1d = EI("T1j", [128, 64, 64], BF16); T2d = EI("T2j", [128, 64, 64], BF16)
    dcd = EI("dctx", [128, 2, 512], BF16); ccs = EI("ccs", [64, 2, 64])
    ln_g = EI("ln_g", [2, 256]); ln_b = EI("ln_b", [2, 256])
    w_sT = EI("w_sT", [2, 128, 4, 128]); bs_d = EI("bs64", [2, 64, 4, 128])
    w_br = EI("w_br", [2, 1024, 1024]); w_out = EI("w_out", [2, 1024, 1024])
    y_out = nc.dram_tensor("y", [2048, 1024], F32, kind="ExternalOutput").ap()
    x1 = IN("x1", [TOKS, 1024])
    hT_d = IN("hT_d", [8, 128, TOKS], BF16)
    qT_d = IN("qT_d", [4, 128, TOKS], BF16)
    kT_lat = [IN(f"kT_lat{i}", [256, 2048], BF16) for i in range(2)]
    kT_ag = [IN(f"kT_ag{i}", [1024, 2048], BF16) for i in range(2)]
    kT_ctx = IN("kT_ctx", [4, 128, 256], BF16)
    v_lat = [IN(f"v_lat{i}", [1024, 512], BF16) for i in range(2)]
    v_ag = [IN(f"v_ag{i}", [4096, 512], BF16) for i in range(2)]
    v_ctx = IN("v_ctx", [256, 512], BF16)
    f_lat = IN("f_lat", [4 * 2048, 64], BF16)
    f_ag = IN("f_ag", [16 * 2048, 64], BF16)
    f_ctx = IN("f_ctx", [256, 256], BF16)
    oaT_d = IN("oaT_d", [4, 128, TOKS])
    obT_d = IN("obT_d", [4, 64, TOKS])
    modT_d = IN("modT_d", [128, 24, 2])
    out_keys = set()
    uid = [0]

    def U(name):
        uid[0] += 1
        return f"{name}_{uid[0]}"

    def store(key, out, in_, reads):
        out_keys.add(key)
        kb.dma("pool", key, out, in_, reads=reads, writes=[])

    d_fag = Dep("f_ag"); d_kvag = Dep("kv_ag")

    def collective(kind, src, dst, dep):
        sem = kb._sem("cc")
        inst = nc.gpsimd.collective_compute(kind, ALU.bypass, replica_groups=RG, ins=[src], outs=[dst])
        inst.then_inc(sem, 1)
        kb.cnt["cc"] += 1
        dep.w = ("cc", kb.cnt["cc"])

    def rstd_chain(ssrc, dsrc, dst, ddst, scale):
        kb.op("dve", lambda E: E.tensor_scalar(out=dst, in0=ssrc, scalar1=scale, scalar2=EPS, op0=ALU.mult, op1=ALU.add), reads=[dsrc], writes=[ddst])
        kb.op("act", lambda E: E.activation(out=dst, in_=dst, func=AF.Sqrt), reads=[ddst], writes=[ddst])
        kb.op("dve", lambda E: E.reciprocal(out=dst, in_=dst), reads=[ddst], writes=[ddst])

    def make_ident(ident, dep):
        kb.op("pool", lambda E: E.memset(ident[:], 0.0), writes=[dep])
        kb.op("pool", lambda E: E.affine_select(out=ident[:], in_=ident[:], pattern=[[-1, 128]], compare_op=ALU.not_equal,
                                                fill=1.0, base=0, channel_multiplier=1), reads=[dep], writes=[dep])

    def stage_a(l, x_src):
        with ExitStack() as es:
            sb = lambda name, shape, dty: es.enter_context(nc.sbuf_tensor(U(name), shape, dty))
            ps = [es.enter_context(nc.psum_tensor(U(f"psA{i}"), [128, 512], F32)) for i in range(6)]
            ps += [es.enter_context(nc.psum_tensor(U(f"psA{i}"), [128, 1024], BF16)) for i in (6, 7)]
            ident = sb("ident", [128, 128], F32); identb = sb("identb", [128, 128], BF16)
            cT = sb("cT", [128, 8, 2], F32); sg = sb("sg", [128, 8, 2], F32); sc = sb("sc", [128, 8, 2], F32)
            bT = sb("bT", [128, 24], F32); gn = sb("gn", [128, 8], F32)
            modT = sb("modT_s", [128, 24, 2], F32); Aff = sb("Aff", [128, 8, 2], F32)
            wst = [sb(f"wst{i}", [128, 8, 512], F32) for i in range(2)]
            wA = sb("wA", [128, 8, 1792], BF16)
            GQ = sb("GQ", [128, 64], F32); GK = sb("GK", [128, 64], F32)
            xt = [sb(f"xt{i}", [128, 1024], F32) for i in range(2)]
            junk = sb("junk", [128, 1024], F32)
            ss = sb("ss", [128, 1], F32); rstd = sb("rstd", [128, 1], F32)
            hTt = [sb(f"hTt{i}", [128, 8, 128], BF16) for i in range(2)]
            cs = [sb(f"cs{i}", [128, 2, 64], F32) for i in range(2)]
            sq = sb("sq", [128, 512], F32); ss8 = sb("ss8", [128, 8], F32)
            qn = sb("qn", [128, 512], F32); t1 = sb("t1", [128, 512], F32); t2 = sb("t2", [128, 512], F32)
            qr = [sb(f"qr{i}", [128, 512], BF16) for i in range(2)]
            qTt = [sb(f"qTt{i}", [128, 4, 128], BF16) for i in range(2)]
            kTt = [sb(f"kTt{i}", [128, 4, 128], BF16) for i in range(2)]
            vt = [sb(f"vt{i}", [128, 512], BF16) for i in range(2)]
            ft = [sb(f"ft{i}", [128, 256], BF16) for i in range(2)]
            D = lambda n: Dep(n)
            d_ident, d_identb, d_cT, d_sg, d_sc, d_bT, d_gn, d_modT, d_Aff = [D(n) for n in "ident identb cT sg sc bT gn modT Aff".split()]
            d_wst = [D("wst0"), D("wst1")]; d_wA = D("wA"); d_G = D("G")
            d_xt = [D("xt0"), D("xt1")]; d_junk = D("junk"); d_ss = D("ss"); d_rstd = D("rstd")
            d_hTt = [D("hTt0"), D("hTt1")]; d_cs = [D("cs0"), D("cs1")]
            d_sq, d_ss8, d_qn, d_t1, d_t2 = D("sq"), D("ss8"), D("qn"), D("t1"), D("t2")
            d_qr = [D("qr0"), D("qr1")]; d_qTt = [D("qTt0"), D("qTt1")]; d_kTt = [D("kTt0"), D("kTt1")]
            d_vt = [D("vt0"), D("vt1")]; d_ft = [D("ft0"), D("ft1")]
            d_ps = [D(f"ps{i}") for i in range(8)]

            make_ident(ident, d_ident)
            kb.op("pool", lambda E: E.tensor_copy(out=identb[:], in_=ident[:]), reads=[d_ident], writes=[d_identb])
            kb.dma("sp", "ld_c0", cT[:], cvec, writes=[d_cT])
            kb.dma("sp", "ld_c1", bT[:], b_ada[l], writes=[d_bT])
            kb.dma("sp", "ld_c2", gn[:], g_norm[l], writes=[d_gn])
            kb.dma("sp", "ld_c3", GQ[:], g_q[l].partition_broadcast(128), writes=[d_G])
            kb.dma("sp", "ld_c3", GK[:], g_k[l].partition_broadcast(128), writes=[d_G])
            kb.op("act", lambda E: E.activation(out=sg[:], in_=cT[:], func=AF.Sigmoid), reads=[d_cT], writes=[d_sg])
            kb.op("dve", lambda E: E.tensor_tensor(out=sc[:], in0=cT[:], in1=sg[:], op=ALU.mult), reads=[d_cT, d_sg], writes=[d_sc])
            w_ada_v = w_ada[l].rearrange("(k p) n -> p k n", p=128)
            for g in range(6):
                s = g % 2
                kb.dma("sp", f"ld_w{s}", wst[s][:], w_ada_v[:, :, g * 512:(g + 1) * 512], writes=[d_wst[s]])
                for jj in range(4):
                    j = g * 4 + jj
                    for k in range(8):
                        kb.op("pe", lambda E, k=k, jj=jj, s=s, j=j: E.matmul(ps[0][:, 2 * j:2 * j + 2], lhsT=wst[s][:, k, jj * 128:(jj + 1) * 128],
                                                                            rhs=sc[:, k, :], start=(k == 0), stop=(k == 7)),
                              reads=[d_wst[s], d_sc], writes=[d_ps[0]])
            kb.op("dve", lambda E: E.tensor_tensor(out=modT[:], in0=ps[0][:, 0:48].rearrange("p (j n) -> p j n", n=2),
                                                   in1=bT[:].unsqueeze(2).to_broadcast([128, 24, 2]), op=ALU.add),
                  reads=[d_ps[0], d_bT], writes=[d_modT])
            store("st_mod", modT_d, modT[:], [d_modT])
            kb.op("dve", lambda E: E.tensor_scalar(out=Aff[:], in0=modT[:, 8:16, :], scalar1=1.0, scalar2=None, op0=ALU.add), reads=[d_modT], writes=[d_Aff])
            kb.op("dve", lambda E: E.tensor_tensor(out=Aff[:], in0=Aff[:], in1=gn[:].unsqueeze(2).to_broadcast([128, 8, 2]), op=ALU.mult),
                  reads=[d_Aff, d_gn], writes=[d_Aff])
            w_in_v = w_in[l].rearrange("(k p) n -> p k n", p=128)
            for g in range(4):
                s = g % 2
                n = 512 if g < 3 else 256
                kb.dma("sp", f"ld_w{s}", wst[s][:, :, 0:n], w_in_v[:, :, g * 512:g * 512 + n], writes=[d_wst[s]])
                e = "pool" if g % 2 == 0 else "dve"
                kb.op(e, lambda E, s=s, n=n, g=g: E.tensor_copy(out=wA[:, :, g * 512:g * 512 + n], in_=wst[s][:, :, 0:n]), reads=[d_wst[s]], writes=[d_wA])

            def load_tile(i):
                s = i % 2
                kb.dma("sp", f"ld_x{s}", xt[s][:], x_src[i * 128:(i + 1) * 128, :], writes=[d_xt[s]])
                if i < NLAT_T:
                    kb.dma("sp", f"ld_cs{s}", cs[s][:, 0, :], cos[i * 128:(i + 1) * 128, :], writes=[d_cs[s]])
                    kb.dma("sp", f"ld_cs{s}", cs[s][:, 1, :], ssin[i * 128:(i + 1) * 128, :], writes=[d_cs[s]])

            G2 = sb("G2", [128, 2, 64], F32)
            sq2 = sb("sq2", [128, 1024], F32); ss16 = sb("ss16", [128, 16], F32)
            qn2 = sb("qn2", [128, 1024], F32); t1b = sb("t1b", [128, 1024], F32); t2b = sb("t2b", [128, 1024], F32)
            QR2 = sb("QR2", [128, 1024], BF16)
            TT2 = [sb(f"TT2_{i}", [128, 8, 128], BF16) for i in range(2)]
            d_G2, d_sq2, d_ss16, d_qn2, d_t1b, d_t2b, d_QR2 = [D(n) for n in "G2 sq2 ss16 qn2 t1b t2b QR2".split()]
            d_TT2 = [D("TT2_0"), D("TT2_1")]
            kb.op("dve", lambda E: E.tensor_copy(out=G2[:, 0, :], in_=GQ[:]), reads=[d_G], writes=[d_G2])
            kb.op("dve", lambda E: E.tensor_copy(out=G2[:, 1, :], in_=GK[:]), reads=[d_G], writes=[d_G2])
            psQK = [ps[2], ps[3]]
            g64 = lambda ap: ap.rearrange("p (g c) -> p g c", c=64)

            def head(i):
                s = i % 2
                var = 0 if i < NLAT_T else 1
                load_tile(i)
                X = xt[s]
                kb.op("act", lambda E: E.activation(out=junk[:], in_=X[:], func=AF.Square, accum_out=ss[:]), reads=[d_xt[s]], writes=[d_junk, d_ss])
                rstd_chain(ss[:], d_ss, rstd[:], d_rstd, 1.0 / 1024)
                kb.op("dve", lambda E: E.tensor_scalar(out=X[:], in0=X[:], scalar1=rstd[:, 0:1], scalar2=None, op0=ALU.mult),
                      reads=[d_xt[s], d_rstd], writes=[d_xt[s]])
                for k in range(8):
                    b = k // 4
                    kb.op("pe", lambda E, k=k, b=b: E.transpose(out=ps[b][:, (k % 4) * 128:(k % 4 + 1) * 128], in_=X[:, k * 128:(k + 1) * 128], identity=ident[:]),
                          reads=[d_xt[s], d_ident], writes=[d_ps[b]])
                for k in range(8):
                    b = k // 4
                    src = ps[b][:, (k % 4) * 128:(k % 4 + 1) * 128]
                    if k % 2 == 0:
                        kb.op("act", lambda E, k=k, src=src: E.activation(out=hTt[s][:, k, :], in_=src, func=AF.Identity,
                                                                          scale=Aff[:, k, var:var + 1], bias=modT[:, k, var:var + 1]),
                              reads=[d_ps[b], d_Aff, d_modT], writes=[d_hTt[s]])
                    else:
                        kb.op("dve", lambda E, k=k, src=src: E.tensor_scalar(out=hTt[s][:, k, :], in0=src, scalar1=Aff[:, k, var:var + 1],
                                                                             scalar2=modT[:, k, var:var + 1], op0=ALU.mult, op1=ALU.add),
                              reads=[d_ps[b], d_Aff, d_modT], writes=[d_hTt[s]])
                store(f"st_h{s}", hT_d[:, :, i * 128:(i + 1) * 128].rearrange("k p t -> p k t"), hTt[s][:], [d_hTt[s]])

            def mid(i):
                s = i % 2
                for cb in range(4):
                    n = 512 if cb < 3 else 256
                    for k in range(8):
                        kb.op("pe", lambda E, k=k, cb=cb, n=n: E.matmul(ps[2 + cb][:, 0:n], lhsT=hTt[s][:, k, :], rhs=wA[:, k, cb * 512:cb * 512 + n],
                                                                       start=(k == 0), stop=(k == 7)),
                              reads=[d_hTt[s], d_wA], writes=[d_ps[2 + cb]])

            def tail_a(i):
                s = i % 2
                lat = i < NLAT_T
                for w in range(2):
                    kb.op("act", lambda E, w=w: E.activation(out=sq2[:, w * 512:(w + 1) * 512], in_=psQK[w][:], func=AF.Square), reads=[d_ps[2 + w]], writes=[d_sq2])
                kb.op("dve", lambda E: E.tensor_reduce(out=ss16[:], in_=g64(sq2[:]), axis=AX.X, op=ALU.add), reads=[d_sq2], writes=[d_ss16])
                rstd_chain(ss16[:], d_ss16, ss16[:], d_ss16, 1.0 / 64)
                for w in range(2):
                    kb.op("dve", lambda E, w=w: E.tensor_tensor(out=g64(qn2[:, w * 512:(w + 1) * 512]), in0=g64(psQK[w][:]),
                                                           in1=ss16[:, w * 8:(w + 1) * 8].unsqueeze(2).to_broadcast([128, 8, 64]), op=ALU.mult),
                          reads=[d_ps[2 + w], d_ss16], writes=[d_qn2])
                kb.op("act", lambda E: E.activation(out=vt[s][:], in_=ps[4][:], func=AF.Copy), reads=[d_ps[4]], writes=[d_vt[s]])
                kb.op("act", lambda E: E.activation(out=ft[s][:], in_=ps[5][:, 0:256], func=AF.Copy), reads=[d_ps[5]], writes=[d_ft[s]])
                if lat:
                    store(f"st_v{s}", v_lat[i // 8][(i % 8) * 128:(i % 8 + 1) * 128, :], vt[s][:], [d_vt[s]])
                    store(f"st_f{s}", f_lat.rearrange("(g t) c -> t g c", g=4)[i * 128:(i + 1) * 128, :, :], ft[s][:].rearrange("p (g c) -> p g c", c=64), [d_ft[s]])
                else:
                    ic = i - NLAT_T
                    store(f"st_v{s}", v_ctx[ic * 128:(ic + 1) * 128, :], vt[s][:], [d_vt[s]])
                    store(f"st_f{s}", f_ctx[ic * 128:(ic + 1) * 128, :], ft[s][:], [d_ft[s]])

            def tail_b(i):
                s = i % 2
                lat = i < NLAT_T
                qv4 = qn2[:].rearrange("p (w g c) -> p w g c", w=2, c=64)
                G2b = G2[:].unsqueeze(2).to_broadcast([128, 2, 8, 64])
                if lat:
                    for w in range(2):
                        kb.op("dve", lambda E, w=w: E.tensor_tensor(out=qv4[:, w], in0=qv4[:, w], in1=G2[:, w, :].unsqueeze(1).to_broadcast([128, 8, 64]), op=ALU.mult),
                              reads=[d_qn2, d_G2], writes=[d_qn2])
                    kb.op("dve", lambda E: E.tensor_tensor(out=g64(t1b[:]), in0=g64(qn2[:]), in1=cs[s][:, 0, :].unsqueeze(1).to_broadcast([128, 16, 64]), op=ALU.mult),
                          reads=[d_qn2, d_cs[s]], writes=[d_t1b])
                    qv = qn2[:].rearrange("p (g a h c) -> p g a h c", a=2, h=2, c=16)
                    tv = t2b[:].rearrange("p (g a h c) -> p g a h c", a=2, h=2, c=16)
                    sv = cs[s][:, 1, :].rearrange("p (a h c) -> p a h c", a=2, h=2)
                    for hf in range(2):
                        for a in range(2):
                            kb.op("dve", lambda E, hf=hf, a=a: E.tensor_tensor(out=tv[:, :, a, hf, :], in0=qv[:, :, a, 1 - hf, :],
                                                                                in1=sv[:, a, hf, :].unsqueeze(1).to_broadcast([128, 16, 16]), op=ALU.mult),
                                  reads=[d_qn2, d_cs[s]], writes=[d_t2b])
                    kb.op("dve", lambda E: E.tensor_tensor(out=QR2[:], in0=t1b[:], in1=t2b[:], op=ALU.add), reads=[d_t1b, d_t2b], writes=[d_QR2])
                else:
                    QRv = QR2[:].rearrange("p (w g c) -> p w g c", w=2, c=64)
                    for w in range(2):
                        kb.op("dve", lambda E, w=w: E.tensor_tensor(out=QRv[:, w], in0=qv4[:, w], in1=G2[:, w, :].unsqueeze(1).to_broadcast([128, 8, 64]), op=ALU.mult),
                              reads=[d_qn2, d_G2], writes=[d_QR2])
                PT = ps[6]; dPT = d_ps[6]
                for hh in range(8):
                    kb.op("pe", lambda E, hh=hh: E.transpose(out=PT[:, hh * 128:(hh + 1) * 128], in_=QR2[:, hh * 128:(hh + 1) * 128], identity=identb[:]),
                          reads=[d_QR2, d_identb], writes=[dPT])
                TT = TT2[s]; dTT = d_TT2[s]
                kb.op("act", lambda E: E.activation(out=TT[:].rearrange("p h t -> p (h t)"), in_=PT[:, 0:1024], func=AF.Copy), reads=[dPT], writes=[dTT])
                store(f"st_q{s}", qT_d[:, :, i * 128:(i + 1) * 128].rearrange("h p t -> p h t"), TT[:, 0:4, :], [dTT])
                if lat:
                    for hp in range(2):
                        store(f"st_k{s}", kT_lat[hp].rearrange("(h p) t -> p h t", p=128)[:, :, i * 128:(i + 1) * 128], TT[:, 4 + hp * 2:4 + hp * 2 + 2, :], [dTT])
                else:
                    store(f"st_k{s}", kT_ctx[:, :, (i - NLAT_T) * 128:(i - NLAT_T + 1) * 128].rearrange("h p t -> p h t"), TT[:, 4:8, :], [dTT])

            head(0)
            mid(0)
            head(1)
            for i in range(NT_A):
                tail_a(i)
                if i + 1 < NT_A:
                    mid(i + 1)
                tail_b(i)
                if i == NLAT_T - 1:
                    for key in ("st_k0", "st_k1", "st_v0", "st_v1", "st_f0", "st_f1"):
                        nc.gpsimd.wait_ge(kb.sems[key], kb.cnt[key])
                        kb.seen["pool"][key] = kb.cnt[key]
                    collective("AllGather", f_lat, f_ag, d_fag)
                    for ci in range(2):
                        collective("AllGather", kT_lat[ci], kT_ag[ci], d_kvag)
                        collective("AllGather", v_lat[ci], v_ag[ci], d_kvag)
                if i + 2 < NT_A:
                    head(i + 2)
            kb.barrier()

    def stage_b(l, last):
        lam_init = 0.8 - 0.6 * math.exp(-0.3 * l)
        with ExitStack() as es:
            sbt = lambda name, shape, dty: es.enter_context(nc.sbuf_tensor(U(name), shape, dty))
            ps = [es.enter_context(nc.psum_tensor(U(f"psF{i}"), [128, 512], F32)) for i in range(8)]
            d_ps = [Dep(f"ps{i}") for i in range(8)]
            X1 = [sbt(f"X1_{i}", [64, 8192], BF16) for i in range(2)]
            Xc = sbt("Xc", [128, 2, 256], BF16)
            CS = sbt("CS", [64, 128], BF16)
            T1 = sbt("T1s", [128, 64, 64], BF16)
            T2 = sbt("T2s", [128, 64, 64], BF16)
            DC = sbt("DC", [128, 2, 512], BF16)
            Bsb = sbt("Bsb", [128, 64, 2, 64], BF16)
            YT = sbt("YT", [64, 64, 2, 32], BF16)
            YC = sbt("YC", [64, 2, 256], BF16)
            CC = sbt("CC", [64, 2, 64], F32)
            WF = sbt("WF", [64, 4, 64], F32)
            BF_ = sbt("BF", [64, 4], F32)
            MC = sbt("MC", [64, 4, 2, 64], BF16)
            ob = [sbt(f"ob{i}", [64, 512], F32) for i in range(2)]
            dn = {n: Dep(n) for n in "X1_0 X1_1 Xc CS T1 T2 DC Bsb YT YC CC WF BF MC ob0 ob1".split()}
            f_ag_v = f_ag.rearrange("(r g t p) c -> r g t (p c)", r=4, g=4, p=128)

            def load_x1(g):
                s = g % 2
                for r in range(4):
                    kb.dma("sp", f"f_x1{s}", X1[s][r * 16:(r + 1) * 16, :], f_ag_v[r, g], reads=[d_fag], writes=[dn[f"X1_{s}"]])
            load_x1(0)
            if not last:
                kb.dma("sp", "f_xc", Xc[:], f_ctx.rearrange("(k p) c -> p k c", p=128), writes=[dn["Xc"]])
                kb.dma("sp", "f_dc", DC[:], dcd, writes=[dn["DC"]])
            kb.dma("sp", "f_cs", CS[:], cs64, writes=[dn["CS"]])
            kb.dma("sp", "f_cc", CC[:], ccs, writes=[dn["CC"]])
            kb.dma("sp", "f_wf", WF[:], w_f[l].rearrange("g c d -> c g d"), writes=[dn["WF"]])
            kb.dma("sp", "f_bf", BF_[:], b_f[l], writes=[dn["BF"]])
            kb.dma("sp", "f_t1", T1[:], T1d, writes=[dn["T1"]])
            kb.dma("sp", "f_t2", T2[:], T2d, writes=[dn["T2"]])
            for g in range(4):
                for i in range(2):
                    kb.op("pe", lambda E, i=i, g=g: E.matmul(ps[0][0:64, (g * 2 + i) * 64:(g * 2 + i + 1) * 64], lhsT=CC[:, i, :], rhs=WF[:, g, :], start=True, stop=True),
                          reads=[dn["CC"], dn["WF"]], writes=[d_ps[0]])
            kb.op("dve", lambda E: E.tensor_copy(out=MC[:].rearrange("c g i d -> c (g i d)"), in_=ps[0][0:64, 0:512]), reads=[d_ps[0]], writes=[dn["MC"]])
            sc_lat = 1.0 / math.sqrt(8192 * 64); sc_ctx = 1.0 / math.sqrt(256 * 64)
            blkc = [0]
            for g in range(4):
                s = g % 2
                if g + 1 < 4:
                    load_x1(g + 1)
                X1v = X1[s][:].rearrange("t (p c) -> t p c", c=64)
                for c4 in range(16):
                    b = 1 + c4 % 2
                    for cc in range(4):
                        c = c4 * 4 + cc
                        kb.op("pe", lambda E, c=c, cc=cc, b=b: E.matmul(ps[b][:, cc * 128:(cc + 1) * 128], lhsT=X1v[:, :, c], rhs=CS[:], start=True, stop=True),
                              reads=[dn[f"X1_{s}"], dn["CS"]], writes=[d_ps[b]])
                    dst = Bsb[:, c4 * 4:(c4 + 1) * 4, :, :].rearrange("p c r k -> p (c r k)")
                    if c4 % 2 == 0:
                        kb.op("act", lambda E, dst=dst, b=b: E.activation(out=dst, in_=ps[b][:], func=AF.Copy), reads=[d_ps[b]], writes=[dn["Bsb"]])
                    else:
                        kb.op("dve", lambda E, dst=dst, b=b: E.tensor_copy(out=dst, in_=ps[b][:]), reads=[d_ps[b]], writes=[dn["Bsb"]])
                for k8 in range(8):
                    b = 3 + k8 % 2
                    for ki in range(8):
                        kbi = k8 * 8 + ki
                        o = ps[b][0:64, ki * 64:(ki + 1) * 64]
                        kb.op("pe", lambda E, kbi=kbi, o=o: E.matmul(o, lhsT=Bsb[:, :, 0, kbi], rhs=T1[:, kbi, :], start=True, stop=False),
                              reads=[dn["Bsb"], dn["T1"]], writes=[d_ps[b]])
                        kb.op("pe", lambda E, kbi=kbi, o=o: E.matmul(o, lhsT=Bsb[:, :, 1, kbi], rhs=T2[:, kbi, :], start=False, stop=True),
                              reads=[dn["Bsb"], dn["T2"]], writes=[d_ps[b]])
                    dst = YT[:, k8 * 8:(k8 + 1) * 8, :, :].rearrange("c k r a -> c (k r a)")
                    if k8 % 2 == 0:
                        kb.op("act", lambda E, dst=dst, b=b: E.activation(out=dst, in_=ps[b][0:64, :], func=AF.Copy), reads=[d_ps[b]], writes=[dn["YT"]])
                    else:
                        kb.op("dve", lambda E, dst=dst, b=b: E.tensor_copy(out=dst, in_=ps[b][0:64, :]), reads=[d_ps[b]], writes=[dn["YT"]])
                nblk = 4
                if not last:
                    for k in range(2):
                        kb.op("pe", lambda E, k=k, g=g: E.matmul(ps[5][0:64, :], lhsT=Xc[:, k, g * 64:(g + 1) * 64], rhs=DC[:, k, :], start=(k == 0), stop=(k == 1)),
                              reads=[dn["Xc"], dn["DC"]], writes=[d_ps[5]])
                    kb.op("dve", lambda E: E.tensor_copy(out=YC[:].rearrange("c r k -> c (r k)"), in_=ps[5][0:64, :]), reads=[d_ps[5]], writes=[dn["YC"]])
                    nblk = 5
                for blk in range(nblk):
                    b = 6 + blkc[0] % 2
                    so = blkc[0] % 2
                    blkc[0] += 1
                    if blk < 4:
                        n = 512; scl = sc_lat; dy = dn["YT"]
                        r0 = YT[:, :, 0, blk * 8:(blk + 1) * 8].rearrange("c k a -> c a k")
                        r1 = YT[:, :, 1, blk * 8:(blk + 1) * 8].rearrange("c k a -> c a k")
                    else:
                        n = 256; r0 = YC[:, 0, :]; r1 = YC[:, 1, :]; scl = sc_ctx; dy = dn["YC"]
                    kb.op("pe", lambda E, r0=r0, n=n, b=b, g=g: E.matmul(ps[b][0:64, 0:n], lhsT=MC[:, g, 0, :], rhs=r0, start=True, stop=False), reads=[dn["MC"], dy], writes=[d_ps[b]])
                    kb.op("pe", lambda E, r1=r1, n=n, b=b, g=g: E.matmul(ps[b][0:64, 0:n], lhsT=MC[:, g, 1, :], rhs=r1, start=False, stop=True), reads=[dn["MC"], dy], writes=[d_ps[b]])
                    kb.op("act", lambda E, n=n, b=b, so=so, scl=scl, g=g: E.activation(out=ob[so][:, 0:n], in_=ps[b][0:64, 0:n], func=AF.Identity, scale=scl, bias=BF_[:, g:g + 1]),
                          reads=[d_ps[b], dn["BF"]], writes=[dn[f"ob{so}"]])
                    store(f"st_ob{so}", obT_d[g, :, blk * 512:blk * 512 + n], ob[so][:, 0:n], [dn[f"ob{so}"]])
            kb.barrier()

        with ExitStack() as es:
            sbt = lambda name, shape, dty: es.enter_context(nc.sbuf_tensor(U(name), shape, dty))
            psS = [es.enter_context(nc.psum_tensor(U(f"psS{i}"), [128, 1024], F32)) for i in range(2)]
            ps = [None] * 4 + [es.enter_context(nc.psum_tensor(U(f"psB{i}"), [128, 512], F32)) for i in range(4, 8)]
            d_psS = [Dep("psS0"), Dep("psS1")]
            d_ps = [Dep(f"ps{i}") for i in range(8)]
            kTh = [sbt(f"kTh{i}", [128, NKEY], BF16) for i in range(2)]
            vh = [sbt(f"vh{i}", [128, NKT, 128], BF16) for i in range(2)]
            qTh = [sbt(f"qTh{i}", [128, TOKS], BF16) for i in range(2)]
            NPT = 4
            pt = [sbt(f"pt{i}", [128, 1024], BF16) for i in range(NPT)]
            ones = sbt("ones", [128, 128], F32)
            onesb = sbt("onesb", [128, 32], BF16)
            sel = sbt("sel", [128, 2, 128], F32)
            rsb = sbt("rsb", [128, 512], F32)
            lv = sbt("lv", [128, 4, 64], F32); lt = sbt("lt", [128, 2, 64], F32); l2 = sbt("l2", [128, 2], F32)
            nlam = sbt("nlam", [128, 1], F32); gs = sbt("gs", [128, 1], F32)
            rec = [sbt(f"rec{c}", [128, 512], F32) for c in range(2)]
            o0 = sbt("o0", [128, 512], F32); o1 = sbt("o1", [128, 512], F32)
            osq = sbt("osq", [128, 512], F32); rs = sbt("rs", [128, 512], F32)
            of = [sbt(f"of{i}", [128, 512], F32) for i in range(2)]
            d_kv = [Dep("kv0"), Dep("kv1")]
            d_pt = [Dep(f"pt{i}") for i in range(NPT)]
            dm = {n: Dep(n) for n in "ones onesb sel rsb lv lt l2 nlam gs rec0 rec1 o0 o1 osq rs of0 of1".split()}
            kb.op("pool", lambda E: E.memset(ones[:], 1.0), writes=[dm["ones"]])
            kb.op("pool", lambda E: E.memset(onesb[:], 1.0), writes=[dm["onesb"]])
            kb.op("pool", lambda E: E.memset(sel[:], 0.0), writes=[dm["sel"]])
            for (p0, p1, c, val) in ((0, 32, 0, 1.0 / 32), (64, 96, 0, 1.0 / 32), (32, 64, 1, 1.0 / 32), (64, 128, 1, 1.0 / 32), (64, 96, 1, 0.0)):
                kb.op("pool", lambda E, p0=p0, p1=p1, c=c, val=val: E.memset(sel[p0:p1, c, :], val), reads=[dm["sel"]], writes=[dm["sel"]])
            kb.dma("sp", "a_lv", lv[:].rearrange("p a c -> p (a c)"), lamv[l].partition_broadcast(128), writes=[dm["lv"]])
            kb.dma("sp", "a_gs", gs[:], g_sub[l], writes=[dm["gs"]])
            lvv = lv[:].rearrange("p (i j) c -> p i j c", j=2)
            kb.op("dve", lambda E: E.tensor_tensor(out=lt[:], in0=lvv[:, :, 0, :], in1=lvv[:, :, 1, :], op=ALU.mult), reads=[dm["lv"]], writes=[dm["lt"]])
            kb.op("dve", lambda E: E.tensor_reduce(out=l2[:], in_=lt[:], axis=AX.X, op=ALU.add), reads=[dm["lt"]], writes=[dm["l2"]])
            kb.op("act", lambda E: E.activation(out=l2[:], in_=l2[:], func=AF.Exp), reads=[dm["l2"]], writes=[dm["l2"]])
            kb.op("dve", lambda E: E.tensor_tensor(out=nlam[:], in0=l2[:, 1:2], in1=l2[:, 0:1], op=ALU.subtract), reads=[dm["l2"]], writes=[dm["nlam"]])
            kb.op("dve", lambda E: E.tensor_scalar(out=nlam[:], in0=nlam[:], scalar1=-lam_init, scalar2=None, op0=ALU.add), reads=[dm["nlam"]], writes=[dm["nlam"]])
            kb.op("dve", lambda E: E.tensor_scalar(out=gs[:], in0=gs[:], scalar1=(1.0 - lam_init), scalar2=None, op0=ALU.mult), reads=[dm["gs"]], writes=[dm["gs"]])
            kT_ag_v = [a.rearrange("(r h p) t -> h p r t", r=4, h=2) for a in kT_ag]
            v_ag_v = [a.rearrange("(r t p) e -> p r t e", r=4, p=128) for a in v_ag]
            v_ctx_v = v_ctx.rearrange("(t p) e -> p t e", p=128)

            def load_head(h):
                s = h % 2
                kb.dma("sp", f"a_k{s}", kTh[s][:, 0:8192].rearrange("p (r t) -> p r t", r=4), kT_ag_v[h // 2][h % 2], reads=[d_kvag], writes=[d_kv[s]])
                kb.dma("sp", f"a_k{s}", kTh[s][:, 8192:NKEY], kT_ctx[h], writes=[d_kv[s]])
                for half in range(2):
                    for r in range(4):
                        kt0 = r * 16 + half * 8
                        kb.dma("sp", f"a_k{s}", vh[s][:, kt0:kt0 + 8, :], v_ag_v[half][:, r, :, h * 128:(h + 1) * 128], reads=[d_kvag], writes=[d_kv[s]])
                kb.dma("sp", f"a_k{s}", vh[s][:, 64:66, :], v_ctx_v[:, :, h * 128:(h + 1) * 128], writes=[d_kv[s]])
                kb.dma("sp", f"a_k{s}", qTh[s][:], qT_d[h], writes=[d_kv[s]])

            load_head(0)
            blk_id = 0
            for h in range(4):
                s = h % 2
                if h + 1 < 4:
                    load_head(h + 1)
                for qb in range(4 if last else 5):
                    if qb < 4:
                        q0 = qb * 512; nq = 512; kts = list(range(NKT))
                    else:
                        q0 = 2048; nq = 256; kts = [64, 65]
                    nk = len(kts)

                    def scores(idx):
                        kt = kts[idx]; sl = idx % 2
                        for c in range(2):
                            kb.op("pe", lambda E, c=c, kt=kt, sl=sl: E.matmul(psS[sl][:, c * 512:c * 512 + nq], lhsT=kTh[s][c * 64:(c + 1) * 64, kt * 128:(kt + 1) * 128],
                                                                           rhs=qTh[s][c * 64:(c + 1) * 64, q0:q0 + nq], start=True, stop=True, tile_position=(64 * c, 0)),
                                  reads=[d_kv[s]], writes=[d_psS[sl]])

                    def exps(idx):
                        sl = idx % 2; p = idx % NPT
                        kb.op("act", lambda E, sl=sl, p=p: E.activation(out=pt[p][:].rearrange("p (c q) -> p c q", c=2)[:, :, 0:nq],
                                                                        in_=psS[sl][:].rearrange("p (c q) -> p c q", c=2)[:, :, 0:nq], func=AF.Exp, scale=0.125),
                              reads=[d_psS[sl]], writes=[d_pt[p]])

                    def av(idx):
                        kt = kts[idx]; p = idx % NPT
                        for c in range(2):
                            kb.op("pe", lambda E, c=c, kt=kt, p=p, idx=idx: E.matmul(ps[4 + c][:, 0:nq], lhsT=vh[s][:, kt, :], rhs=pt[p][:, c * 512:c * 512 + nq],
                                                                                  start=(idx == 0), stop=(idx == nk - 1)),
                                  reads=[d_kv[s], d_pt[p]], writes=[d_ps[4 + c]])

                    def rowsums(idxs):
                        for idx in idxs:
                            p = idx % NPT
                            for c in range(2):
                                g4 = (idx % 2) * 2 + c
                                kb.op("pe", lambda E, c=c, p=p, idx=idx, g4=g4: E.matmul(ps[6][32 * g4:32 * g4 + 32, 0:nq], lhsT=onesb[:], rhs=pt[p][:, c * 512:c * 512 + nq],
                                                                                       start=(idx < 2), stop=(idx >= nk - 2), tile_position=(0, 32 * g4)),
                                      reads=[dm["onesb"], d_pt[p]], writes=[d_ps[6]])

                    scores(0)
                    pend = []

                    def av_rs(i2):
                        av(i2)
                        pend.append(i2)
                        if len(pend) == 2 or i2 == nk - 1:
                            rowsums(list(pend))
                            del pend[:]
                    for idx in range(nk):
                        exps(idx)
                        if idx + 1 < nk:
                            scores(idx + 1)
                        if idx >= 1:
                            av_rs(idx - 1)
                    av_rs(nk - 1)
                    kb.op("act", lambda E: E.activation(out=rsb[:, 0:nq], in_=ps[6][:, 0:nq], func=AF.Copy), reads=[d_ps[6]], writes=[dm["rsb"]])
                    for c in range(2):
                        bnk = 7 if c == 0 else 6
                        kb.op("pe", lambda E, c=c, bnk=bnk: E.matmul(ps[bnk][:, 0:nq], lhsT=sel[:, c, :], rhs=rsb[:, 0:nq], start=True, stop=True),
                              reads=[dm["sel"], dm["rsb"]], writes=[d_ps[bnk]])
                        kb.op("dve", lambda E, c=c, bnk=bnk: E.reciprocal(out=rec[c][:, 0:nq], in_=ps[bnk][:, 0:nq]), reads=[d_ps[bnk]], writes=[dm[f"rec{c}"]])
                    kb.op("dve", lambda E: E.tensor_tensor(out=o0[:, 0:nq], in0=ps[4][:, 0:nq], in1=rec[0][:, 0:nq], op=ALU.mult), reads=[d_ps[4], dm["rec0"]], writes=[dm["o0"]])
                    kb.op("dve", lambda E: E.tensor_tensor(out=o1[:, 0:nq], in0=ps[5][:, 0:nq], in1=rec[1][:, 0:nq], op=ALU.mult), reads=[d_ps[5], dm["rec1"]], writes=[dm["o1"]])
                    kb.op("dve", lambda E: E.scalar_tensor_tensor(out=o0[:, 0:nq], in0=o1[:, 0:nq], scalar=nlam[:, 0:1], in1=o0[:, 0:nq], op0=ALU.mult, op1=ALU.add),
                          reads=[dm["o0"], dm["o1"], dm["nlam"]], writes=[dm["o0"]])
                    kb.op("dve", lambda E: E.tensor_tensor(out=osq[:, 0:nq], in0=o0[:, 0:nq], in1=o0[:, 0:nq], op=ALU.mult), reads=[dm["o0"]], writes=[dm["osq"]])
                    kb.op("pe", lambda E: E.matmul(ps[6][:, 0:nq], lhsT=ones[:], rhs=osq[:, 0:nq], start=True, stop=True), reads=[dm["ones"], dm["osq"]], writes=[d_ps[6]])
                    kb.op("dve", lambda E: E.tensor_scalar(out=rs[:, 0:nq], in0=ps[6][:, 0:nq], scalar1=1.0 / 128, scalar2=EPS, op0=ALU.mult, op1=ALU.add),
                          reads=[d_ps[6]], writes=[dm["rs"]])
                    kb.op("act", lambda E: E.activation(out=rs[:, 0:nq], in_=rs[:, 0:nq], func=AF.Ln), reads=[dm["rs"]], writes=[dm["rs"]])
                    kb.op("act", lambda E: E.activation(out=rs[:, 0:nq], in_=rs[:, 0:nq], func=AF.Exp, scale=-0.5), reads=[dm["rs"]], writes=[dm["rs"]])
                    so = blk_id % 2
                    kb.op("dve", lambda E, so=so: E.scalar_tensor_tensor(out=of[so][:, 0:nq], in0=o0[:, 0:nq], scalar=gs[:, 0:1], in1=rs[:, 0:nq], op0=ALU.mult, op1=ALU.mult),
                          reads=[dm["o0"], dm["gs"], dm["rs"]], writes=[dm[f"of{so}"]])
                    store(f"st_oa{so}", oaT_d[h, :, q0:q0 + nq], of[so][:, 0:nq], [dm[f"of{so}"]])
                    blk_id += 1
            kb.barrier()

    def stage_c(l, last, x_src, x_dst):
        with ExitStack() as es0:
            sb = lambda name, shape, dty: es0.enter_context(nc.sbuf_tensor(U(name), shape, dty))
            ps = [es0.enter_context(nc.psum_tensor(U(f"psC{i}"), [128, 512], F32)) for i in range(8)]
            d_ps = [Dep(f"ps{i}") for i in range(8)]
            wC = sb("wC", [128, 8, 4608], BF16)
            wBR = sb("wBR", [128, 4, 1024], BF16)
            wBRb = sb("wBRb", [64, 4, 1024], BF16)
            wBRc = sb("wBRc", [64, 4, 1024], BF16)
            wO = sb("wO", [128, 8, 1024], BF16)
            wS = sb("wS", [128, 4, 128], BF16)
            LG = sb("LG", [128, 256], F32); LB = sb("LB", [128, 256], F32)
            BS = sb("BS", [64, 4, 128], F32)
            modT = sb("modT_s", [128, 24, 2], F32)
            G = [sb(f"G{i}", [128, 1024], F32) for i in range(2)]
            ident = sb("ident", [128, 128], F32); ones = sb("ones", [128, 128], F32)
            dw = {n: Dep(n) for n in "wC wBR wBRb wBRc wO wS LG LB BS modT G ident ones".split()}
            with ExitStack() as es:
                sbt = lambda name, shape, dty: es.enter_context(nc.sbuf_tensor(U(name), shape, dty))
                wst = [sbt(f"wst{i}", [128, 8, 512], F32) for i in range(2)]
                d_wst = [Dep("wst0"), Dep("wst1")]
                wsf = sbt("wsf", [128, 4, 128], F32); diag = sbt("diag", [128, 128], F32)
                d_wsf = Dep("wsf"); d_diag = Dep("diag")
                make_ident(ident, dw["ident"])
                kb.op("pool", lambda E: E.memset(ones[:], 1.0), writes=[dw["ones"]])
                kb.dma("sp", "c_mod", modT[:], modT_d, writes=[dw["modT"]])
                kb.dma("sp", "c_lg", LG[:], ln_g[l].partition_broadcast(128), writes=[dw["LG"]])
                kb.dma("sp", "c_lb", LB[:], ln_b[l].partition_broadcast(128), writes=[dw["LB"]])
                kb.dma("sp", "c_bs", BS[:], bs_d[l], writes=[dw["BS"]])
                kb.dma("sp", "c_ws", wsf[:], w_sT[l], writes=[d_wsf])
                kb.op("dve", lambda E: E.tensor_copy(out=wS[:], in_=wsf[:]), reads=[d_wsf], writes=[dw["wS"]])
                for var in range(1 if last else 2):
                    for k in range(8):
                        kb.op("dve", lambda E, k=k, var=var: E.tensor_scalar(out=diag[:], in0=ident[:], scalar1=modT[:, 16 + k, var:var + 1], scalar2=None, op0=ALU.mult),
                              reads=[dw["ident"], dw["modT"]], writes=[d_diag])
                        b = k // 4
                        kb.op("pe", lambda E, k=k, b=b: E.matmul(ps[b][:, (k % 4) * 128:(k % 4 + 1) * 128], lhsT=ones[:], rhs=diag[:], start=True, stop=True),
                              reads=[dw["ones"], d_diag], writes=[d_ps[b]])
                    for b in range(2):
                        kb.op("act", lambda E, b=b, var=var: E.activation(out=G[var][:, b * 512:(b + 1) * 512], in_=ps[b][:], func=AF.Copy), reads=[d_ps[b]], writes=[dw["G"]])
                cnt = [0]

                def load_cast(src_v, dst, ddst, np_=128, nk=8):
                    s = cnt[0] % 2; cnt[0] += 1
                    kb.dma("sp", f"c_w{s}", wst[s][0:np_, 0:nk, :], src_v, writes=[d_wst[s]])
                    e = ("dve", "pool", "act")[cnt[0] % 3]
                    if e == "act":
                        kb.op("act", lambda E: E.activation(out=dst, in_=wst[s][0:np_, 0:nk, :], func=AF.Copy), reads=[d_wst[s]], writes=[ddst])
                    else:
                        kb.op(e, lambda E: E.tensor_copy(out=dst, in_=wst[s][0:np_, 0:nk, :]), reads=[d_wst[s]], writes=[ddst])
                w_in_v = w_in[l].rearrange("(k p) n -> p k n", p=128)
                for g in range(9):
                    load_cast(w_in_v[:, :, O_U + g * 512:O_U + (g + 1) * 512], wC[:, :, g * 512:(g + 1) * 512], dw["wC"])
                w_br_v = w_br[l, 0:512, :].rearrange("(k p) n -> p k n", p=128)
                w_brb_v = w_br[l, 512:768, :].rearrange("(g c) n -> c g n", c=64)
                w_brc_v = w_br[l, 768:1024, :].rearrange("(g c) n -> c g n", c=64)
                w_out_v = w_out[l].rearrange("(k p) n -> p k n", p=128)
                for g in range(2):
                    cs_ = slice(g * 512, (g + 1) * 512)
                    load_cast(w_br_v[:, :, cs_], wBR[:, :, cs_], dw["wBR"], nk=4)
                    load_cast(w_brb_v[:, :, cs_], wBRb[:, :, cs_], dw["wBRb"], np_=64, nk=4)
                    load_cast(w_brc_v[:, :, cs_], wBRc[:, :, cs_], dw["wBRc"], np_=64, nk=4)
                    load_cast(w_out_v[:, :, cs_], wO[:, :, cs_], dw["wO"])
                kb.barrier()

            hTb = sb("hTb", [128, 8, BT], BF16)
            oab = sb("oab", [128, 4, BT], F32)
            obb = sb("obb", [64, 4, BT], F32)
            uT = sb("uT", [64, 4, BT], F32)
            gcT = sb("gcT", [64, 4, BT], F32)
            sgt = [sb(f"sgt{i}", [128, BT], F32) for i in range(4)]
            og = sb("og", [128, 4, BT], BF16)
            ogb = sb("ogb", [64, 4, BT], BF16)
            ogc = sb("ogc", [64, 4, BT], BF16)
            st6 = sb("st6", [128, 6], F32); mv = sb("mv", [128, 2], F32); rstd = sb("rstd", [128, 1], F32)
            vcn = sb("vcn", [128, 256], F32); vnb = sb("vnb", [128, 256], BF16)
            sT = sb("sT", [64, 4, 128], F32)
            mm = [sb(f"mm{i}", [128, 3, BT], F32) for i in range(2)]
            t0 = sb("t0", [128, BT], F32); t1 = sb("t1", [128, BT], F32)
            yT = sb("yT", [128, 8, BT], BF16)
            xt = [sb(f"xt{i}", [128, 1024], F32) for i in range(2)]
            tmpo = sb("tmpo", [128, 1024], F32)
            dn = {n: Dep(n) for n in "hTb oab obb uT gcT sgt0 sgt1 sgt2 sgt3 og ogb ogc st6 mv rstd vcn vnb sT mm0 mm1 t0 t1 yT xt0 xt1 tmpo".split()}

            def proj_fm(col0, M, nt, bank):
                for k in range(8):
                    kb.op("pe", lambda E, k=k: E.matmul(ps[bank][0:M, 0:nt], lhsT=wC[:, k, col0:col0 + M], rhs=hTb[:, k, 0:nt], start=(k == 0), stop=(k == 7)),
                          reads=[dw["wC"], dn["hTb"]], writes=[d_ps[bank]])
            pj = [0]

            def next_bank():
                pj[0] += 1
                return 6 + pj[0] % 2
            gj = [0]

            def gate_bank():
                gj[0] += 1
                return (0, 1, 2, 6, 7)[gj[0] % 5]
            GC0 = O_GATE - O_U
            MC0 = O_MERGE - O_U
            for blk in range(8 if last else 9):
                tok0 = blk * BT; nt = BT; var = 0 if tok0 < 2048 else 1
                ntile = nt // 128
                kb.dma("sp", "c_h", hTb[:, :, 0:nt], hT_d[:, :, tok0:tok0 + nt].rearrange("k p t -> p k t"), writes=[dn["hTb"]])
                kb.dma("sp", "c_oa", oab[:, :, 0:nt], oaT_d[:, :, tok0:tok0 + nt].rearrange("h p t -> p h t"), writes=[dn["oab"]])
                kb.dma("sp", "c_ob", obb[:, :, 0:nt], obT_d[:, :, tok0:tok0 + nt].rearrange("g c t -> c g t"), writes=[dn["obb"]])
                for g in range(4):
                    b = gate_bank()
                    proj_fm(g * 64, 64, nt, b)
                    kb.op("act", lambda E, g=g, b=b: E.activation(out=uT[:, g, 0:nt], in_=ps[b][0:64, 0:nt], func=AF.Copy), reads=[d_ps[b]], writes=[dn["uT"]])
                for br in range(2):
                    for g in range(4):
                        b = gate_bank(); s = (br * 4 + g) % 4
                        proj_fm(GC0 + 512 + br * 256 + g * 64, 64, nt, b)
                        kb.op("act", lambda E, b=b, s=s: E.activation(out=sgt[s][0:64, 0:nt], in_=ps[b][0:64, 0:nt], func=AF.Sigmoid), reads=[d_ps[b]], writes=[dn[f"sgt{s}"]])
                        if br == 1:
                            kb.op("dve", lambda E, g=g, b=b, s=s: E.tensor_tensor(out=gcT[:, g, 0:nt], in0=ps[b][0:64, 0:nt], in1=sgt[s][0:64, 0:nt], op=ALU.mult),
                                  reads=[d_ps[b], dn[f"sgt{s}"]], writes=[dn["gcT"]])
                        else:
                            kb.op("dve", lambda E, b=b, s=s: E.tensor_tensor(out=sgt[s][0:64, 0:nt], in0=ps[b][0:64, 0:nt], in1=sgt[s][0:64, 0:nt], op=ALU.mult),
                                  reads=[d_ps[b], dn[f"sgt{s}"]], writes=[dn[f"sgt{s}"]])
                            kb.op("dve", lambda E, g=g, s=s: E.tensor_tensor(out=ogb[:, g, 0:nt], in0=sgt[s][0:64, 0:nt], in1=obb[:, g, 0:nt], op=ALU.mult),
                                  reads=[dn[f"sgt{s}"], dn["obb"]], writes=[dn["ogb"]])
                for j in range(4):
                    b = gate_bank(); s = j % 4
                    proj_fm(GC0 + j * 128, 128, nt, b)
                    kb.op("act", lambda E, b=b, s=s: E.activation(out=sgt[s][:, 0:nt], in_=ps[b][:, 0:nt], func=AF.Sigmoid), reads=[d_ps[b]], writes=[dn[f"sgt{s}"]])
                    kb.op("dve", lambda E, b=b, s=s: E.tensor_tensor(out=sgt[s][:, 0:nt], in0=ps[b][:, 0:nt], in1=sgt[s][:, 0:nt], op=ALU.mult),
                          reads=[d_ps[b], dn[f"sgt{s}"]], writes=[dn[f"sgt{s}"]])
                    kb.op("dve", lambda E, j=j, s=s: E.tensor_tensor(out=og[:, j, 0:nt], in0=sgt[s][:, 0:nt], in1=oab[:, j, 0:nt], op=ALU.mult),
                          reads=[dn[f"sgt{s}"], dn["oab"]], writes=[dn["og"]])
                for t in range(ntile):
                    b = next_bank()
                    for k in range(8):
                        kb.op("pe", lambda E, k=k, t=t, b=b: E.matmul(ps[b][:, 0:256], lhsT=hTb[:, k, t * 128:(t + 1) * 128], rhs=wC[:, k, 256:512], start=(k == 0), stop=(k == 7)),
                              reads=[dw["wC"], dn["hTb"]], writes=[d_ps[b]])
                    kb.op("dve", lambda E, b=b: E.bn_stats(out=st6[:], in_=ps[b][:, 0:256]), reads=[d_ps[b]], writes=[dn["st6"]])
                    kb.op("dve", lambda E: E.bn_aggr(out=mv[:], in_=st6[:]), reads=[dn["st6"]], writes=[dn["mv"]])
                    kb.op("dve", lambda E: E.tensor_scalar(out=rstd[:], in0=mv[:, 1:2], scalar1=EPS, scalar2=None, op0=ALU.add), reads=[dn["mv"]], writes=[dn["rstd"]])
                    kb.op("act", lambda E: E.activation(out=rstd[:], in_=rstd[:], func=AF.Sqrt), reads=[dn["rstd"]], writes=[dn["rstd"]])
                    kb.op("dve", lambda E: E.reciprocal(out=rstd[:], in_=rstd[:]), reads=[dn["rstd"]], writes=[dn["rstd"]])
                    kb.op("dve", lambda E, b=b: E.tensor_scalar(out=vcn[:], in0=ps[b][:, 0:256], scalar1=mv[:, 0:1], scalar2=rstd[:, 0:1], op0=ALU.subtract, op1=ALU.mult),
                          reads=[d_ps[b], dn["mv"], dn["rstd"]], writes=[dn["vcn"]])
                    kb.op("dve", lambda E: E.tensor_tensor(out=vcn[:], in0=vcn[:], in1=LG[:], op=ALU.mult), reads=[dn["vcn"], dw["LG"]], writes=[dn["vcn"]])
                    kb.op("dve", lambda E: E.tensor_tensor(out=vnb[:], in0=vcn[:], in1=LB[:], op=ALU.add), reads=[dn["vcn"], dw["LB"]], writes=[dn["vnb"]])
                    b2 = next_bank()
                    for g in range(4):
                        kb.op("pe", lambda E, g=g, b2=b2: E.matmul(ps[b2][0:64, g * 128:(g + 1) * 128], lhsT=vnb[:, g * 64:(g + 1) * 64], rhs=wS[:, g, :], start=True, stop=True),
                              reads=[dn["vnb"], dw["wS"]], writes=[d_ps[b2]])
                    kb.op("dve", lambda E, b2=b2: E.tensor_tensor(out=sT[:].rearrange("c g p -> c (g p)"), in0=ps[b2][0:64, :], in1=BS[:].rearrange("c g p -> c (g p)"), op=ALU.add),
                          reads=[d_ps[b2], dw["BS"]], writes=[dn["sT"]])
                    kb.op("dve", lambda E, t=t: E.tensor_tensor(out=sT[:], in0=sT[:], in1=uT[:, :, t * 128:(t + 1) * 128], op=ALU.mult), reads=[dn["sT"], dn["uT"]], writes=[dn["sT"]])
                    kb.op("dve", lambda E, t=t: E.tensor_tensor(out=ogc[:, :, t * 128:(t + 1) * 128], in0=sT[:], in1=gcT[:, :, t * 128:(t + 1) * 128], op=ALU.mult),
                          reads=[dn["sT"], dn["gcT"]], writes=[dn["ogc"]])
                for dc in range(8):
                    dsl = slice(dc * 128, (dc + 1) * 128)
                    ms = dc % 2
                    for i in range(3):
                        proj_fm(MC0 + i * 1024 + dc * 128, 128, nt, 3 + i)
                        kb.op("act", lambda E, i=i, ms=ms: E.activation(out=mm[ms][:, i, 0:nt], in_=ps[3 + i][:, 0:nt], func=AF.Sigmoid), reads=[d_ps[3 + i]], writes=[dn[f"mm{ms}"]])
                    for e in range(4):
                        kb.op("pe", lambda E, e=e: E.matmul(ps[0][:, 0:nt], lhsT=wBR[:, e, dsl], rhs=og[:, e, 0:nt], start=(e == 0), stop=(e == 3)),
                              reads=[dw["wBR"], dn["og"]], writes=[d_ps[0]])
                    for g in range(4):
                        kb.op("pe", lambda E, g=g: E.matmul(ps[1][:, 0:nt], lhsT=wBRb[:, g, dsl], rhs=ogb[:, g, 0:nt], start=(g == 0), stop=(g == 3)),
                              reads=[dw["wBRb"], dn["ogb"]], writes=[d_ps[1]])
                    for g in range(4):
                        kb.op("pe", lambda E, g=g: E.matmul(ps[2][:, 0:nt], lhsT=wBRc[:, g, dsl], rhs=ogc[:, g, 0:nt], start=(g == 0), stop=(g == 3)),
                              reads=[dw["wBRc"], dn["ogc"]], writes=[d_ps[2]])
                    kb.op("dve", lambda E, ms=ms: E.tensor_tensor(out=t0[:, 0:nt], in0=ps[0][:, 0:nt], in1=mm[ms][:, 0, 0:nt], op=ALU.mult), reads=[d_ps[0], dn[f"mm{ms}"]], writes=[dn["t0"]])
                    kb.op("dve", lambda E, ms=ms: E.tensor_tensor(out=t1[:, 0:nt], in0=ps[1][:, 0:nt], in1=mm[ms][:, 1, 0:nt], op=ALU.mult), reads=[d_ps[1], dn[f"mm{ms}"]], writes=[dn["t1"]])
                    kb.op("dve", lambda E: E.tensor_tensor(out=t0[:, 0:nt], in0=t0[:, 0:nt], in1=t1[:, 0:nt], op=ALU.add), reads=[dn["t0"], dn["t1"]], writes=[dn["t0"]])
                    kb.op("dve", lambda E, ms=ms: E.tensor_tensor(out=t1[:, 0:nt], in0=ps[2][:, 0:nt], in1=mm[ms][:, 2, 0:nt], op=ALU.mult), reads=[d_ps[2], dn[f"mm{ms}"]], writes=[dn["t1"]])
                    kb.op("dve", lambda E, dc=dc: E.tensor_tensor(out=yT[:, dc, 0:nt], in0=t0[:, 0:nt], in1=t1[:, 0:nt], op=ALU.add), reads=[dn["t0"], dn["t1"]], writes=[dn["yT"]])
                for t in range(ntile):
                    gt = (tok0 // 128) + t
                    s = gt % 2
                    kb.dma("sp", f"c_x{s}", xt[s][:], x_src[gt * 128:(gt + 1) * 128, :], writes=[dn[f"xt{s}"]])
                    for cb in range(2):
                        b = next_bank()
                        for k in range(8):
                            kb.op("pe", lambda E, k=k, cb=cb, b=b, t=t: E.matmul(ps[b][:], lhsT=yT[:, k, t * 128:(t + 1) * 128], rhs=wO[:, k, cb * 512:(cb + 1) * 512],
                                                                              start=(k == 0), stop=(k == 7)),
                                  reads=[dn["yT"], dw["wO"]], writes=[d_ps[b]])
                        kb.op("dve", lambda E, cb=cb, b=b: E.tensor_tensor(out=tmpo[:, cb * 512:(cb + 1) * 512], in0=ps[b][:], in1=G[var][:, cb * 512:(cb + 1) * 512], op=ALU.mult),
                              reads=[d_ps[b], dw["G"]], writes=[dn["tmpo"]])
                    kb.op("dve", lambda E, s=s: E.tensor_tensor(out=xt[s][:], in0=xt[s][:], in1=tmpo[:], op=ALU.add), reads=[dn["tmpo"], dn[f"xt{s}"]], writes=[dn[f"xt{s}"]])
                    store(f"st_x{s}", x_dst[gt * 128:(gt + 1) * 128, :], xt[s][:], [dn[f"xt{s}"]])
            kb.barrier()

    import os
    FS = os.environ.get("FSTOP", "")
    for l in range(1 if FS else 2):
        last = l == 1
        x_src = x_in if l == 0 else x1
        stage_a(l, x_src)
        if FS == "a":
            break
        if FS == "ag":
            kb.barrier()
            break
        stage_b(l, last)
        if FS == "b":
            break
        stage_c(l, last, x_src, y_out if last else x1)
    kb.barrier()
    for k in sorted(out_keys):
        nc.gpsimd.wait_ge(kb.sems[k], kb.cnt[k])
    return kb


import numpy as np
import ml_dtypes
BF = ml_dtypes.bfloat16
NLT = 2048


def rope_tabs():
    n = 8192
    row = np.repeat(np.arange(n // 64), 64).astype(np.float32)
    col = np.tile(np.arange(64), n // 64).astype(np.float32)
    freqs = (10000.0 ** (-np.arange(0, 32, 2, dtype=np.float32) / 32)).astype(np.float32)
    ar = row[:, None] * freqs; ac = col[:, None] * freqs
    ang = np.concatenate([ar, ar, ac, ac], -1)
    cos = np.cos(ang).astype(np.float32); sin = np.sin(ang).astype(np.float32)
    sgn = np.tile(np.concatenate([-np.ones(16), np.ones(16)]), 2).astype(np.float32)
    return cos, sin * sgn


def col_layout(v, k):
    return np.ascontiguousarray(v.reshape(k, 128).T)


def fourier_consts(j):
    t = np.arange(64)[:, None]; kb = np.arange(64)[None, :]
    a = 2 * np.pi * ((t * kb) % 64) / 64
    cs64 = np.concatenate([np.cos(a), -np.sin(a)], 1).astype(BF)
    p = np.arange(128)[:, None, None]; kbb = np.arange(64)[None, :, None]; ka = (32 * j + np.arange(32))[None, None, :]
    a = 2 * np.pi * ((p * (64 * ka + kbb)) % 8192) / 8192
    T1 = np.concatenate([np.cos(a), -np.sin(a)], 2).astype(BF)
    T2 = np.concatenate([np.sin(a), np.cos(a)], 2).astype(BF)
    n = np.arange(128)[:, None, None]; ch = np.arange(2)[None, :, None]; k = np.arange(256)[None, None, :]
    a = 2 * np.pi * (((ch * 128 + n) * k) % 256) / 256
    dctx = np.concatenate([np.cos(a), -np.sin(a)], 2).astype(BF)
    c1 = np.arange(64)[:, None]; c2 = np.arange(64)[None, :]
    a = 2 * np.pi * ((c1 * c2) % 64) / 64
    ccs = np.stack([np.cos(a), np.sin(a)], 1).astype(np.float32)
    return dict(cs64=cs64, T1j=np.ascontiguousarray(T1), T2j=np.ascontiguousarray(T2), dctx=np.ascontiguousarray(dctx), ccs=np.ascontiguousarray(ccs))


def fused_inputs(I):
    cos, ssin = rope_tabs()
    C = np.ascontiguousarray
    shared = {
        "w_ada": C(I['w_ada']), "b_ada": C(np.stack([col_layout(I['b_ada'][l], 24) for l in range(2)])),
        "g_norm": C(np.stack([col_layout(I['g_norm'][l], 8) for l in range(2)])),
        "w_in": C(I['w_in']), "g_q": C(I['g_q']), "g_k": C(I['g_k']),
        "lamv": C(np.concatenate([I['lam_q1'], I['lam_k1'], I['lam_q2'], I['lam_k2']], 1)),
        "g_sub": C(I['g_sub'].reshape(2, 128, 1)),
        "w_f": C(I['w_f']), "b_f": C(I['b_f'].transpose(0, 2, 1)),
        "ln_g": C(I['ln_g']), "ln_b": C(I['ln_b']),
        "w_sT": C(I['w_s'].transpose(0, 3, 1, 2)),
        "bs64": C(np.broadcast_to(I['b_s'][:, None, :, :], (2, 64, 4, 128))),
        "w_br": C(np.concatenate([I['w_br_a'], I['w_br_b'], I['w_br_c']], 1)), "w_out": C(I['w_out']),
    }
    maps = []
    for core in range(8):
        b, j = core // 4, core % 4
        m = dict(shared)
        m["x_tok"] = C(np.concatenate([I['x'][b, j * NLT:(j + 1) * NLT], I['ctx'][b]], 0))
        m["cvec"] = C(np.stack([col_layout(I['c'][b], 8), col_layout(I['c_ctx'], 8)], -1))
        m["cos"] = C(cos[j * NLT:(j + 1) * NLT]); m["ssin"] = C(ssin[j * NLT:(j + 1) * NLT])
        m.update(fourier_consts(j))
        maps.append(m)
    return maps


def kernel(**inputs):
    I = {k: np.asarray(v, dtype=np.float32) for k, v in inputs.items()}
    kb = KB()
    build_fused(kb)
    res = kb.run(fused_inputs(I))
    out = np.empty((2, 8192, 1024), np.float32)
    for core in range(8):
        b, j = core // 4, core % 4
        out[b, j * NLT:(j + 1) * NLT] = np.asarray(res.results[core]["y"])
    return out
```

```python
import numpy as np
import concourse.bass as bass
import concourse.mybir as mybir
from concourse.bass_utils import run_bass_kernel_spmd

F32 = mybir.dt.float32
BF16 = mybir.dt.bfloat16
AF = mybir.ActivationFunctionType
ALU = mybir.AluOpType
AX = mybir.AxisListType


class Dep:
    __slots__ = ("w", "r", "name")

    def __init__(self, name=""):
        self.w = None
        self.r = []
        self.name = name


class KB:
    COMPUTE = ("pe", "act", "dve", "pool")

    def __init__(self):
        self.nc = bass.Bass("TRN2", target_bir_lowering=False)
        nc = self.nc
        self.eng = {"pe": nc.tensor, "act": nc.scalar, "dve": nc.vector,
                    "pool": nc.gpsimd, "sp": nc.sync}
        self.sems = {}
        self.cnt = {}
        for e in self.COMPUTE:
            self.sems[e] = nc.alloc_semaphore(name="s_" + e)
            self.cnt[e] = 0
        self.seen = {e: {} for e in self.eng}
        self.n_inst = 0

    def _sem(self, key):
        if key not in self.sems:
            self.sems[key] = self.nc.alloc_semaphore(name="d_" + str(key))
            self.cnt[key] = 0
        return self.sems[key]

    def _waits(self, e, reads, writes):
        need = {}

        def add(t, war=False):
            if t is None:
                return
            sk, v = t
            if sk == e and (war or e == "pe"):
                return
            if need.get(sk, 0) < v:
                need[sk] = v
        for d in reads:
            add(d.w)
        for d in writes:
            add(d.w)
            for t in d.r:
                add(t, war=True)
        E = self.eng[e]
        for sk, v in need.items():
            if self.seen[e].get(sk, 0) >= v:
                continue
            E.wait_ge(self.sems[sk], v)
            self.seen[e][sk] = v

    def _mark(self, tok, reads, writes):
        for d in reads:
            d.r.append(tok)
            if len(d.r) > 64:
                m = {}
                for sk, v in d.r:
                    if m.get(sk, 0) < v:
                        m[sk] = v
                d.r = list(m.items())
        for d in writes:
            d.w = tok
            d.r = []

    def op(self, e, fn, reads=(), writes=()):
        self._waits(e, reads, writes)
        inst = fn(self.eng[e])
        self.cnt[e] += 1
        inst.then_inc(self.sems[e], 1)
        self._mark((e, self.cnt[e]), reads, writes)
        self.n_inst += 1
        return inst

    def mm(self, fn, reads=(), writes=(), last=True):
        return self.op("pe", fn, reads, writes)

    def dma(self, q, key, out, in_, reads=(), writes=(), **kw):
        sem = self._sem(key)
        self._waits(q, reads, writes)
        inst = self.eng[q].dma_start(out=out, in_=in_, **kw)
        self.cnt[key] += 16
        inst.then_inc(sem, 16)
        self._mark((key, self.cnt[key]), reads, writes)
        self.n_inst += 1
        return inst

    def barrier(self, skip=()):
        for e, E in self.eng.items():
            for sk, sem in self.sems.items():
                if sk in skip:
                    continue
                v = self.cnt[sk]
                if v == 0 or self.seen[e].get(sk, 0) >= v:
                    continue
                E.wait_ge(sem, v)
                self.seen[e][sk] = v

    def wait_all(self, e, deps):
        self._waits(e, deps, ())

    def run(self, in_maps, n=8, trace=False):
        return run_bass_kernel_spmd(self.nc, in_maps, core_ids=list(range(n)), trace=trace)


import math
from contextlib import ExitStack

NT_A = 18
NLAT_T = 16
TOKS = NT_A * 128
NKT = 66
NKEY = NKT * 128
EPS = 1e-6
BT = 256
O_Q, O_K, O_V, O_F, O_U, O_VC, O_GATE, O_MERGE = 0, 512, 1024, 1536, 1792, 2048, 2304, 3328
RG = [[0, 1, 2, 3], [4, 5, 6, 7]]


def build_fused(kb):
    nc = kb.nc
    EI = lambda name, shape, d=F32: nc.dram_tensor(name, shape, d, kind="ExternalInput").ap()
    IN = lambda name, shape, d=F32: nc.dram_tensor(name, shape, d, kind="Internal").ap()
    x_in = EI("x_tok", [TOKS, 1024])
    cvec = EI("cvec", [128, 8, 2])
    w_ada = EI("w_ada", [2, 1024, 3072]); b_ada = EI("b_ada", [2, 128, 24]); g_norm = EI("g_norm", [2, 128, 8])
    w_in = EI("w_in", [2, 1024, 6400])
    g_q = EI("g_q", [2, 64]); g_k = EI("g_k", [2, 64])
    cos = EI("cos", [2048, 64]); ssin = EI("ssin", [2048, 64])
    lamv = EI("lamv", [2, 256]); g_sub = EI("g_sub", [2, 128, 1])
    w_f = EI("w_f", [2, 4, 64, 64]); b_f = EI("b_f", [2, 64, 4])
    cs64 = EI("cs64", [64, 128], BF16); T1d = EI("T1j", [128, 64, 64], BF16); T2d = EI("T2j", [128, 64, 64], BF16)
    dcd = EI("dctx", [128, 2, 512], BF16); ccs = EI("ccs", [64, 2, 64])
    ln_g = EI("ln_g", [2, 256]); ln_b = EI("ln_b", [2, 256])
    w_sT = EI("w_sT", [2, 128, 4, 128]); bs_d = EI("bs64", [2, 64, 4, 128])
    w_br = EI("w_br", [2, 1024, 1024]); w_out = EI("w_out", [2, 1024, 1024])
    y_out = nc.dram_tensor("y", [2048, 1024], F32, kind="ExternalOutput").ap()
    x1 = IN("x1", [TOKS, 1024])
    hT_d = IN("hT_d", [8, 128, TOKS], BF16)
    qT_d = IN("qT_d", [4, 128, TOKS], BF16)
    kT_lat = [IN(f"kT_lat{i}", [256, 2048], BF16) for i in range(2)]
    kT_ag = [IN(f"kT_ag{i}", [1024, 2048], BF16) for i in range(2)]
    kT_ctx = IN("kT_ctx", [4, 128, 256], BF16)
    v_lat = [IN(f"v_lat{i}", [1024, 512], BF16) for i in range(2)]
    v_ag = [IN(f"v_ag{i}", [4096, 512], BF16) for i in range(2)]
    v_ctx = IN("v_ctx", [256, 512], BF16)
    f_lat = IN("f_lat", [4 * 2048, 64], BF16)
    f_ag = IN("f_ag", [16 * 2048, 64], BF16)
    f_ctx = IN("f_ctx", [256, 256], BF16)
    oaT_d = IN("oaT_d", [4, 128, TOKS])
    obT_d = IN("obT_d", [4, 64, TOKS])
    modT_d = IN("modT_d", [128, 24, 2])
    out_keys = set()
    uid = [0]

    def U(name):
        uid[0] += 1
        return f"{name}_{uid[0]}"

    def store(key, out, in_, reads):
        out_keys.add(key)
        kb.dma("pool", key, out, in_, reads=reads, writes=[])

    d_fag = Dep("f_ag"); d_kvag = Dep("kv_ag")

    def collective(kind, src, dst, dep):
        sem = kb._sem("cc")
        inst = nc.gpsimd.collective_compute(kind, ALU.bypass, replica_groups=RG, ins=[src], outs=[dst])
        inst.then_inc(sem, 1)
        kb.cnt["cc"] += 1
        dep.w = ("cc", kb.cnt["cc"])

    def rstd_chain(ssrc, dsrc, dst, ddst, scale):
        kb.op("dve", lambda E: E.tensor_scalar(out=dst, in0=ssrc, scalar1=scale, scalar2=EPS, op0=ALU.mult, op1=ALU.add), reads=[dsrc], writes=[ddst])
        kb.op("act", lambda E: E.activation(out=dst, in_=dst, func=AF.Sqrt), reads=[ddst], writes=[ddst])
        kb.op("dve", lambda E: E.reciprocal(out=dst, in_=dst), reads=[ddst], writes=[ddst])

    def make_ident(ident, dep):
        kb.op("pool", lambda E: E.memset(ident[:], 0.0), writes=[dep])
        kb.op("pool", lambda E: E.affine_select(out=ident[:], in_=ident[:], pattern=[[-1, 128]], compare_op=ALU.not_equal,
                                                fill=1.0, base=0, channel_multiplier=1), reads=[dep], writes=[dep])

    def stage_a(l, x_src):
        with ExitStack() as es:
            sb = lambda name, shape, dty: es.enter_context(nc.sbuf_tensor(U(name), shape, dty))
            ps = [es.enter_context(nc.psum_tensor(U(f"psA{i}"), [128, 512], F32)) for i in range(6)]
            ps += [es.enter_context(nc.psum_tensor(U(f"psA{i}"), [128, 1024], BF16)) for i in (6, 7)]
            ident = sb("ident", [128, 128], F32); identb = sb("identb", [128, 128], BF16)
            cT = sb("cT", [128, 8, 2], F32); sg = sb("sg", [128, 8, 2], F32); sc = sb("sc", [128, 8, 2], F32)
            bT = sb("bT", [128, 24], F32); gn = sb("gn", [128, 8], F32)
            modT = sb("modT_s", [128, 24, 2], F32); Aff = sb("Aff", [128, 8, 2], F32)
            wst = [sb(f"wst{i}", [128, 8, 512], F32) for i in range(2)]
            wA = sb("wA", [128, 8, 1792], BF16)
            GQ = sb("GQ", [128, 64], F32); GK = sb("GK", [128, 64], F32)
            xt = [sb(f"xt{i}", [128, 1024], F32) for i in range(2)]
            junk = sb("junk", [128, 1024], F32)
            ss = sb("ss", [128, 1], F32); rstd = sb("rstd", [128, 1], F32)
            hTt = [sb(f"hTt{i}", [128, 8, 128], BF16) for i in range(2)]
            cs = [sb(f"cs{i}", [128, 2, 64], F32) for i in range(2)]
            sq = sb("sq", [128, 512], F32); ss8 = sb("ss8", [128, 8], F32)
            qn = sb("qn", [128, 512], F32); t1 = sb("t1", [128, 512], F32); t2 = sb("t2", [128, 512], F32)
            qr = [sb(f"qr{i}", [128, 512], BF16) for i in range(2)]
            qTt = [sb(f"qTt{i}", [128, 4, 128], BF16) for i in range(2)]
            kTt = [sb(f"kTt{i}", [128, 4, 128], BF16) for i in range(2)]
            vt = [sb(f"vt{i}", [128, 512], BF16) for i in range(2)]
            ft = [sb(f"ft{i}", [128, 256], BF16) for i in range(2)]
            D = lambda n: Dep(n)
            d_ident, d_identb, d_cT, d_sg, d_sc, d_bT, d_gn, d_modT, d_Aff = [D(n) for n in "ident identb cT sg sc bT gn modT Aff".split()]
            d_wst = [D("wst0"), D("wst1")]; d_wA = D("wA"); d_G = D("G")
            d_xt = [D("xt0"), D("xt1")]; d_junk = D("junk"); d_ss = D("ss"); d_rstd = D("rstd")
            d_hTt = [D("hTt0"), D("hTt1")]; d_cs = [D("cs0"), D("cs1")]
            d_sq, d_ss8, d_qn, d_t1, d_t2 = D("sq"), D("ss8"), D("qn"), D("t1"), D("t2")
            d_qr = [D("qr0"), D("qr1")]; d_qTt = [D("qTt0"), D("qTt1")]; d_kTt = [D("kTt0"), D("kTt1")]
            d_vt = [D("vt0"), D("vt1")]; d_ft = [D("ft0"), D("ft1")]
            d_ps = [D(f"ps{i}") for i in range(8)]

            make_ident(ident, d_ident)
            kb.op("pool", lambda E: E.tensor_copy(out=identb[:], in_=ident[:]), reads=[d_ident], writes=[d_identb])
            kb.dma("sp", "ld_c0", cT[:], cvec, writes=[d_cT])
            kb.dma("sp", "ld_c1", bT[:], b_ada[l], writes=[d_bT])
            kb.dma("sp", "ld_c2", gn[:], g_norm[l], writes=[d_gn])
            kb.dma("sp", "ld_c3", GQ[:], g_q[l].partition_broadcast(128), writes=[d_G])
            kb.dma("sp", "ld_c3", GK[:], g_k[l].partition_broadcast(128), writes=[d_G])
            kb.op("act", lambda E: E.activation(out=sg[:], in_=cT[:], func=AF.Sigmoid), reads=[d_cT], writes=[d_sg])
            kb.op("dve", lambda E: E.tensor_tensor(out=sc[:], in0=cT[:], in1=sg[:], op=ALU.mult), reads=[d_cT, d_sg], writes=[d_sc])
            w_ada_v = w_ada[l].rearrange("(k p) n -> p k n", p=128)
            for g in range(6):
                s = g % 2
                kb.dma("sp", f"ld_w{s}", wst[s][:], w_ada_v[:, :, g * 512:(g + 1) * 512], writes=[d_wst[s]])
                for jj in range(4):
                    j = g * 4 + jj
                    for k in range(8):
                        kb.op("pe", lambda E, k=k, jj=jj, s=s, j=j: E.matmul(ps[0][:, 2 * j:2 * j + 2], lhsT=wst[s][:, k, jj * 128:(jj + 1) * 128],
                                                                            rhs=sc[:, k, :], start=(k == 0), stop=(k == 7)),
                              reads=[d_wst[s], d_sc], writes=[d_ps[0]])
            kb.op("dve", lambda E: E.tensor_tensor(out=modT[:], in0=ps[0][:, 0:48].rearrange("p (j n) -> p j n", n=2),
                                                   in1=bT[:].unsqueeze(2).to_broadcast([128, 24, 2]), op=ALU.add),
                  reads=[d_ps[0], d_bT], writes=[d_modT])
            store("st_mod", modT_d, modT[:], [d_modT])
            kb.op("dve", lambda E: E.tensor_scalar(out=Aff[:], in0=modT[:, 8:16, :], scalar1=1.0, scalar2=None, op0=ALU.add), reads=[d_modT], writes=[d_Aff])
            kb.op("dve", lambda E: E.tensor_tensor(out=Aff[:], in0=Aff[:], in1=gn[:].unsqueeze(2).to_broadcast([128, 8, 2]), op=ALU.mult),
                  reads=[d_Aff, d_gn], writes=[d_Aff])
            w_in_v = w_in[l].rearrange("(k p) n -> p k n", p=128)
            for g in range(4):
                s = g % 2
                n = 512 if g < 3 else 256
                kb.dma("sp", f"ld_w{s}", wst[s][:, :, 0:n], w_in_v[:, :, g * 512:g * 512 + n], writes=[d_wst[s]])
                e = "pool" if g % 2 == 0 else "dve"
                kb.op(e, lambda E, s=s, n=n, g=g: E.tensor_copy(out=wA[:, :, g * 512:g * 512 + n], in_=wst[s][:, :, 0:n]), reads=[d_wst[s]], writes=[d_wA])

            def load_tile(i):
                s = i % 2
                kb.dma("sp", f"ld_x{s}", xt[s][:], x_src[i * 128:(i + 1) * 128, :], writes=[d_xt[s]])
                if i < NLAT_T:
                    kb.dma("sp", f"ld_cs{s}", cs[s][:, 0, :], cos[i * 128:(i + 1) * 128, :], writes=[d_cs[s]])
                    kb.dma("sp", f"ld_cs{s}", cs[s][:, 1, :], ssin[i * 128:(i + 1) * 128, :], writes=[d_cs[s]])

            G2 = sb("G2", [128, 2, 64], F32)
            sq2 = sb("sq2", [128, 1024], F32); ss16 = sb("ss16", [128, 16], F32)
            qn2 = sb("qn2", [128, 1024], F32); t1b = sb("t1b", [128, 1024], F32); t2b = sb("t2b", [128, 1024], F32)
            QR2 = sb("QR2", [128, 1024], BF16)
            TT2 = [sb(f"TT2_{i}", [128, 8, 128], BF16) for i in range(2)]
            d_G2, d_sq2, d_ss16, d_qn2, d_t1b, d_t2b, d_QR2 = [D(n) for n in "G2 sq2 ss16 qn2 t1b t2b QR2".split()]
            d_TT2 = [D("TT2_0"), D("TT2_1")]
            kb.op("dve", lambda E: E.tensor_copy(out=G2[:, 0, :], in_=GQ[:]), reads=[d_G], writes=[d_G2])
            kb.op("dve", lambda E: E.tensor_copy(out=G2[:, 1, :], in_=GK[:]), reads=[d_G], writes=[d_G2])
            psQK = [ps[2], ps[3]]
            g64 = lambda ap: ap.rearrange("p (g c) -> p g c", c=64)

            def head(i):
                s = i % 2
                var = 0 if i < NLAT_T else 1
                load_tile(i)
                X = xt[s]
                kb.op("act", lambda E: E.activation(out=junk[:], in_=X[:], func=AF.Square, accum_out=ss[:]), reads=[d_xt[s]], writes=[d_junk, d_ss])
                rstd_chain(ss[:], d_ss, rstd[:], d_rstd, 1.0 / 1024)
                kb.op("dve", lambda E: E.tensor_scalar(out=X[:], in0=X[:], scalar1=rstd[:, 0:1], scalar2=None, op0=ALU.mult),
                      reads=[d_xt[s], d_rstd], writes=[d_xt[s]])
                for k in range(8):
                    b = k // 4
                    kb.op("pe", lambda E, k=k, b=b: E.transpose(out=ps[b][:, (k % 4) * 128:(k % 4 + 1) * 128], in_=X[:, k * 128:(k + 1) * 128], identity=ident[:]),
                          reads=[d_xt[s], d_ident], writes=[d_ps[b]])
                for k in range(8):
                    b = k // 4
                    src = ps[b][:, (k % 4) * 128:(k % 4 + 1) * 128]
                    if k % 2 == 0:
                        kb.op("act", lambda E, k=k, src=src: E.activation(out=hTt[s][:, k, :], in_=src, func=AF.Identity,
                                                                          scale=Aff[:, k, var:var + 1], bias=modT[:, k, var:var + 1]),
                              reads=[d_ps[b], d_Aff, d_modT], writes=[d_hTt[s]])
                    else:
                        kb.op("dve", lambda E, k=k, src=src: E.tensor_scalar(out=hTt[s][:, k, :], in0=src, scalar1=Aff[:, k, var:var + 1],
                                                                             scalar2=modT[:, k, var:var + 1], op0=ALU.mult, op1=ALU.add),
                              reads=[d_ps[b], d_Aff, d_modT], writes=[d_hTt[s]])
                store(f"st_h{s}", hT_d[:, :, i * 128:(i + 1) * 128].rearrange("k p t -> p k t"), hTt[s][:], [d_hTt[s]])

            def mid(i):
                s = i % 2
                for cb in range(4):
                    n = 512 if cb < 3 else 256
                    for k in range(8):
                        kb.op("pe", lambda E, k=k, cb=cb, n=n: E.matmul(ps[2 + cb][:, 0:n], lhsT=hTt[s][:, k, :], rhs=wA[:, k, cb * 512:cb * 512 + n],
                                                                       start=(k == 0), stop=(k == 7)),
                              reads=[d_hTt[s], d_wA], writes=[d_ps[2 + cb]])

            def tail_a(i):
                s = i % 2
                lat = i < NLAT_T
                for w in range(2):
                    kb.op("act", lambda E, w=w: E.activation(out=sq2[:, w * 512:(w + 1) * 512], in_=psQK[w][:], func=AF.Square), reads=[d_ps[2 + w]], writes=[d_sq2])
                kb.op("dve", lambda E: E.tensor_reduce(out=ss16[:], in_=g64(sq2[:]), axis=AX.X, op=ALU.add), reads=[d_sq2], writes=[d_ss16])
                rstd_chain(ss16[:], d_ss16, ss16[:], d_ss16, 1.0 / 64)
                for w in range(2):
                    kb.op("dve", lambda E, w=w: E.tensor_tensor(out=g64(qn2[:, w * 512:(w + 1) * 512]), in0=g64(psQK[w][:]),
                                                           in1=ss16[:, w * 8:(w + 1) * 8].unsqueeze(2).to_broadcast([128, 8, 64]), op=ALU.mult),
                          reads=[d_ps[2 + w], d_ss16], writes=[d_qn2])
                kb.op("act", lambda E: E.activation(out=vt[s][:], in_=ps[4][:], func=AF.Copy), reads=[d_ps[4]], writes=[d_vt[s]])
                kb.op("act", lambda E: E.activation(out=ft[s][:], in_=ps[5][:, 0:256], func=AF.Copy), reads=[d_ps[5]], writes=[d_ft[s]])
                if lat:
                    store(f"st_v{s}", v_lat[i // 8][(i % 8) * 128:(i % 8 + 1) * 128, :], vt[s][:], [d_vt[s]])
                    store(f"st_f{s}", f_lat.rearrange("(g t) c -> t g c", g=4)[i * 128:(i + 1) * 128, :, :], ft[s][:].rearrange("p (g c) -> p g c", c=64), [d_ft[s]])
                else:
                    ic = i - NLAT_T
                    store(f"st_v{s}", v_ctx[ic * 128:(ic + 1) * 128, :], vt[s][:], [d_vt[s]])
                    store(f"st_f{s}", f_ctx[ic * 128:(ic + 1) * 128, :], ft[s][:], [d_ft[s]])

            def tail_b(i):
                s = i % 2
                lat = i < NLAT_T
                qv4 = qn2[:].rearrange("p (w g c) -> p w g c", w=2, c=64)
                G2b = G2[:].unsqueeze(2).to_broadcast([128, 2, 8, 64])
                if lat:
                    for w in range(2):
                        kb.op("dve", lambda E, w=w: E.tensor_tensor(out=qv4[:, w], in0=qv4[:, w], in1=G2[:, w, :].unsqueeze(1).to_broadcast([128, 8, 64]), op=ALU.mult),
                              reads=[d_qn2, d_G2], writes=[d_qn2])
                    kb.op("dve", lambda E: E.tensor_tensor(out=g64(t1b[:]), in0=g64(qn2[:]), in1=cs[s][:, 0, :].unsqueeze(1).to_broadcast([128, 16, 64]), op=ALU.mult),
                          reads=[d_qn2, d_cs[s]], writes=[d_t1b])
                    qv = qn2[:].rearrange("p (g a h c) -> p g a h c", a=2, h=2, c=16)
                    tv = t2b[:].rearrange("p (g a h c) -> p g a h c", a=2, h=2, c=16)
                    sv = cs[s][:, 1, :].rearrange("p (a h c) -> p a h c", a=2, h=2)
                    for hf in range(2):
                        for a in range(2):
                            kb.op("dve", lambda E, hf=hf, a=a: E.tensor_tensor(out=tv[:, :, a, hf, :], in0=qv[:, :, a, 1 - hf, :],
                                                                                in1=sv[:, a, hf, :].unsqueeze(1).to_broadcast([128, 16, 16]), op=ALU.mult),
                                  reads=[d_qn2, d_cs[s]], writes=[d_t2b])
                    kb.op("dve", lambda E: E.tensor_tensor(out=QR2[:], in0=t1b[:], in1=t2b[:], op=ALU.add), reads=[d_t1b, d_t2b], writes=[d_QR2])
                else:
                    QRv = QR2[:].rearrange("p (w g c) -> p w g c", w=2, c=64)
                    for w in range(2):
                        kb.op("dve", lambda E, w=w: E.tensor_tensor(out=QRv[:, w], in0=qv4[:, w], in1=G2[:, w, :].unsqueeze(1).to_broadcast([128, 8, 64]), op=ALU.mult),
                              reads=[d_qn2, d_G2], writes=[d_QR2])
                PT = ps[6]; dPT = d_ps[6]
                for hh in range(8):
                    kb.op("pe", lambda E, hh=hh: E.transpose(out=PT[:, hh * 128:(hh + 1) * 128], in_=QR2[:, hh * 128:(hh + 1) * 128], identity=identb[:]),
                          reads=[d_QR2, d_identb], writes=[dPT])
                TT = TT2[s]; dTT = d_TT2[s]
                kb.op("act", lambda E: E.activation(out=TT[:].rearrange("p h t -> p (h t)"), in_=PT[:, 0:1024], func=AF.Copy), reads=[dPT], writes=[dTT])
                store(f"st_q{s}", qT_d[:, :, i * 128:(i + 1) * 128].rearrange("h p t -> p h t"), TT[:, 0:4, :], [dTT])
                if lat:
                    for hp in range(2):
                        store(f"st_k{s}", kT_lat[hp].rearrange("(h p) t -> p h t", p=128)[:, :, i * 128:(i + 1) * 128], TT[:, 4 + hp * 2:4 + hp * 2 + 2, :], [dTT])
                else:
                    store(f"st_k{s}", kT_ctx[:, :, (i - NLAT_T) * 128:(i - NLAT_T + 1) * 128].rearrange("h p t -> p h t"), TT[:, 4:8, :], [dTT])

            head(0)
            mid(0)
            head(1)
            for i in range(NT_A):
                tail_a(i)
                if i + 1 < NT_A:
                    mid(i + 1)
                tail_b(i)
                if i == NLAT_T - 1:
                    for key in ("st_k0", "st_k1", "st_v0", "st_v1", "st_f0", "st_f1"):
                        nc.gpsimd.wait_ge(kb.sems[key], kb.cnt[key])
                        kb.seen["pool"][key] = kb.cnt[key]
                    collective("AllGather", f_lat, f_ag, d_fag)
                    for ci in range(2):
                        collective("AllGather", kT_lat[ci], kT_ag[ci], d_kvag)
                        collective("AllGather", v_lat[ci], v_ag[ci], d_kvag)
                if i + 2 < NT_A:
                    head(i + 2)
            kb.barrier(skip=("cc",))

    def stage_b(l, last):
        lam_init = 0.8 - 0.6 * math.exp(-0.3 * l)
        with ExitStack() as es:
            sbt = lambda name, shape, dty: es.enter_context(nc.sbuf_tensor(U(name), shape, dty))
            ps = [es.enter_context(nc.psum_tensor(U(f"psF{i}"), [128, 512], F32)) for i in range(8)]
            d_ps = [Dep(f"ps{i}") for i in range(8)]
            X1 = [sbt(f"X1_{i}", [64, 8192], BF16) for i in range(2)]
            Xc = sbt("Xc", [128, 2, 256], BF16)
            CS = sbt("CS", [64, 128], BF16)
            T1 = sbt("T1s", [128, 64, 64], BF16)
            T2 = sbt("T2s", [128, 64, 64], BF16)
            DC = sbt("DC", [128, 2, 512], BF16)
            Bsb2 = [sbt(f"Bsb{i}", [128, 64, 2, 64], BF16) for i in range(2)]
            YT2 = [sbt(f"YT{i}", [64, 64, 2, 32], BF16) for i in range(2)]
            YC = sbt("YC", [64, 2, 256], BF16)
            CC = sbt("CC", [64, 2, 64], F32)
            WF = sbt("WF", [64, 4, 64], F32)
            BF_ = sbt("BF", [64, 4], F32)
            MC = sbt("MC", [64, 4, 2, 64], BF16)
            ob = [sbt(f"ob{i}", [64, 512], F32) for i in range(2)]
            dn = {n: Dep(n) for n in "X1_0 X1_1 Xc CS T1 T2 DC Bsb0 Bsb1 YT0 YT1 YC CC WF BF MC ob0 ob1".split()}
            f_ag_v = f_ag.rearrange("(r g t p) c -> r g t (p c)", r=4, g=4, p=128)

            def load_x1(g):
                s = g % 2
                for r in range(4):
                    kb.dma("sp", f"f_x1{s}", X1[s][r * 16:(r + 1) * 16, :], f_ag_v[r, g], reads=[d_fag], writes=[dn[f"X1_{s}"]])
            load_x1(0)
            if not last:
                kb.dma("sp", "f_xc", Xc[:], f_ctx.rearrange("(k p) c -> p k c", p=128), writes=[dn["Xc"]])
                kb.dma("sp", "f_dc", DC[:], dcd, writes=[dn["DC"]])
            kb.dma("sp", "f_cs", CS[:], cs64, writes=[dn["CS"]])
            kb.dma("sp", "f_cc", CC[:], ccs, writes=[dn["CC"]])
            kb.dma("sp", "f_wf", WF[:], w_f[l].rearrange("g c d -> c g d"), writes=[dn["WF"]])
            kb.dma("sp", "f_bf", BF_[:], b_f[l], writes=[dn["BF"]])
            kb.dma("sp", "f_t1", T1[:], T1d, writes=[dn["T1"]])
            kb.dma("sp", "f_t2", T2[:], T2d, writes=[dn["T2"]])
            for g in range(4):
                for i in range(2):
                    kb.op("pe", lambda E, i=i, g=g: E.matmul(ps[0][0:64, (g * 2 + i) * 64:(g * 2 + i + 1) * 64], lhsT=CC[:, i, :], rhs=WF[:, g, :], start=True, stop=True),
                          reads=[dn["CC"], dn["WF"]], writes=[d_ps[0]])
            kb.op("dve", lambda E: E.tensor_copy(out=MC[:].rearrange("c g i d -> c (g i d)"), in_=ps[0][0:64, 0:512]), reads=[d_ps[0]], writes=[dn["MC"]])
            sc_lat = 1.0 / math.sqrt(8192 * 64); sc_ctx = 1.0 / math.sqrt(256 * 64)
            blkc = [0]
            def f_s1(g):
                s = g % 2
                if g + 1 < 4:
                    load_x1(g + 1)
                X1v = X1[s][:].rearrange("t (p c) -> t p c", c=64)
                Bsb = Bsb2[s]; dBsb = dn[f"Bsb{s}"]
                for c4 in range(16):
                    b = 1 + c4 % 2
                    for cc in range(4):
                        c = c4 * 4 + cc
                        kb.op("pe", lambda E, c=c, cc=cc, b=b: E.matmul(ps[b][:, cc * 128:(cc + 1) * 128], lhsT=X1v[:, :, c], rhs=CS[:], start=True, stop=True),
                              reads=[dn[f"X1_{s}"], dn["CS"]], writes=[d_ps[b]])
                    dst = Bsb[:, c4 * 4:(c4 + 1) * 4, :, :].rearrange("p c r k -> p (c r k)")
                    if c4 % 2 == 0:
                        kb.op("act", lambda E, dst=dst, b=b: E.activation(out=dst, in_=ps[b][:], func=AF.Copy), reads=[d_ps[b]], writes=[dBsb])
                    else:
                        kb.op("dve", lambda E, dst=dst, b=b: E.tensor_copy(out=dst, in_=ps[b][:]), reads=[d_ps[b]], writes=[dBsb])

            def f_s3(g):
                s = g % 2
                Bsb = Bsb2[s]; YT = YT2[s]; dBsb = dn[f"Bsb{s}"]; dYT = dn[f"YT{s}"]
                for k8 in range(8):
                    b = 3 + k8 % 2
                    for ki in range(8):
                        kbi = k8 * 8 + ki
                        o = ps[b][0:64, ki * 64:(ki + 1) * 64]
                        kb.op("pe", lambda E, kbi=kbi, o=o: E.matmul(o, lhsT=Bsb[:, :, 0, kbi], rhs=T1[:, kbi, :], start=True, stop=False),
                              reads=[dBsb, dn["T1"]], writes=[d_ps[b]])
                        kb.op("pe", lambda E, kbi=kbi, o=o: E.matmul(o, lhsT=Bsb[:, :, 1, kbi], rhs=T2[:, kbi, :], start=False, stop=True),
                              reads=[dBsb, dn["T2"]], writes=[d_ps[b]])
                    dst = YT[:, k8 * 8:(k8 + 1) * 8, :, :].rearrange("c k r a -> c (k r a)")
                    if k8 % 2 == 0:
                        kb.op("act", lambda E, dst=dst, b=b: E.activation(out=dst, in_=ps[b][0:64, :], func=AF.Copy), reads=[d_ps[b]], writes=[dYT])
                    else:
                        kb.op("dve", lambda E, dst=dst, b=b: E.tensor_copy(out=dst, in_=ps[b][0:64, :]), reads=[d_ps[b]], writes=[dYT])

            def f_s4(g):
                s = g % 2
                YT = YT2[s]; dYT = dn[f"YT{s}"]
                nblk = 4
                if not last:
                    for k in range(2):
                        kb.op("pe", lambda E, k=k, g=g: E.matmul(ps[5][0:64, :], lhsT=Xc[:, k, g * 64:(g + 1) * 64], rhs=DC[:, k, :], start=(k == 0), stop=(k == 1)),
                              reads=[dn["Xc"], dn["DC"]], writes=[d_ps[5]])
                    kb.op("dve", lambda E: E.tensor_copy(out=YC[:].rearrange("c r k -> c (r k)"), in_=ps[5][0:64, :]), reads=[d_ps[5]], writes=[dn["YC"]])
                    nblk = 5
                for blk in range(nblk):
                    b = 6 + blkc[0] % 2
                    so = blkc[0] % 2
                    blkc[0] += 1
                    if blk < 4:
                        n = 512; scl = sc_lat; dy = dYT
                        r0 = YT[:, :, 0, blk * 8:(blk + 1) * 8].rearrange("c k a -> c a k")
                        r1 = YT[:, :, 1, blk * 8:(blk + 1) * 8].rearrange("c k a -> c a k")
                    else:
                        n = 256; r0 = YC[:, 0, :]; r1 = YC[:, 1, :]; scl = sc_ctx; dy = dn["YC"]
                    kb.op("pe", lambda E, r0=r0, n=n, b=b, g=g: E.matmul(ps[b][0:64, 0:n], lhsT=MC[:, g, 0, :], rhs=r0, start=True, stop=False), reads=[dn["MC"], dy], writes=[d_ps[b]])
                    kb.op("pe", lambda E, r1=r1, n=n, b=b, g=g: E.matmul(ps[b][0:64, 0:n], lhsT=MC[:, g, 1, :], rhs=r1, start=False, stop=True), reads=[dn["MC"], dy], writes=[d_ps[b]])
                    kb.op("act", lambda E, n=n, b=b, so=so, scl=scl, g=g: E.activation(out=ob[so][:, 0:n], in_=ps[b][0:64, 0:n], func=AF.Identity, scale=scl, bias=BF_[:, g:g + 1]),
                          reads=[d_ps[b], dn["BF"]], writes=[dn[f"ob{so}"]])
                    store(f"st_ob{so}", obT_d[g, :, blk * 512:blk * 512 + n], ob[so][:, 0:n], [dn[f"ob{so}"]])

            f_s1(0)
            for g in range(4):
                if g + 1 < 4:
                    f_s1(g + 1)
                f_s3(g)
                f_s4(g)
            kb.barrier()

        with ExitStack() as es:
            sbt = lambda name, shape, dty: es.enter_context(nc.sbuf_tensor(U(name), shape, dty))
            psS = [es.enter_context(nc.psum_tensor(U(f"psS{i}"), [128, 1024], F32)) for i in range(2)]
            ps = [None] * 4 + [es.enter_context(nc.psum_tensor(U(f"psB{i}"), [128, 512], F32)) for i in range(4, 8)]
            d_psS = [Dep("psS0"), Dep("psS1")]
            d_ps = [Dep(f"ps{i}") for i in range(8)]
            kTh = [sbt(f"kTh{i}", [128, NKEY], BF16) for i in range(2)]
            vh = [sbt(f"vh{i}", [128, NKT, 128], BF16) for i in range(2)]
            qTh = [sbt(f"qTh{i}", [128, TOKS], BF16) for i in range(2)]
            NPT = 4
            pt = [sbt(f"pt{i}", [128, 1024], BF16) for i in range(NPT)]
            ones = sbt("ones", [128, 128], F32)
            onesb = sbt("onesb", [128, 32], BF16)
            sel = sbt("sel", [128, 2, 128], F32)
            rsb = sbt("rsb", [128, 512], F32)
            lv = sbt("lv", [128, 4, 64], F32); lt = sbt("lt", [128, 2, 64], F32); l2 = sbt("l2", [128, 2], F32)
            nlam = sbt("nlam", [128, 1], F32); gs = sbt("gs", [128, 1], F32)
            rec = [sbt(f"rec{c}", [128, 512], F32) for c in range(2)]
            o0 = sbt("o0", [128, 512], F32); o1 = sbt("o1", [128, 512], F32)
            osq = sbt("osq", [128, 512], F32); rs = sbt("rs", [128, 512], F32)
            of = [sbt(f"of{i}", [128, 512], F32) for i in range(2)]
            d_kv = [Dep("kv0"), Dep("kv1")]
            d_pt = [Dep(f"pt{i}") for i in range(NPT)]
            dm = {n: Dep(n) for n in "ones onesb sel rsb lv lt l2 nlam gs rec0 rec1 o0 o1 osq rs of0 of1".split()}
            kb.op("pool", lambda E: E.memset(ones[:], 1.0), writes=[dm["ones"]])
            kb.op("pool", lambda E: E.memset(onesb[:], 1.0), writes=[dm["onesb"]])
            kb.op("pool", lambda E: E.memset(sel[:], 0.0), writes=[dm["sel"]])
            for (p0, p1, c, val) in ((0, 32, 0, 1.0 / 32), (64, 96, 0, 1.0 / 32), (32, 64, 1, 1.0 / 32), (64, 128, 1, 1.0 / 32), (64, 96, 1, 0.0)):
                kb.op("pool", lambda E, p0=p0, p1=p1, c=c, val=val: E.memset(sel[p0:p1, c, :], val), reads=[dm["sel"]], writes=[dm["sel"]])
            kb.dma("sp", "a_lv", lv[:].rearrange("p a c -> p (a c)"), lamv[l].partition_broadcast(128), writes=[dm["lv"]])
            kb.dma("sp", "a_gs", gs[:], g_sub[l], writes=[dm["gs"]])
            lvv = lv[:].rearrange("p (i j) c -> p i j c", j=2)
            kb.op("dve", lambda E: E.tensor_tensor(out=lt[:], in0=lvv[:, :, 0, :], in1=lvv[:, :, 1, :], op=ALU.mult), reads=[dm["lv"]], writes=[dm["lt"]])
            kb.op("dve", lambda E: E.tensor_reduce(out=l2[:], in_=lt[:], axis=AX.X, op=ALU.add), reads=[dm["lt"]], writes=[dm["l2"]])
            kb.op("act", lambda E: E.activation(out=l2[:], in_=l2[:], func=AF.Exp), reads=[dm["l2"]], writes=[dm["l2"]])
            kb.op("dve", lambda E: E.tensor_tensor(out=nlam[:], in0=l2[:, 1:2], in1=l2[:, 0:1], op=ALU.subtract), reads=[dm["l2"]], writes=[dm["nlam"]])
            kb.op("dve", lambda E: E.tensor_scalar(out=nlam[:], in0=nlam[:], scalar1=-lam_init, scalar2=None, op0=ALU.add), reads=[dm["nlam"]], writes=[dm["nlam"]])
            kb.op("dve", lambda E: E.tensor_scalar(out=gs[:], in0=gs[:], scalar1=(1.0 - lam_init), scalar2=None, op0=ALU.mult), reads=[dm["gs"]], writes=[dm["gs"]])
            kT_ag_v = [a.rearrange("(r h p) t -> h p r t", r=4, h=2) for a in kT_ag]
            v_ag_v = [a.rearrange("(r t p) e -> p r t e", r=4, p=128) for a in v_ag]
            v_ctx_v = v_ctx.rearrange("(t p) e -> p t e", p=128)

            def load_head(h):
                s = h % 2
                kb.dma("sp", f"a_k{s}", kTh[s][:, 0:8192].rearrange("p (r t) -> p r t", r=4), kT_ag_v[h // 2][h % 2], reads=[d_kvag], writes=[d_kv[s]])
                kb.dma("sp", f"a_k{s}", kTh[s][:, 8192:NKEY], kT_ctx[h], writes=[d_kv[s]])
                for half in range(2):
                    for r in range(4):
                        kt0 = r * 16 + half * 8
                        kb.dma("sp", f"a_k{s}", vh[s][:, kt0:kt0 + 8, :], v_ag_v[half][:, r, :, h * 128:(h + 1) * 128], reads=[d_kvag], writes=[d_kv[s]])
                kb.dma("sp", f"a_k{s}", vh[s][:, 64:66, :], v_ctx_v[:, :, h * 128:(h + 1) * 128], writes=[d_kv[s]])
                kb.dma("sp", f"a_k{s}", qTh[s][:], qT_d[h], writes=[d_kv[s]])

            deferred = []

            def make_epilogue(h, q0, nq, so):
                def e_sel(c):
                    kb.op("pe", lambda E: E.matmul(ps[7][:, 0:nq], lhsT=sel[:, c, :], rhs=rsb[:, 0:nq], start=True, stop=True),
                          reads=[dm["sel"], dm["rsb"]], writes=[d_ps[7]])
                    kb.op("dve", lambda E: E.reciprocal(out=rec[c][:, 0:nq], in_=ps[7][:, 0:nq]), reads=[d_ps[7]], writes=[dm[f"rec{c}"]])

                def e_comb():
                    kb.op("dve", lambda E: E.tensor_tensor(out=o0[:, 0:nq], in0=o0[:, 0:nq], in1=rec[0][:, 0:nq], op=ALU.mult), reads=[dm["o0"], dm["rec0"]], writes=[dm["o0"]])
                    kb.op("dve", lambda E: E.tensor_tensor(out=o1[:, 0:nq], in0=o1[:, 0:nq], in1=rec[1][:, 0:nq], op=ALU.mult), reads=[dm["o1"], dm["rec1"]], writes=[dm["o1"]])
                    kb.op("dve", lambda E: E.scalar_tensor_tensor(out=o0[:, 0:nq], in0=o1[:, 0:nq], scalar=nlam[:, 0:1], in1=o0[:, 0:nq], op0=ALU.mult, op1=ALU.add),
                          reads=[dm["o0"], dm["o1"], dm["nlam"]], writes=[dm["o0"]])
                    kb.op("dve", lambda E: E.tensor_tensor(out=osq[:, 0:nq], in0=o0[:, 0:nq], in1=o0[:, 0:nq], op=ALU.mult), reads=[dm["o0"]], writes=[dm["osq"]])

                def e_norm_mm():
                    kb.op("pe", lambda E: E.matmul(ps[7][:, 0:nq], lhsT=ones[:], rhs=osq[:, 0:nq], start=True, stop=True), reads=[dm["ones"], dm["osq"]], writes=[d_ps[7]])
                    kb.op("dve", lambda E: E.tensor_scalar(out=rs[:, 0:nq], in0=ps[7][:, 0:nq], scalar1=1.0 / 128, scalar2=EPS, op0=ALU.mult, op1=ALU.add),
                          reads=[d_ps[7]], writes=[dm["rs"]])

                def e_act():
                    kb.op("act", lambda E: E.activation(out=rs[:, 0:nq], in_=rs[:, 0:nq], func=AF.Ln), reads=[dm["rs"]], writes=[dm["rs"]])
                    kb.op("act", lambda E: E.activation(out=rs[:, 0:nq], in_=rs[:, 0:nq], func=AF.Exp, scale=-0.5), reads=[dm["rs"]], writes=[dm["rs"]])

                def e_fin():
                    kb.op("dve", lambda E: E.scalar_tensor_tensor(out=of[so][:, 0:nq], in0=o0[:, 0:nq], scalar=gs[:, 0:1], in1=rs[:, 0:nq], op0=ALU.mult, op1=ALU.mult),
                          reads=[dm["o0"], dm["gs"], dm["rs"]], writes=[dm[f"of{so}"]])
                    store(f"st_oa{so}", oaT_d[h, :, q0:q0 + nq], of[so][:, 0:nq], [dm[f"of{so}"]])
                return [(3, lambda: e_sel(0)), (6, lambda: e_sel(1)), (9, e_comb), (14, e_norm_mm), (18, e_act), (22, e_fin)]

            def run_deferred(upto):
                while deferred and deferred[0][0] <= upto:
                    deferred.pop(0)[1]()

            load_head(0)
            blk_id = 0
            for h in range(4):
                s = h % 2
                if h + 1 < 4:
                    load_head(h + 1)
                for qb in range(4 if last else 5):
                    if qb < 4:
                        q0 = qb * 512; nq = 512; kts = list(range(NKT))
                    else:
                        q0 = 2048; nq = 256; kts = [64, 65]
                    nk = len(kts)

                    def scores(idx):
                        kt = kts[idx]; sl = idx % 2
                        for c in range(2):
                            kb.op("pe", lambda E, c=c, kt=kt, sl=sl: E.matmul(psS[sl][:, c * 512:c * 512 + nq], lhsT=kTh[s][c * 64:(c + 1) * 64, kt * 128:(kt + 1) * 128],
                                                                           rhs=qTh[s][c * 64:(c + 1) * 64, q0:q0 + nq], start=True, stop=True, tile_position=(64 * c, 0)),
                                  reads=[d_kv[s]], writes=[d_psS[sl]])

                    def exps(idx):
                        sl = idx % 2; p = idx % NPT
                        kb.op("act", lambda E, sl=sl, p=p: E.activation(out=pt[p][:].rearrange("p (c q) -> p c q", c=2)[:, :, 0:nq],
                                                                        in_=psS[sl][:].rearrange("p (c q) -> p c q", c=2)[:, :, 0:nq], func=AF.Exp, scale=0.125),
                              reads=[d_psS[sl]], writes=[d_pt[p]])

                    def av(idx):
                        kt = kts[idx]; p = idx % NPT
                        for c in range(2):
                            kb.op("pe", lambda E, c=c, kt=kt, p=p, idx=idx: E.matmul(ps[4 + c][:, 0:nq], lhsT=vh[s][:, kt, :], rhs=pt[p][:, c * 512:c * 512 + nq],
                                                                                  start=(idx == 0), stop=(idx == nk - 1)),
                                  reads=[d_kv[s], d_pt[p]], writes=[d_ps[4 + c]])

                    def rowsums(idxs):
                        for idx in idxs:
                            p = idx % NPT
                            for c in range(2):
                                g4 = (idx % 2) * 2 + c
                                kb.op("pe", lambda E, c=c, p=p, idx=idx, g4=g4: E.matmul(ps[6][32 * g4:32 * g4 + 32, 0:nq], lhsT=onesb[:], rhs=pt[p][:, c * 512:c * 512 + nq],
                                                                                       start=(idx < 2), stop=(idx >= nk - 2), tile_position=(0, 32 * g4)),
                                      reads=[dm["onesb"], d_pt[p]], writes=[d_ps[6]])

                    scores(0)
                    pend = []

                    def av_rs(i2):
                        av(i2)
                        pend.append(i2)
                        if len(pend) == 2 or i2 == nk - 1:
                            rowsums(list(pend))
                            del pend[:]
                    for idx in range(nk):
                        run_deferred(idx)
                        exps(idx)
                        if idx + 1 < nk:
                            scores(idx + 1)
                        if idx >= 1:
                            av_rs(idx - 1)
                    av_rs(nk - 1)
                    run_deferred(10 ** 9)
                    kb.op("dve", lambda E: E.tensor_copy(out=rsb[:, 0:nq], in_=ps[6][:, 0:nq]), reads=[d_ps[6]], writes=[dm["rsb"]])
                    kb.op("dve", lambda E: E.tensor_copy(out=o0[:, 0:nq], in_=ps[4][:, 0:nq]), reads=[d_ps[4]], writes=[dm["o0"]])
                    kb.op("dve", lambda E: E.tensor_copy(out=o1[:, 0:nq], in_=ps[5][:, 0:nq]), reads=[d_ps[5]], writes=[dm["o1"]])
                    deferred.extend(make_epilogue(h, q0, nq, blk_id % 2))
                    blk_id += 1
            run_deferred(10 ** 9)
            kb.barrier()

    def stage_c(l, last, x_src, x_dst):
        with ExitStack() as es0:
            sb = lambda name, shape, dty: es0.enter_context(nc.sbuf_tensor(U(name), shape, dty))
            ps = [es0.enter_context(nc.psum_tensor(U(f"psC{i}"), [128, 512], F32)) for i in range(8)]
            d_ps = [Dep(f"ps{i}") for i in range(8)]
            wC = sb("wC", [128, 8, 4608], BF16)
            wBR = sb("wBR", [128, 4, 1024], BF16)
            wBRb = sb("wBRb", [64, 4, 1024], BF16)
            wBRc = sb("wBRc", [64, 4, 1024], BF16)
            wO = sb("wO", [128, 8, 1024], BF16)
            wS = sb("wS", [128, 4, 128], BF16)
            LG = sb("LG", [128, 256], F32); LB = sb("LB", [128, 256], F32)
            BS = sb("BS", [64, 4, 128], F32)
            modT = sb("modT_s", [128, 24, 2], F32)
            G = [sb(f"G{i}", [128, 1024], F32) for i in range(2)]
            ident = sb("ident", [128, 128], F32); ones = sb("ones", [128, 128], F32)
            dw = {n: Dep(n) for n in "wC wBR wBRb wBRc wO wS LG LB BS modT G ident ones".split()}
            with ExitStack() as es:
                sbt = lambda name, shape, dty: es.enter_context(nc.sbuf_tensor(U(name), shape, dty))
                wst = [sbt(f"wst{i}", [128, 8, 512], F32) for i in range(2)]
                d_wst = [Dep("wst0"), Dep("wst1")]
                wsf = sbt("wsf", [128, 4, 128], F32); diag = sbt("diag", [128, 128], F32)
                d_wsf = Dep("wsf"); d_diag = Dep("diag")
                make_ident(ident, dw["ident"])
                kb.op("pool", lambda E: E.memset(ones[:], 1.0), writes=[dw["ones"]])
                kb.dma("sp", "c_mod", modT[:], modT_d, writes=[dw["modT"]])
                kb.dma("sp", "c_lg", LG[:], ln_g[l].partition_broadcast(128), writes=[dw["LG"]])
                kb.dma("sp", "c_lb", LB[:], ln_b[l].partition_broadcast(128), writes=[dw["LB"]])
                kb.dma("sp", "c_bs", BS[:], bs_d[l], writes=[dw["BS"]])
                kb.dma("sp", "c_ws", wsf[:], w_sT[l], writes=[d_wsf])
                kb.op("dve", lambda E: E.tensor_copy(out=wS[:], in_=wsf[:]), reads=[d_wsf], writes=[dw["wS"]])
                for var in range(1 if last else 2):
                    for k in range(8):
                        kb.op("dve", lambda E, k=k, var=var: E.tensor_scalar(out=diag[:], in0=ident[:], scalar1=modT[:, 16 + k, var:var + 1], scalar2=None, op0=ALU.mult),
                              reads=[dw["ident"], dw["modT"]], writes=[d_diag])
                        b = k // 4
                        kb.op("pe", lambda E, k=k, b=b: E.matmul(ps[b][:, (k % 4) * 128:(k % 4 + 1) * 128], lhsT=ones[:], rhs=diag[:], start=True, stop=True),
                              reads=[dw["ones"], d_diag], writes=[d_ps[b]])
                    for b in range(2):
                        kb.op("act", lambda E, b=b, var=var: E.activation(out=G[var][:, b * 512:(b + 1) * 512], in_=ps[b][:], func=AF.Copy), reads=[d_ps[b]], writes=[dw["G"]])
                cnt = [0]

                def load_cast(src_v, dst, ddst, np_=128, nk=8):
                    s = cnt[0] % 2; cnt[0] += 1
                    kb.dma("sp", f"c_w{s}", wst[s][0:np_, 0:nk, :], src_v, writes=[d_wst[s]])
                    e = ("dve", "pool", "act")[cnt[0] % 3]
                    if e == "act":
                        kb.op("act", lambda E: E.activation(out=dst, in_=wst[s][0:np_, 0:nk, :], func=AF.Copy), reads=[d_wst[s]], writes=[ddst])
                    else:
                        kb.op(e, lambda E: E.tensor_copy(out=dst, in_=wst[s][0:np_, 0:nk, :]), reads=[d_wst[s]], writes=[ddst])
                w_in_v = w_in[l].rearrange("(k p) n -> p k n", p=128)
                for g in range(9):
                    load_cast(w_in_v[:, :, O_U + g * 512:O_U + (g + 1) * 512], wC[:, :, g * 512:(g + 1) * 512], dw["wC"])
                w_br_v = w_br[l, 0:512, :].rearrange("(k p) n -> p k n", p=128)
                w_brb_v = w_br[l, 512:768, :].rearrange("(g c) n -> c g n", c=64)
                w_brc_v = w_br[l, 768:1024, :].rearrange("(g c) n -> c g n", c=64)
                w_out_v = w_out[l].rearrange("(k p) n -> p k n", p=128)
                for g in range(2):
                    cs_ = slice(g * 512, (g + 1) * 512)
                    load_cast(w_br_v[:, :, cs_], wBR[:, :, cs_], dw["wBR"], nk=4)
                    load_cast(w_brb_v[:, :, cs_], wBRb[:, :, cs_], dw["wBRb"], np_=64, nk=4)
                    load_cast(w_brc_v[:, :, cs_], wBRc[:, :, cs_], dw["wBRc"], np_=64, nk=4)
                    load_cast(w_out_v[:, :, cs_], wO[:, :, cs_], dw["wO"])
                kb.barrier()

            hTb = sb("hTb", [128, 8, BT], BF16)
            oab = sb("oab", [128, 4, BT], F32)
            obb = sb("obb", [64, 4, BT], F32)
            uT = sb("uT", [64, 4, BT], F32)
            gcT = sb("gcT", [64, 4, BT], F32)
            sgt = [sb(f"sgt{i}", [128, BT], F32) for i in range(4)]
            og = sb("og", [128, 4, BT], BF16)
            ogb = sb("ogb", [64, 4, BT], BF16)
            ogc = sb("ogc", [64, 4, BT], BF16)
            st6 = sb("st6", [128, 6], F32); mv = sb("mv", [128, 2], F32); rstd = sb("rstd", [128, 1], F32)
            vcn = sb("vcn", [128, 256], F32); vnb = sb("vnb", [128, 256], BF16)
            sT = sb("sT", [64, 4, 128], F32)
            mm = [sb(f"mm{i}", [128, 3, BT], F32) for i in range(2)]
            t0 = sb("t0", [128, BT], F32); t1 = sb("t1", [128, BT], F32)
            yT = sb("yT", [128, 8, BT], BF16)
            xt = [sb(f"xt{i}", [128, 1024], F32) for i in range(2)]
            tmpo = sb("tmpo", [128, 1024], F32)
            dn = {n: Dep(n) for n in "hTb oab obb uT gcT sgt0 sgt1 sgt2 sgt3 og ogb ogc st6 mv rstd vcn vnb sT mm0 mm1 t0 t1 yT xt0 xt1 tmpo".split()}

            def proj_fm(col0, M, nt, bank):
                for k in range(8):
                    kb.op("pe", lambda E, k=k: E.matmul(ps[bank][0:M, 0:nt], lhsT=wC[:, k, col0:col0 + M], rhs=hTb[:, k, 0:nt], start=(k == 0), stop=(k == 7)),
                          reads=[dw["wC"], dn["hTb"]], writes=[d_ps[bank]])
            pj = [0]

            def next_bank():
                pj[0] += 1
                return 6 + pj[0] % 2
            gj = [0]

            def gate_bank():
                gj[0] += 1
                return (0, 1, 2, 6, 7)[gj[0] % 5]
            GC0 = O_GATE - O_U
            MC0 = O_MERGE - O_U
            for blk in range(8 if last else 9):
                tok0 = blk * BT; nt = BT; var = 0 if tok0 < 2048 else 1
                ntile = nt // 128
                kb.dma("sp", "c_h", hTb[:, :, 0:nt], hT_d[:, :, tok0:tok0 + nt].rearrange("k p t -> p k t"), writes=[dn["hTb"]])
                kb.dma("sp", "c_oa", oab[:, :, 0:nt], oaT_d[:, :, tok0:tok0 + nt].rearrange("h p t -> p h t"), writes=[dn["oab"]])
                kb.dma("sp", "c_ob", obb[:, :, 0:nt], obT_d[:, :, tok0:tok0 + nt].rearrange("g c t -> c g t"), writes=[dn["obb"]])
                for g in range(4):
                    b = gate_bank()
                    proj_fm(g * 64, 64, nt, b)
                    kb.op("act", lambda E, g=g, b=b: E.activation(out=uT[:, g, 0:nt], in_=ps[b][0:64, 0:nt], func=AF.Copy), reads=[d_ps[b]], writes=[dn["uT"]])
                for br in range(2):
                    for g in range(4):
                        b = gate_bank(); s = (br * 4 + g) % 4
                        proj_fm(GC0 + 512 + br * 256 + g * 64, 64, nt, b)
                        kb.op("act", lambda E, b=b, s=s: E.activation(out=sgt[s][0:64, 0:nt], in_=ps[b][0:64, 0:nt], func=AF.Sigmoid), reads=[d_ps[b]], writes=[dn[f"sgt{s}"]])
                        if br == 1:
                            kb.op("dve", lambda E, g=g, b=b, s=s: E.tensor_tensor(out=gcT[:, g, 0:nt], in0=ps[b][0:64, 0:nt], in1=sgt[s][0:64, 0:nt], op=ALU.mult),
                                  reads=[d_ps[b], dn[f"sgt{s}"]], writes=[dn["gcT"]])
                        else:
                            kb.op("dve", lambda E, b=b, s=s: E.tensor_tensor(out=sgt[s][0:64, 0:nt], in0=ps[b][0:64, 0:nt], in1=sgt[s][0:64, 0:nt], op=ALU.mult),
                                  reads=[d_ps[b], dn[f"sgt{s}"]], writes=[dn[f"sgt{s}"]])
                            kb.op("dve", lambda E, g=g, s=s: E.tensor_tensor(out=ogb[:, g, 0:nt], in0=sgt[s][0:64, 0:nt], in1=obb[:, g, 0:nt], op=ALU.mult),
                                  reads=[dn[f"sgt{s}"], dn["obb"]], writes=[dn["ogb"]])
                for j in range(4):
                    b = gate_bank(); s = j % 4
                    proj_fm(GC0 + j * 128, 128, nt, b)
                    kb.op("act", lambda E, b=b, s=s: E.activation(out=sgt[s][:, 0:nt], in_=ps[b][:, 0:nt], func=AF.Sigmoid), reads=[d_ps[b]], writes=[dn[f"sgt{s}"]])
                    kb.op("dve", lambda E, b=b, s=s: E.tensor_tensor(out=sgt[s][:, 0:nt], in0=ps[b][:, 0:nt], in1=sgt[s][:, 0:nt], op=ALU.mult),
                          reads=[d_ps[b], dn[f"sgt{s}"]], writes=[dn[f"sgt{s}"]])
                    kb.op("dve", lambda E, j=j, s=s: E.tensor_tensor(out=og[:, j, 0:nt], in0=sgt[s][:, 0:nt], in1=oab[:, j, 0:nt], op=ALU.mult),
                          reads=[dn[f"sgt{s}"], dn["oab"]], writes=[dn["og"]])
                for t in range(ntile):
                    b = next_bank()
                    for k in range(8):
                        kb.op("pe", lambda E, k=k, t=t, b=b: E.matmul(ps[b][:, 0:256], lhsT=hTb[:, k, t * 128:(t + 1) * 128], rhs=wC[:, k, 256:512], start=(k == 0), stop=(k == 7)),
                              reads=[dw["wC"], dn["hTb"]], writes=[d_ps[b]])
                    kb.op("dve", lambda E, b=b: E.bn_stats(out=st6[:], in_=ps[b][:, 0:256]), reads=[d_ps[b]], writes=[dn["st6"]])
                    kb.op("dve", lambda E: E.bn_aggr(out=mv[:], in_=st6[:]), reads=[dn["st6"]], writes=[dn["mv"]])
                    kb.op("dve", lambda E: E.tensor_scalar(out=rstd[:], in0=mv[:, 1:2], scalar1=EPS, scalar2=None, op0=ALU.add), reads=[dn["mv"]], writes=[dn["rstd"]])
                    kb.op("act", lambda E: E.activation(out=rstd[:], in_=rstd[:], func=AF.Sqrt), reads=[dn["rstd"]], writes=[dn["rstd"]])
                    kb.op("dve", lambda E: E.reciprocal(out=rstd[:], in_=rstd[:]), reads=[dn["rstd"]], writes=[dn["rstd"]])
                    kb.op("dve", lambda E, b=b: E.tensor_scalar(out=vcn[:], in0=ps[b][:, 0:256], scalar1=mv[:, 0:1], scalar2=rstd[:, 0:1], op0=ALU.subtract, op1=ALU.mult),
                          reads=[d_ps[b], dn["mv"], dn["rstd"]], writes=[dn["vcn"]])
                    kb.op("dve", lambda E: E.tensor_tensor(out=vcn[:], in0=vcn[:], in1=LG[:], op=ALU.mult), reads=[dn["vcn"], dw["LG"]], writes=[dn["vcn"]])
                    kb.op("dve", lambda E: E.tensor_tensor(out=vnb[:], in0=vcn[:], in1=LB[:], op=ALU.add), reads=[dn["vcn"], dw["LB"]], writes=[dn["vnb"]])
                    b2 = next_bank()
                    for g in range(4):
                        kb.op("pe", lambda E, g=g, b2=b2: E.matmul(ps[b2][0:64, g * 128:(g + 1) * 128], lhsT=vnb[:, g * 64:(g + 1) * 64], rhs=wS[:, g, :], start=True, stop=True),
                              reads=[dn["vnb"], dw["wS"]], writes=[d_ps[b2]])
                    kb.op("dve", lambda E, b2=b2: E.tensor_tensor(out=sT[:].rearrange("c g p -> c (g p)"), in0=ps[b2][0:64, :], in1=BS[:].rearrange("c g p -> c (g p)"), op=ALU.add),
                          reads=[d_ps[b2], dw["BS"]], writes=[dn["sT"]])
                    kb.op("dve", lambda E, t=t: E.tensor_tensor(out=sT[:], in0=sT[:], in1=uT[:, :, t * 128:(t + 1) * 128], op=ALU.mult), reads=[dn["sT"], dn["uT"]], writes=[dn["sT"]])
                    kb.op("dve", lambda E, t=t: E.tensor_tensor(out=ogc[:, :, t * 128:(t + 1) * 128], in0=sT[:], in1=gcT[:, :, t * 128:(t + 1) * 128], op=ALU.mult),
                          reads=[dn["sT"], dn["gcT"]], writes=[dn["ogc"]])
                for dc in range(8):
                    dsl = slice(dc * 128, (dc + 1) * 128)
                    ms = dc % 2
                    for i in range(3):
                        proj_fm(MC0 + i * 1024 + dc * 128, 128, nt, 3 + i)
                        kb.op("act", lambda E, i=i, ms=ms: E.activation(out=mm[ms][:, i, 0:nt], in_=ps[3 + i][:, 0:nt], func=AF.Sigmoid), reads=[d_ps[3 + i]], writes=[dn[f"mm{ms}"]])
                    for e in range(4):
                        kb.op("pe", lambda E, e=e: E.matmul(ps[0][:, 0:nt], lhsT=wBR[:, e, dsl], rhs=og[:, e, 0:nt], start=(e == 0), stop=(e == 3)),
                              reads=[dw["wBR"], dn["og"]], writes=[d_ps[0]])
                    for g in range(4):
                        kb.op("pe", lambda E, g=g: E.matmul(ps[1][:, 0:nt], lhsT=wBRb[:, g, dsl], rhs=ogb[:, g, 0:nt], start=(g == 0), stop=(g == 3)),
                              reads=[dw["wBRb"], dn["ogb"]], writes=[d_ps[1]])
                    for g in range(4):
                        kb.op("pe", lambda E, g=g: E.matmul(ps[2][:, 0:nt], lhsT=wBRc[:, g, dsl], rhs=ogc[:, g, 0:nt], start=(g == 0), stop=(g == 3)),
                              reads=[dw["wBRc"], dn["ogc"]], writes=[d_ps[2]])
                    kb.op("dve", lambda E, ms=ms: E.tensor_tensor(out=t0[:, 0:nt], in0=ps[0][:, 0:nt], in1=mm[ms][:, 0, 0:nt], op=ALU.mult), reads=[d_ps[0], dn[f"mm{ms}"]], writes=[dn["t0"]])
                    kb.op("dve", lambda E, ms=ms: E.tensor_tensor(out=t1[:, 0:nt], in0=ps[1][:, 0:nt], in1=mm[ms][:, 1, 0:nt], op=ALU.mult), reads=[d_ps[1], dn[f"mm{ms}"]], writes=[dn["t1"]])
                    kb.op("dve", lambda E: E.tensor_tensor(out=t0[:, 0:nt], in0=t0[:, 0:nt], in1=t1[:, 0:nt], op=ALU.add), reads=[dn["t0"], dn["t1"]], writes=[dn["t0"]])
                    kb.op("dve", lambda E, ms=ms: E.tensor_tensor(out=t1[:, 0:nt], in0=ps[2][:, 0:nt], in1=mm[ms][:, 2, 0:nt], op=ALU.mult), reads=[d_ps[2], dn[f"mm{ms}"]], writes=[dn["t1"]])
                    kb.op("dve", lambda E, dc=dc: E.tensor_tensor(out=yT[:, dc, 0:nt], in0=t0[:, 0:nt], in1=t1[:, 0:nt], op=ALU.add), reads=[dn["t0"], dn["t1"]], writes=[dn["yT"]])
                for t in range(ntile):
                    gt = (tok0 // 128) + t
                    s = gt % 2
                    kb.dma("sp", f"c_x{s}", xt[s][:], x_src[gt * 128:(gt + 1) * 128, :], writes=[dn[f"xt{s}"]])
                    for cb in range(2):
                        b = next_bank()
                        for k in range(8):
                            kb.op("pe", lambda E, k=k, cb=cb, b=b, t=t: E.matmul(ps[b][:], lhsT=yT[:, k, t * 128:(t + 1) * 128], rhs=wO[:, k, cb * 512:(cb + 1) * 512],
                                                                              start=(k == 0), stop=(k == 7)),
                                  reads=[dn["yT"], dw["wO"]], writes=[d_ps[b]])
                        kb.op("dve", lambda E, cb=cb, b=b: E.tensor_tensor(out=tmpo[:, cb * 512:(cb + 1) * 512], in0=ps[b][:], in1=G[var][:, cb * 512:(cb + 1) * 512], op=ALU.mult),
                              reads=[d_ps[b], dw["G"]], writes=[dn["tmpo"]])
                    kb.op("dve", lambda E, s=s: E.tensor_tensor(out=xt[s][:], in0=xt[s][:], in1=tmpo[:], op=ALU.add), reads=[dn["tmpo"], dn[f"xt{s}"]], writes=[dn[f"xt{s}"]])
                    store(f"st_x{s}", x_dst[gt * 128:(gt + 1) * 128, :], xt[s][:], [dn[f"xt{s}"]])
            kb.barrier()

    import os
    FS = os.environ.get("FSTOP", "")
    for l in range(1 if FS else 2):
        last = l == 1
        x_src = x_in if l == 0 else x1
        stage_a(l, x_src)
        if FS == "a":
            break
        if FS == "ag":
            kb.barrier()
            break
        stage_b(l, last)
        if FS == "b":
            break
        stage_c(l, last, x_src, y_out if last else x1)
    kb.barrier()
    for k in sorted(out_keys):
        nc.gpsimd.wait_ge(kb.sems[k], kb.cnt[k])
    return kb


import numpy as np
import ml_dtypes
BF = ml_dtypes.bfloat16
NLT = 2048


def rope_tabs():
    n = 8192
    row = np.repeat(np.arange(n // 64), 64).astype(np.float32)
    col = np.tile(np.arange(64), n // 64).astype(np.float32)
    freqs = (10000.0 ** (-np.arange(0, 32, 2, dtype=np.float32) / 32)).astype(np.float32)
    ar = row[:, None] * freqs; ac = col[:, None] * freqs
    ang = np.concatenate([ar, ar, ac, ac], -1)
    cos = np.cos(ang).astype(np.float32); sin = np.sin(ang).astype(np.float32)
    sgn = np.tile(np.concatenate([-np.ones(16), np.ones(16)]), 2).astype(np.float32)
    return cos, sin * sgn


def col_layout(v, k):
    return np.ascontiguousarray(v.reshape(k, 128).T)


def fourier_consts(j):
    t = np.arange(64)[:, None]; kb = np.arange(64)[None, :]
    a = 2 * np.pi * ((t * kb) % 64) / 64
    cs64 = np.concatenate([np.cos(a), -np.sin(a)], 1).astype(BF)
    p = np.arange(128)[:, None, None]; kbb = np.arange(64)[None, :, None]; ka = (32 * j + np.arange(32))[None, None, :]
    a = 2 * np.pi * ((p * (64 * ka + kbb)) % 8192) / 8192
    T1 = np.concatenate([np.cos(a), -np.sin(a)], 2).astype(BF)
    T2 = np.concatenate([np.sin(a), np.cos(a)], 2).astype(BF)
    n = np.arange(128)[:, None, None]; ch = np.arange(2)[None, :, None]; k = np.arange(256)[None, None, :]
    a = 2 * np.pi * (((ch * 128 + n) * k) % 256) / 256
    dctx = np.concatenate([np.cos(a), -np.sin(a)], 2).astype(BF)
    c1 = np.arange(64)[:, None]; c2 = np.arange(64)[None, :]
    a = 2 * np.pi * ((c1 * c2) % 64) / 64
    ccs = np.stack([np.cos(a), np.sin(a)], 1).astype(np.float32)
    return dict(cs64=cs64, T1j=np.ascontiguousarray(T1), T2j=np.ascontiguousarray(T2), dctx=np.ascontiguousarray(dctx), ccs=np.ascontiguousarray(ccs))


def fused_inputs(I):
    cos, ssin = rope_tabs()
    C = np.ascontiguousarray
    shared = {
        "w_ada": C(I['w_ada']), "b_ada": C(np.stack([col_layout(I['b_ada'][l], 24) for l in range(2)])),
        "g_norm": C(np.stack([col_layout(I['g_norm'][l], 8) for l in range(2)])),
        "w_in": C(I['w_in']), "g_q": C(I['g_q']), "g_k": C(I['g_k']),
        "lamv": C(np.concatenate([I['lam_q1'], I['lam_k1'], I['lam_q2'], I['lam_k2']], 1)),
        "g_sub": C(I['g_sub'].reshape(2, 128, 1)),
        "w_f": C(I['w_f']), "b_f": C(I['b_f'].transpose(0, 2, 1)),
        "ln_g": C(I['ln_g']), "ln_b": C(I['ln_b']),
        "w_sT": C(I['w_s'].transpose(0, 3, 1, 2)),
        "bs64": C(np.broadcast_to(I['b_s'][:, None, :, :], (2, 64, 4, 128))),
        "w_br": C(np.concatenate([I['w_br_a'], I['w_br_b'], I['w_br_c']], 1)), "w_out": C(I['w_out']),
    }
    maps = []
    for core in range(8):
        b, j = core // 4, core % 4
        m = dict(shared)
        m["x_tok"] = C(np.concatenate([I['x'][b, j * NLT:(j + 1) * NLT], I['ctx'][b]], 0))
        m["cvec"] = C(np.stack([col_layout(I['c'][b], 8), col_layout(I['c_ctx'], 8)], -1))
        m["cos"] = C(cos[j * NLT:(j + 1) * NLT]); m["ssin"] = C(ssin[j * NLT:(j + 1) * NLT])
        m.update(fourier_consts(j))
        maps.append(m)
    return maps


def kernel(**inputs):
    I = {k: np.asarray(v, dtype=np.float32) for k, v in inputs.items()}
    kb = KB()
    build_fused(kb)
    res = kb.run(fused_inputs(I))
    out = np.empty((2, 8192, 1024), np.float32)
    for core in range(8):
        b, j = core // 4, core % 4
        out[b, j * NLT:(j + 1) * NLT] = np.asarray(res.results[core]["y"])
    return out
```

```python
import numpy as np
import concourse.bass as bass
import concourse.mybir as mybir
from concourse.bass_utils import run_bass_kernel_spmd

F32 = mybir.dt.float32
BF16 = mybir.dt.bfloat16
AF = mybir.ActivationFunctionType
ALU = mybir.AluOpType
AX = mybir.AxisListType


class Dep:
    __slots__ = ("w", "r", "name")

    def __init__(self, name=""):
        self.w = None
        self.r = []
        self.name = name


class KB:
    COMPUTE = ("pe", "act", "dve", "pool")

    def __init__(self):
        self.nc = bass.Bass("TRN2", target_bir_lowering=False)
        nc = self.nc
        self.eng = {"pe": nc.tensor, "act": nc.scalar, "dve": nc.vector,
                    "pool": nc.gpsimd, "sp": nc.sync}
        self.sems = {}
        self.cnt = {}
        for e in self.COMPUTE:
            self.sems[e] = nc.alloc_semaphore(name="s_" + e)
            self.cnt[e] = 0
        self.seen = {e: {} for e in self.eng}
        self.n_inst = 0

    def _sem(self, key):
        if key not in self.sems:
            self.sems[key] = self.nc.alloc_semaphore(name="d_" + str(key))
            self.cnt[key] = 0
        return self.sems[key]

    def _waits(self, e, reads, writes):
        need = {}

        def add(t, war=False):
            if t is None:
                return
            sk, v = t
            if sk == e and (war or e == "pe"):
                return
            if need.get(sk, 0) < v:
                need[sk] = v
        for d in reads:
            add(d.w)
        for d in writes:
            add(d.w)
            for t in d.r:
                add(t, war=True)
        E = self.eng[e]
        for sk, v in need.items():
            if self.seen[e].get(sk, 0) >= v:
                continue
            E.wait_ge(self.sems[sk], v)
            self.seen[e][sk] = v

    def _mark(self, tok, reads, writes):
        for d in reads:
            d.r.append(tok)
            if len(d.r) > 64:
                m = {}
                for sk, v in d.r:
                    if m.get(sk, 0) < v:
                        m[sk] = v
                d.r = list(m.items())
        for d in writes:
            d.w = tok
            d.r = []

    def op(self, e, fn, reads=(), writes=()):
        self._waits(e, reads, writes)
        inst = fn(self.eng[e])
        self.cnt[e] += 1
        inst.then_inc(self.sems[e], 1)
        self._mark((e, self.cnt[e]), reads, writes)
        self.n_inst += 1
        return inst

    def mm(self, fn, reads=(), writes=(), last=True):
        return self.op("pe", fn, reads, writes)

    def dma(self, q, key, out, in_, reads=(), writes=(), **kw):
        sem = self._sem(key)
        self._waits(q, reads, writes)
        inst = self.eng[q].dma_start(out=out, in_=in_, **kw)
        self.cnt[key] += 16
        inst.then_inc(sem, 16)
        self._mark((key, self.cnt[key]), reads, writes)
        self.n_inst += 1
        return inst

    def barrier(self, skip=()):
        for e, E in self.eng.items():
            for sk, sem in self.sems.items():
                if sk in skip:
                    continue
                v = self.cnt[sk]
                if v == 0 or self.seen[e].get(sk, 0) >= v:
                    continue
                E.wait_ge(sem, v)
                self.seen[e][sk] = v

    def wait_all(self, e, deps):
        self._waits(e, deps, ())

    def run(self, in_maps, n=8, trace=False):
        return run_bass_kernel_spmd(self.nc, in_maps, core_ids=list(range(n)), trace=trace)


import math
from contextlib import ExitStack

NT_A = 18
NLAT_T = 16
TOKS = NT_A * 128
NKT = 66
NKEY = NKT * 128
EPS = 1e-6
BT = 256
O_Q, O_K, O_V, O_F, O_U, O_VC, O_GATE, O_MERGE = 0, 512, 1024, 1536, 1792, 2048, 2304, 3328
RG = [[0, 1, 2, 3], [4, 5, 6, 7]]


def build_fused(kb):
    nc = kb.nc
    EI = lambda name, shape, d=F32: nc.dram_tensor(name, shape, d, kind="ExternalInput").ap()
    IN = lambda name, shape, d=F32: nc.dram_tensor(name, shape, d, kind="Internal").ap()
    x_in = EI("x_tok", [TOKS, 1024])
    cvec = EI("cvec", [128, 8, 2])
    w_ada = EI("w_ada", [2, 1024, 3072]); b_ada = EI("b_ada", [2, 128, 24]); g_norm = EI("g_norm", [2, 128, 8])
    w_in = EI("w_in", [2, 1024, 6400])
    g_q = EI("g_q", [2, 64]); g_k = EI("g_k", [2, 64])
    cos = EI("cos", [2048, 64]); ssin = EI("ssin", [2048, 64])
    lamv = EI("lamv", [2, 256]); g_sub = EI("g_sub", [2, 128, 1])
    w_f = EI("w_f", [2, 4, 64, 64]); b_f = EI("b_f", [2, 64, 4])
    cs64 = EI("cs64", [64, 128], BF16); T1d = EI("T1j", [128, 64, 64], BF16); T2d = EI("T2j", [128, 64, 64], BF16)
    dcd = EI("dctx", [128, 2, 512], BF16); ccs = EI("ccs", [64, 2, 64])
    ln_g = EI("ln_g", [2, 256]); ln_b = EI("ln_b", [2, 256])
    w_sT = EI("w_sT", [2, 128, 4, 128]); bs_d = EI("bs64", [2, 64, 4, 128])
    w_br = EI("w_br", [2, 1024, 1024]); w_out = EI("w_out", [2, 1024, 1024])
    y_out = nc.dram_tensor("y", [2048, 1024], F32, kind="ExternalOutput").ap()
    x1 = IN("x1", [TOKS, 1024])
    hT_d = IN("hT_d", [8, 128, TOKS], BF16)
    qT_d = IN("qT_d", [4, 128, TOKS], BF16)
    kT_lat = [IN(f"kT_lat{i}", [256, 2048], BF16) for i in range(2)]
    kT_ag = [IN(f"kT_ag{i}", [1024, 2048], BF16) for i in range(2)]
    kT_ctx = IN("kT_ctx", [4, 128, 256], BF16)
    v_lat = [IN(f"v_lat{i}", [1024, 512], BF16) for i in range(2)]
    v_ag = [IN(f"v_ag{i}", [4096, 512], BF16) for i in range(2)]
    v_ctx = IN("v_ctx", [256, 512], BF16)
    f_lat = IN("f_lat", [4 * 2048, 64], BF16)
    f_ag = IN("f_ag", [16 * 2048, 64], BF16)
    f_ctx = IN("f_ctx", [256, 256], BF16)
    oaT_d = IN("oaT_d", [4, 128, TOKS])
    obT_d = IN("obT_d", [4, 64, TOKS])
    modT_d = IN("modT_d", [128, 24, 2])
    wC_bf = IN("wC_bf", [128, 8, 4608], BF16)
    wBR_bf = IN("wBR_bf", [128, 4, 1024], BF16)
    wBRb_bf = IN("wBRb_bf", [64, 4, 1024], BF16)
    wBRc_bf = IN("wBRc_bf", [64, 4, 1024], BF16)
    wO_bf = IN("wO_bf", [128, 8, 1024], BF16)
    out_keys = set()
    uid = [0]

    def U(name):
        uid[0] += 1
        return f"{name}_{uid[0]}"

    def store(key, out, in_, reads):
        out_keys.add(key)
        kb.dma("pool", key, out, in_, reads=reads, writes=[])

    d_fag = Dep("f_ag"); d_kvag = Dep("kv_ag")

    def collective(kind, src, dst, dep):
        sem = kb._sem("cc")
        inst = nc.gpsimd.collective_compute(kind, ALU.bypass, replica_groups=RG, ins=[src], outs=[dst])
        inst.then_inc(sem, 1)
        kb.cnt["cc"] += 1
        dep.w = ("cc", kb.cnt["cc"])

    def rstd_chain(ssrc, dsrc, dst, ddst, scale, epsb=None):
        if epsb is not None:
            kb.op("act", lambda E: E.activation(out=dst, in_=ssrc, func=AF.Ln, scale=scale, bias=epsb[0]), reads=[dsrc, epsb[1]], writes=[ddst])
            kb.op("act", lambda E: E.activation(out=dst, in_=dst, func=AF.Exp, scale=-0.5), reads=[ddst], writes=[ddst])
            return
        kb.op("dve", lambda E: E.tensor_scalar(out=dst, in0=ssrc, scalar1=scale, scalar2=EPS, op0=ALU.mult, op1=ALU.add), reads=[dsrc], writes=[ddst])
        kb.op("act", lambda E: E.activation(out=dst, in_=dst, func=AF.Sqrt), reads=[ddst], writes=[ddst])
        kb.op("dve", lambda E: E.reciprocal(out=dst, in_=dst), reads=[ddst], writes=[ddst])

    def make_ident(ident, dep):
        kb.op("pool", lambda E: E.memset(ident[:], 0.0), writes=[dep])
        kb.op("pool", lambda E: E.affine_select(out=ident[:], in_=ident[:], pattern=[[-1, 128]], compare_op=ALU.not_equal,
                                                fill=1.0, base=0, channel_multiplier=1), reads=[dep], writes=[dep])

    def stage_a(l, x_src):
        with ExitStack() as es:
            sb = lambda name, shape, dty: es.enter_context(nc.sbuf_tensor(U(name), shape, dty))
            ps = [es.enter_context(nc.psum_tensor(U(f"psA{i}"), [128, 512], F32)) for i in range(6)]
            ps += [es.enter_context(nc.psum_tensor(U(f"psA{i}"), [128, 1024], BF16)) for i in (6, 7)]
            ident = sb("ident", [128, 128], F32); identb = sb("identb", [128, 128], BF16)
            cT = sb("cT", [128, 8, 2], F32); sg = sb("sg", [128, 8, 2], F32); sc = sb("sc", [128, 8, 2], F32)
            bT = sb("bT", [128, 24], F32); gn = sb("gn", [128, 8], F32)
            modT = sb("modT_s", [128, 24, 2], F32); Aff = sb("Aff", [128, 8, 2], F32)
            wst = [sb(f"wst{i}", [128, 8, 512], F32) for i in range(2)]
            wA = sb("wA", [128, 8, 1792], BF16)
            GQ = sb("GQ", [128, 64], F32); GK = sb("GK", [128, 64], F32)
            xt = [sb(f"xt{i}", [128, 1024], F32) for i in range(2)]
            junk = sb("junk", [128, 1024], F32)
            ss = sb("ss", [128, 1], F32); rstd = sb("rstd", [128, 1], F32)
            hTt = [sb(f"hTt{i}", [128, 8, 128], BF16) for i in range(2)]
            cs = [sb(f"cs{i}", [128, 2, 64], F32) for i in range(2)]
            sq = sb("sq", [128, 512], F32); ss8 = sb("ss8", [128, 8], F32)
            qn = sb("qn", [128, 512], F32); t1 = sb("t1", [128, 512], F32); t2 = sb("t2", [128, 512], F32)
            qr = [sb(f"qr{i}", [128, 512], BF16) for i in range(2)]
            qTt = [sb(f"qTt{i}", [128, 4, 128], BF16) for i in range(2)]
            kTt = [sb(f"kTt{i}", [128, 4, 128], BF16) for i in range(2)]
            vt = [sb(f"vt{i}", [128, 512], BF16) for i in range(2)]
            ft = [sb(f"ft{i}", [128, 256], BF16) for i in range(2)]
            D = lambda n: Dep(n)
            d_ident, d_identb, d_cT, d_sg, d_sc, d_bT, d_gn, d_modT, d_Aff = [D(n) for n in "ident identb cT sg sc bT gn modT Aff".split()]
            d_wst = [D("wst0"), D("wst1")]; d_wA = D("wA"); d_G = D("G")
            d_xt = [D("xt0"), D("xt1")]; d_junk = D("junk"); d_ss = D("ss"); d_rstd = D("rstd")
            d_hTt = [D("hTt0"), D("hTt1")]; d_cs = [D("cs0"), D("cs1")]
            d_sq, d_ss8, d_qn, d_t1, d_t2 = D("sq"), D("ss8"), D("qn"), D("t1"), D("t2")
            d_qr = [D("qr0"), D("qr1")]; d_qTt = [D("qTt0"), D("qTt1")]; d_kTt = [D("kTt0"), D("kTt1")]
            d_vt = [D("vt0"), D("vt1")]; d_ft = [D("ft0"), D("ft1")]
            d_ps = [D(f"ps{i}") for i in range(8)]

            make_ident(ident, d_ident)
            kb.op("pool", lambda E: E.tensor_copy(out=identb[:], in_=ident[:]), reads=[d_ident], writes=[d_identb])
            epst = sb("epst", [128, 1], F32); d_eps = D("eps")
            kb.op("pool", lambda E: E.memset(epst[:], EPS), writes=[d_eps])
            EPSB = (epst[:, 0:1], d_eps)
            kb.dma("sp", "ld_c0", cT[:], cvec, writes=[d_cT])
            kb.dma("sp", "ld_c1", bT[:], b_ada[l], writes=[d_bT])
            kb.dma("sp", "ld_c2", gn[:], g_norm[l], writes=[d_gn])
            kb.dma("sp", "ld_c3", GQ[:], g_q[l].partition_broadcast(128), writes=[d_G])
            kb.dma("sp", "ld_c3", GK[:], g_k[l].partition_broadcast(128), writes=[d_G])
            kb.op("act", lambda E: E.activation(out=sg[:], in_=cT[:], func=AF.Sigmoid), reads=[d_cT], writes=[d_sg])
            kb.op("dve", lambda E: E.tensor_tensor(out=sc[:], in0=cT[:], in1=sg[:], op=ALU.mult), reads=[d_cT, d_sg], writes=[d_sc])
            w_ada_v = w_ada[l].rearrange("(k p) n -> p k n", p=128)
            for g in range(6):
                s = g % 2
                kb.dma("sp", f"ld_w{s}", wst[s][:], w_ada_v[:, :, g * 512:(g + 1) * 512], writes=[d_wst[s]])
                for jj in range(4):
                    j = g * 4 + jj
                    for k in range(8):
                        kb.op("pe", lambda E, k=k, jj=jj, s=s, j=j: E.matmul(ps[0][:, 2 * j:2 * j + 2], lhsT=wst[s][:, k, jj * 128:(jj + 1) * 128],
                                                                            rhs=sc[:, k, :], start=(k == 0), stop=(k == 7)),
                              reads=[d_wst[s], d_sc], writes=[d_ps[0]])
            kb.op("dve", lambda E: E.tensor_tensor(out=modT[:], in0=ps[0][:, 0:48].rearrange("p (j n) -> p j n", n=2),
                                                   in1=bT[:].unsqueeze(2).to_broadcast([128, 24, 2]), op=ALU.add),
                  reads=[d_ps[0], d_bT], writes=[d_modT])
            store("st_mod", modT_d, modT[:], [d_modT])
            kb.op("dve", lambda E: E.tensor_scalar(out=Aff[:], in0=modT[:, 8:16, :], scalar1=1.0, scalar2=None, op0=ALU.add), reads=[d_modT], writes=[d_Aff])
            kb.op("dve", lambda E: E.tensor_tensor(out=Aff[:], in0=Aff[:], in1=gn[:].unsqueeze(2).to_broadcast([128, 8, 2]), op=ALU.mult),
                  reads=[d_Aff, d_gn], writes=[d_Aff])
            w_in_v = w_in[l].rearrange("(k p) n -> p k n", p=128)
            for g in range(4):
                s = g % 2
                n = 512 if g < 3 else 256
                kb.dma("sp", f"ld_w{s}", wst[s][:, :, 0:n], w_in_v[:, :, g * 512:g * 512 + n], writes=[d_wst[s]])
                e = "pool" if g % 2 == 0 else "dve"
                kb.op(e, lambda E, s=s, n=n, g=g: E.tensor_copy(out=wA[:, :, g * 512:g * 512 + n], in_=wst[s][:, :, 0:n]), reads=[d_wst[s]], writes=[d_wA])

            def load_tile(i):
                s = i % 2
                kb.dma("sp", f"ld_x{s}", xt[s][:], x_src[i * 128:(i + 1) * 128, :], writes=[d_xt[s]])
                if i < NLAT_T:
                    kb.dma("sp", f"ld_cs{s}", cs[s][:, 0, :], cos[i * 128:(i + 1) * 128, :], writes=[d_cs[s]])
                    kb.dma("sp", f"ld_cs{s}", cs[s][:, 1, :], ssin[i * 128:(i + 1) * 128, :], writes=[d_cs[s]])

            G2 = sb("G2", [128, 2, 64], F32)
            sq2 = sb("sq2", [128, 1024], F32); ss16 = sb("ss16", [128, 16], F32)
            qn2 = sb("qn2", [128, 1024], F32); t1b = sb("t1b", [128, 1024], F32); t2b = sb("t2b", [128, 1024], F32)
            QR2 = sb("QR2", [128, 1024], BF16)
            TT2 = [sb(f"TT2_{i}", [128, 8, 128], BF16) for i in range(2)]
            d_G2, d_sq2, d_ss16, d_qn2, d_t1b, d_t2b, d_QR2 = [D(n) for n in "G2 sq2 ss16 qn2 t1b t2b QR2".split()]
            d_TT2 = [D("TT2_0"), D("TT2_1")]
            kb.op("dve", lambda E: E.tensor_copy(out=G2[:, 0, :], in_=GQ[:]), reads=[d_G], writes=[d_G2])
            kb.op("dve", lambda E: E.tensor_copy(out=G2[:, 1, :], in_=GK[:]), reads=[d_G], writes=[d_G2])
            psQK = [ps[2], ps[3]]
            g64 = lambda ap: ap.rearrange("p (g c) -> p g c", c=64)

            def head(i):
                s = i % 2
                var = 0 if i < NLAT_T else 1
                load_tile(i)
                X = xt[s]
                kb.op("act", lambda E: E.activation(out=junk[:], in_=X[:], func=AF.Square, accum_out=ss[:]), reads=[d_xt[s]], writes=[d_junk, d_ss])
                rstd_chain(ss[:], d_ss, rstd[:], d_rstd, 1.0 / 1024, EPSB)
                kb.op("dve", lambda E: E.tensor_scalar(out=X[:], in0=X[:], scalar1=rstd[:, 0:1], scalar2=None, op0=ALU.mult),
                      reads=[d_xt[s], d_rstd], writes=[d_xt[s]])
                for k in range(8):
                    b = k // 4
                    kb.op("pe", lambda E, k=k, b=b: E.transpose(out=ps[b][:, (k % 4) * 128:(k % 4 + 1) * 128], in_=X[:, k * 128:(k + 1) * 128], identity=ident[:]),
                          reads=[d_xt[s], d_ident], writes=[d_ps[b]])
                for k in range(8):
                    b = k // 4
                    src = ps[b][:, (k % 4) * 128:(k % 4 + 1) * 128]
                    if k % 2 == 0:
                        kb.op("act", lambda E, k=k, src=src: E.activation(out=hTt[s][:, k, :], in_=src, func=AF.Identity,
                                                                          scale=Aff[:, k, var:var + 1], bias=modT[:, k, var:var + 1]),
                              reads=[d_ps[b], d_Aff, d_modT], writes=[d_hTt[s]])
                    else:
                        kb.op("dve", lambda E, k=k, src=src: E.tensor_scalar(out=hTt[s][:, k, :], in0=src, scalar1=Aff[:, k, var:var + 1],
                                                                             scalar2=modT[:, k, var:var + 1], op0=ALU.mult, op1=ALU.add),
                              reads=[d_ps[b], d_Aff, d_modT], writes=[d_hTt[s]])
                store(f"st_h{s}", hT_d[:, :, i * 128:(i + 1) * 128].rearrange("k p t -> p k t"), hTt[s][:], [d_hTt[s]])

            def mid(i):
                s = i % 2
                for cb in range(4):
                    n = 512 if cb < 3 else 256
                    for k in range(8):
                        kb.op("pe", lambda E, k=k, cb=cb, n=n: E.matmul(ps[2 + cb][:, 0:n], lhsT=hTt[s][:, k, :], rhs=wA[:, k, cb * 512:cb * 512 + n],
                                                                       start=(k == 0), stop=(k == 7)),
                              reads=[d_hTt[s], d_wA], writes=[d_ps[2 + cb]])

            def tail_a(i):
                s = i % 2
                lat = i < NLAT_T
                for w in range(2):
                    kb.op("act", lambda E, w=w: E.activation(out=sq2[:, w * 512:(w + 1) * 512], in_=psQK[w][:], func=AF.Square), reads=[d_ps[2 + w]], writes=[d_sq2])
                kb.op("dve", lambda E: E.tensor_reduce(out=ss16[:], in_=g64(sq2[:]), axis=AX.X, op=ALU.add), reads=[d_sq2], writes=[d_ss16])
                rstd_chain(ss16[:], d_ss16, ss16[:], d_ss16, 1.0 / 64, EPSB)
                for w in range(2):
                    kb.op("dve", lambda E, w=w: E.tensor_tensor(out=g64(qn2[:, w * 512:(w + 1) * 512]), in0=g64(psQK[w][:]),
                                                           in1=ss16[:, w * 8:(w + 1) * 8].unsqueeze(2).to_broadcast([128, 8, 64]), op=ALU.mult),
                          reads=[d_ps[2 + w], d_ss16], writes=[d_qn2])
                kb.op("act", lambda E: E.activation(out=vt[s][:], in_=ps[4][:], func=AF.Copy), reads=[d_ps[4]], writes=[d_vt[s]])
                kb.op("act", lambda E: E.activation(out=ft[s][:], in_=ps[5][:, 0:256], func=AF.Copy), reads=[d_ps[5]], writes=[d_ft[s]])
                if lat:
                    store(f"st_v{s}", v_lat[i // 8][(i % 8) * 128:(i % 8 + 1) * 128, :], vt[s][:], [d_vt[s]])
                    store(f"st_f{s}", f_lat.rearrange("(g t) c -> t g c", g=4)[i * 128:(i + 1) * 128, :, :], ft[s][:].rearrange("p (g c) -> p g c", c=64), [d_ft[s]])
                else:
                    ic = i - NLAT_T
                    store(f"st_v{s}", v_ctx[ic * 128:(ic + 1) * 128, :], vt[s][:], [d_vt[s]])
                    store(f"st_f{s}", f_ctx[ic * 128:(ic + 1) * 128, :], ft[s][:], [d_ft[s]])

            def tail_b(i):
                s = i % 2
                lat = i < NLAT_T
                qv4 = qn2[:].rearrange("p (w g c) -> p w g c", w=2, c=64)
                G2b = G2[:].unsqueeze(2).to_broadcast([128, 2, 8, 64])
                if lat:
                    for w in range(2):
                        kb.op("dve", lambda E, w=w: E.tensor_tensor(out=qv4[:, w], in0=qv4[:, w], in1=G2[:, w, :].unsqueeze(1).to_broadcast([128, 8, 64]), op=ALU.mult),
                              reads=[d_qn2, d_G2], writes=[d_qn2])
                    kb.op("dve", lambda E: E.tensor_tensor(out=g64(t1b[:]), in0=g64(qn2[:]), in1=cs[s][:, 0, :].unsqueeze(1).to_broadcast([128, 16, 64]), op=ALU.mult),
                          reads=[d_qn2, d_cs[s]], writes=[d_t1b])
                    qv = qn2[:].rearrange("p (g a h c) -> p g a h c", a=2, h=2, c=16)
                    tv = t2b[:].rearrange("p (g a h c) -> p g a h c", a=2, h=2, c=16)
                    sv = cs[s][:, 1, :].rearrange("p (a h c) -> p a h c", a=2, h=2)
                    for hf in range(2):
                        for a in range(2):
                            kb.op("dve", lambda E, hf=hf, a=a: E.tensor_tensor(out=tv[:, :, a, hf, :], in0=qv[:, :, a, 1 - hf, :],
                                                                                in1=sv[:, a, hf, :].unsqueeze(1).to_broadcast([128, 16, 16]), op=ALU.mult),
                                  reads=[d_qn2, d_cs[s]], writes=[d_t2b])
                    kb.op("dve", lambda E: E.tensor_tensor(out=QR2[:], in0=t1b[:], in1=t2b[:], op=ALU.add), reads=[d_t1b, d_t2b], writes=[d_QR2])
                else:
                    QRv = QR2[:].rearrange("p (w g c) -> p w g c", w=2, c=64)
                    for w in range(2):
                        kb.op("dve", lambda E, w=w: E.tensor_tensor(out=QRv[:, w], in0=qv4[:, w], in1=G2[:, w, :].unsqueeze(1).to_broadcast([128, 8, 64]), op=ALU.mult),
                              reads=[d_qn2, d_G2], writes=[d_QR2])
                PT = ps[6]; dPT = d_ps[6]
                for hh in range(8):
                    kb.op("pe", lambda E, hh=hh: E.transpose(out=PT[:, hh * 128:(hh + 1) * 128], in_=QR2[:, hh * 128:(hh + 1) * 128], identity=identb[:]),
                          reads=[d_QR2, d_identb], writes=[dPT])
                TT = TT2[s]; dTT = d_TT2[s]
                kb.op("act", lambda E: E.activation(out=TT[:].rearrange("p h t -> p (h t)"), in_=PT[:, 0:1024], func=AF.Copy), reads=[dPT], writes=[dTT])
                store(f"st_q{s}", qT_d[:, :, i * 128:(i + 1) * 128].rearrange("h p t -> p h t"), TT[:, 0:4, :], [dTT])
                if lat:
                    for hp in range(2):
                        store(f"st_k{s}", kT_lat[hp].rearrange("(h p) t -> p h t", p=128)[:, :, i * 128:(i + 1) * 128], TT[:, 4 + hp * 2:4 + hp * 2 + 2, :], [dTT])
                else:
                    store(f"st_k{s}", kT_ctx[:, :, (i - NLAT_T) * 128:(i - NLAT_T + 1) * 128].rearrange("h p t -> p h t"), TT[:, 4:8, :], [dTT])

            wcb = [sb(f"wcb{i}", [128, 8, 512], BF16) for i in range(2)]
            d_wcb = [D("wcb0"), D("wcb1")]
            w_in_v2 = w_in[l].rearrange("(k p) n -> p k n", p=128)
            pieces = []
            for g in range(9):
                pieces.append((w_in_v2[:, :, O_U + g * 512:O_U + (g + 1) * 512], 128, 8, wC_bf[:, :, g * 512:(g + 1) * 512]))
            for g in range(2):
                cs_ = slice(g * 512, (g + 1) * 512)
                pieces.append((w_br[l, 0:512, :].rearrange("(k p) n -> p k n", p=128)[:, :, cs_], 128, 4, wBR_bf[:, :, cs_]))
                pieces.append((w_br[l, 512:768, :].rearrange("(g c) n -> c g n", c=64)[:, :, cs_], 64, 4, wBRb_bf[:, :, cs_]))
                pieces.append((w_br[l, 768:1024, :].rearrange("(g c) n -> c g n", c=64)[:, :, cs_], 64, 4, wBRc_bf[:, :, cs_]))
                pieces.append((w_out[l].rearrange("(k p) n -> p k n", p=128)[:, :, cs_], 128, 8, wO_bf[:, :, cs_]))

            def side_load(j):
                if j >= len(pieces):
                    return
                src, np_, nk, dst = pieces[j]
                sj = j % 2
                kb.dma("pool", f"ld_sj{sj}", wst[sj][0:np_, 0:nk, :], src, writes=[d_wst[sj]])

            def side_job(j):
                side_load(j + 1)
                if j >= len(pieces):
                    return
                src, np_, nk, dst = pieces[j]
                sj = j % 2
                kb.op("act", lambda E: E.activation(out=wcb[sj][0:np_, 0:nk, :], in_=wst[sj][0:np_, 0:nk, :], func=AF.Copy), reads=[d_wst[sj]], writes=[d_wcb[sj]])
                store(f"st_wc{sj}", dst, wcb[sj][0:np_, 0:nk, :], [d_wcb[sj]])

            side_load(0)
            head(0)
            mid(0)
            head(1)
            for i in range(NT_A):
                side_job(i)
                tail_a(i)
                if i + 1 < NT_A:
                    mid(i + 1)
                tail_b(i)
                if i == NLAT_T - 1:
                    for key in ("st_k0", "st_k1", "st_v0", "st_v1", "st_f0", "st_f1"):
                        nc.gpsimd.wait_ge(kb.sems[key], kb.cnt[key])
                        kb.seen["pool"][key] = kb.cnt[key]
                    collective("AllGather", f_lat, f_ag, d_fag)
                    for ci in range(2):
                        collective("AllGather", kT_lat[ci], kT_ag[ci], d_kvag)
                        collective("AllGather", v_lat[ci], v_ag[ci], d_kvag)
                if i + 2 < NT_A:
                    head(i + 2)
            kb.barrier(skip=("cc",))

    def stage_b(l, last):
        lam_init = 0.8 - 0.6 * math.exp(-0.3 * l)
        with ExitStack() as es:
            sbt = lambda name, shape, dty: es.enter_context(nc.sbuf_tensor(U(name), shape, dty))
            ps = [es.enter_context(nc.psum_tensor(U(f"psF{i}"), [128, 512], F32)) for i in range(8)]
            d_ps = [Dep(f"ps{i}") for i in range(8)]
            X1 = [sbt(f"X1_{i}", [64, 8192], BF16) for i in range(2)]
            Xc = sbt("Xc", [128, 2, 256], BF16)
            CS = sbt("CS", [64, 128], BF16)
            T1 = sbt("T1s", [128, 64, 64], BF16)
            T2 = sbt("T2s", [128, 64, 64], BF16)
            DC = sbt("DC", [128, 2, 512], BF16)
            Bsb2 = [sbt(f"Bsb{i}", [128, 64, 2, 64], BF16) for i in range(2)]
            YT2 = [sbt(f"YT{i}", [64, 64, 2, 32], BF16) for i in range(2)]
            YC = sbt("YC", [64, 2, 256], BF16)
            CC = sbt("CC", [64, 2, 64], F32)
            WF = sbt("WF", [64, 4, 64], F32)
            BF_ = sbt("BF", [64, 4], F32)
            MC = sbt("MC", [64, 4, 2, 64], BF16)
            ob = [sbt(f"ob{i}", [64, 512], F32) for i in range(2)]
            dn = {n: Dep(n) for n in "X1_0 X1_1 Xc CS T1 T2 DC Bsb0 Bsb1 YT0 YT1 YC CC WF BF MC ob0 ob1".split()}
            f_ag_v = f_ag.rearrange("(r g t p) c -> r g t (p c)", r=4, g=4, p=128)

            def load_x1(g):
                s = g % 2
                for r in range(4):
                    kb.dma("sp", f"f_x1{s}", X1[s][r * 16:(r + 1) * 16, :], f_ag_v[r, g], reads=[d_fag], writes=[dn[f"X1_{s}"]])
            load_x1(0)
            if not last:
                kb.dma("sp", "f_xc", Xc[:], f_ctx.rearrange("(k p) c -> p k c", p=128), writes=[dn["Xc"]])
                kb.dma("sp", "f_dc", DC[:], dcd, writes=[dn["DC"]])
            kb.dma("sp", "f_cs", CS[:], cs64, writes=[dn["CS"]])
            kb.dma("sp", "f_cc", CC[:], ccs, writes=[dn["CC"]])
            kb.dma("sp", "f_wf", WF[:], w_f[l].rearrange("g c d -> c g d"), writes=[dn["WF"]])
            kb.dma("sp", "f_bf", BF_[:], b_f[l], writes=[dn["BF"]])
            kb.dma("sp", "f_t1", T1[:], T1d, writes=[dn["T1"]])
            kb.dma("sp", "f_t2", T2[:], T2d, writes=[dn["T2"]])
            for g in range(4):
                for i in range(2):
                    kb.op("pe", lambda E, i=i, g=g: E.matmul(ps[0][0:64, (g * 2 + i) * 64:(g * 2 + i + 1) * 64], lhsT=CC[:, i, :], rhs=WF[:, g, :], start=True, stop=True),
                          reads=[dn["CC"], dn["WF"]], writes=[d_ps[0]])
            kb.op("dve", lambda E: E.tensor_copy(out=MC[:].rearrange("c g i d -> c (g i d)"), in_=ps[0][0:64, 0:512]), reads=[d_ps[0]], writes=[dn["MC"]])
            sc_lat = 1.0 / math.sqrt(8192 * 64); sc_ctx = 1.0 / math.sqrt(256 * 64)
            blkc = [0]
            def f_s1(g):
                s = g % 2
                if g + 1 < 4:
                    load_x1(g + 1)
                X1v = X1[s][:].rearrange("t (p c) -> t p c", c=64)
                Bsb = Bsb2[s]; dBsb = dn[f"Bsb{s}"]
                for c4 in range(16):
                    b = 1 + c4 % 2
                    for cc in range(4):
                        c = c4 * 4 + cc
                        kb.op("pe", lambda E, c=c, cc=cc, b=b: E.matmul(ps[b][:, cc * 128:(cc + 1) * 128], lhsT=X1v[:, :, c], rhs=CS[:], start=True, stop=True),
                              reads=[dn[f"X1_{s}"], dn["CS"]], writes=[d_ps[b]])
                    dst = Bsb[:, c4 * 4:(c4 + 1) * 4, :, :].rearrange("p c r k -> p (c r k)")
                    if c4 % 2 == 0:
                        kb.op("act", lambda E, dst=dst, b=b: E.activation(out=dst, in_=ps[b][:], func=AF.Copy), reads=[d_ps[b]], writes=[dBsb])
                    else:
                        kb.op("dve", lambda E, dst=dst, b=b: E.tensor_copy(out=dst, in_=ps[b][:]), reads=[d_ps[b]], writes=[dBsb])

            def f_s3(g):
                s = g % 2
                Bsb = Bsb2[s]; YT = YT2[s]; dBsb = dn[f"Bsb{s}"]; dYT = dn[f"YT{s}"]
                for k8 in range(8):
                    b = 3 + k8 % 2
                    for ki in range(8):
                        kbi = k8 * 8 + ki
                        o = ps[b][0:64, ki * 64:(ki + 1) * 64]
                        kb.op("pe", lambda E, kbi=kbi, o=o: E.matmul(o, lhsT=Bsb[:, :, 0, kbi], rhs=T1[:, kbi, :], start=True, stop=False),
                              reads=[dBsb, dn["T1"]], writes=[d_ps[b]])
                        kb.op("pe", lambda E, kbi=kbi, o=o: E.matmul(o, lhsT=Bsb[:, :, 1, kbi], rhs=T2[:, kbi, :], start=False, stop=True),
                              reads=[dBsb, dn["T2"]], writes=[d_ps[b]])
                    dst = YT[:, k8 * 8:(k8 + 1) * 8, :, :].rearrange("c k r a -> c (k r a)")
                    if k8 % 2 == 0:
                        kb.op("act", lambda E, dst=dst, b=b: E.activation(out=dst, in_=ps[b][0:64, :], func=AF.Copy), reads=[d_ps[b]], writes=[dYT])
                    else:
                        kb.op("dve", lambda E, dst=dst, b=b: E.tensor_copy(out=dst, in_=ps[b][0:64, :]), reads=[d_ps[b]], writes=[dYT])

            def f_s4(g):
                s = g % 2
                YT = YT2[s]; dYT = dn[f"YT{s}"]
                nblk = 4
                if not last:
                    for k in range(2):
                        kb.op("pe", lambda E, k=k, g=g: E.matmul(ps[5][0:64, :], lhsT=Xc[:, k, g * 64:(g + 1) * 64], rhs=DC[:, k, :], start=(k == 0), stop=(k == 1)),
                              reads=[dn["Xc"], dn["DC"]], writes=[d_ps[5]])
                    kb.op("dve", lambda E: E.tensor_copy(out=YC[:].rearrange("c r k -> c (r k)"), in_=ps[5][0:64, :]), reads=[d_ps[5]], writes=[dn["YC"]])
                    nblk = 5
                for blk in range(nblk):
                    b = 6 + blkc[0] % 2
                    so = blkc[0] % 2
                    blkc[0] += 1
                    if blk < 4:
                        n = 512; scl = sc_lat; dy = dYT
                        r0 = YT[:, :, 0, blk * 8:(blk + 1) * 8].rearrange("c k a -> c a k")
                        r1 = YT[:, :, 1, blk * 8:(blk + 1) * 8].rearrange("c k a -> c a k")
                    else:
                        n = 256; r0 = YC[:, 0, :]; r1 = YC[:, 1, :]; scl = sc_ctx; dy = dn["YC"]
                    kb.op("pe", lambda E, r0=r0, n=n, b=b, g=g: E.matmul(ps[b][0:64, 0:n], lhsT=MC[:, g, 0, :], rhs=r0, start=True, stop=False), reads=[dn["MC"], dy], writes=[d_ps[b]])
                    kb.op("pe", lambda E, r1=r1, n=n, b=b, g=g: E.matmul(ps[b][0:64, 0:n], lhsT=MC[:, g, 1, :], rhs=r1, start=False, stop=True), reads=[dn["MC"], dy], writes=[d_ps[b]])
                    kb.op("act", lambda E, n=n, b=b, so=so, scl=scl, g=g: E.activation(out=ob[so][:, 0:n], in_=ps[b][0:64, 0:n], func=AF.Identity, scale=scl, bias=BF_[:, g:g + 1]),
                          reads=[d_ps[b], dn["BF"]], writes=[dn[f"ob{so}"]])
                    store(f"st_ob{so}", obT_d[g, :, blk * 512:blk * 512 + n], ob[so][:, 0:n], [dn[f"ob{so}"]])

            f_s1(0)
            for g in range(4):
                if g + 1 < 4:
                    f_s1(g + 1)
                f_s3(g)
                f_s4(g)
            kb.barrier()

        with ExitStack() as es:
            sbt = lambda name, shape, dty: es.enter_context(nc.sbuf_tensor(U(name), shape, dty))
            psS = [es.enter_context(nc.psum_tensor(U(f"psS{i}"), [128, 1024], F32)) for i in range(2)]
            ps = [None] * 4 + [es.enter_context(nc.psum_tensor(U(f"psB{i}"), [128, 512], F32)) for i in range(4, 8)]
            d_psS = [Dep("psS0"), Dep("psS1")]
            d_ps = [Dep(f"ps{i}") for i in range(8)]
            kTh = [sbt(f"kTh{i}", [128, NKEY], BF16) for i in range(2)]
            vh = [sbt(f"vh{i}", [128, NKT, 128], BF16) for i in range(2)]
            qTh = [sbt(f"qTh{i}", [128, TOKS], BF16) for i in range(2)]
            NPT = 4
            pt = [sbt(f"pt{i}", [128, 1024], BF16) for i in range(NPT)]
            ones = sbt("ones", [128, 128], F32)
            onesb = sbt("onesb", [128, 32], BF16)
            sel = sbt("sel", [128, 2, 128], F32)
            rsb = sbt("rsb", [128, 512], F32)
            lv = sbt("lv", [128, 4, 64], F32); lt = sbt("lt", [128, 2, 64], F32); l2 = sbt("l2", [128, 2], F32)
            nlam = sbt("nlam", [128, 1], F32); gs = sbt("gs", [128, 1], F32)
            rec = [sbt(f"rec{c}", [128, 512], F32) for c in range(2)]
            o0 = sbt("o0", [128, 512], F32); o1 = sbt("o1", [128, 512], F32)
            osq = sbt("osq", [128, 512], F32); rs = sbt("rs", [128, 512], F32)
            of = [sbt(f"of{i}", [128, 512], F32) for i in range(2)]
            d_kv = [Dep("kv0"), Dep("kv1")]
            d_pt = [Dep(f"pt{i}") for i in range(NPT)]
            dm = {n: Dep(n) for n in "ones onesb sel rsb lv lt l2 nlam gs rec0 rec1 o0 o1 osq rs of0 of1".split()}
            kb.op("pool", lambda E: E.memset(ones[:], 1.0), writes=[dm["ones"]])
            kb.op("pool", lambda E: E.memset(onesb[:], 1.0), writes=[dm["onesb"]])
            kb.op("pool", lambda E: E.memset(sel[:], 0.0), writes=[dm["sel"]])
            for (p0, p1, c, val) in ((0, 32, 0, 1.0 / 32), (64, 96, 0, 1.0 / 32), (32, 64, 1, 1.0 / 32), (64, 128, 1, 1.0 / 32), (64, 96, 1, 0.0)):
                kb.op("pool", lambda E, p0=p0, p1=p1, c=c, val=val: E.memset(sel[p0:p1, c, :], val), reads=[dm["sel"]], writes=[dm["sel"]])
            kb.dma("sp", "a_lv", lv[:].rearrange("p a c -> p (a c)"), lamv[l].partition_broadcast(128), writes=[dm["lv"]])
            kb.dma("sp", "a_gs", gs[:], g_sub[l], writes=[dm["gs"]])
            lvv = lv[:].rearrange("p (i j) c -> p i j c", j=2)
            kb.op("dve", lambda E: E.tensor_tensor(out=lt[:], in0=lvv[:, :, 0, :], in1=lvv[:, :, 1, :], op=ALU.mult), reads=[dm["lv"]], writes=[dm["lt"]])
            kb.op("dve", lambda E: E.tensor_reduce(out=l2[:], in_=lt[:], axis=AX.X, op=ALU.add), reads=[dm["lt"]], writes=[dm["l2"]])
            kb.op("act", lambda E: E.activation(out=l2[:], in_=l2[:], func=AF.Exp), reads=[dm["l2"]], writes=[dm["l2"]])
            kb.op("dve", lambda E: E.tensor_tensor(out=nlam[:], in0=l2[:, 1:2], in1=l2[:, 0:1], op=ALU.subtract), reads=[dm["l2"]], writes=[dm["nlam"]])
            kb.op("dve", lambda E: E.tensor_scalar(out=nlam[:], in0=nlam[:], scalar1=-lam_init, scalar2=None, op0=ALU.add), reads=[dm["nlam"]], writes=[dm["nlam"]])
            kb.op("dve", lambda E: E.tensor_scalar(out=gs[:], in0=gs[:], scalar1=(1.0 - lam_init), scalar2=None, op0=ALU.mult), reads=[dm["gs"]], writes=[dm["gs"]])
            kT_ag_v = [a.rearrange("(r h p) t -> h p r t", r=4, h=2) for a in kT_ag]
            v_ag_v = [a.rearrange("(r t p) e -> p r t e", r=4, p=128) for a in v_ag]
            v_ctx_v = v_ctx.rearrange("(t p) e -> p t e", p=128)

            def load_head(h):
                s = h % 2
                kb.dma("sp", f"a_k{s}", kTh[s][:, 0:8192].rearrange("p (r t) -> p r t", r=4), kT_ag_v[h // 2][h % 2], reads=[d_kvag], writes=[d_kv[s]])
                kb.dma("sp", f"a_k{s}", kTh[s][:, 8192:NKEY], kT_ctx[h], writes=[d_kv[s]])
                for half in range(2):
                    for r in range(4):
                        kt0 = r * 16 + half * 8
                        kb.dma("sp", f"a_k{s}", vh[s][:, kt0:kt0 + 8, :], v_ag_v[half][:, r, :, h * 128:(h + 1) * 128], reads=[d_kvag], writes=[d_kv[s]])
                kb.dma("sp", f"a_k{s}", vh[s][:, 64:66, :], v_ctx_v[:, :, h * 128:(h + 1) * 128], writes=[d_kv[s]])
                kb.dma("sp", f"a_k{s}", qTh[s][:], qT_d[h], writes=[d_kv[s]])

            deferred = []

            def make_epilogue(h, q0, nq, so):
                def e_sel(c):
                    kb.op("pe", lambda E: E.matmul(ps[7][:, 0:nq], lhsT=sel[:, c, :], rhs=rsb[:, 0:nq], start=True, stop=True),
                          reads=[dm["sel"], dm["rsb"]], writes=[d_ps[7]])
                    kb.op("dve", lambda E: E.reciprocal(out=rec[c][:, 0:nq], in_=ps[7][:, 0:nq]), reads=[d_ps[7]], writes=[dm[f"rec{c}"]])

                def e_comb():
                    kb.op("dve", lambda E: E.tensor_tensor(out=o0[:, 0:nq], in0=o0[:, 0:nq], in1=rec[0][:, 0:nq], op=ALU.mult), reads=[dm["o0"], dm["rec0"]], writes=[dm["o0"]])
                    kb.op("dve", lambda E: E.tensor_tensor(out=o1[:, 0:nq], in0=o1[:, 0:nq], in1=rec[1][:, 0:nq], op=ALU.mult), reads=[dm["o1"], dm["rec1"]], writes=[dm["o1"]])
                    kb.op("dve", lambda E: E.scalar_tensor_tensor(out=o0[:, 0:nq], in0=o1[:, 0:nq], scalar=nlam[:, 0:1], in1=o0[:, 0:nq], op0=ALU.mult, op1=ALU.add),
                          reads=[dm["o0"], dm["o1"], dm["nlam"]], writes=[dm["o0"]])
                    kb.op("dve", lambda E: E.tensor_tensor(out=osq[:, 0:nq], in0=o0[:, 0:nq], in1=o0[:, 0:nq], op=ALU.mult), reads=[dm["o0"]], writes=[dm["osq"]])

                def e_norm_mm():
                    kb.op("pe", lambda E: E.matmul(ps[7][:, 0:nq], lhsT=ones[:], rhs=osq[:, 0:nq], start=True, stop=True), reads=[dm["ones"], dm["osq"]], writes=[d_ps[7]])
                    kb.op("dve", lambda E: E.tensor_scalar(out=rs[:, 0:nq], in0=ps[7][:, 0:nq], scalar1=1.0 / 128, scalar2=EPS, op0=ALU.mult, op1=ALU.add),
                          reads=[d_ps[7]], writes=[dm["rs"]])

                def e_act():
                    kb.op("act", lambda E: E.activation(out=rs[:, 0:nq], in_=rs[:, 0:nq], func=AF.Ln), reads=[dm["rs"]], writes=[dm["rs"]])
                    kb.op("act", lambda E: E.activation(out=rs[:, 0:nq], in_=rs[:, 0:nq], func=AF.Exp, scale=-0.5), reads=[dm["rs"]], writes=[dm["rs"]])

                def e_fin():
                    kb.op("dve", lambda E: E.scalar_tensor_tensor(out=of[so][:, 0:nq], in0=o0[:, 0:nq], scalar=gs[:, 0:1], in1=rs[:, 0:nq], op0=ALU.mult, op1=ALU.mult),
                          reads=[dm["o0"], dm["gs"], dm["rs"]], writes=[dm[f"of{so}"]])
                    store(f"st_oa{so}", oaT_d[h, :, q0:q0 + nq], of[so][:, 0:nq], [dm[f"of{so}"]])
                return [(3, lambda: e_sel(0)), (6, lambda: e_sel(1)), (9, e_comb), (14, e_norm_mm), (18, e_act), (22, e_fin)]

            def run_deferred(upto):
                while deferred and deferred[0][0] <= upto:
                    deferred.pop(0)[1]()

            load_head(0)
            blk_id = 0
            for h in range(4):
                s = h % 2
                if h + 1 < 4:
                    load_head(h + 1)
                for qb in range(4 if last else 5):
                    if qb < 4:
                        q0 = qb * 512; nq = 512; kts = list(range(NKT))
                    else:
                        q0 = 2048; nq = 256; kts = [64, 65]
                    nk = len(kts)

                    def scores(idx):
                        kt = kts[idx]; sl = idx % 2
                        for c in range(2):
                            kb.op("pe", lambda E, c=c, kt=kt, sl=sl: E.matmul(psS[sl][:, c * 512:c * 512 + nq], lhsT=kTh[s][c * 64:(c + 1) * 64, kt * 128:(kt + 1) * 128],
                                                                           rhs=qTh[s][c * 64:(c + 1) * 64, q0:q0 + nq], start=True, stop=True, tile_position=(64 * c, 0)),
                                  reads=[d_kv[s]], writes=[d_psS[sl]])

                    def exps(idx):
                        sl = idx % 2; p = idx % NPT
                        kb.op("act", lambda E, sl=sl, p=p: E.activation(out=pt[p][:].rearrange("p (c q) -> p c q", c=2)[:, :, 0:nq],
                                                                        in_=psS[sl][:].rearrange("p (c q) -> p c q", c=2)[:, :, 0:nq], func=AF.Exp, scale=0.125),
                              reads=[d_psS[sl]], writes=[d_pt[p]])

                    def av(idx):
                        kt = kts[idx]; p = idx % NPT
                        for c in range(2):
                            kb.op("pe", lambda E, c=c, kt=kt, p=p, idx=idx: E.matmul(ps[4 + c][:, 0:nq], lhsT=vh[s][:, kt, :], rhs=pt[p][:, c * 512:c * 512 + nq],
                                                                                  start=(idx == 0), stop=(idx == nk - 1)),
                                  reads=[d_kv[s], d_pt[p]], writes=[d_ps[4 + c]])

                    def rowsums(idxs):
                        for idx in idxs:
                            p = idx % NPT
                            for c in range(2):
                                g4 = (idx % 2) * 2 + c
                                kb.op("pe", lambda E, c=c, p=p, idx=idx, g4=g4: E.matmul(ps[6][32 * g4:32 * g4 + 32, 0:nq], lhsT=onesb[:], rhs=pt[p][:, c * 512:c * 512 + nq],
                                                                                       start=(idx < 2), stop=(idx >= nk - 2), tile_position=(0, 32 * g4)),
                                      reads=[dm["onesb"], d_pt[p]], writes=[d_ps[6]])

                    scores(0)
                    pend = []

                    def av_rs(i2):
                        av(i2)
                        pend.append(i2)
                        if len(pend) == 2 or i2 == nk - 1:
                            rowsums(list(pend))
                            del pend[:]
                    for idx in range(nk):
                        run_deferred(idx)
                        exps(idx)
                        if idx + 1 < nk:
                            scores(idx + 1)
                        if idx >= 1:
                            av_rs(idx - 1)
                    av_rs(nk - 1)
                    run_deferred(10 ** 9)
                    kb.op("dve", lambda E: E.tensor_copy(out=rsb[:, 0:nq], in_=ps[6][:, 0:nq]), reads=[d_ps[6]], writes=[dm["rsb"]])
                    kb.op("dve", lambda E: E.tensor_copy(out=o0[:, 0:nq], in_=ps[4][:, 0:nq]), reads=[d_ps[4]], writes=[dm["o0"]])
                    kb.op("dve", lambda E: E.tensor_copy(out=o1[:, 0:nq], in_=ps[5][:, 0:nq]), reads=[d_ps[5]], writes=[dm["o1"]])
                    deferred.extend(make_epilogue(h, q0, nq, blk_id % 2))
                    blk_id += 1
            run_deferred(10 ** 9)
            kb.barrier()

    def stage_c(l, last, x_src, x_dst):
        with ExitStack() as es0:
            sb = lambda name, shape, dty: es0.enter_context(nc.sbuf_tensor(U(name), shape, dty))
            ps = [es0.enter_context(nc.psum_tensor(U(f"psC{i}"), [128, 512], F32)) for i in range(8)]
            d_ps = [Dep(f"ps{i}") for i in range(8)]
            wC = sb("wC", [128, 8, 4608], BF16)
            wBR = sb("wBR", [128, 4, 1024], BF16)
            wBRb = sb("wBRb", [64, 4, 1024], BF16)
            wBRc = sb("wBRc", [64, 4, 1024], BF16)
            wO = sb("wO", [128, 8, 1024], BF16)
            wS = sb("wS", [128, 4, 128], BF16)
            LG = sb("LG", [128, 256], F32); LB = sb("LB", [128, 256], F32)
            BS = sb("BS", [64, 4, 128], F32)
            modT = sb("modT_s", [128, 24, 2], F32)
            G = [sb(f"G{i}", [128, 1024], F32) for i in range(2)]
            ident = sb("ident", [128, 128], F32); ones = sb("ones", [128, 128], F32)
            dw = {n: Dep(n) for n in "wC wBR wBRb wBRc wO wS LG LB BS modT G ident ones".split()}
            with ExitStack() as es:
                sbt = lambda name, shape, dty: es.enter_context(nc.sbuf_tensor(U(name), shape, dty))
                wst = [sbt(f"wst{i}", [128, 8, 512], F32) for i in range(2)]
                d_wst = [Dep("wst0"), Dep("wst1")]
                wsf = sbt("wsf", [128, 4, 128], F32); diag = sbt("diag", [128, 128], F32)
                d_wsf = Dep("wsf"); d_diag = Dep("diag")
                make_ident(ident, dw["ident"])
                kb.op("pool", lambda E: E.memset(ones[:], 1.0), writes=[dw["ones"]])
                kb.dma("sp", "c_mod", modT[:], modT_d, writes=[dw["modT"]])
                kb.dma("sp", "c_lg", LG[:], ln_g[l].partition_broadcast(128), writes=[dw["LG"]])
                kb.dma("sp", "c_lb", LB[:], ln_b[l].partition_broadcast(128), writes=[dw["LB"]])
                kb.dma("sp", "c_bs", BS[:], bs_d[l], writes=[dw["BS"]])
                kb.dma("sp", "c_ws", wsf[:], w_sT[l], writes=[d_wsf])
                kb.op("dve", lambda E: E.tensor_copy(out=wS[:], in_=wsf[:]), reads=[d_wsf], writes=[dw["wS"]])
                for var in range(1 if last else 2):
                    for k in range(8):
                        kb.op("dve", lambda E, k=k, var=var: E.tensor_scalar(out=diag[:], in0=ident[:], scalar1=modT[:, 16 + k, var:var + 1], scalar2=None, op0=ALU.mult),
                              reads=[dw["ident"], dw["modT"]], writes=[d_diag])
                        b = k // 4
                        kb.op("pe", lambda E, k=k, b=b: E.matmul(ps[b][:, (k % 4) * 128:(k % 4 + 1) * 128], lhsT=ones[:], rhs=diag[:], start=True, stop=True),
                              reads=[dw["ones"], d_diag], writes=[d_ps[b]])
                    for b in range(2):
                        kb.op("act", lambda E, b=b, var=var: E.activation(out=G[var][:, b * 512:(b + 1) * 512], in_=ps[b][:], func=AF.Copy), reads=[d_ps[b]], writes=[dw["G"]])
                kb.dma("sp", "c_w0", wC[:, :, 0:2304], wC_bf[:, :, 0:2304], writes=[dw["wC"]])
                kb.dma("sp", "c_w1", wC[:, :, 2304:4608], wC_bf[:, :, 2304:4608], writes=[dw["wC"]])
                kb.dma("sp", "c_w2", wBR[:], wBR_bf, writes=[dw["wBR"]])
                kb.dma("sp", "c_w3", wBRb[:], wBRb_bf, writes=[dw["wBRb"]])
                kb.dma("sp", "c_w4", wBRc[:], wBRc_bf, writes=[dw["wBRc"]])
                kb.dma("sp", "c_w5", wO[:], wO_bf, writes=[dw["wO"]])
                kb.barrier()

            hTb = sb("hTb", [128, 8, BT], BF16)
            oab = sb("oab", [128, 4, BT], F32)
            obb = sb("obb", [64, 4, BT], F32)
            uT = sb("uT", [64, 4, BT], F32)
            gcT = sb("gcT", [64, 4, BT], F32)
            sgt = [sb(f"sgt{i}", [128, BT], F32) for i in range(4)]
            og = sb("og", [128, 4, BT], BF16)
            ogb = sb("ogb", [64, 4, BT], BF16)
            ogc = sb("ogc", [64, 4, BT], BF16)
            st6 = sb("st6", [128, 6], F32); mv = sb("mv", [128, 2], F32); rstd = sb("rstd", [128, 1], F32)
            vcn = sb("vcn", [128, 256], F32); vnb = sb("vnb", [128, 256], BF16)
            sT = sb("sT", [64, 4, 128], F32)
            mm = [sb(f"mm{i}", [128, 3, BT], F32) for i in range(2)]
            t0 = sb("t0", [128, BT], F32); t1 = sb("t1", [128, BT], F32)
            yT = sb("yT", [128, 8, BT], BF16)
            xt = [sb(f"xt{i}", [128, 1024], F32) for i in range(2)]
            tmpo = sb("tmpo", [128, 1024], F32)
            dn = {n: Dep(n) for n in "hTb oab obb uT gcT sgt0 sgt1 sgt2 sgt3 og ogb ogc st6 mv rstd vcn vnb sT mm0 mm1 t0 t1 yT xt0 xt1 tmpo".split()}

            def proj_fm(col0, M, nt, bank):
                for k in range(8):
                    kb.op("pe", lambda E, k=k: E.matmul(ps[bank][0:M, 0:nt], lhsT=wC[:, k, col0:col0 + M], rhs=hTb[:, k, 0:nt], start=(k == 0), stop=(k == 7)),
                          reads=[dw["wC"], dn["hTb"]], writes=[d_ps[bank]])
            pj = [0]

            def next_bank():
                pj[0] += 1
                return 6 + pj[0] % 2
            gj = [0]

            def gate_bank():
                gj[0] += 1
                return (0, 1, 2, 6, 7)[gj[0] % 5]
            GC0 = O_GATE - O_U
            MC0 = O_MERGE - O_U
            for blk in range(8 if last else 9):
                tok0 = blk * BT; nt = BT; var = 0 if tok0 < 2048 else 1
                ntile = nt // 128
                kb.dma("sp", "c_h", hTb[:, :, 0:nt], hT_d[:, :, tok0:tok0 + nt].rearrange("k p t -> p k t"), writes=[dn["hTb"]])
                kb.dma("sp", "c_oa", oab[:, :, 0:nt], oaT_d[:, :, tok0:tok0 + nt].rearrange("h p t -> p h t"), writes=[dn["oab"]])
                kb.dma("sp", "c_ob", obb[:, :, 0:nt], obT_d[:, :, tok0:tok0 + nt].rearrange("g c t -> c g t"), writes=[dn["obb"]])
                for g in range(4):
                    b = gate_bank()
                    proj_fm(g * 64, 64, nt, b)
                    kb.op("act", lambda E, g=g, b=b: E.activation(out=uT[:, g, 0:nt], in_=ps[b][0:64, 0:nt], func=AF.Copy), reads=[d_ps[b]], writes=[dn["uT"]])
                for br in range(2):
                    for g in range(4):
                        b = gate_bank(); s = (br * 4 + g) % 4
                        proj_fm(GC0 + 512 + br * 256 + g * 64, 64, nt, b)
                        kb.op("act", lambda E, b=b, s=s: E.activation(out=sgt[s][0:64, 0:nt], in_=ps[b][0:64, 0:nt], func=AF.Sigmoid), reads=[d_ps[b]], writes=[dn[f"sgt{s}"]])
                        if br == 1:
                            kb.op("dve", lambda E, g=g, b=b, s=s: E.tensor_tensor(out=gcT[:, g, 0:nt], in0=ps[b][0:64, 0:nt], in1=sgt[s][0:64, 0:nt], op=ALU.mult),
                                  reads=[d_ps[b], dn[f"sgt{s}"]], writes=[dn["gcT"]])
                        else:
                            kb.op("dve", lambda E, b=b, s=s: E.tensor_tensor(out=sgt[s][0:64, 0:nt], in0=ps[b][0:64, 0:nt], in1=sgt[s][0:64, 0:nt], op=ALU.mult),
                                  reads=[d_ps[b], dn[f"sgt{s}"]], writes=[dn[f"sgt{s}"]])
                            kb.op("dve", lambda E, g=g, s=s: E.tensor_tensor(out=ogb[:, g, 0:nt], in0=sgt[s][0:64, 0:nt], in1=obb[:, g, 0:nt], op=ALU.mult),
                                  reads=[dn[f"sgt{s}"], dn["obb"]], writes=[dn["ogb"]])
                for j in range(4):
                    b = gate_bank(); s = j % 4
                    proj_fm(GC0 + j * 128, 128, nt, b)
                    kb.op("act", lambda E, b=b, s=s: E.activation(out=sgt[s][:, 0:nt], in_=ps[b][:, 0:nt], func=AF.Sigmoid), reads=[d_ps[b]], writes=[dn[f"sgt{s}"]])
                    kb.op("dve", lambda E, b=b, s=s: E.tensor_tensor(out=sgt[s][:, 0:nt], in0=ps[b][:, 0:nt], in1=sgt[s][:, 0:nt], op=ALU.mult),
                          reads=[d_ps[b], dn[f"sgt{s}"]], writes=[dn[f"sgt{s}"]])
                    kb.op("dve", lambda E, j=j, s=s: E.tensor_tensor(out=og[:, j, 0:nt], in0=sgt[s][:, 0:nt], in1=oab[:, j, 0:nt], op=ALU.mult),
                          reads=[dn[f"sgt{s}"], dn["oab"]], writes=[dn["og"]])
                for t in range(ntile):
                    b = next_bank()
                    for k in range(8):
                        kb.op("pe", lambda E, k=k, t=t, b=b: E.matmul(ps[b][:, 0:256], lhsT=hTb[:, k, t * 128:(t + 1) * 128], rhs=wC[:, k, 256:512], start=(k == 0), stop=(k == 7)),
                              reads=[dw["wC"], dn["hTb"]], writes=[d_ps[b]])
                    kb.op("dve", lambda E, b=b: E.bn_stats(out=st6[:], in_=ps[b][:, 0:256]), reads=[d_ps[b]], writes=[dn["st6"]])
                    kb.op("dve", lambda E: E.bn_aggr(out=mv[:], in_=st6[:]), reads=[dn["st6"]], writes=[dn["mv"]])
                    kb.op("dve", lambda E: E.tensor_scalar(out=rstd[:], in0=mv[:, 1:2], scalar1=EPS, scalar2=None, op0=ALU.add), reads=[dn["mv"]], writes=[dn["rstd"]])
                    kb.op("act", lambda E: E.activation(out=rstd[:], in_=rstd[:], func=AF.Sqrt), reads=[dn["rstd"]], writes=[dn["rstd"]])
                    kb.op("dve", lambda E: E.reciprocal(out=rstd[:], in_=rstd[:]), reads=[dn["rstd"]], writes=[dn["rstd"]])
                    kb.op("dve", lambda E, b=b: E.tensor_scalar(out=vcn[:], in0=ps[b][:, 0:256], scalar1=mv[:, 0:1], scalar2=rstd[:, 0:1], op0=ALU.subtract, op1=ALU.mult),
                          reads=[d_ps[b], dn["mv"], dn["rstd"]], writes=[dn["vcn"]])
                    kb.op("dve", lambda E: E.tensor_tensor(out=vcn[:], in0=vcn[:], in1=LG[:], op=ALU.mult), reads=[dn["vcn"], dw["LG"]], writes=[dn["vcn"]])
                    kb.op("dve", lambda E: E.tensor_tensor(out=vnb[:], in0=vcn[:], in1=LB[:], op=ALU.add), reads=[dn["vcn"], dw["LB"]], writes=[dn["vnb"]])
                    b2 = next_bank()
                    for g in range(4):
                        kb.op("pe", lambda E, g=g, b2=b2: E.matmul(ps[b2][0:64, g * 128:(g + 1) * 128], lhsT=vnb[:, g * 64:(g + 1) * 64], rhs=wS[:, g, :], start=True, stop=True),
                              reads=[dn["vnb"], dw["wS"]], writes=[d_ps[b2]])
                    kb.op("dve", lambda E, b2=b2: E.tensor_tensor(out=sT[:].rearrange("c g p -> c (g p)"), in0=ps[b2][0:64, :], in1=BS[:].rearrange("c g p -> c (g p)"), op=ALU.add),
                          reads=[d_ps[b2], dw["BS"]], writes=[dn["sT"]])
                    kb.op("dve", lambda E, t=t: E.tensor_tensor(out=sT[:], in0=sT[:], in1=uT[:, :, t * 128:(t + 1) * 128], op=ALU.mult), reads=[dn["sT"], dn["uT"]], writes=[dn["sT"]])
                    kb.op("dve", lambda E, t=t: E.tensor_tensor(out=ogc[:, :, t * 128:(t + 1) * 128], in0=sT[:], in1=gcT[:, :, t * 128:(t + 1) * 128], op=ALU.mult),
                          reads=[dn["sT"], dn["gcT"]], writes=[dn["ogc"]])
                for dc in range(8):
                    dsl = slice(dc * 128, (dc + 1) * 128)
                    ms = dc % 2
                    for i in range(3):
                        proj_fm(MC0 + i * 1024 + dc * 128, 128, nt, 3 + i)
                        kb.op("act", lambda E, i=i, ms=ms: E.activation(out=mm[ms][:, i, 0:nt], in_=ps[3 + i][:, 0:nt], func=AF.Sigmoid), reads=[d_ps[3 + i]], writes=[dn[f"mm{ms}"]])
                    for e in range(4):
                        kb.op("pe", lambda E, e=e: E.matmul(ps[0][:, 0:nt], lhsT=wBR[:, e, dsl], rhs=og[:, e, 0:nt], start=(e == 0), stop=(e == 3)),
                              reads=[dw["wBR"], dn["og"]], writes=[d_ps[0]])
                    for g in range(4):
                        kb.op("pe", lambda E, g=g: E.matmul(ps[1][:, 0:nt], lhsT=wBRb[:, g, dsl], rhs=ogb[:, g, 0:nt], start=(g == 0), stop=(g == 3)),
                              reads=[dw["wBRb"], dn["ogb"]], writes=[d_ps[1]])
                    for g in range(4):
                        kb.op("pe", lambda E, g=g: E.matmul(ps[2][:, 0:nt], lhsT=wBRc[:, g, dsl], rhs=ogc[:, g, 0:nt], start=(g == 0), stop=(g == 3)),
                              reads=[dw["wBRc"], dn["ogc"]], writes=[d_ps[2]])
                    kb.op("dve", lambda E, ms=ms: E.tensor_tensor(out=t0[:, 0:nt], in0=ps[0][:, 0:nt], in1=mm[ms][:, 0, 0:nt], op=ALU.mult), reads=[d_ps[0], dn[f"mm{ms}"]], writes=[dn["t0"]])
                    kb.op("dve", lambda E, ms=ms: E.tensor_tensor(out=t1[:, 0:nt], in0=ps[1][:, 0:nt], in1=mm[ms][:, 1, 0:nt], op=ALU.mult), reads=[d_ps[1], dn[f"mm{ms}"]], writes=[dn["t1"]])
                    kb.op("dve", lambda E: E.tensor_tensor(out=t0[:, 0:nt], in0=t0[:, 0:nt], in1=t1[:, 0:nt], op=ALU.add), reads=[dn["t0"], dn["t1"]], writes=[dn["t0"]])
                    kb.op("dve", lambda E, ms=ms: E.tensor_tensor(out=t1[:, 0:nt], in0=ps[2][:, 0:nt], in1=mm[ms][:, 2, 0:nt], op=ALU.mult), reads=[d_ps[2], dn[f"mm{ms}"]], writes=[dn["t1"]])
                    kb.op("dve", lambda E, dc=dc: E.tensor_tensor(out=yT[:, dc, 0:nt], in0=t0[:, 0:nt], in1=t1[:, 0:nt], op=ALU.add), reads=[dn["t0"], dn["t1"]], writes=[dn["yT"]])
                for t in range(ntile):
                    gt = (tok0 // 128) + t
                    s = gt % 2
                    kb.dma("sp", f"c_x{s}", xt[s][:], x_src[gt * 128:(gt + 1) * 128, :], writes=[dn[f"xt{s}"]])
                    for cb in range(2):
                        b = next_bank()
                        for k in range(8):
                            kb.op("pe", lambda E, k=k, cb=cb, b=b, t=t: E.matmul(ps[b][:], lhsT=yT[:, k, t * 128:(t + 1) * 128], rhs=wO[:, k, cb * 512:(cb + 1) * 512],
                                                                              start=(k == 0), stop=(k == 7)),
                                  reads=[dn["yT"], dw["wO"]], writes=[d_ps[b]])
                        kb.op("dve", lambda E, cb=cb, b=b: E.tensor_tensor(out=tmpo[:, cb * 512:(cb + 1) * 512], in0=ps[b][:], in1=G[var][:, cb * 512:(cb + 1) * 512], op=ALU.mult),
                              reads=[d_ps[b], dw["G"]], writes=[dn["tmpo"]])
                    kb.op("dve", lambda E, s=s: E.tensor_tensor(out=xt[s][:], in0=xt[s][:], in1=tmpo[:], op=ALU.add), reads=[dn["tmpo"], dn[f"xt{s}"]], writes=[dn[f"xt{s}"]])
                    store(f"st_x{s}", x_dst[gt * 128:(gt + 1) * 128, :], xt[s][:], [dn[f"xt{s}"]])
            kb.barrier()

    import os
    FS = os.environ.get("FSTOP", "")
    for l in range(1 if FS else 2):
        last = l == 1
        x_src = x_in if l == 0 else x1
        stage_a(l, x_src)
        if FS == "a":
            break
        if FS == "ag":
            kb.barrier()
            break
        stage_b(l, last)
        if FS == "b":
            break
        stage_c(l, last, x_src, y_out if last else x1)
    kb.barrier()
    for k in sorted(out_keys):
        nc.gpsimd.wait_ge(kb.sems[k], kb.cnt[k])
    return kb


import numpy as np
import ml_dtypes
BF = ml_dtypes.bfloat16
NLT = 2048


def rope_tabs():
    n = 8192
    row = np.repeat(np.arange(n // 64), 64).astype(np.float32)
    col = np.tile(np.arange(64), n // 64).astype(np.float32)
    freqs = (10000.0 ** (-np.arange(0, 32, 2, dtype=np.float32) / 32)).astype(np.float32)
    ar = row[:, None] * freqs; ac = col[:, None] * freqs
    ang = np.concatenate([ar, ar, ac, ac], -1)
    cos = np.cos(ang).astype(np.float32); sin = np.sin(ang).astype(np.float32)
    sgn = np.tile(np.concatenate([-np.ones(16), np.ones(16)]), 2).astype(np.float32)
    return cos, sin * sgn


def col_layout(v, k):
    return np.ascontiguousarray(v.reshape(k, 128).T)


def fourier_consts(j):
    t = np.arange(64)[:, None]; kb = np.arange(64)[None, :]
    a = 2 * np.pi * ((t * kb) % 64) / 64
    cs64 = np.concatenate([np.cos(a), -np.sin(a)], 1).astype(BF)
    p = np.arange(128)[:, None, None]; kbb = np.arange(64)[None, :, None]; ka = (32 * j + np.arange(32))[None, None, :]
    a = 2 * np.pi * ((p * (64 * ka + kbb)) % 8192) / 8192
    T1 = np.concatenate([np.cos(a), -np.sin(a)], 2).astype(BF)
    T2 = np.concatenate([np.sin(a), np.cos(a)], 2).astype(BF)
    n = np.arange(128)[:, None, None]; ch = np.arange(2)[None, :, None]; k = np.arange(256)[None, None, :]
    a = 2 * np.pi * (((ch * 128 + n) * k) % 256) / 256
    dctx = np.concatenate([np.cos(a), -np.sin(a)], 2).astype(BF)
    c1 = np.arange(64)[:, None]; c2 = np.arange(64)[None, :]
    a = 2 * np.pi * ((c1 * c2) % 64) / 64
    ccs = np.stack([np.cos(a), np.sin(a)], 1).astype(np.float32)
    return dict(cs64=cs64, T1j=np.ascontiguousarray(T1), T2j=np.ascontiguousarray(T2), dctx=np.ascontiguousarray(dctx), ccs=np.ascontiguousarray(ccs))


def fused_inputs(I):
    cos, ssin = rope_tabs()
    C = np.ascontiguousarray
    shared = {
        "w_ada": C(I['w_ada']), "b_ada": C(np.stack([col_layout(I['b_ada'][l], 24) for l in range(2)])),
        "g_norm": C(np.stack([col_layout(I['g_norm'][l], 8) for l in range(2)])),
        "w_in": C(I['w_in']), "g_q": C(I['g_q']), "g_k": C(I['g_k']),
        "lamv": C(np.concatenate([I['lam_q1'], I['lam_k1'], I['lam_q2'], I['lam_k2']], 1)),
        "g_sub": C(I['g_sub'].reshape(2, 128, 1)),
        "w_f": C(I['w_f']), "b_f": C(I['b_f'].transpose(0, 2, 1)),
        "ln_g": C(I['ln_g']), "ln_b": C(I['ln_b']),
        "w_sT": C(I['w_s'].transpose(0, 3, 1, 2)),
        "bs64": C(np.broadcast_to(I['b_s'][:, None, :, :], (2, 64, 4, 128))),
        "w_br": C(np.concatenate([I['w_br_a'], I['w_br_b'], I['w_br_c']], 1)), "w_out": C(I['w_out']),
    }
    maps = []
    for core in range(8):
        b, j = core // 4, core % 4
        m = dict(shared)
        m["x_tok"] = C(np.concatenate([I['x'][b, j * NLT:(j + 1) * NLT], I['ctx'][b]], 0))
        m["cvec"] = C(np.stack([col_layout(I['c'][b], 8), col_layout(I['c_ctx'], 8)], -1))
        m["cos"] = C(cos[j * NLT:(j + 1) * NLT]); m["ssin"] = C(ssin[j * NLT:(j + 1) * NLT])
        m.update(fourier_consts(j))
        maps.append(m)
    return maps


def kernel(**inputs):
    I = {k: np.asarray(v, dtype=np.float32) for k, v in inputs.items()}
    kb = KB()
    build_fused(kb)
    res = kb.run(fused_inputs(I))
    out = np.empty((2, 8192, 1024), np.float32)
    for core in range(8):
        b, j = core // 4, core % 4
        out[b, j * NLT:(j + 1) * NLT] = np.asarray(res.results[core]["y"])
    return out
```

```python
import numpy as np
import concourse.bass as bass
import concourse.mybir as mybir
from concourse.bass_utils import run_bass_kernel_spmd

F32 = mybir.dt.float32
BF16 = mybir.dt.bfloat16
AF = mybir.ActivationFunctionType
ALU = mybir.AluOpType
AX = mybir.AxisListType


class Dep:
    __slots__ = ("w", "r", "name")

    def __init__(self, name=""):
        self.w = None
        self.r = []
        self.name = name


class KB:
    COMPUTE = ("pe", "act", "dve", "pool")

    def __init__(self):
        self.nc = bass.Bass("TRN2", target_bir_lowering=False)
        nc = self.nc
        self.eng = {"pe": nc.tensor, "act": nc.scalar, "dve": nc.vector,
                    "pool": nc.gpsimd, "sp": nc.sync}
        self.sems = {}
        self.cnt = {}
        for e in self.COMPUTE:
            self.sems[e] = nc.alloc_semaphore(name="s_" + e)
            self.cnt[e] = 0
        self.seen = {e: {} for e in self.eng}
        self.n_inst = 0

    def _sem(self, key):
        if key not in self.sems:
            self.sems[key] = self.nc.alloc_semaphore(name="d_" + str(key))
            self.cnt[key] = 0
        return self.sems[key]

    def _waits(self, e, reads, writes):
        need = {}

        def add(t, war=False):
            if t is None:
                return
            sk, v = t
            if sk == e and (war or e == "pe"):
                return
            if need.get(sk, 0) < v:
                need[sk] = v
        for d in reads:
            add(d.w)
        for d in writes:
            add(d.w)
            for t in d.r:
                add(t, war=True)
        E = self.eng[e]
        for sk, v in need.items():
            if self.seen[e].get(sk, 0) >= v:
                continue
            E.wait_ge(self.sems[sk], v)
            self.seen[e][sk] = v

    def _mark(self, tok, reads, writes):
        for d in reads:
            d.r.append(tok)
            if len(d.r) > 64:
                m = {}
                for sk, v in d.r:
                    if m.get(sk, 0) < v:
                        m[sk] = v
                d.r = list(m.items())
        for d in writes:
            d.w = tok
            d.r = []

    def op(self, e, fn, reads=(), writes=()):
        self._waits(e, reads, writes)
        inst = fn(self.eng[e])
        self.cnt[e] += 1
        inst.then_inc(self.sems[e], 1)
        self._mark((e, self.cnt[e]), reads, writes)
        self.n_inst += 1
        return inst

    def mm(self, fn, reads=(), writes=(), last=True):
        return self.op("pe", fn, reads, writes)

    def dma(self, q, key, out, in_, reads=(), writes=(), **kw):
        sem = self._sem(key)
        self._waits(q, reads, writes)
        inst = self.eng[q].dma_start(out=out, in_=in_, **kw)
        self.cnt[key] += 16
        inst.then_inc(sem, 16)
        self._mark((key, self.cnt[key]), reads, writes)
        self.n_inst += 1
        return inst

    def barrier(self, skip=()):
        for e, E in self.eng.items():
            for sk, sem in self.sems.items():
                if sk in skip:
                    continue
                v = self.cnt[sk]
                if v == 0 or self.seen[e].get(sk, 0) >= v:
                    continue
                E.wait_ge(sem, v)
                self.seen[e][sk] = v

    def wait_all(self, e, deps):
        self._waits(e, deps, ())

    def run(self, in_maps, n=8, trace=False):
        return run_bass_kernel_spmd(self.nc, in_maps, core_ids=list(range(n)), trace=trace)


import math
from contextlib import ExitStack

NT_A = 18
NLAT_T = 16
TOKS = NT_A * 128
NKT = 66
NKEY = NKT * 128
EPS = 1e-6
BT = 256
O_Q, O_K, O_V, O_F, O_U, O_VC, O_GATE, O_MERGE = 0, 512, 1024, 1536, 1792, 2048, 2304, 3328
RG = [[0, 1, 2, 3], [4, 5, 6, 7]]


def build_fused(kb):
    nc = kb.nc
    EI = lambda name, shape, d=F32: nc.dram_tensor(name, shape, d, kind="ExternalInput").ap()
    IN = lambda name, shape, d=F32: nc.dram_tensor(name, shape, d, kind="Internal").ap()
    x_in = EI("x_tok", [TOKS, 1024])
    cvec = EI("cvec", [128, 8, 2])
    w_ada = EI("w_ada", [2, 1024, 3072]); b_ada = EI("b_ada", [2, 128, 24]); g_norm = EI("g_norm", [2, 128, 8])
    w_in = EI("w_in", [2, 1024, 6400])
    g_q = EI("g_q", [2, 64]); g_k = EI("g_k", [2, 64])
    cos = EI("cos", [2048, 64]); ssin = EI("ssin", [2048, 64])
    lamv = EI("lamv", [2, 256]); g_sub = EI("g_sub", [2, 128, 1])
    w_f = EI("w_f", [2, 4, 64, 64]); b_f = EI("b_f", [2, 64, 4])
    cs64 = EI("cs64", [64, 128], BF16); T1d = EI("T1j", [128, 64, 64], BF16); T2d = EI("T2j", [128, 64, 64], BF16)
    dcd = EI("dctx", [128, 2, 512], BF16); ccs = EI("ccs", [64, 2, 64])
    ln_g = EI("ln_g", [2, 256]); ln_b = EI("ln_b", [2, 256])
    w_sT = EI("w_sT", [2, 128, 4, 128]); bs_d = EI("bs128", [2, 128, 2, 128])
    w_br = EI("w_br", [2, 1024, 1024]); w_out = EI("w_out", [2, 1024, 1024])
    y_out = nc.dram_tensor("y", [2048, 1024], F32, kind="ExternalOutput").ap()
    x1 = IN("x1", [TOKS, 1024])
    hT_d = IN("hT_d", [8, 128, TOKS], BF16)
    qT_d = IN("qT_d", [4, 128, TOKS], BF16)
    kT_lat = [IN(f"kT_lat{i}", [256, 2048], BF16) for i in range(2)]
    kT_ag = [IN(f"kT_ag{i}", [1024, 2048], BF16) for i in range(2)]
    kT_ctx = IN("kT_ctx", [4, 128, 256], BF16)
    v_lat = [IN(f"v_lat{i}", [1024, 512], BF16) for i in range(2)]
    v_ag = [IN(f"v_ag{i}", [4096, 512], BF16) for i in range(2)]
    v_ctx = IN("v_ctx", [256, 512], BF16)
    f_lat = IN("f_lat", [4 * 2048, 64], BF16)
    f_ag = IN("f_ag", [16 * 2048, 64], BF16)
    f_ctx = IN("f_ctx", [256, 256], BF16)
    oaT_d = IN("oaT_d", [4, 128, TOKS])
    obT_d = IN("obT_d", [4, 64, TOKS])
    modT_d = IN("modT_d", [128, 24, 2])
    wC_bf = IN("wC_bf", [128, 8, 4608], BF16)
    wBR_bf = IN("wBR_bf", [128, 4, 1024], BF16)
    wBRb_bf = IN("wBRb_bf", [128, 2, 1024], BF16)
    wBRc_bf = IN("wBRc_bf", [128, 2, 1024], BF16)
    wO_bf = IN("wO_bf", [128, 8, 1024], BF16)
    out_keys = set()
    uid = [0]

    def U(name):
        uid[0] += 1
        return f"{name}_{uid[0]}"

    def store(key, out, in_, reads):
        out_keys.add(key)
        kb.dma("pool", key, out, in_, reads=reads, writes=[])

    d_fag = Dep("f_ag"); d_kvag = Dep("kv_ag")

    def collective(kind, src, dst, dep):
        sem = kb._sem("cc")
        inst = nc.gpsimd.collective_compute(kind, ALU.bypass, replica_groups=RG, ins=[src], outs=[dst])
        inst.then_inc(sem, 1)
        kb.cnt["cc"] += 1
        dep.w = ("cc", kb.cnt["cc"])

    def rstd_chain(ssrc, dsrc, dst, ddst, scale, epsb=None):
        if epsb is not None:
            kb.op("act", lambda E: E.activation(out=dst, in_=ssrc, func=AF.Ln, scale=scale, bias=epsb[0]), reads=[dsrc, epsb[1]], writes=[ddst])
            kb.op("act", lambda E: E.activation(out=dst, in_=dst, func=AF.Exp, scale=-0.5), reads=[ddst], writes=[ddst])
            return
        kb.op("dve", lambda E: E.tensor_scalar(out=dst, in0=ssrc, scalar1=scale, scalar2=EPS, op0=ALU.mult, op1=ALU.add), reads=[dsrc], writes=[ddst])
        kb.op("act", lambda E: E.activation(out=dst, in_=dst, func=AF.Sqrt), reads=[ddst], writes=[ddst])
        kb.op("dve", lambda E: E.reciprocal(out=dst, in_=dst), reads=[ddst], writes=[ddst])

    def make_ident(ident, dep):
        kb.op("pool", lambda E: E.memset(ident[:], 0.0), writes=[dep])
        kb.op("pool", lambda E: E.affine_select(out=ident[:], in_=ident[:], pattern=[[-1, 128]], compare_op=ALU.not_equal,
                                                fill=1.0, base=0, channel_multiplier=1), reads=[dep], writes=[dep])

    def stage_a(l, x_src):
        with ExitStack() as es:
            sb = lambda name, shape, dty: es.enter_context(nc.sbuf_tensor(U(name), shape, dty))
            ps = [es.enter_context(nc.psum_tensor(U(f"psA{i}"), [128, 512], F32)) for i in range(6)]
            ps += [es.enter_context(nc.psum_tensor(U(f"psA{i}"), [128, 1024], BF16)) for i in (6, 7)]
            ident = sb("ident", [128, 128], F32); identb = sb("identb", [128, 128], BF16)
            cT = sb("cT", [128, 8, 2], F32); sg = sb("sg", [128, 8, 2], F32); sc = sb("sc", [128, 8, 2], F32)
            bT = sb("bT", [128, 24], F32); gn = sb("gn", [128, 8], F32)
            modT = sb("modT_s", [128, 24, 2], F32); Aff = sb("Aff", [128, 8, 2], F32)
            wst = [sb(f"wst{i}", [128, 8, 512], F32) for i in range(4)]
            wA = sb("wA", [128, 8, 1792], BF16)
            GQ = sb("GQ", [128, 64], F32); GK = sb("GK", [128, 64], F32)
            xt = [sb(f"xt{i}", [128, 1024], F32) for i in range(2)]
            junk = sb("junk", [128, 1024], F32)
            ss = sb("ss", [128, 1], F32); rstd = sb("rstd", [128, 1], F32)
            hTt = [sb(f"hTt{i}", [128, 8, 128], BF16) for i in range(2)]
            cs = [sb(f"cs{i}", [128, 2, 64], F32) for i in range(2)]
            sq = sb("sq", [128, 512], F32); ss8 = sb("ss8", [128, 8], F32)
            qn = sb("qn", [128, 512], F32); t1 = sb("t1", [128, 512], F32); t2 = sb("t2", [128, 512], F32)
            qr = [sb(f"qr{i}", [128, 512], BF16) for i in range(2)]
            qTt = [sb(f"qTt{i}", [128, 4, 128], BF16) for i in range(2)]
            kTt = [sb(f"kTt{i}", [128, 4, 128], BF16) for i in range(2)]
            vt = [sb(f"vt{i}", [128, 512], BF16) for i in range(2)]
            ft = [sb(f"ft{i}", [128, 256], BF16) for i in range(2)]
            D = lambda n: Dep(n)
            d_ident, d_identb, d_cT, d_sg, d_sc, d_bT, d_gn, d_modT, d_Aff = [D(n) for n in "ident identb cT sg sc bT gn modT Aff".split()]
            d_wst = [D(f"wst{i}") for i in range(4)]; d_wA = D("wA"); d_G = D("G")
            d_xt = [D("xt0"), D("xt1")]; d_junk = D("junk"); d_ss = D("ss"); d_rstd = D("rstd")
            d_hTt = [D("hTt0"), D("hTt1")]; d_cs = [D("cs0"), D("cs1")]
            d_sq, d_ss8, d_qn, d_t1, d_t2 = D("sq"), D("ss8"), D("qn"), D("t1"), D("t2")
            d_qr = [D("qr0"), D("qr1")]; d_qTt = [D("qTt0"), D("qTt1")]; d_kTt = [D("kTt0"), D("kTt1")]
            d_vt = [D("vt0"), D("vt1")]; d_ft = [D("ft0"), D("ft1")]
            d_ps = [D(f"ps{i}") for i in range(8)]

            make_ident(ident, d_ident)
            kb.op("pool", lambda E: E.tensor_copy(out=identb[:], in_=ident[:]), reads=[d_ident], writes=[d_identb])
            epst = sb("epst", [128, 1], F32); d_eps = D("eps")
            kb.op("pool", lambda E: E.memset(epst[:], EPS), writes=[d_eps])
            EPSB = (epst[:, 0:1], d_eps)
            kb.dma("sp", "ld_c0", cT[:], cvec, writes=[d_cT])
            kb.dma("sp", "ld_c1", bT[:], b_ada[l], writes=[d_bT])
            kb.dma("sp", "ld_c2", gn[:], g_norm[l], writes=[d_gn])
            kb.dma("sp", "ld_c3", GQ[:], g_q[l].partition_broadcast(128), writes=[d_G])
            kb.dma("sp", "ld_c3", GK[:], g_k[l].partition_broadcast(128), writes=[d_G])
            kb.op("act", lambda E: E.activation(out=sg[:], in_=cT[:], func=AF.Sigmoid), reads=[d_cT], writes=[d_sg])
            kb.op("dve", lambda E: E.tensor_tensor(out=sc[:], in0=cT[:], in1=sg[:], op=ALU.mult), reads=[d_cT, d_sg], writes=[d_sc])
            w_ada_v = w_ada[l].rearrange("(k p) n -> p k n", p=128)
            for g in range(6):
                s = g % 4
                kb.dma("sp", f"ld_w{s}", wst[s][:], w_ada_v[:, :, g * 512:(g + 1) * 512], writes=[d_wst[s]])
                for jj in range(4):
                    j = g * 4 + jj
                    for k in range(8):
                        kb.op("pe", lambda E, k=k, jj=jj, s=s, j=j: E.matmul(ps[0][:, 2 * j:2 * j + 2], lhsT=wst[s][:, k, jj * 128:(jj + 1) * 128],
                                                                            rhs=sc[:, k, :], start=(k == 0), stop=(k == 7)),
                              reads=[d_wst[s], d_sc], writes=[d_ps[0]])
            kb.op("dve", lambda E: E.tensor_tensor(out=modT[:], in0=ps[0][:, 0:48].rearrange("p (j n) -> p j n", n=2),
                                                   in1=bT[:].unsqueeze(2).to_broadcast([128, 24, 2]), op=ALU.add),
                  reads=[d_ps[0], d_bT], writes=[d_modT])
            store("st_mod", modT_d, modT[:], [d_modT])
            kb.op("dve", lambda E: E.tensor_scalar(out=Aff[:], in0=modT[:, 8:16, :], scalar1=1.0, scalar2=None, op0=ALU.add), reads=[d_modT], writes=[d_Aff])
            kb.op("dve", lambda E: E.tensor_tensor(out=Aff[:], in0=Aff[:], in1=gn[:].unsqueeze(2).to_broadcast([128, 8, 2]), op=ALU.mult),
                  reads=[d_Aff, d_gn], writes=[d_Aff])
            w_in_v = w_in[l].rearrange("(k p) n -> p k n", p=128)
            for g in range(4):
                s = (g + 2) % 4
                n = 512 if g < 3 else 256
                kb.dma("sp", f"ld_w{s}", wst[s][:, :, 0:n], w_in_v[:, :, g * 512:g * 512 + n], writes=[d_wst[s]])
                e = "pool" if g % 2 == 0 else "dve"
                kb.op(e, lambda E, s=s, n=n, g=g: E.tensor_copy(out=wA[:, :, g * 512:g * 512 + n], in_=wst[s][:, :, 0:n]), reads=[d_wst[s]], writes=[d_wA])

            def load_tile(i):
                s = i % 2
                kb.dma("sp", f"ld_x{s}", xt[s][:], x_src[i * 128:(i + 1) * 128, :], writes=[d_xt[s]])
                if i < NLAT_T:
                    kb.dma("sp", f"ld_cs{s}", cs[s][:, 0, :], cos[i * 128:(i + 1) * 128, :], writes=[d_cs[s]])
                    kb.dma("sp", f"ld_cs{s}", cs[s][:, 1, :], ssin[i * 128:(i + 1) * 128, :], writes=[d_cs[s]])

            G2 = sb("G2", [128, 2, 64], F32)
            sq2 = sb("sq2", [128, 1024], F32); ss16 = sb("ss16", [128, 16], F32)
            qn2 = sb("qn2", [128, 1024], F32); t1b = sb("t1b", [128, 1024], F32); t2b = sb("t2b", [128, 1024], F32)
            QR2 = sb("QR2", [128, 1024], BF16)
            TT2 = [sb(f"TT2_{i}", [128, 8, 128], BF16) for i in range(2)]
            d_G2, d_sq2, d_ss16, d_qn2, d_t1b, d_t2b, d_QR2 = [D(n) for n in "G2 sq2 ss16 qn2 t1b t2b QR2".split()]
            d_TT2 = [D("TT2_0"), D("TT2_1")]
            kb.op("dve", lambda E: E.tensor_copy(out=G2[:, 0, :], in_=GQ[:]), reads=[d_G], writes=[d_G2])
            kb.op("dve", lambda E: E.tensor_copy(out=G2[:, 1, :], in_=GK[:]), reads=[d_G], writes=[d_G2])
            psQK = [ps[2], ps[3]]
            g64 = lambda ap: ap.rearrange("p (g c) -> p g c", c=64)

            def head(i):
                s = i % 2
                var = 0 if i < NLAT_T else 1
                load_tile(i)
                X = xt[s]
                kb.op("act", lambda E: E.activation(out=junk[:], in_=X[:], func=AF.Square, accum_out=ss[:]), reads=[d_xt[s]], writes=[d_junk, d_ss])
                rstd_chain(ss[:], d_ss, rstd[:], d_rstd, 1.0 / 1024, EPSB)
                kb.op("dve", lambda E: E.tensor_scalar(out=X[:], in0=X[:], scalar1=rstd[:, 0:1], scalar2=None, op0=ALU.mult),
                      reads=[d_xt[s], d_rstd], writes=[d_xt[s]])
                for k in range(8):
                    b = k // 4
                    kb.op("pe", lambda E, k=k, b=b: E.transpose(out=ps[b][:, (k % 4) * 128:(k % 4 + 1) * 128], in_=X[:, k * 128:(k + 1) * 128], identity=ident[:]),
                          reads=[d_xt[s], d_ident], writes=[d_ps[b]])
                for k in range(8):
                    b = k // 4
                    src = ps[b][:, (k % 4) * 128:(k % 4 + 1) * 128]
                    if k % 2 == 0:
                        kb.op("act", lambda E, k=k, src=src: E.activation(out=hTt[s][:, k, :], in_=src, func=AF.Identity,
                                                                          scale=Aff[:, k, var:var + 1], bias=modT[:, k, var:var + 1]),
                              reads=[d_ps[b], d_Aff, d_modT], writes=[d_hTt[s]])
                    else:
                        kb.op("dve", lambda E, k=k, src=src: E.tensor_scalar(out=hTt[s][:, k, :], in0=src, scalar1=Aff[:, k, var:var + 1],
                                                                             scalar2=modT[:, k, var:var + 1], op0=ALU.mult, op1=ALU.add),
                              reads=[d_ps[b], d_Aff, d_modT], writes=[d_hTt[s]])
                store(f"st_h{s}", hT_d[:, :, i * 128:(i + 1) * 128].rearrange("k p t -> p k t"), hTt[s][:], [d_hTt[s]])

            def mid(i):
                s = i % 2
                for cb in range(4):
                    n = 512 if cb < 3 else 256
                    for k in range(8):
                        kb.op("pe", lambda E, k=k, cb=cb, n=n: E.matmul(ps[2 + cb][:, 0:n], lhsT=hTt[s][:, k, :], rhs=wA[:, k, cb * 512:cb * 512 + n],
                                                                       start=(k == 0), stop=(k == 7)),
                              reads=[d_hTt[s], d_wA], writes=[d_ps[2 + cb]])

            def tail_a(i):
                s = i % 2
                lat = i < NLAT_T
                for w in range(2):
                    kb.op("act", lambda E, w=w: E.activation(out=sq2[:, w * 512:(w + 1) * 512], in_=psQK[w][:], func=AF.Square), reads=[d_ps[2 + w]], writes=[d_sq2])
                kb.op("dve", lambda E: E.tensor_reduce(out=ss16[:], in_=g64(sq2[:]), axis=AX.X, op=ALU.add), reads=[d_sq2], writes=[d_ss16])
                rstd_chain(ss16[:], d_ss16, ss16[:], d_ss16, 1.0 / 64, EPSB)
                for w in range(2):
                    kb.op("dve", lambda E, w=w: E.tensor_tensor(out=g64(qn2[:, w * 512:(w + 1) * 512]), in0=g64(psQK[w][:]),
                                                           in1=ss16[:, w * 8:(w + 1) * 8].unsqueeze(2).to_broadcast([128, 8, 64]), op=ALU.mult),
                          reads=[d_ps[2 + w], d_ss16], writes=[d_qn2])
                kb.op("act", lambda E: E.activation(out=vt[s][:], in_=ps[4][:], func=AF.Copy), reads=[d_ps[4]], writes=[d_vt[s]])
                kb.op("act", lambda E: E.activation(out=ft[s][:], in_=ps[5][:, 0:256], func=AF.Copy), reads=[d_ps[5]], writes=[d_ft[s]])
                if lat:
                    store(f"st_v{s}", v_lat[i // 8][(i % 8) * 128:(i % 8 + 1) * 128, :], vt[s][:], [d_vt[s]])
                    store(f"st_f{s}", f_lat.rearrange("(g t) c -> t g c", g=4)[i * 128:(i + 1) * 128, :, :], ft[s][:].rearrange("p (g c) -> p g c", c=64), [d_ft[s]])
                else:
                    ic = i - NLAT_T
                    store(f"st_v{s}", v_ctx[ic * 128:(ic + 1) * 128, :], vt[s][:], [d_vt[s]])
                    store(f"st_f{s}", f_ctx[ic * 128:(ic + 1) * 128, :], ft[s][:], [d_ft[s]])

            def tail_b(i):
                s = i % 2
                lat = i < NLAT_T
                qv4 = qn2[:].rearrange("p (w g c) -> p w g c", w=2, c=64)
                G2b = G2[:].unsqueeze(2).to_broadcast([128, 2, 8, 64])
                if lat:
                    for w in range(2):
                        kb.op("dve", lambda E, w=w: E.tensor_tensor(out=qv4[:, w], in0=qv4[:, w], in1=G2[:, w, :].unsqueeze(1).to_broadcast([128, 8, 64]), op=ALU.mult),
                              reads=[d_qn2, d_G2], writes=[d_qn2])
                    kb.op("dve", lambda E: E.tensor_tensor(out=g64(t1b[:]), in0=g64(qn2[:]), in1=cs[s][:, 0, :].unsqueeze(1).to_broadcast([128, 16, 64]), op=ALU.mult),
                          reads=[d_qn2, d_cs[s]], writes=[d_t1b])
                    qv = qn2[:].rearrange("p (g a h c) -> p g a h c", a=2, h=2, c=16)
                    tv = t2b[:].rearrange("p (g a h c) -> p g a h c", a=2, h=2, c=16)
                    sv = cs[s][:, 1, :].rearrange("p (a h c) -> p a h c", a=2, h=2)
                    for hf in range(2):
                        for a in range(2):
                            kb.op("dve", lambda E, hf=hf, a=a: E.tensor_tensor(out=tv[:, :, a, hf, :], in0=qv[:, :, a, 1 - hf, :],
                                                                                in1=sv[:, a, hf, :].unsqueeze(1).to_broadcast([128, 16, 16]), op=ALU.mult),
                                  reads=[d_qn2, d_cs[s]], writes=[d_t2b])
                    kb.op("dve", lambda E: E.tensor_tensor(out=QR2[:], in0=t1b[:], in1=t2b[:], op=ALU.add), reads=[d_t1b, d_t2b], writes=[d_QR2])
                else:
                    QRv = QR2[:].rearrange("p (w g c) -> p w g c", w=2, c=64)
                    for w in range(2):
                        kb.op("dve", lambda E, w=w: E.tensor_tensor(out=QRv[:, w], in0=qv4[:, w], in1=G2[:, w, :].unsqueeze(1).to_broadcast([128, 8, 64]), op=ALU.mult),
                              reads=[d_qn2, d_G2], writes=[d_QR2])
                PT = ps[6]; dPT = d_ps[6]
                for hh in range(8):
                    kb.op("pe", lambda E, hh=hh: E.transpose(out=PT[:, hh * 128:(hh + 1) * 128], in_=QR2[:, hh * 128:(hh + 1) * 128], identity=identb[:]),
                          reads=[d_QR2, d_identb], writes=[dPT])
                TT = TT2[s]; dTT = d_TT2[s]
                kb.op("act", lambda E: E.activation(out=TT[:].rearrange("p h t -> p (h t)"), in_=PT[:, 0:1024], func=AF.Copy), reads=[dPT], writes=[dTT])
                store(f"st_q{s}", qT_d[:, :, i * 128:(i + 1) * 128].rearrange("h p t -> p h t"), TT[:, 0:4, :], [dTT])
                if lat:
                    for hp in range(2):
                        store(f"st_k{s}", kT_lat[hp].rearrange("(h p) t -> p h t", p=128)[:, :, i * 128:(i + 1) * 128], TT[:, 4 + hp * 2:4 + hp * 2 + 2, :], [dTT])
                else:
                    store(f"st_k{s}", kT_ctx[:, :, (i - NLAT_T) * 128:(i - NLAT_T + 1) * 128].rearrange("h p t -> p h t"), TT[:, 4:8, :], [dTT])

            wcb = [sb(f"wcb{i}", [128, 8, 512], BF16) for i in range(2)]
            d_wcb = [D("wcb0"), D("wcb1")]
            w_in_v2 = w_in[l].rearrange("(k p) n -> p k n", p=128)
            pieces = []
            for g in range(9):
                pieces.append((w_in_v2[:, :, O_U + g * 512:O_U + (g + 1) * 512], 128, 8, wC_bf[:, :, g * 512:(g + 1) * 512]))
            for g in range(2):
                cs_ = slice(g * 512, (g + 1) * 512)
                pieces.append((w_br[l, 0:512, :].rearrange("(k p) n -> p k n", p=128)[:, :, cs_], 128, 4, wBR_bf[:, :, cs_]))
                pieces.append((w_br[l, 512:768, :].rearrange("(k p) n -> p k n", p=128)[:, :, cs_], 128, 2, wBRb_bf[:, :, cs_]))
                pieces.append((w_br[l, 768:1024, :].rearrange("(k p) n -> p k n", p=128)[:, :, cs_], 128, 2, wBRc_bf[:, :, cs_]))
                pieces.append((w_out[l].rearrange("(k p) n -> p k n", p=128)[:, :, cs_], 128, 8, wO_bf[:, :, cs_]))

            def side_load(j):
                if j >= len(pieces):
                    return
                src, np_, nk, dst = pieces[j]
                sj = j % 2
                kb.dma("pool", f"ld_sj{sj}", wst[sj][0:np_, 0:nk, :], src, writes=[d_wst[sj]])

            def side_job(j):
                side_load(j + 1)
                if j >= len(pieces):
                    return
                src, np_, nk, dst = pieces[j]
                sj = j % 2
                kb.op("act", lambda E: E.activation(out=wcb[sj][0:np_, 0:nk, :], in_=wst[sj][0:np_, 0:nk, :], func=AF.Copy), reads=[d_wst[sj]], writes=[d_wcb[sj]])
                store(f"st_wc{sj}", dst, wcb[sj][0:np_, 0:nk, :], [d_wcb[sj]])

            side_load(0)
            head(0)
            mid(0)
            head(1)
            for i in range(NT_A):
                side_job(i)
                tail_a(i)
                if i + 1 < NT_A:
                    mid(i + 1)
                tail_b(i)
                if i == NLAT_T - 1:
                    for key in ("st_k0", "st_k1", "st_v0", "st_v1", "st_f0", "st_f1"):
                        nc.gpsimd.wait_ge(kb.sems[key], kb.cnt[key])
                        kb.seen["pool"][key] = kb.cnt[key]
                    collective("AllGather", f_lat, f_ag, d_fag)
                    for ci in range(2):
                        collective("AllGather", kT_lat[ci], kT_ag[ci], d_kvag)
                        collective("AllGather", v_lat[ci], v_ag[ci], d_kvag)
                if i + 2 < NT_A:
                    head(i + 2)
            kb.barrier(skip=("cc",))

    def stage_b(l, last):
        lam_init = 0.8 - 0.6 * math.exp(-0.3 * l)
        with ExitStack() as es:
            sbt = lambda name, shape, dty: es.enter_context(nc.sbuf_tensor(U(name), shape, dty))
            ps = [es.enter_context(nc.psum_tensor(U(f"psF{i}"), [128, 512], F32)) for i in range(8)]
            d_ps = [Dep(f"ps{i}") for i in range(8)]
            X1 = [sbt(f"X1_{i}", [64, 8192], BF16) for i in range(2)]
            Xc = sbt("Xc", [128, 2, 256], BF16)
            CS = sbt("CS", [64, 128], BF16)
            T1 = sbt("T1s", [128, 64, 64], BF16)
            T2 = sbt("T2s", [128, 64, 64], BF16)
            DC = sbt("DC", [128, 2, 512], BF16)
            Bsb2 = [sbt(f"Bsb{i}", [128, 64, 2, 64], BF16) for i in range(2)]
            YT2 = [sbt(f"YT{i}", [64, 64, 2, 32], BF16) for i in range(2)]
            YC = sbt("YC", [64, 2, 256], BF16)
            CC = sbt("CC", [64, 2, 64], F32)
            WF = sbt("WF", [64, 4, 64], F32)
            BF_ = sbt("BF", [64, 4], F32)
            MC = sbt("MC", [64, 4, 2, 64], BF16)
            ob = [sbt(f"ob{i}", [64, 512], F32) for i in range(2)]
            dn = {n: Dep(n) for n in "X1_0 X1_1 Xc CS T1 T2 DC Bsb0 Bsb1 YT0 YT1 YC CC WF BF MC ob0 ob1".split()}
            f_ag_v = f_ag.rearrange("(r g t p) c -> r g t (p c)", r=4, g=4, p=128)

            def load_x1(g):
                s = g % 2
                for r in range(4):
                    kb.dma("sp", f"f_x1{s}", X1[s][r * 16:(r + 1) * 16, :], f_ag_v[r, g], reads=[d_fag], writes=[dn[f"X1_{s}"]])
            load_x1(0)
            if not last:
                kb.dma("sp", "f_xc", Xc[:], f_ctx.rearrange("(k p) c -> p k c", p=128), writes=[dn["Xc"]])
                kb.dma("sp", "f_dc", DC[:], dcd, writes=[dn["DC"]])
            kb.dma("sp", "f_cs", CS[:], cs64, writes=[dn["CS"]])
            kb.dma("sp", "f_cc", CC[:], ccs, writes=[dn["CC"]])
            kb.dma("sp", "f_wf", WF[:], w_f[l].rearrange("g c d -> c g d"), writes=[dn["WF"]])
            kb.dma("sp", "f_bf", BF_[:], b_f[l], writes=[dn["BF"]])
            kb.dma("sp", "f_t1", T1[:], T1d, writes=[dn["T1"]])
            kb.dma("sp", "f_t2", T2[:], T2d, writes=[dn["T2"]])
            for g in range(4):
                for i in range(2):
                    kb.op("pe", lambda E, i=i, g=g: E.matmul(ps[0][0:64, (g * 2 + i) * 64:(g * 2 + i + 1) * 64], lhsT=CC[:, i, :], rhs=WF[:, g, :], start=True, stop=True),
                          reads=[dn["CC"], dn["WF"]], writes=[d_ps[0]])
            kb.op("dve", lambda E: E.tensor_copy(out=MC[:].rearrange("c g i d -> c (g i d)"), in_=ps[0][0:64, 0:512]), reads=[d_ps[0]], writes=[dn["MC"]])
            sc_lat = 1.0 / math.sqrt(8192 * 64); sc_ctx = 1.0 / math.sqrt(256 * 64)
            blkc = [0]
            def f_s1(g):
                s = g % 2
                if g + 1 < 4:
                    load_x1(g + 1)
                X1v = X1[s][:].rearrange("t (p c) -> t p c", c=64)
                Bsb = Bsb2[s]; dBsb = dn[f"Bsb{s}"]
                for c4 in range(16):
                    b = 1 + c4 % 2
                    for cc in range(4):
                        c = c4 * 4 + cc
                        kb.op("pe", lambda E, c=c, cc=cc, b=b: E.matmul(ps[b][:, cc * 128:(cc + 1) * 128], lhsT=X1v[:, :, c], rhs=CS[:], start=True, stop=True),
                              reads=[dn[f"X1_{s}"], dn["CS"]], writes=[d_ps[b]])
                    dst = Bsb[:, c4 * 4:(c4 + 1) * 4, :, :].rearrange("p c r k -> p (c r k)")
                    if c4 % 2 == 0:
                        kb.op("act", lambda E, dst=dst, b=b: E.activation(out=dst, in_=ps[b][:], func=AF.Copy), reads=[d_ps[b]], writes=[dBsb])
                    else:
                        kb.op("dve", lambda E, dst=dst, b=b: E.tensor_copy(out=dst, in_=ps[b][:]), reads=[d_ps[b]], writes=[dBsb])

            def f_s3(g):
                s = g % 2
                Bsb = Bsb2[s]; YT = YT2[s]; dBsb = dn[f"Bsb{s}"]; dYT = dn[f"YT{s}"]
                for k8 in range(8):
                    b = 3 + k8 % 2
                    for ki in range(8):
                        kbi = k8 * 8 + ki
                        o = ps[b][0:64, ki * 64:(ki + 1) * 64]
                        kb.op("pe", lambda E, kbi=kbi, o=o: E.matmul(o, lhsT=Bsb[:, :, 0, kbi], rhs=T1[:, kbi, :], start=True, stop=False),
                              reads=[dBsb, dn["T1"]], writes=[d_ps[b]])
                        kb.op("pe", lambda E, kbi=kbi, o=o: E.matmul(o, lhsT=Bsb[:, :, 1, kbi], rhs=T2[:, kbi, :], start=False, stop=True),
                              reads=[dBsb, dn["T2"]], writes=[d_ps[b]])
                    dst = YT[:, k8 * 8:(k8 + 1) * 8, :, :].rearrange("c k r a -> c (k r a)")
                    if k8 % 2 == 0:
                        kb.op("act", lambda E, dst=dst, b=b: E.activation(out=dst, in_=ps[b][0:64, :], func=AF.Copy), reads=[d_ps[b]], writes=[dYT])
                    else:
                        kb.op("dve", lambda E, dst=dst, b=b: E.tensor_copy(out=dst, in_=ps[b][0:64, :]), reads=[d_ps[b]], writes=[dYT])

            def f_s4(g):
                s = g % 2
                YT = YT2[s]; dYT = dn[f"YT{s}"]
                nblk = 4
                if not last:
                    for k in range(2):
                        kb.op("pe", lambda E, k=k, g=g: E.matmul(ps[5][0:64, :], lhsT=Xc[:, k, g * 64:(g + 1) * 64], rhs=DC[:, k, :], start=(k == 0), stop=(k == 1)),
                              reads=[dn["Xc"], dn["DC"]], writes=[d_ps[5]])
                    kb.op("dve", lambda E: E.tensor_copy(out=YC[:].rearrange("c r k -> c (r k)"), in_=ps[5][0:64, :]), reads=[d_ps[5]], writes=[dn["YC"]])
                    nblk = 5
                for blk in range(nblk):
                    b = 6 + blkc[0] % 2
                    so = blkc[0] % 2
                    blkc[0] += 1
                    if blk < 4:
                        n = 512; scl = sc_lat; dy = dYT
                        r0 = YT[:, :, 0, blk * 8:(blk + 1) * 8].rearrange("c k a -> c a k")
                        r1 = YT[:, :, 1, blk * 8:(blk + 1) * 8].rearrange("c k a -> c a k")
                    else:
                        n = 256; r0 = YC[:, 0, :]; r1 = YC[:, 1, :]; scl = sc_ctx; dy = dn["YC"]
                    kb.op("pe", lambda E, r0=r0, n=n, b=b, g=g: E.matmul(ps[b][0:64, 0:n], lhsT=MC[:, g, 0, :], rhs=r0, start=True, stop=False), reads=[dn["MC"], dy], writes=[d_ps[b]])
                    kb.op("pe", lambda E, r1=r1, n=n, b=b, g=g: E.matmul(ps[b][0:64, 0:n], lhsT=MC[:, g, 1, :], rhs=r1, start=False, stop=True), reads=[dn["MC"], dy], writes=[d_ps[b]])
                    kb.op("act", lambda E, n=n, b=b, so=so, scl=scl, g=g: E.activation(out=ob[so][:, 0:n], in_=ps[b][0:64, 0:n], func=AF.Identity, scale=scl, bias=BF_[:, g:g + 1]),
                          reads=[d_ps[b], dn["BF"]], writes=[dn[f"ob{so}"]])
                    store(f"st_ob{so}", obT_d[g, :, blk * 512:blk * 512 + n], ob[so][:, 0:n], [dn[f"ob{so}"]])

            f_s1(0)
            for g in range(4):
                if g + 1 < 4:
                    f_s1(g + 1)
                f_s3(g)
                f_s4(g)
            kb.barrier()

        with ExitStack() as es:
            sbt = lambda name, shape, dty: es.enter_context(nc.sbuf_tensor(U(name), shape, dty))
            psS = [es.enter_context(nc.psum_tensor(U(f"psS{i}"), [128, 1024], F32)) for i in range(2)]
            ps = [None] * 4 + [es.enter_context(nc.psum_tensor(U(f"psB{i}"), [128, 512], F32)) for i in range(4, 8)]
            d_psS = [Dep("psS0"), Dep("psS1")]
            d_ps = [Dep(f"ps{i}") for i in range(8)]
            kTh = [sbt(f"kTh{i}", [128, NKEY], BF16) for i in range(2)]
            vh = [sbt(f"vh{i}", [128, NKT, 128], BF16) for i in range(2)]
            qTh = [sbt(f"qTh{i}", [128, TOKS], BF16) for i in range(2)]
            NPT = 4
            pt = [sbt(f"pt{i}", [128, 1024], BF16) for i in range(NPT)]
            ones = sbt("ones", [128, 128], F32)
            onesb = sbt("onesb", [128, 32], BF16)
            sel = sbt("sel", [128, 2, 128], F32)
            rsb = sbt("rsb", [128, 512], F32)
            lv = sbt("lv", [128, 4, 64], F32); lt = sbt("lt", [128, 2, 64], F32); l2 = sbt("l2", [128, 2], F32)
            nlam = sbt("nlam", [128, 1], F32); gs = sbt("gs", [128, 1], F32)
            rec = [sbt(f"rec{c}", [128, 512], F32) for c in range(2)]
            o0 = sbt("o0", [128, 512], F32); o1 = sbt("o1", [128, 512], F32)
            osq = sbt("osq", [128, 512], F32); rs = sbt("rs", [128, 512], F32)
            of = [sbt(f"of{i}", [128, 512], F32) for i in range(2)]
            d_kv = [Dep("kv0"), Dep("kv1")]
            d_pt = [Dep(f"pt{i}") for i in range(NPT)]
            dm = {n: Dep(n) for n in "ones onesb sel rsb lv lt l2 nlam gs rec0 rec1 o0 o1 osq rs of0 of1".split()}
            kb.op("pool", lambda E: E.memset(ones[:], 1.0), writes=[dm["ones"]])
            kb.op("pool", lambda E: E.memset(onesb[:], 1.0), writes=[dm["onesb"]])
            kb.op("pool", lambda E: E.memset(sel[:], 0.0), writes=[dm["sel"]])
            for (p0, p1, c, val) in ((0, 32, 0, 1.0 / 32), (64, 96, 0, 1.0 / 32), (32, 64, 1, 1.0 / 32), (64, 128, 1, 1.0 / 32), (64, 96, 1, 0.0)):
                kb.op("pool", lambda E, p0=p0, p1=p1, c=c, val=val: E.memset(sel[p0:p1, c, :], val), reads=[dm["sel"]], writes=[dm["sel"]])
            kb.dma("sp", "a_lv", lv[:].rearrange("p a c -> p (a c)"), lamv[l].partition_broadcast(128), writes=[dm["lv"]])
            kb.dma("sp", "a_gs", gs[:], g_sub[l], writes=[dm["gs"]])
            lvv = lv[:].rearrange("p (i j) c -> p i j c", j=2)
            kb.op("dve", lambda E: E.tensor_tensor(out=lt[:], in0=lvv[:, :, 0, :], in1=lvv[:, :, 1, :], op=ALU.mult), reads=[dm["lv"]], writes=[dm["lt"]])
            kb.op("dve", lambda E: E.tensor_reduce(out=l2[:], in_=lt[:], axis=AX.X, op=ALU.add), reads=[dm["lt"]], writes=[dm["l2"]])
            kb.op("act", lambda E: E.activation(out=l2[:], in_=l2[:], func=AF.Exp), reads=[dm["l2"]], writes=[dm["l2"]])
            kb.op("dve", lambda E: E.tensor_tensor(out=nlam[:], in0=l2[:, 1:2], in1=l2[:, 0:1], op=ALU.subtract), reads=[dm["l2"]], writes=[dm["nlam"]])
            kb.op("dve", lambda E: E.tensor_scalar(out=nlam[:], in0=nlam[:], scalar1=-lam_init, scalar2=None, op0=ALU.add), reads=[dm["nlam"]], writes=[dm["nlam"]])
            kb.op("dve", lambda E: E.tensor_scalar(out=gs[:], in0=gs[:], scalar1=(1.0 - lam_init), scalar2=None, op0=ALU.mult), reads=[dm["gs"]], writes=[dm["gs"]])
            kT_ag_v = [a.rearrange("(r h p) t -> h p r t", r=4, h=2) for a in kT_ag]
            v_ag_v = [a.rearrange("(r t p) e -> p r t e", r=4, p=128) for a in v_ag]
            v_ctx_v = v_ctx.rearrange("(t p) e -> p t e", p=128)

            def load_head(h):
                s = h % 2
                kb.dma("sp", f"a_k{s}", kTh[s][:, 0:8192].rearrange("p (r t) -> p r t", r=4), kT_ag_v[h // 2][h % 2], reads=[d_kvag], writes=[d_kv[s]])
                kb.dma("sp", f"a_k{s}", kTh[s][:, 8192:NKEY], kT_ctx[h], writes=[d_kv[s]])
                for half in range(2):
                    for r in range(4):
                        kt0 = r * 16 + half * 8
                        kb.dma("sp", f"a_k{s}", vh[s][:, kt0:kt0 + 8, :], v_ag_v[half][:, r, :, h * 128:(h + 1) * 128], reads=[d_kvag], writes=[d_kv[s]])
                kb.dma("sp", f"a_k{s}", vh[s][:, 64:66, :], v_ctx_v[:, :, h * 128:(h + 1) * 128], writes=[d_kv[s]])
                kb.dma("sp", f"a_k{s}", qTh[s][:], qT_d[h], writes=[d_kv[s]])

            deferred = []

            def make_epilogue(h, q0, nq, so):
                def e_sel(c):
                    kb.op("pe", lambda E: E.matmul(ps[7][:, 0:nq], lhsT=sel[:, c, :], rhs=rsb[:, 0:nq], start=True, stop=True),
                          reads=[dm["sel"], dm["rsb"]], writes=[d_ps[7]])
                    kb.op("dve", lambda E: E.reciprocal(out=rec[c][:, 0:nq], in_=ps[7][:, 0:nq]), reads=[d_ps[7]], writes=[dm[f"rec{c}"]])

                def e_comb():
                    kb.op("dve", lambda E: E.tensor_tensor(out=o0[:, 0:nq], in0=o0[:, 0:nq], in1=rec[0][:, 0:nq], op=ALU.mult), reads=[dm["o0"], dm["rec0"]], writes=[dm["o0"]])
                    kb.op("dve", lambda E: E.tensor_tensor(out=o1[:, 0:nq], in0=o1[:, 0:nq], in1=rec[1][:, 0:nq], op=ALU.mult), reads=[dm["o1"], dm["rec1"]], writes=[dm["o1"]])
                    kb.op("dve", lambda E: E.scalar_tensor_tensor(out=o0[:, 0:nq], in0=o1[:, 0:nq], scalar=nlam[:, 0:1], in1=o0[:, 0:nq], op0=ALU.mult, op1=ALU.add),
                          reads=[dm["o0"], dm["o1"], dm["nlam"]], writes=[dm["o0"]])
                    kb.op("dve", lambda E: E.tensor_tensor(out=osq[:, 0:nq], in0=o0[:, 0:nq], in1=o0[:, 0:nq], op=ALU.mult), reads=[dm["o0"]], writes=[dm["osq"]])

                def e_norm_mm():
                    kb.op("pe", lambda E: E.matmul(ps[7][:, 0:nq], lhsT=ones[:], rhs=osq[:, 0:nq], start=True, stop=True), reads=[dm["ones"], dm["osq"]], writes=[d_ps[7]])
                    kb.op("dve", lambda E: E.tensor_scalar(out=rs[:, 0:nq], in0=ps[7][:, 0:nq], scalar1=1.0 / 128, scalar2=EPS, op0=ALU.mult, op1=ALU.add),
                          reads=[d_ps[7]], writes=[dm["rs"]])

                def e_act():
                    kb.op("act", lambda E: E.activation(out=rs[:, 0:nq], in_=rs[:, 0:nq], func=AF.Ln), reads=[dm["rs"]], writes=[dm["rs"]])
                    kb.op("act", lambda E: E.activation(out=rs[:, 0:nq], in_=rs[:, 0:nq], func=AF.Exp, scale=-0.5), reads=[dm["rs"]], writes=[dm["rs"]])

                def e_fin():
                    kb.op("dve", lambda E: E.scalar_tensor_tensor(out=of[so][:, 0:nq], in0=o0[:, 0:nq], scalar=gs[:, 0:1], in1=rs[:, 0:nq], op0=ALU.mult, op1=ALU.mult),
                          reads=[dm["o0"], dm["gs"], dm["rs"]], writes=[dm[f"of{so}"]])
                    store(f"st_oa{so}", oaT_d[h, :, q0:q0 + nq], of[so][:, 0:nq], [dm[f"of{so}"]])
                return [(3, lambda: e_sel(0)), (6, lambda: e_sel(1)), (9, e_comb), (14, e_norm_mm), (18, e_act), (22, e_fin)]

            def run_deferred(upto):
                while deferred and deferred[0][0] <= upto:
                    deferred.pop(0)[1]()

            load_head(0)
            blk_id = 0
            for h in range(4):
                s = h % 2
                if h + 1 < 4:
                    load_head(h + 1)
                for qb in range(4 if last else 5):
                    if qb < 4:
                        q0 = qb * 512; nq = 512; kts = list(range(NKT))
                    else:
                        q0 = 2048; nq = 256; kts = [64, 65]
                    nk = len(kts)

                    def scores(idx):
                        kt = kts[idx]; sl = idx % 2
                        for c in range(2):
                            kb.op("pe", lambda E, c=c, kt=kt, sl=sl: E.matmul(psS[sl][:, c * 512:c * 512 + nq], lhsT=kTh[s][c * 64:(c + 1) * 64, kt * 128:(kt + 1) * 128],
                                                                           rhs=qTh[s][c * 64:(c + 1) * 64, q0:q0 + nq], start=True, stop=True, tile_position=(64 * c, 0)),
                                  reads=[d_kv[s]], writes=[d_psS[sl]])

                    def exps(idx):
                        sl = idx % 2; p = idx % NPT
                        kb.op("act", lambda E, sl=sl, p=p: E.activation(out=pt[p][:].rearrange("p (c q) -> p c q", c=2)[:, :, 0:nq],
                                                                        in_=psS[sl][:].rearrange("p (c q) -> p c q", c=2)[:, :, 0:nq], func=AF.Exp, scale=0.125),
                              reads=[d_psS[sl]], writes=[d_pt[p]])

                    def av(idx):
                        kt = kts[idx]; p = idx % NPT
                        for c in range(2):
                            kb.op("pe", lambda E, c=c, kt=kt, p=p, idx=idx: E.matmul(ps[4 + c][:, 0:nq], lhsT=vh[s][:, kt, :], rhs=pt[p][:, c * 512:c * 512 + nq],
                                                                                  start=(idx == 0), stop=(idx == nk - 1)),
                                  reads=[d_kv[s], d_pt[p]], writes=[d_ps[4 + c]])

                    def rowsums(idxs):
                        for idx in idxs:
                            p = idx % NPT
                            for c in range(2):
                                g4 = (idx % 2) * 2 + c
                                kb.op("pe", lambda E, c=c, p=p, idx=idx, g4=g4: E.matmul(ps[6][32 * g4:32 * g4 + 32, 0:nq], lhsT=onesb[:], rhs=pt[p][:, c * 512:c * 512 + nq],
                                                                                       start=(idx < 2), stop=(idx >= nk - 2), tile_position=(0, 32 * g4)),
                                      reads=[dm["onesb"], d_pt[p]], writes=[d_ps[6]])

                    scores(0)
                    pend = []

                    def av_rs(i2):
                        av(i2)
                        pend.append(i2)
                        if len(pend) == 2 or i2 == nk - 1:
                            rowsums(list(pend))
                            del pend[:]
                    for idx in range(nk):
                        run_deferred(idx)
                        exps(idx)
                        if idx + 1 < nk:
                            scores(idx + 1)
                        if idx >= 1:
                            av_rs(idx - 1)
                    av_rs(nk - 1)
                    run_deferred(10 ** 9)
                    kb.op("dve", lambda E: E.tensor_copy(out=rsb[:, 0:nq], in_=ps[6][:, 0:nq]), reads=[d_ps[6]], writes=[dm["rsb"]])
                    kb.op("dve", lambda E: E.tensor_copy(out=o0[:, 0:nq], in_=ps[4][:, 0:nq]), reads=[d_ps[4]], writes=[dm["o0"]])
                    kb.op("dve", lambda E: E.tensor_copy(out=o1[:, 0:nq], in_=ps[5][:, 0:nq]), reads=[d_ps[5]], writes=[dm["o1"]])
                    deferred.extend(make_epilogue(h, q0, nq, blk_id % 2))
                    blk_id += 1
            run_deferred(10 ** 9)
            kb.barrier()

    def stage_c(l, last, x_src, x_dst):
        with ExitStack() as es0:
            sb = lambda name, shape, dty: es0.enter_context(nc.sbuf_tensor(U(name), shape, dty))
            ps = [es0.enter_context(nc.psum_tensor(U(f"psC{i}"), [128, 512], F32)) for i in range(8)]
            d_ps = [Dep(f"ps{i}") for i in range(8)]
            wC = sb("wC", [128, 8, 4608], BF16)
            wBR = sb("wBR", [128, 4, 1024], BF16)
            wBRb = sb("wBRb", [128, 2, 1024], BF16)
            wBRc = sb("wBRc", [128, 2, 1024], BF16)
            wO = sb("wO", [128, 8, 1024], BF16)
            wS = sb("wS", [128, 4, 128], BF16)
            LG = sb("LG", [128, 256], F32); LB = sb("LB", [128, 256], F32)
            BS = sb("BS", [128, 2, 128], F32)
            modT = sb("modT_s", [128, 24, 2], F32)
            G = [sb(f"G{i}", [128, 1024], F32) for i in range(2)]
            ident = sb("ident", [128, 128], F32); ones = sb("ones", [128, 128], F32)
            dw = {n: Dep(n) for n in "wC wBR wBRb wBRc wO wS LG LB BS modT G ident ones".split()}
            with ExitStack() as es:
                sbt = lambda name, shape, dty: es.enter_context(nc.sbuf_tensor(U(name), shape, dty))
                wst = [sbt(f"wst{i}", [128, 8, 512], F32) for i in range(2)]
                d_wst = [Dep("wst0"), Dep("wst1")]
                wsf = sbt("wsf", [128, 4, 128], F32); diag = sbt("diag", [128, 128], F32)
                d_wsf = Dep("wsf"); d_diag = Dep("diag")
                make_ident(ident, dw["ident"])
                kb.op("pool", lambda E: E.memset(ones[:], 1.0), writes=[dw["ones"]])
                kb.dma("sp", "c_mod", modT[:], modT_d, writes=[dw["modT"]])
                kb.dma("sp", "c_lg", LG[:], ln_g[l].partition_broadcast(128), writes=[dw["LG"]])
                kb.dma("sp", "c_lb", LB[:], ln_b[l].partition_broadcast(128), writes=[dw["LB"]])
                kb.dma("sp", "c_bs", BS[:], bs_d[l], writes=[dw["BS"]])
                kb.dma("sp", "c_ws", wsf[:], w_sT[l], writes=[d_wsf])
                kb.op("dve", lambda E: E.tensor_copy(out=wS[:], in_=wsf[:]), reads=[d_wsf], writes=[dw["wS"]])
                for var in range(1 if last else 2):
                    for k in range(8):
                        kb.op("dve", lambda E, k=k, var=var: E.tensor_scalar(out=diag[:], in0=ident[:], scalar1=modT[:, 16 + k, var:var + 1], scalar2=None, op0=ALU.mult),
                              reads=[dw["ident"], dw["modT"]], writes=[d_diag])
                        b = k // 4
                        kb.op("pe", lambda E, k=k, b=b: E.matmul(ps[b][:, (k % 4) * 128:(k % 4 + 1) * 128], lhsT=ones[:], rhs=diag[:], start=True, stop=True),
                              reads=[dw["ones"], d_diag], writes=[d_ps[b]])
                    for b in range(2):
                        kb.op("act", lambda E, b=b, var=var: E.activation(out=G[var][:, b * 512:(b + 1) * 512], in_=ps[b][:], func=AF.Copy), reads=[d_ps[b]], writes=[dw["G"]])
                kb.dma("sp", "c_w0", wC[:, :, 0:2304], wC_bf[:, :, 0:2304], writes=[dw["wC"]])
                kb.dma("sp", "c_w1", wC[:, :, 2304:4608], wC_bf[:, :, 2304:4608], writes=[dw["wC"]])
                kb.dma("sp", "c_w2", wBR[:], wBR_bf, writes=[dw["wBR"]])
                kb.dma("sp", "c_w3", wBRb[:], wBRb_bf, writes=[dw["wBRb"]])
                kb.dma("sp", "c_w4", wBRc[:], wBRc_bf, writes=[dw["wBRc"]])
                kb.dma("sp", "c_w5", wO[:], wO_bf, writes=[dw["wO"]])
                kb.barrier()

            hTb = sb("hTb", [128, 8, BT], BF16)
            oab = sb("oab", [128, 4, BT], F32)
            obb = sb("obb", [128, 2, BT], F32)
            uT = sb("uT", [128, 2, BT], F32)
            gcT = sb("gcT", [128, 2, BT], F32)
            sgt = [sb(f"sgt{i}", [128, BT], F32) for i in range(4)]
            og = sb("og", [128, 4, BT], BF16)
            ogb = sb("ogb", [128, 2, BT], BF16)
            ogc = sb("ogc", [128, 2, BT], BF16)
            st6 = sb("st6", [128, 6], F32); mv = sb("mv", [128, 2], F32); rstd = sb("rstd", [128, 1], F32)
            vcn = sb("vcn", [128, 256], F32); vnb = sb("vnb", [128, 256], BF16)
            sT = sb("sT", [128, 2, 128], F32)
            mm = [sb(f"mm{i}", [128, 3, BT], F32) for i in range(2)]
            t0 = sb("t0", [128, BT], F32); t1 = sb("t1", [128, BT], F32)
            yT = sb("yT", [128, 8, BT], BF16)
            xt = [sb(f"xt{i}", [128, 1024], F32) for i in range(2)]
            tmpo = sb("tmpo", [128, 1024], F32)
            dn = {n: Dep(n) for n in "hTb oab obb uT gcT sgt0 sgt1 sgt2 sgt3 og ogb ogc st6 mv rstd vcn vnb sT mm0 mm1 t0 t1 yT xt0 xt1 tmpo".split()}

            def proj_fm(col0, M, nt, bank):
                for k in range(8):
                    kb.op("pe", lambda E, k=k: E.matmul(ps[bank][0:M, 0:nt], lhsT=wC[:, k, col0:col0 + M], rhs=hTb[:, k, 0:nt], start=(k == 0), stop=(k == 7)),
                          reads=[dw["wC"], dn["hTb"]], writes=[d_ps[bank]])
            pj = [0]

            def next_bank():
                pj[0] += 1
                return 6 + pj[0] % 2
            gj = [0]

            def gate_bank():
                gj[0] += 1
                return (0, 1, 2, 6, 7)[gj[0] % 5]
            GC0 = O_GATE - O_U
            MC0 = O_MERGE - O_U
            for blk in range(8 if last else 9):
                tok0 = blk * BT; nt = BT; var = 0 if tok0 < 2048 else 1
                ntile = nt // 128
                kb.dma("sp", "c_h", hTb[:, :, 0:nt], hT_d[:, :, tok0:tok0 + nt].rearrange("k p t -> p k t"), writes=[dn["hTb"]])
                kb.dma("sp", "c_oa", oab[:, :, 0:nt], oaT_d[:, :, tok0:tok0 + nt].rearrange("h p t -> p h t"), writes=[dn["oab"]])
                kb.dma("sp", "c_ob", obb[:, :, 0:nt], obT_d.rearrange("(gp g2) c t -> (g2 c) gp t", g2=2)[:, :, tok0:tok0 + nt], writes=[dn["obb"]])
                for gp in range(2):
                    b = gate_bank()
                    proj_fm(gp * 128, 128, nt, b)
                    kb.op("act", lambda E, gp=gp, b=b: E.activation(out=uT[:, gp, 0:nt], in_=ps[b][:, 0:nt], func=AF.Copy), reads=[d_ps[b]], writes=[dn["uT"]])
                for br in range(2):
                    for gp in range(2):
                        b = gate_bank(); s = (br * 2 + gp) % 4
                        proj_fm(GC0 + 512 + br * 256 + gp * 128, 128, nt, b)
                        kb.op("act", lambda E, b=b, s=s: E.activation(out=sgt[s][:, 0:nt], in_=ps[b][:, 0:nt], func=AF.Sigmoid), reads=[d_ps[b]], writes=[dn[f"sgt{s}"]])
                        if br == 1:
                            kb.op("dve", lambda E, gp=gp, b=b, s=s: E.tensor_tensor(out=gcT[:, gp, 0:nt], in0=ps[b][:, 0:nt], in1=sgt[s][:, 0:nt], op=ALU.mult),
                                  reads=[d_ps[b], dn[f"sgt{s}"]], writes=[dn["gcT"]])
                        else:
                            kb.op("dve", lambda E, b=b, s=s: E.tensor_tensor(out=sgt[s][:, 0:nt], in0=ps[b][:, 0:nt], in1=sgt[s][:, 0:nt], op=ALU.mult),
                                  reads=[d_ps[b], dn[f"sgt{s}"]], writes=[dn[f"sgt{s}"]])
                            kb.op("dve", lambda E, gp=gp, s=s: E.tensor_tensor(out=ogb[:, gp, 0:nt], in0=sgt[s][:, 0:nt], in1=obb[:, gp, 0:nt], op=ALU.mult),
                                  reads=[dn[f"sgt{s}"], dn["obb"]], writes=[dn["ogb"]])
                for j in range(4):
                    b = gate_bank(); s = j % 4
                    proj_fm(GC0 + j * 128, 128, nt, b)
                    kb.op("act", lambda E, b=b, s=s: E.activation(out=sgt[s][:, 0:nt], in_=ps[b][:, 0:nt], func=AF.Sigmoid), reads=[d_ps[b]], writes=[dn[f"sgt{s}"]])
                    kb.op("dve", lambda E, b=b, s=s: E.tensor_tensor(out=sgt[s][:, 0:nt], in0=ps[b][:, 0:nt], in1=sgt[s][:, 0:nt], op=ALU.mult),
                          reads=[d_ps[b], dn[f"sgt{s}"]], writes=[dn[f"sgt{s}"]])
                    kb.op("dve", lambda E, j=j, s=s: E.tensor_tensor(out=og[:, j, 0:nt], in0=sgt[s][:, 0:nt], in1=oab[:, j, 0:nt], op=ALU.mult),
                          reads=[dn[f"sgt{s}"], dn["oab"]], writes=[dn["og"]])
                for t in range(ntile):
                    b = next_bank()
                    for k in range(8):
                        kb.op("pe", lambda E, k=k, t=t, b=b: E.matmul(ps[b][:, 0:256], lhsT=hTb[:, k, t * 128:(t + 1) * 128], rhs=wC[:, k, 256:512], start=(k == 0), stop=(k == 7)),
                              reads=[dw["wC"], dn["hTb"]], writes=[d_ps[b]])
                    kb.op("dve", lambda E, b=b: E.bn_stats(out=st6[:], in_=ps[b][:, 0:256]), reads=[d_ps[b]], writes=[dn["st6"]])
                    kb.op("dve", lambda E: E.bn_aggr(out=mv[:], in_=st6[:]), reads=[dn["st6"]], writes=[dn["mv"]])
                    kb.op("dve", lambda E: E.tensor_scalar(out=rstd[:], in0=mv[:, 1:2], scalar1=EPS, scalar2=None, op0=ALU.add), reads=[dn["mv"]], writes=[dn["rstd"]])
                    kb.op("act", lambda E: E.activation(out=rstd[:], in_=rstd[:], func=AF.Sqrt), reads=[dn["rstd"]], writes=[dn["rstd"]])
                    kb.op("dve", lambda E: E.reciprocal(out=rstd[:], in_=rstd[:]), reads=[dn["rstd"]], writes=[dn["rstd"]])
                    kb.op("dve", lambda E, b=b: E.tensor_scalar(out=vcn[:], in0=ps[b][:, 0:256], scalar1=mv[:, 0:1], scalar2=rstd[:, 0:1], op0=ALU.subtract, op1=ALU.mult),
                          reads=[d_ps[b], dn["mv"], dn["rstd"]], writes=[dn["vcn"]])
                    kb.op("dve", lambda E: E.tensor_tensor(out=vcn[:], in0=vcn[:], in1=LG[:], op=ALU.mult), reads=[dn["vcn"], dw["LG"]], writes=[dn["vcn"]])
                    kb.op("dve", lambda E: E.tensor_tensor(out=vnb[:], in0=vcn[:], in1=LB[:], op=ALU.add), reads=[dn["vcn"], dw["LB"]], writes=[dn["vnb"]])
                    b2 = next_bank()
                    for g in range(4):
                        kb.op("pe", lambda E, g=g, b2=b2: E.matmul(ps[b2][(g % 2) * 64:(g % 2) * 64 + 64, (g // 2) * 128:(g // 2 + 1) * 128], lhsT=vnb[:, g * 64:(g + 1) * 64],
                                                                   rhs=wS[:, g, :], start=True, stop=True, tile_position=(0, 64 * (g % 2))),
                              reads=[dn["vnb"], dw["wS"]], writes=[d_ps[b2]])
                    kb.op("dve", lambda E, b2=b2: E.tensor_tensor(out=sT[:].rearrange("c g p -> c (g p)"), in0=ps[b2][:, 0:256], in1=BS[:].rearrange("c g p -> c (g p)"), op=ALU.add),
                          reads=[d_ps[b2], dw["BS"]], writes=[dn["sT"]])
                    kb.op("dve", lambda E, t=t: E.tensor_tensor(out=sT[:], in0=sT[:], in1=uT[:, :, t * 128:(t + 1) * 128], op=ALU.mult), reads=[dn["sT"], dn["uT"]], writes=[dn["sT"]])
                    kb.op("dve", lambda E, t=t: E.tensor_tensor(out=ogc[:, :, t * 128:(t + 1) * 128], in0=sT[:], in1=gcT[:, :, t * 128:(t + 1) * 128], op=ALU.mult),
                          reads=[dn["sT"], dn["gcT"]], writes=[dn["ogc"]])
                for dc in range(8):
                    dsl = slice(dc * 128, (dc + 1) * 128)
                    ms = dc % 2
                    for i in range(3):
                        proj_fm(MC0 + i * 1024 + dc * 128, 128, nt, 3 + i)
                        kb.op("act", lambda E, i=i, ms=ms: E.activation(out=mm[ms][:, i, 0:nt], in_=ps[3 + i][:, 0:nt], func=AF.Sigmoid), reads=[d_ps[3 + i]], writes=[dn[f"mm{ms}"]])
                    for e in range(4):
                        kb.op("pe", lambda E, e=e: E.matmul(ps[0][:, 0:nt], lhsT=wBR[:, e, dsl], rhs=og[:, e, 0:nt], start=(e == 0), stop=(e == 3)),
                              reads=[dw["wBR"], dn["og"]], writes=[d_ps[0]])
                    for g in range(2):
                        kb.op("pe", lambda E, g=g: E.matmul(ps[1][:, 0:nt], lhsT=wBRb[:, g, dsl], rhs=ogb[:, g, 0:nt], start=(g == 0), stop=(g == 1)),
                              reads=[dw["wBRb"], dn["ogb"]], writes=[d_ps[1]])
                    for g in range(2):
                        kb.op("pe", lambda E, g=g: E.matmul(ps[2][:, 0:nt], lhsT=wBRc[:, g, dsl], rhs=ogc[:, g, 0:nt], start=(g == 0), stop=(g == 1)),
                              reads=[dw["wBRc"], dn["ogc"]], writes=[d_ps[2]])
                    kb.op("dve", lambda E, ms=ms: E.tensor_tensor(out=t0[:, 0:nt], in0=ps[0][:, 0:nt], in1=mm[ms][:, 0, 0:nt], op=ALU.mult), reads=[d_ps[0], dn[f"mm{ms}"]], writes=[dn["t0"]])
                    kb.op("dve", lambda E, ms=ms: E.tensor_tensor(out=t1[:, 0:nt], in0=ps[1][:, 0:nt], in1=mm[ms][:, 1, 0:nt], op=ALU.mult), reads=[d_ps[1], dn[f"mm{ms}"]], writes=[dn["t1"]])
                    kb.op("dve", lambda E: E.tensor_tensor(out=t0[:, 0:nt], in0=t0[:, 0:nt], in1=t1[:, 0:nt], op=ALU.add), reads=[dn["t0"], dn["t1"]], writes=[dn["t0"]])
                    kb.op("dve", lambda E, ms=ms: E.tensor_tensor(out=t1[:, 0:nt], in0=ps[2][:, 0:nt], in1=mm[ms][:, 2, 0:nt], op=ALU.mult), reads=[d_ps[2], dn[f"mm{ms}"]], writes=[dn["t1"]])
                    kb.op("dve", lambda E, dc=dc: E.tensor_tensor(out=yT[:, dc, 0:nt], in0=t0[:, 0:nt], in1=t1[:, 0:nt], op=ALU.add), reads=[dn["t0"], dn["t1"]], writes=[dn["yT"]])
                for t in range(ntile):
                    gt = (tok0 // 128) + t
                    s = gt % 2
                    kb.dma("sp", f"c_x{s}", xt[s][:], x_src[gt * 128:(gt + 1) * 128, :], writes=[dn[f"xt{s}"]])
                    for cb in range(2):
                        b = next_bank()
                        for k in range(8):
                            kb.op("pe", lambda E, k=k, cb=cb, b=b, t=t: E.matmul(ps[b][:], lhsT=yT[:, k, t * 128:(t + 1) * 128], rhs=wO[:, k, cb * 512:(cb + 1) * 512],
                                                                              start=(k == 0), stop=(k == 7)),
                                  reads=[dn["yT"], dw["wO"]], writes=[d_ps[b]])
                        kb.op("dve", lambda E, cb=cb, b=b: E.tensor_tensor(out=tmpo[:, cb * 512:(cb + 1) * 512], in0=ps[b][:], in1=G[var][:, cb * 512:(cb + 1) * 512], op=ALU.mult),
                              reads=[d_ps[b], dw["G"]], writes=[dn["tmpo"]])
                    kb.op("dve", lambda E, s=s: E.tensor_tensor(out=xt[s][:], in0=xt[s][:], in1=tmpo[:], op=ALU.add), reads=[dn["tmpo"], dn[f"xt{s}"]], writes=[dn[f"xt{s}"]])
                    store(f"st_x{s}", x_dst[gt * 128:(gt + 1) * 128, :], xt[s][:], [dn[f"xt{s}"]])
            kb.barrier()

    import os
    FS = os.environ.get("FSTOP", "")
    for l in range(1 if FS else 2):
        last = l == 1
        x_src = x_in if l == 0 else x1
        stage_a(l, x_src)
        if FS == "a":
            break
        if FS == "ag":
            kb.barrier()
            break
        stage_b(l, last)
        if FS == "b":
            break
        stage_c(l, last, x_src, y_out if last else x1)
    kb.barrier()
    for k in sorted(out_keys):
        nc.gpsimd.wait_ge(kb.sems[k], kb.cnt[k])
    return kb


import numpy as np
import ml_dtypes
BF = ml_dtypes.bfloat16
NLT = 2048


def rope_tabs():
    n = 8192
    row = np.repeat(np.arange(n // 64), 64).astype(np.float32)
    col = np.tile(np.arange(64), n // 64).astype(np.float32)
    freqs = (10000.0 ** (-np.arange(0, 32, 2, dtype=np.float32) / 32)).astype(np.float32)
    ar = row[:, None] * freqs; ac = col[:, None] * freqs
    ang = np.concatenate([ar, ar, ac, ac], -1)
    cos = np.cos(ang).astype(np.float32); sin = np.sin(ang).astype(np.float32)
    sgn = np.tile(np.concatenate([-np.ones(16), np.ones(16)]), 2).astype(np.float32)
    return cos, sin * sgn


def col_layout(v, k):
    return np.ascontiguousarray(v.reshape(k, 128).T)


def fourier_consts(j):
    t = np.arange(64)[:, None]; kb = np.arange(64)[None, :]
    a = 2 * np.pi * ((t * kb) % 64) / 64
    cs64 = np.concatenate([np.cos(a), -np.sin(a)], 1).astype(BF)
    p = np.arange(128)[:, None, None]; kbb = np.arange(64)[None, :, None]; ka = (32 * j + np.arange(32))[None, None, :]
    a = 2 * np.pi * ((p * (64 * ka + kbb)) % 8192) / 8192
    T1 = np.concatenate([np.cos(a), -np.sin(a)], 2).astype(BF)
    T2 = np.concatenate([np.sin(a), np.cos(a)], 2).astype(BF)
    n = np.arange(128)[:, None, None]; ch = np.arange(2)[None, :, None]; k = np.arange(256)[None, None, :]
    a = 2 * np.pi * (((ch * 128 + n) * k) % 256) / 256
    dctx = np.concatenate([np.cos(a), -np.sin(a)], 2).astype(BF)
    c1 = np.arange(64)[:, None]; c2 = np.arange(64)[None, :]
    a = 2 * np.pi * ((c1 * c2) % 64) / 64
    ccs = np.stack([np.cos(a), np.sin(a)], 1).astype(np.float32)
    return dict(cs64=cs64, T1j=np.ascontiguousarray(T1), T2j=np.ascontiguousarray(T2), dctx=np.ascontiguousarray(dctx), ccs=np.ascontiguousarray(ccs))


def fused_inputs(I):
    cos, ssin = rope_tabs()
    C = np.ascontiguousarray
    shared = {
        "w_ada": C(I['w_ada']), "b_ada": C(np.stack([col_layout(I['b_ada'][l], 24) for l in range(2)])),
        "g_norm": C(np.stack([col_layout(I['g_norm'][l], 8) for l in range(2)])),
        "w_in": C(I['w_in']), "g_q": C(I['g_q']), "g_k": C(I['g_k']),
        "lamv": C(np.concatenate([I['lam_q1'], I['lam_k1'], I['lam_q2'], I['lam_k2']], 1)),
        "g_sub": C(I['g_sub'].reshape(2, 128, 1)),
        "w_f": C(I['w_f']), "b_f": C(I['b_f'].transpose(0, 2, 1)),
        "ln_g": C(I['ln_g']), "ln_b": C(I['ln_b']),
        "w_sT": C(I['w_s'].transpose(0, 3, 1, 2)),
        "bs128": C(np.broadcast_to(I['b_s'].reshape(2, 2, 2, 1, 128).transpose(0, 2, 3, 1, 4), (2, 2, 64, 2, 128)).reshape(2, 128, 2, 128)),
        "w_br": C(np.concatenate([I['w_br_a'], I['w_br_b'], I['w_br_c']], 1)), "w_out": C(I['w_out']),
    }
    maps = []
    for core in range(8):
        b, j = core // 4, core % 4
        m = dict(shared)
        m["x_tok"] = C(np.concatenate([I['x'][b, j * NLT:(j + 1) * NLT], I['ctx'][b]], 0))
        m["cvec"] = C(np.stack([col_layout(I['c'][b], 8), col_layout(I['c_ctx'], 8)], -1))
        m["cos"] = C(cos[j * NLT:(j + 1) * NLT]); m["ssin"] = C(ssin[j * NLT:(j + 1) * NLT])
        m.update(fourier_consts(j))
        maps.append(m)
    return maps


def kernel(**inputs):
    I = {k: np.asarray(v, dtype=np.float32) for k, v in inputs.items()}
    kb = KB()
    build_fused(kb)
    res = kb.run(fused_inputs(I))
    out = np.empty((2, 8192, 1024), np.float32)
    for core in range(8):
        b, j = core // 4, core % 4
        out[b, j * NLT:(j + 1) * NLT] = np.asarray(res.results[core]["y"])
    return out
```
